# Optimizing a Trainium2 kernel written in Bass

```python
import math
import jax, jax.numpy as jnp
from jax import lax
import numpy as np

D_MODEL = 1024
BATCH = 8
SEQ = 4096
DEPTH = 4

GRID_W = 64
CTX_LEN = 256
BR_W = 512
D_A = BR_W
N_A = 64
H_A = D_A // N_A
R_DECAY = 64
R_ICLR = 64
GN_EPS = 64e-5
D_B = BR_W
CONV_K = 3
H_C = 4
DH_C = 64
DV_C = 2 * DH_C
D_QC = H_C * 2 * DH_C
D_C = H_C * DV_C
Q_BLOCK = 128
ROPE_BASE = 10000.0
N_BRANCH = 3
RMS_EPS = 1e-6
IN_WIDTHS = (D_A, D_A, D_A, R_DECAY, R_DECAY, R_ICLR, R_ICLR, D_A,
             D_B, D_B, D_B, D_B,
             D_QC, D_QC, D_C, D_C,
             N_BRANCH * D_MODEL)
N_IN = 4 * D_A + 2 * R_DECAY + 2 * R_ICLR + 4 * D_B + 2 * D_QC + 2 * D_C + N_BRANCH * D_MODEL

kernel_name = "hybrid_rwkv7_shortconv_diffattn_prefix_dit"


def rms_norm(x, g):
    xf = x.astype(jnp.float32)
    y = xf * lax.rsqrt(jnp.mean(xf * xf, axis=-1, keepdims=True) + RMS_EPS)
    return (y * g.astype(jnp.float32)).astype(x.dtype)


def heads(t, n):
    return t.reshape(t.shape[:-1] + (n, t.shape[-1] // n))


def split_columns(p):
    offsets = [int(o) for o in np.cumsum(IN_WIDTHS)[:-1]]
    return jnp.split(p, offsets, axis=-1)


def adaln_modulate(x, g, cond, w_mod, b_mod):
    mod = jax.nn.silu(cond) @ w_mod + b_mod
    shift, scale, gate = jnp.split(mod, 3, axis=-1)
    return rms_norm(x, g) * (1 + scale) + shift, gate


def axial_rope_angles(n_tokens):
    rows = n_tokens // GRID_W
    row = jnp.repeat(jnp.arange(rows), GRID_W).astype(jnp.float32)
    col = jnp.tile(jnp.arange(GRID_W), rows).astype(jnp.float32)
    n_freq = DH_C // 4
    inv_freq = ROPE_BASE ** (-jnp.arange(n_freq, dtype=jnp.float32) / n_freq)
    return jnp.concatenate([row[:, None] * inv_freq, col[:, None] * inv_freq], axis=-1)


def apply_rope(x, ang):
    n_tok = ang.shape[0]
    cos = jnp.cos(ang).reshape(n_tok, 2, DH_C // 4)[None, :, None, None].astype(x.dtype)
    sin = jnp.sin(ang).reshape(n_tok, 2, DH_C // 4)[None, :, None, None].astype(x.dtype)
    xr = x.reshape(x.shape[:-1] + (2, 2, DH_C // 4))
    x1, x2 = xr[..., 0, :], xr[..., 1, :]
    out = jnp.stack([x1 * cos - x2 * sin, x2 * cos + x1 * sin], axis=-2)
    return out.reshape(x.shape)


def rwkv_streams(r, k, v, lw, la, decay_w0, decay_up, iclr_a0, iclr_up, k_k, k_a):
    kk = heads(k * k_k, H_A).astype(jnp.float32)
    kk = kk * lax.rsqrt(jnp.sum(kk * kk, axis=-1, keepdims=True) + 1e-12)
    dirs = []
    for d in range(2):
        w_raw = (decay_w0[d] + jnp.tanh(lw[d]) @ decay_up[d]).astype(jnp.float32)
        log_w = -jnp.exp(-jax.nn.softplus(-w_raw) - 0.5)
        a = jax.nn.sigmoid(iclr_a0[d] + la[d] @ iclr_up[d])
        k_dir = k * (1 + (a - 1) * k_a)
        dirs.append((heads(jnp.exp(log_w), H_A), heads(k_dir, H_A), heads(a, H_A) * kk))
    return heads(r, H_A), heads(k, H_A), heads(v, H_A), kk, dirs


def rwkv7_scan(r, v, kk, decay, k_dir, akk, state0, reverse):
    xs = tuple(jnp.swapaxes(t.astype(jnp.float32), 0, 1) for t in (r, v, kk, decay, k_dir, akk))

    def step(s, inp):
        r_t, v_t, kk_t, w_t, k_t, akk_t = inp
        s_kk = jnp.einsum('bhvk,bhk->bhv', s, kk_t)
        s = s * w_t[:, :, None, :] - s_kk[..., None] * akk_t[:, :, None, :] + v_t[..., None] * k_t[:, :, None, :]
        return s, jnp.einsum('bhvk,bhk->bhv', s, r_t)

    s_fin, ys = lax.scan(step, state0, xs, reverse=reverse)
    return jnp.swapaxes(ys, 0, 1).astype(r.dtype), s_fin


def rwkv_output(y, r, k, v, r_k, gn_g, gn_b):
    yf = y.astype(jnp.float32)
    mu = jnp.mean(yf, axis=-1, keepdims=True)
    var = jnp.mean(jnp.square(yf - mu), axis=-1, keepdims=True)
    yn = ((yf - mu) * lax.rsqrt(var + GN_EPS)).astype(y.dtype)
    bonus = jnp.sum(r * k * r_k, axis=-1, keepdims=True) * v
    flat = lambda t: t.reshape(t.shape[:-2] + (D_A,))
    return flat(yn) * gn_g + gn_b + flat(bonus)


def short_conv(u, conv_w, conv_b):
    n_tok = u.shape[1]
    up = jnp.pad(u, ((0, 0), (1, 1), (0, 0)))
    return up[:, :n_tok] * conv_w[0] + up[:, 1:n_tok + 1] * conv_w[1] + up[:, 2:] * conv_w[2] + conv_b


def qk_heads(t):
    return t.reshape(t.shape[:-1] + (H_C, 2, DH_C))


def diff_attend(q, k, v, lam):
    s = jnp.einsum('bqhcd,bkhcd->bhcqk', q, k).astype(jnp.float32) * (DH_C ** -0.5)
    p = jax.nn.softmax(s, axis=-1)
    attn = p[:, :, 0] - lam * p[:, :, 1]
    return jnp.einsum('bhqk,bkhv->bqhv', attn.astype(v.dtype), v)


def diff_out(o, subln_g, lam_init):
    o = rms_norm(o, subln_g) * (1 - lam_init)
    return o.reshape(o.shape[:-2] + (D_C,))


def merge_branches(y_a, y_b, y_c, z_a, z_b, z_c, g_merge, w_branch, w_out):
    y = jnp.stack([y_a * jax.nn.silu(z_a), y_b * jax.nn.silu(z_b), y_c * jax.nn.silu(z_c)], axis=-2)
    br = jnp.einsum('btiw,iwd->btid', y, w_branch)
    gates = jax.nn.sigmoid(g_merge.reshape(g_merge.shape[:-1] + (N_BRANCH, D_MODEL)))
    return jnp.sum(gates * br, axis=-2) @ w_out


def hybrid_layer(x, ctx, c, c_ctx, lam_init, update_ctx, w_mod, b_mod, norm_g, w_in,
                 decay_w0, decay_up, iclr_a0, iclr_up, k_k, k_a, r_k, gn_g, gn_b,
                 conv_w, conv_b, qk_norm_g, lambda_qk, subln_g, w_branch, w_out):
    bsz, n_tok, _ = x.shape
    h, gate = adaln_modulate(x, norm_g, c[:, None, :], w_mod, b_mod)
    hc, gate_c = adaln_modulate(ctx, norm_g, c_ctx[None, None, :], w_mod, b_mod)
    (r, k, v, lwf, lwb, laf, lab, z_a, gb, gc, u, z_b, qd, kd, vd, z_c, g_merge) = split_columns(h @ w_in)
    (r_c, k_c, v_c, lwf_c, lwb_c, laf_c, lab_c, z_a_c, gb_c, gc_c, u_c, z_b_c,
     qd_c, kd_c, vd_c, z_c_c, g_merge_c) = split_columns(hc @ w_in)

    rh, kh, vh, kk, dirs = rwkv_streams(r, k, v, (lwf, lwb), (laf, lab), decay_w0, decay_up, iclr_a0, iclr_up, k_k, k_a)
    rh_c, kh_c, vh_c, kk_c, dirs_c = rwkv_streams(r_c, k_c, v_c, (lwf_c, lwb_c), (laf_c, lab_c),
                                                  decay_w0, decay_up, iclr_a0, iclr_up, k_k, k_a)
    s0 = jnp.zeros((bsz, H_A, N_A, N_A), jnp.float32)
    yc_f, sc_f = rwkv7_scan(rh_c, vh_c, kk_c, *dirs_c[0], s0, False)
    yc_b, sc_b = rwkv7_scan(rh_c, vh_c, kk_c, *dirs_c[1], s0, True)
    y_f, _ = rwkv7_scan(rh, vh, kk, *dirs[0], sc_f, False)
    y_b, _ = rwkv7_scan(rh, vh, kk, *dirs[1], sc_b, True)
    y_a = rwkv_output(y_f + y_b, rh, kh, vh, r_k, gn_g, gn_b)

    y_bc = gb * short_conv(gc * u, conv_w, conv_b)

    lam = (jnp.exp(jnp.sum(lambda_qk[0] * lambda_qk[1]).astype(jnp.float32))
           - jnp.exp(jnp.sum(lambda_qk[2] * lambda_qk[3]).astype(jnp.float32)) + lam_init)
    ang = axial_rope_angles(n_tok)
    q = apply_rope(rms_norm(qk_heads(qd), qk_norm_g[0]), ang)
    kl = apply_rope(rms_norm(qk_heads(kd), qk_norm_g[1]), ang)
    kc_att = rms_norm(qk_heads(kd_c), qk_norm_g[1])
    vc_att = heads(vd_c, H_C)
    k_all = jnp.concatenate([kl, kc_att], axis=1)
    v_all = jnp.concatenate([heads(vd, H_C), vc_att], axis=1)
    n_blk = n_tok // Q_BLOCK
    qb = jnp.swapaxes(q.reshape(bsz, n_blk, Q_BLOCK, H_C, 2, DH_C), 0, 1)
    ob = lax.map(lambda qi: diff_attend(qi, k_all, v_all, lam), qb)
    o = jnp.swapaxes(ob, 0, 1).reshape(bsz, n_tok, H_C, DV_C)
    y_c = diff_out(o, subln_g, lam_init)

    x = x + gate * merge_branches(y_a, y_bc, y_c, z_a, z_b, z_c, g_merge, w_branch, w_out)

    if update_ctx:
        y_a_c = rwkv_output(yc_f + yc_b, rh_c, kh_c, vh_c, r_k, gn_g, gn_b)
        y_b_c = gb_c * short_conv(gc_c * u_c, conv_w, conv_b)
        q_c = rms_norm(qk_heads(qd_c), qk_norm_g[0])
        y_c_c = diff_out(diff_attend(q_c, kc_att, vc_att, lam), subln_g, lam_init)
        ctx = ctx + gate_c * merge_branches(y_a_c, y_b_c, y_c_c, z_a_c, z_b_c, z_c_c, g_merge_c, w_branch, w_out)
    return x, ctx


def setup_inputs(seed: int = 0) -> dict:
    key = jax.random.key(seed)
    ks = jax.random.split(key, 26)
    L, D = DEPTH, D_MODEL

    def nrm(k, shape, scale):
        return jax.random.normal(k, shape, jnp.float32) * scale

    return {
        "x": nrm(ks[0], (BATCH, SEQ, D), 1.0),
        "c": nrm(ks[1], (BATCH, D), 1.0),
        "ctx": nrm(ks[2], (BATCH, CTX_LEN, D), 1.0),
        "c_ctx": nrm(ks[3], (D,), 1.0),
        "w_mod": nrm(ks[4], (L, D, 3 * D), 0.5 * D ** -0.5),
        "b_mod": nrm(ks[5], (L, 3 * D), 0.02),
        "norm_g": 1.0 + nrm(ks[6], (L, D), 0.02),
        "w_in": nrm(ks[7], (L, D, N_IN), D ** -0.5),
        "decay_w0": jax.random.uniform(ks[8], (L, 2, D_A), jnp.float32, -5.0, 1.0),
        "decay_up": nrm(ks[9], (L, 2, R_DECAY, D_A), 0.1 * R_DECAY ** -0.5),
        "iclr_a0": nrm(ks[10], (L, 2, D_A), 0.5),
        "iclr_up": nrm(ks[11], (L, 2, R_ICLR, D_A), 0.1 * R_ICLR ** -0.5),
        "k_k": 0.85 + nrm(ks[12], (L, D_A), 0.05),
        "k_a": 1.0 + nrm(ks[13], (L, D_A), 0.05),
        "r_k": nrm(ks[14], (L, H_A, N_A), 0.1),
        "gn_g": 1.0 + nrm(ks[15], (L, D_A), 0.02),
        "gn_b": nrm(ks[16], (L, D_A), 0.02),
        "conv_w": nrm(ks[17], (L, CONV_K, D_B), CONV_K ** -0.5),
        "conv_b": nrm(ks[18], (L, D_B), 0.02),
        "qk_norm_g": 1.0 + nrm(ks[19], (L, 2, DH_C), 0.02),
        "lambda_qk": nrm(ks[20], (L, 4, DH_C), 0.1),
        "subln_g": 1.0 + nrm(ks[21], (L, DV_C), 0.02),
        "w_branch": nrm(ks[22], (L, N_BRANCH, BR_W, D), BR_W ** -0.5),
        "w_out": nrm(ks[23], (L, D, D), D ** -0.5),
    }


def reference(x, c, ctx, c_ctx, w_mod, b_mod, norm_g, w_in, decay_w0, decay_up, iclr_a0, iclr_up,
              k_k, k_a, r_k, gn_g, gn_b, conv_w, conv_b, qk_norm_g, lambda_qk, subln_g, w_branch, w_out):
    for l in range(DEPTH):
        lam_init = 0.8 - 0.6 * math.exp(-0.3 * l)
        x, ctx = hybrid_layer(x, ctx, c, c_ctx, lam_init, l < DEPTH - 1,
                              w_mod[l], b_mod[l], norm_g[l], w_in[l],
                              decay_w0[l], decay_up[l], iclr_a0[l], iclr_up[l], k_k[l], k_a[l], r_k[l],
                              gn_g[l], gn_b[l], conv_w[l], conv_b[l], qk_norm_g[l], lambda_qk[l],
                              subln_g[l], w_branch[l], w_out[l])
    return x
```

```python
import math
import threading
import numpy as np
from contextlib import ExitStack
import concourse.bass as bass
import concourse.mybir as mybir
from concourse.bass_utils import run_bass_kernel_spmd

F32 = mybir.dt.float32
BF16 = mybir.dt.bfloat16
AF = mybir.ActivationFunctionType
ALU = mybir.AluOpType
AX = mybir.AxisListType

D = 1024
CTX = 256
NIN = 9472
NPL = 79
C0 = math.exp(-0.5)
GN_EPS = 64e-5
RMS_EPS = 1e-6


class Buf:
    __slots__ = ("name", "w", "r")

    def __init__(self, name):
        self.name = name
        self.w = None
        self.r = []


class Coop:
    def __init__(self):
        self.cur = None

    def run(self, specs):
        n = len(specs)
        self.evs = [threading.Event() for _ in range(n)]
        self.alive = [True] * n
        self.quota = [q for _, q in specs]
        self.used = [0] * n
        self.done_ev = threading.Event()
        self.err = None

        def wrap(i, fn):
            self.evs[i].wait()
            self.evs[i].clear()
            self.cur = i
            try:
                fn()
            except BaseException as e:
                self.err = e
            self.alive[i] = False
            self._pass(i)
        ths = [threading.Thread(target=wrap, args=(i, f)) for i, (f, _) in enumerate(specs)]
        for t in ths:
            t.start()
        self.evs[0].set()
        self.done_ev.wait()
        for t in ths:
            t.join()
        self.cur = None
        if self.err is not None:
            raise self.err

    def _pass(self, i):
        n = len(self.alive)
        for k in range(1, n + 1):
            j = (i + k) % n
            if self.alive[j]:
                if j == i:
                    return True
                self.evs[j].set()
                return False
        self.done_ev.set()
        return False

    def switch(self):
        i = self.cur
        if i is None:
            return
        self.used[i] += 1
        if self.used[i] < self.quota[i]:
            return
        self.used[i] = 0
        if not self._pass(i):
            self.evs[i].wait()
            self.evs[i].clear()
            self.cur = i


class KB:
    def __init__(self, nc, stack):
        self.nc = nc
        self.stack = stack
        self.eng = {"pe": nc.tensor, "dve": nc.vector, "act": nc.scalar, "pool": nc.gpsimd, "sp": nc.sync}
        self.sems = {}
        self.cnt = {}
        self.seen = {e: {} for e in self.eng}
        for e in self.eng:
            self.sems[e] = stack.enter_context(nc.semaphore("s_" + e))
            self.cnt[e] = 0
        self.n_inst = 0
        self.dma_names = {}
        self.dma_pool = []
        self.coop = Coop()

    def dma_sem(self, name):
        if name not in self.dma_names:
            i = len(self.dma_names)
            if i >= len(self.dma_pool):
                key = "d_%d" % i
                self.sems[key] = self.stack.enter_context(self.nc.semaphore(key))
                self.cnt[key] = 0
                self.dma_pool.append(key)
            self.dma_names[name] = self.dma_pool[i]
        return self.dma_names[name]

    def phase_reset(self):
        self.dma_names = {}

    def _wait(self, e, ev):
        if ev is None:
            return
        key, val = ev
        if key == "pe" and e == "pe":
            return
        if self.seen[e].get(key, 0) >= val:
            return
        self.eng[e].wait_ge(self.sems[key], val)
        self.seen[e][key] = val
        self.n_inst += 1

    def deps(self, e, reads, writes):
        for b in reads:
            self._wait(e, b.w)
        for b in writes:
            self._wait(e, b.w)
            for ev in b.r:
                self._wait(e, ev)

    def done(self, ev, reads, writes):
        for b in reads:
            b.r.append(ev)
            if len(b.r) > 16:
                d = {}
                for k, v in b.r:
                    d[k] = max(d.get(k, 0), v)
                b.r = list(d.items())
        for b in writes:
            b.w = ev
            b.r = []

    def op(self, e, fn, reads=(), writes=()):
        self.deps(e, reads, writes)
        ins = fn()
        self.cnt[e] += 1
        ins.then_inc(self.sems[e], 1)
        ev = (e, self.cnt[e])
        self.done(ev, reads, writes)
        self.n_inst += 1
        self.coop.switch()
        return ev

    def pe(self, fns, reads=(), writes=()):
        self.deps("pe", reads, writes)
        ins = None
        for fn in fns:
            ins = fn()
            self.n_inst += 1
        self.cnt["pe"] += 1
        ins.then_inc(self.sems["pe"], 1)
        ev = ("pe", self.cnt["pe"])
        self.done(ev, reads, writes)
        self.coop.switch()
        return ev

    def dma(self, q, semname, out, in_, reads=(), writes=()):
        key = self.dma_sem(semname)
        self.deps(q, reads, writes)
        ins = self.eng[q].dma_start(out=out, in_=in_)
        self.cnt[key] += 16
        ins.then_inc(self.sems[key], 16)
        ev = (key, self.cnt[key])
        self.done(ev, reads, writes)
        self.n_inst += 1
        self.coop.switch()
        return ev

    def dma_group(self, q, semname, pairs, reads=(), writes=()):
        key = self.dma_sem(semname)
        self.deps(q, reads, writes)
        for out, in_ in pairs:
            ins = self.eng[q].dma_start(out=out, in_=in_)
            self.cnt[key] += 16
            ins.then_inc(self.sems[key], 16)
            self.n_inst += 1
        ev = (key, self.cnt[key])
        self.done(ev, reads, writes)
        self.coop.switch()
        return ev

    def barrier(self):
        for e in self.eng:
            for k in self.cnt:
                if self.cnt[k] > 0:
                    self._wait(e, (k, self.cnt[k]))
        self.phase_reset()


class T:
    uid = [0]

    def __init__(self, st, nc, name, shape, dt, nbuf=1):
        T.uid[0] += 1
        self.t = st.enter_context(nc.sbuf_tensor("%s_u%d" % (name, T.uid[0]), shape, dt))
        self.b = Buf(name)

    def __getitem__(self, k):
        return self.t[k]


SEGS = [(0, 512, "F"), (512, 512, "F"), (1024, 512, "T"), (1536, 256, "X"), (1792, 512, "T"),
        (2304, 512, "F"), (2816, 512, "F"), (3328, 512, "F"), (3840, 512, "F"),
        (4352, 512, "F"), (4864, 512, "F"), (5376, 512, "T"), (5888, 512, "F")] + \
       [(6400 + 512 * i, 512, "F") for i in range(6)]
TCOL = {1024: 0, 1792: 512, 5376: 1024}


def build(SEQ, DEPTH, dbg=None):
    TT = CTX + SEQ
    NCH = TT // 64
    blocks = [(0, CTX, 1)] + [(CTX + 512 * i, 512, 0) for i in range(SEQ // 512)]
    nc = bass.Bass("TRN2", target_bir_lowering=False)
    dram = lambda n, s, d, k="Internal": nc.dram_tensor(n, s, d, kind=k).ap()
    xT_in = dram("xT", [D, TT], F32, "ExternalInput")
    PP = dram("PP", [128, DEPTH * NPL], F32, "ExternalInput")
    CCd = dram("CC", [128, 16], F32, "ExternalInput")
    BCd = dram("BC", [128, DEPTH * 2 * 512], F32, "ExternalInput")
    LAMd = dram("LAM", [1, DEPTH * 256], F32, "ExternalInput")
    KON = dram("KON", [128, 1024], F32, "ExternalInput")
    MSK = dram("MSK", [128, 4 * 512], F32, "ExternalInput")
    ROPE = dram("ROPE", [128, 2 * TT], F32, "ExternalInput")
    LTRI = dram("LTRI", [128, 512], F32, "ExternalInput")
    w_mod = dram("w_mod", [DEPTH, D, 3 * D], F32, "ExternalInput")
    w_in = dram("w_in", [DEPTH, D, NIN], F32, "ExternalInput")
    dup = dram("dup", [DEPTH, 128, 512], F32, "ExternalInput")
    iup = dram("iup", [DEPTH, 128, 512], F32, "ExternalInput")
    w_br = dram("w_br", [DEPTH, 1536, D], F32, "ExternalInput")
    w_out = dram("w_out", [DEPTH, D, D], F32, "ExternalInput")
    yout = dram("yout", [D, SEQ], F32, "ExternalOutput")
    xT = dram("xTs", [D, TT], F32)
    PF = dram("PF", [NIN, TT], BF16)
    PX = dram("PX", [256, TT], F32)
    PT = dram("PT", [TT, 1536], BF16)
    QGs = [dram("QGs%d" % d, [512, NCH, 2, 64], BF16) for d in range(2)]
    AKT = [dram("AKT%d" % d, [512, TT], BF16) for d in range(2)]
    KIT = [dram("KIT%d" % d, [512, TT], BF16) for d in range(2)]
    AKK = [dram("AKK%d" % d, [TT, 512], BF16) for d in range(2)]
    KIK = [dram("KIK%d" % d, [TT, 512], BF16) for d in range(2)]
    GCs = [dram("GCs%d" % d, [512, NCH], F32) for d in range(2)]
    YS = [dram("YS%d" % d, [TT, 512], F32) for d in range(2)]
    YG = dram("YG", [1536, TT], BF16)
    QRs = dram("QRs", [512, TT], BF16)
    KRs = dram("KRs", [512, TT], BF16)
    dbgo = {}
    if dbg:
        for n, shp in dbg.items():
            dbgo[n] = dram("dbg_" + n, shp, F32, "ExternalOutput")

    with ExitStack() as st:
        kb = KB(nc, st)
        V, A, P, PEe = nc.vector, nc.scalar, nc.gpsimd, nc.tensor
        psum = st.enter_context(nc.psum_tensor("psum", [128, 8, 512], F32))
        PB = [Buf("ps%d" % i) for i in range(8)]
        psbf = [psum[:, i, :].bitcast(BF16) for i in range(8)]

        kon = T(st, nc, "kon", [128, 1024], F32)
        konb = T(st, nc, "konb", [128, 1024], BF16)
        msk2 = T(st, nc, "msk2", [128, 4, 8, 64], F32)
        pp = T(st, nc, "pp", [128, DEPTH, NPL], F32)
        cc = T(st, nc, "cc", [128, 8, 2], F32)
        ltri = T(st, nc, "ltri", [128, 512], F32)
        modT = T(st, nc, "modT", [128, DEPTH, 24, 2], F32)
        gsc = T(st, nc, "gsc", [128, DEPTH, 8, 2], F32)
        omka = T(st, nc, "omka", [128, DEPTH, 4], F32)
        lamc = T(st, nc, "lamc", [128, DEPTH, 2], F32)
        sgl = T(st, nc, "sgl", [128, DEPTH], F32)
        kb.dma("sp", "c0", kon[:], KON[:, :], writes=[kon.b])
        kb.dma("sp", "c1", msk2[:].rearrange("p a h x -> p (a h x)"), MSK[:, :], writes=[msk2.b])
        kb.dma("sp", "c2", pp[:].rearrange("p l n -> p (l n)"), PP[:, :], writes=[pp.b])
        kb.dma("sp", "c3", cc[:].rearrange("p a b -> p (a b)"), CCd[:, :], writes=[cc.b])
        kb.dma("sp", "c4", ltri[:], LTRI[:, :], writes=[ltri.b])
        kb.op("dve", lambda: V.tensor_copy(out=konb[:], in_=kon[:]), [kon.b], [konb.b])
        ident_f, ones_f = kon[:, 0:128], kon[:, 128:256]
        ident_b, ones_b, blk_b, perm_f = konb[:, 0:128], konb[:, 128:256], konb[:, 256:384], kon[:, 384:512]
        sel_b = konb[:, 512:544]

        with ExitStack() as s2:
            sc = T(s2, nc, "sc", [128, 8, 2], F32)
            wm = [T(s2, nc, "wm%d" % i, [128, 8, 512], F32) for i in range(2)]
            lamt = T(s2, nc, "lamt", [1, DEPTH, 4, 64], F32)
            lam2 = T(s2, nc, "lam2", [1, DEPTH, 2], F32)
            lam3 = T(s2, nc, "lam3", [1, DEPTH, 2], F32)
            kb.op("act", lambda: A.activation(out=sc[:], in_=cc[:], func=AF.Silu), [cc.b], [sc.b])
            i = 0
            for l in range(DEPTH):
                for nb6 in range(6):
                    w = wm[i % 2]
                    i += 1
                    kb.dma("sp", w.b.name, w[:], w_mod[l].rearrange("(c p) n -> p c n", p=128)[:, :, nb6 * 512:(nb6 + 1) * 512], writes=[w.b])
                    fns = []
                    for n4 in range(4):
                        for c in range(8):
                            fns.append(lambda n4=n4, c=c, w=w: PEe.matmul(psum[:, 0, n4 * 2:n4 * 2 + 2], lhsT=w[:, c, n4 * 128:(n4 + 1) * 128], rhs=sc[:, c, :], start=(c == 0), stop=(c == 7)))
                    kb.pe(fns, [w.b, sc.b], [PB[0]])
                    kb.op("dve", lambda l=l, nb6=nb6: V.tensor_tensor(out=modT[:, l, nb6 * 4:(nb6 + 1) * 4, :], in0=psum[:, 0, 0:8].rearrange("p (a b) -> p a b", b=2),
                                                                   in1=pp[:, l, 8 + nb6 * 4:8 + (nb6 + 1) * 4].unsqueeze(2).to_broadcast([128, 4, 2]), op=ALU.add), [PB[0], pp.b], [modT.b])
                kb.op("dve", lambda l=l: V.tensor_scalar(out=gsc[:, l], in0=modT[:, l, 8:16, :], scalar1=1.0, scalar2=None, op0=ALU.add), [modT.b], [gsc.b])
                kb.op("dve", lambda l=l: V.tensor_tensor(out=gsc[:, l], in0=gsc[:, l], in1=pp[:, l, 0:8].unsqueeze(2).to_broadcast([128, 8, 2]), op=ALU.mult), [gsc.b, pp.b], [gsc.b])
                kb.op("dve", lambda l=l: V.tensor_scalar(out=omka[:, l, :], in0=pp[:, l, 52:56], scalar1=-1.0, scalar2=1.0, op0=ALU.mult, op1=ALU.add), [pp.b], [omka.b])
            kb.dma("sp", "c5", lamt[:].rearrange("p l a x -> p (l a x)"), LAMd[:, :], writes=[lamt.b])
            lv = lamt[:].rearrange("p l (a b) x -> p l a b x", b=2)
            kb.op("dve", lambda: V.tensor_tensor(out=lv[:, :, :, 0, :], in0=lv[:, :, :, 0, :], in1=lv[:, :, :, 1, :], op=ALU.mult), [lamt.b], [lamt.b])
            kb.op("dve", lambda: V.tensor_reduce(out=lam2[:], in_=lv[:, :, :, 0, :], axis=AX.X, op=ALU.add), [lamt.b], [lam2.b])
            kb.op("act", lambda: A.activation(out=lam3[:], in_=lam2[:], func=AF.Exp), [lam2.b], [lam3.b])
            kb.op("dve", lambda: V.tensor_tensor(out=lam2[:, :, 0], in0=lam3[:, :, 1], in1=lam3[:, :, 0], op=ALU.subtract), [lam3.b], [lam2.b])
            for l in range(DEPTH):
                li = 0.8 - 0.6 * math.exp(-0.3 * l)
                kb.op("dve", lambda l=l, li=li: V.tensor_scalar(out=lam2[:, l, 0:1], in0=lam2[:, l, 0:1], scalar1=-li, scalar2=None, op0=ALU.add), [lam2.b], [lam2.b])
                kb.op("dve", lambda l=l, li=li: V.tensor_scalar(out=sgl[:, l:l + 1], in0=pp[:, l, 78:79], scalar1=(1.0 - li), scalar2=None, op0=ALU.mult), [pp.b], [sgl.b])
            kb.pe([lambda: PEe.matmul(psum[:, 0, 0:2 * DEPTH], lhsT=ones_f[0:1, :], rhs=lam2[:].rearrange("p l a -> p (l a)"), start=True, stop=True)], [lam2.b, kon.b], [PB[0]])
            kb.op("dve", lambda: V.tensor_copy(out=lamc[:].rearrange("p l a -> p (l a)"), in_=psum[:, 0, 0:2 * DEPTH]), [PB[0]], [lamc.b])
            kb.barrier()

        with ExitStack() as s2:
            xb_ = [T(s2, nc, "xcp%d" % i, [128, 8, 512], F32) for i in range(2)]
            for bi, (t0, nb, col) in enumerate(blocks):
                x_ = xb_[bi % 2]
                kb.dma("sp", x_.b.name, x_[:, :, 0:nb], xT_in.rearrange("(c p) t -> p c t", p=128)[:, :, t0:t0 + nb], writes=[x_.b])
                kb.dma("sp", x_.b.name, xT.rearrange("(c p) t -> p c t", p=128)[:, :, t0:t0 + nb], x_[:, :, 0:nb], reads=[x_.b])
            kb.barrier()

        xTv = xT.rearrange("(c p) t -> p c t", p=128)
        NTT = TT // 128

        def blk_of(tt):
            t = tt * 128
            for bi, (t0, nb, col) in enumerate(blocks):
                if t0 <= t < t0 + nb:
                    return bi

        evac_rr = [0]

        def evac(out, in_, rb, wb, scale=None):
            evac_rr[0] += 1
            if evac_rr[0] % 2 == 0:
                if scale is None:
                    return kb.op("act", lambda: A.copy(out=out, in_=in_), rb, wb)
                return kb.op("act", lambda: A.mul(out=out, in_=in_, mul=scale), rb, wb)
            if scale is None:
                return kb.op("dve", lambda: V.tensor_copy(out=out, in_=in_), rb, wb)
            return kb.op("dve", lambda: V.tensor_scalar(out=out, in0=in_, scalar1=scale, scalar2=None, op0=ALU.mult), rb, wb)

        def run_threads(specs):
            th = [[g, 0, float(tot)] for g, tot in specs]
            while th:
                t_ = min(th, key=lambda x: x[1] / x[2])
                try:
                    next(t_[0])
                    t_[1] += 1
                except StopIteration:
                    th.remove(t_)

        def phase_AB(l):
            with ExitStack() as s2:
                hT = T(s2, nc, "hT", [128, 8, TT], BF16)
                hTb = [Buf("hT%d" % i) for i in range(len(blocks))]
                xa = T(s2, nc, "xa", [128, 8, 512], F32)
                sq = T(s2, nc, "sq", [128, 8, 512], F32)
                rs = T(s2, nc, "rs", [128, 512], F32)
                tmp = [T(s2, nc, "tmpa%d" % i, [128, 512], F32) for i in range(2)]
                wf = [T(s2, nc, "wf%d" % i, [128, 8, 512], F32) for i in range(2)]
                wb = [T(s2, nc, "wb%d" % i, [128, 8, 512], BF16) for i in range(2)]
                ost = [T(s2, nc, "ost%d" % i, [128, 512], BF16) for i in range(4)]
                osx = [T(s2, nc, "osx%d" % i, [128, 512], F32) for i in range(2)]
                w_inv = w_in[l].rearrange("(c p) n -> p c n", p=128)

                def loadw(si):
                    c0, wd, kind = SEGS[si]
                    f, b = wf[si % 2], wb[si % 2]
                    kb.dma("sp", f.b.name, f[:, :, 0:wd], w_inv[:, :, c0:c0 + wd], writes=[f.b])
                    kb.op("pool", lambda: P.tensor_copy(out=b[:, :, 0:wd], in_=f[:, :, 0:wd]), [f.b], [b.b])

                loadw(0)
                for bi, (t0, nb, col) in enumerate(blocks):
                    kb.dma("sp", "xa", xa[:, :, 0:nb], xTv[:, :, t0:t0 + nb], writes=[xa.b])
                    kb.op("act", lambda: A.activation(out=sq[:, :, 0:nb], in_=xa[:, :, 0:nb], func=AF.Square), [xa.b], [sq.b])
                    kb.pe([lambda c=c: PEe.matmul(psum[:, 0, 0:nb], lhsT=ones_f, rhs=sq[:, c, 0:nb], start=(c == 0), stop=(c == 7)) for c in range(8)],
                          [sq.b, kon.b], [PB[0]])
                    kb.op("act", lambda: A.activation(out=rs[:, 0:nb], in_=psum[:, 0, 0:nb], func=AF.Sqrt, bias=RMS_EPS, scale=1.0 / D), [PB[0]], [rs.b])
                    kb.op("dve", lambda: V.reciprocal(out=rs[:, 0:nb], in_=rs[:, 0:nb]), [rs.b], [rs.b])
                    for c in range(8):
                        tm = tmp[c % 2]
                        kb.op("dve", lambda: V.scalar_tensor_tensor(out=tm[:, 0:nb], in0=xa[:, c, 0:nb], scalar=gsc[:, l, c, col:col + 1], in1=rs[:, 0:nb],
                                                                    op0=ALU.mult, op1=ALU.mult), [xa.b, rs.b, gsc.b], [tm.b])
                        kb.op("act", lambda: A.activation(out=hT[:, c, t0:t0 + nb], in_=tm[:, 0:nb], func=AF.Identity, bias=modT[:, l, c, col:col + 1], scale=1.0),
                              [tm.b, modT.b], [hTb[bi]])
                k = 0
                pk = 0
                for si, (c0, wd, kind) in enumerate(SEGS):
                    if si + 1 < len(SEGS):
                        loadw(si + 1)
                    b = wb[si % 2]
                    if kind in "FX":
                        for bi, (t0, nb, col) in enumerate(blocks):
                            for n4 in range(wd // 128):
                                bank = 1 + pk % 6
                                pk += 1
                                kb.pe([lambda c=c: PEe.matmul(psum[:, bank, 0:nb], lhsT=b[:, c, n4 * 128:(n4 + 1) * 128], rhs=hT[:, c, t0:t0 + nb], start=(c == 0), stop=(c == 7))
                                       for c in range(8)], [b.b, hTb[bi]], [PB[bank]])
                                if kind == "F":
                                    o = ost[k % 4]
                                    k += 1
                                    evac(o[:, 0:nb], psum[:, bank, 0:nb], [PB[bank]], [o.b])
                                    kb.dma("sp", o.b.name, PF[c0 + n4 * 128:c0 + (n4 + 1) * 128, t0:t0 + nb], o[:, 0:nb], reads=[o.b])
                                else:
                                    o = osx[k % 2]
                                    k += 1
                                    evac(o[:, 0:nb], psum[:, bank, 0:nb], [PB[bank]], [o.b])
                                    kb.dma("sp", o.b.name, PX[n4 * 128:(n4 + 1) * 128, t0:t0 + nb], o[:, 0:nb], reads=[o.b])
                    else:
                        for tt in range(NTT):
                            bank = 1 + pk % 6
                            pk += 1
                            kb.pe([lambda c=c: PEe.matmul(psum[:, bank, 0:512], lhsT=hT[:, c, tt * 128:(tt + 1) * 128], rhs=b[:, c, 0:512], start=(c == 0), stop=(c == 7))
                                   for c in range(8)], [b.b, hTb[blk_of(tt)]], [PB[bank]])
                            o = ost[k % 4]
                            k += 1
                            evac(o[:, :], psum[:, bank, :], [PB[bank]], [o.b])
                            kb.dma("sp", o.b.name, PT[tt * 128:(tt + 1) * 128, TCOL[c0]:TCOL[c0] + 512], o[:, :], reads=[o.b])
                kb.barrier()

        def phase_C1(l, s2):
            if True:
                rT = T(s2, nc, "rT", [128, 4, 512], BF16)
                kT = T(s2, nc, "kT", [128, 4, 512], BF16)
                lw = T(s2, nc, "lw", [128, 512], F32)
                la = T(s2, nc, "la", [128, 512], F32)
                tl = T(s2, nc, "tl", [128, 512], BF16)
                lab = T(s2, nc, "lab", [128, 512], BF16)
                upf = T(s2, nc, "upf", [128, 2, 512], F32)
                upb = T(s2, nc, "upb", [128, 2, 512], BF16)
                names = ["kkraw", "rk", "kk", "sg", "a_", "akk", "t_", "kd", "csf", "cs", "cse", "Ep", "Em", "Epv", "tmpb"]
                f = {n: T(s2, nc, "c1_" + n, [128, 512], F32) for n in names}
                sqk = T(s2, nc, "sqk", [128, 512], BF16)
                ob = {n: [T(s2, nc, "c1o_%s%d" % (n, i), [128, 512], BF16) for i in range(2)] for n in ["kkg", "rg", "aki", "ki"]}
                gcs = [T(s2, nc, "gcs%d" % i, [128, 8], F32) for i in range(2)]
                tk = [T(s2, nc, "tk%d" % i, [128, 2, 4, 128], BF16) for i in range(2)]
                kb.dma("sp", "upf", upf[:, 0, :], dup[l], writes=[upf.b])
                kb.dma("sp", "upf", upf[:, 1, :], iup[l], writes=[upf.b])
                kb.op("dve", lambda: V.tensor_copy(out=upb[:], in_=upf[:]), [upf.b], [upb.b])
                it = 0
                for bi, (t0, nb, col) in enumerate(blocks):
                    nchk = nb // 64
                    ch0 = t0 // 64
                    ntt = nb // 128
                    kb.dma("sp", "rT", rT[:, :, 0:nb], PF[0:512, t0:t0 + nb].rearrange("(c p) t -> p c t", p=128), writes=[rT.b])
                    kb.dma("sp", "kT", kT[:, :, 0:nb], PF[512:1024, t0:t0 + nb].rearrange("(c p) t -> p c t", p=128), writes=[kT.b])
                    kb.dma("sp", "lw", lw[:, 0:nb], PX[0:128, t0:t0 + nb], writes=[lw.b])
                    kb.dma("sp", "la", la[:, 0:nb], PX[128:256, t0:t0 + nb], writes=[la.b])
                    kb.op("act", lambda: A.activation(out=tl[:, 0:nb], in_=lw[:, 0:nb], func=AF.Tanh), [lw.b], [tl.b])
                    kb.op("dve", lambda: V.tensor_copy(out=lab[:, 0:nb], in_=la[:, 0:nb]), [la.b], [lab.b])
                    for cc_ in range(4):
                        S = slice(0, nb)
                        kb.op("dve", lambda: V.tensor_scalar(out=f["kkraw"][:, S], in0=kT[:, cc_, S], scalar1=pp[:, l, 48 + cc_:49 + cc_], scalar2=None, op0=ALU.mult),
                              [kT.b, pp.b], [f["kkraw"].b])
                        kb.op("act", lambda: A.activation(out=sqk[:, S], in_=f["kkraw"][:, S], func=AF.Square), [f["kkraw"].b], [sqk.b])
                        kb.pe([lambda: PEe.matmul(psum[:, 0, S], lhsT=blk_b, rhs=sqk[:, S], start=True, stop=True)], [sqk.b, konb.b], [PB[0]])
                        kb.op("act", lambda: A.activation(out=f["rk"][:, S], in_=psum[:, 0, S], func=AF.Sqrt, bias=1e-12, scale=1.0), [PB[0]], [f["rk"].b])
                        kb.op("dve", lambda: V.reciprocal(out=f["rk"][:, S], in_=f["rk"][:, S]), [f["rk"].b], [f["rk"].b])
                        kb.op("dve", lambda: V.tensor_tensor(out=f["kk"][:, S], in0=f["kkraw"][:, S], in1=f["rk"][:, S], op=ALU.mult), [f["kkraw"].b, f["rk"].b], [f["kk"].b])
                        for d in range(2):
                            R = slice(d * 64, (d + 1) * 64)
                            C = slice(cc_ * 128, (cc_ + 1) * 128)
                            o = {n: ob[n][it % 2] for n in ob}
                            g_ = gcs[it % 2]
                            tk_ = tk[it % 2]
                            it += 1
                            kb.pe([lambda: PEe.matmul(psum[:, 1, S], lhsT=upb[R, 0, C], rhs=tl[R, S], start=True, stop=True)], [upb.b, tl.b], [PB[1]])
                            kb.op("act", lambda: A.activation(out=f["sg"][:, S], in_=psum[:, 1, S], func=AF.Sigmoid, bias=pp[:, l, 32 + d * 4 + cc_:33 + d * 4 + cc_], scale=1.0),
                                  [PB[1], pp.b], [f["sg"].b])
                            kb.pe([lambda: PEe.matmul(psum[:, 2, S], lhsT=upb[R, 1, C], rhs=lab[R, S], start=True, stop=True)], [upb.b, lab.b], [PB[2]])
                            kb.op("act", lambda: A.activation(out=f["a_"][:, S], in_=psum[:, 2, S], func=AF.Sigmoid, bias=pp[:, l, 40 + d * 4 + cc_:41 + d * 4 + cc_], scale=1.0),
                                  [PB[2], pp.b], [f["a_"].b])
                            kb.op("pool", lambda: P.tensor_tensor(out=f["akk"][:, S], in0=f["a_"][:, S], in1=f["kk"][:, S], op=ALU.mult), [f["a_"].b, f["kk"].b], [f["akk"].b])
                            kb.op("dve", lambda: V.tensor_scalar(out=f["t_"][:, S], in0=f["a_"][:, S], scalar1=pp[:, l, 52 + cc_:53 + cc_], scalar2=omka[:, l, cc_:cc_ + 1],
                                                                 op0=ALU.mult, op1=ALU.add), [f["a_"].b, pp.b, omka.b], [f["t_"].b])
                            kb.op("pool", lambda: P.tensor_tensor(out=f["kd"][:, S], in0=f["t_"][:, S], in1=kT[:, cc_, S], op=ALU.mult), [f["t_"].b, kT.b], [f["kd"].b])
                            kb.op("dve", lambda: V.tensor_tensor_scan(out=f["csf"][:, S], data0=ltri[:, S], data1=f["sg"][:, S], initial=0.0, op0=ALU.mult, op1=ALU.add),
                                  [ltri.b, f["sg"].b], [f["csf"].b])
                            if d == 0:
                                cs = f["csf"]
                            else:
                                cs = f["cs"]
                                kb.op("dve", lambda: V.tensor_tensor(out=f["tmpb"][:, S], in0=f["sg"][:, S], in1=f["csf"][:, S], op=ALU.subtract), [f["sg"].b, f["csf"].b], [f["tmpb"].b])
                                kb.op("dve", lambda: V.tensor_tensor(out=cs[:, S].rearrange("p (c x) -> p c x", x=64), in0=f["tmpb"][:, S].rearrange("p (c x) -> p c x", x=64),
                                                                     in1=f["csf"][:, S].rearrange("p (c x) -> p c x", x=64)[:, :, 63:64].to_broadcast([128, nchk, 64]), op=ALU.add),
                                      [f["tmpb"].b, f["csf"].b], [cs.b])
                            kb.op("pool", lambda: P.tensor_tensor(out=f["cse"][:, S], in0=cs[:, S], in1=f["sg"][:, S], op=ALU.subtract), [cs.b, f["sg"].b], [f["cse"].b])
                            kb.op("act", lambda: A.activation(out=f["Ep"][:, S], in_=cs[:, S], func=AF.Exp, scale=-C0), [cs.b], [f["Ep"].b])
                            kb.op("act", lambda: A.activation(out=f["Em"][:, S], in_=cs[:, S], func=AF.Exp, scale=C0), [cs.b], [f["Em"].b])
                            kb.op("act", lambda: A.activation(out=f["Epv"][:, S], in_=f["cse"][:, S], func=AF.Exp, scale=-C0), [f["cse"].b], [f["Epv"].b])
                            kb.op("dve", lambda: V.tensor_tensor(out=o["rg"][:, S], in0=rT[:, cc_, S], in1=f["Ep"][:, S], op=ALU.mult), [rT.b, f["Ep"].b], [o["rg"].b])
                            kb.op("pool", lambda: P.tensor_tensor(out=o["kkg"][:, S], in0=f["kk"][:, S], in1=f["Epv"][:, S], op=ALU.mult), [f["kk"].b, f["Epv"].b], [o["kkg"].b])
                            kb.op("dve", lambda: V.tensor_tensor(out=o["aki"][:, S], in0=f["akk"][:, S], in1=f["Em"][:, S], op=ALU.mult), [f["akk"].b, f["Em"].b], [o["aki"].b])
                            kb.op("pool", lambda: P.tensor_tensor(out=o["ki"][:, S], in0=f["kd"][:, S], in1=f["Em"][:, S], op=ALU.mult), [f["kd"].b, f["Em"].b], [o["ki"].b])
                            e_ = 63 if d == 0 else 0
                            kb.op("dve", lambda: V.tensor_copy(out=g_[:, 0:nchk], in_=f["Ep"][:, S].rearrange("p (c x) -> p c x", x=64)[:, :, e_]), [f["Ep"].b], [g_.b])
                            kb.dma("sp", o["kkg"].b.name, QGs[d][C, ch0:ch0 + nchk, 0, :], o["kkg"][:, S].rearrange("p (c x) -> p c x", x=64), reads=[o["kkg"].b])
                            kb.dma("sp", o["rg"].b.name, QGs[d][C, ch0:ch0 + nchk, 1, :], o["rg"][:, S].rearrange("p (c x) -> p c x", x=64), reads=[o["rg"].b])
                            kb.dma("sp", o["aki"].b.name, AKT[d][C, t0:t0 + nb], o["aki"][:, S], reads=[o["aki"].b])
                            kb.dma("sp", o["ki"].b.name, KIT[d][C, t0:t0 + nb], o["ki"][:, S], reads=[o["ki"].b])
                            kb.dma("sp", g_.b.name, GCs[d][C, ch0:ch0 + nchk], g_[:, 0:nchk], reads=[g_.b])
                            fns = []
                            for qi, nm in enumerate(["aki", "ki"]):
                                for tt in range(ntt):
                                    fns.append(lambda qi=qi, nm=nm, tt=tt: PEe.transpose(psbf[3][:, (qi * 4 + tt) * 128:(qi * 4 + tt + 1) * 128], o[nm][:, tt * 128:(tt + 1) * 128], ident_b))
                            kb.pe(fns, [o["aki"].b, o["ki"].b, konb.b], [PB[3]])
                            kb.op("act", lambda: A.copy(out=tk_[:].rearrange("p q t c -> p (q t c)"), in_=psbf[3][:, :]), [PB[3]], [tk_.b])
                            kb.dma("sp", tk_.b.name + "a", AKK[d][t0:t0 + nb, C].rearrange("(t p) c -> p t c", p=128), tk_[:, 0, 0:ntt, :], reads=[tk_.b])
                            kb.dma("sp", tk_.b.name + "b", KIK[d][t0:t0 + nb, C].rearrange("(t p) c -> p t c", p=128), tk_[:, 1, 0:ntt, :], reads=[tk_.b])
                            yield

        def phase_C2(l):
            with ExitStack() as s2:
                def mk(nm, shape, dt):
                    ts = [T(s2, nc, "%s%d" % (nm, i), shape, dt) for i in range(2)]
                    for t_ in ts:
                        t_.b2 = Buf(t_.b.name + "x")
                    return ts
                QG = mk("QG", [128, 8, 8, 128], BF16)
                AT = mk("AT", [128, 8, 512], BF16)
                KT_ = mk("KT", [128, 8, 512], BF16)
                AK = mk("AK", [128, 8, 512], BF16)
                KK_ = mk("KK", [128, 8, 512], BF16)
                VV = mk("VV", [128, 8, 512], BF16)
                GC = mk("GC", [128, 8, 8], F32)
                Z = T(s2, nc, "Z", [128, 8, 64], F32)
                Zb = T(s2, nc, "Zb", [128, 8, 64], BF16)
                NT = [T(s2, nc, "NT%d" % i, [128, 8, 64], BF16) for i in range(5)]
                N_ = [T(s2, nc, "N%d" % i, [128, 8, 64], BF16) for i in range(6)]
                PTt = [T(s2, nc, "PTt%d" % i, [128, 8, 64], BF16) for i in range(2)]
                BabT = T(s2, nc, "BabT", [128, 8, 64], BF16)
                AakT = T(s2, nc, "AakT", [128, 8, 64], BF16)
                BakT = T(s2, nc, "BakT", [128, 8, 64], BF16)
                Wn = T(s2, nc, "Wn", [128, 8, 64], BF16)
                U = T(s2, nc, "U", [128, 8, 64], BF16)
                Yst = [T(s2, nc, "Yst%d" % i, [128, 512], F32) for i in range(2)]
                yi = [0]
                DS = [slice(0, 64), slice(64, 128)]
                nB = len(blocks)
                fo = list(range(nB))
                bo = [0] + list(range(nB - 1, 0, -1))

                def hv(bank):
                    return psum[:, bank, :].rearrange("p (h i) -> p h i", h=8)

                def load(s, par):
                    for d in range(2):
                        t0, nb, col = blocks[fo[s] if d == 0 else bo[s]]
                        nchk, ch0 = nb // 64, t0 // 64
                        if d == 0:
                            csl = slice(ch0, ch0 + nchk)
                        else:
                            csl = slice(ch0 + nchk - 1, (ch0 - 1) if ch0 > 0 else None, -1)
                        D_ = DS[d]
                        bsel = (lambda t_: t_.b) if d == 0 else (lambda t_: t_.b2)
                        sfx = "f" if d == 0 else "b"
                        def ld(t_, dst, src):
                            kb.dma("sp", t_.b.name + sfx, dst, src, writes=[bsel(t_)])

                        def ldh(t_, dstf, srcf):
                            if d == 0:
                                ld(t_, dstf(slice(None)), srcf(slice(None)))
                            else:
                                kb.dma_group("sp", t_.b.name + sfx, [(dstf(h), srcf(h)) for h in range(8)], writes=[bsel(t_)])
                        qsrc = QGs[d].rearrange("(h k) c w i -> k h c (w i)", k=64)
                        ldh(QG[par], lambda h: QG[par][D_, h, 0:nchk, :], lambda h: qsrc[:, h, csl, :])
                        asrc = AKT[d].rearrange("(h k) (c x) -> k h c x", k=64, x=64)
                        ldh(AT[par], lambda h: AT[par][D_, h, 0:nb].rearrange("p (c x) -> p c x", x=64) if not isinstance(h, slice) else AT[par][D_, :, 0:nb].rearrange("p h (c x) -> p h c x", x=64),
                            lambda h: asrc[:, h, csl, :])
                        ksrc = KIT[d].rearrange("(h k) (c x) -> k h c x", k=64, x=64)
                        ldh(KT_[par], lambda h: KT_[par][D_, h, 0:nb].rearrange("p (c x) -> p c x", x=64) if not isinstance(h, slice) else KT_[par][D_, :, 0:nb].rearrange("p h (c x) -> p h c x", x=64),
                            lambda h: ksrc[:, h, csl, :])
                        ld(AK[par], AK[par][D_, 0:nchk, :], AKK[d].rearrange("(c p) n -> p c n", p=64)[:, csl, :])
                        ld(KK_[par], KK_[par][D_, 0:nchk, :], KIK[d].rearrange("(c p) n -> p c n", p=64)[:, csl, :])
                        ld(VV[par], VV[par][D_, 0:nchk, :], PT[:, 0:512].rearrange("(c p) n -> p c n", p=64)[:, csl, :])
                        ld(GC[par], GC[par][D_, :, 0:nchk], GCs[d].rearrange("(h k) c -> k h c", k=64)[:, :, ch0:ch0 + nchk])

                def chunk(par, j, nchk, tokf, tokb):
                    qg, at, kt, ak, kk_, vv, gc = QG[par], AT[par], KT_[par], AK[par], KK_[par], VV[par], GC[par]
                    bb = lambda *ts: [x for t_ in ts for x in (t_.b, t_.b2)]
                    cs = slice(j * 64, j * 64 + 64)
                    H = lambda h: slice(h * 64, (h + 1) * 64)
                    HD = [(h, D_) for h in range(8) for D_ in DS]
                    fns = []
                    for h, D_ in HD:
                        fns.append(lambda h=h, D_=D_: PEe.matmul(psum[D_, h // 4, (h % 4) * 128:(h % 4) * 128 + 128], lhsT=at[D_, h, cs], rhs=qg[D_, h, j, :], start=True, stop=True))
                    for h, D_ in HD:
                        fns.append(lambda h=h, D_=D_: PEe.matmul(psum[D_, 2 + h // 4, (h % 4) * 128:(h % 4) * 128 + 128], lhsT=kt[D_, h, cs], rhs=qg[D_, h, j, :], start=True, stop=True))
                    for h, D_ in HD:
                        fns.append(lambda h=h, D_=D_: PEe.matmul(psum[D_, 4, H(h)], lhsT=qg[D_, h, j, 0:64], rhs=at[D_, h, cs], start=True, stop=True))
                    kb.pe(fns, bb(qg, at, kt), [PB[0], PB[1], PB[2], PB[3], PB[4]])
                    P1v = psum[:, 0:2, :].rearrange("p b (h w i) -> p (b h) w i", h=4, w=2)
                    P2v = psum[:, 2:4, :].rearrange("p b (h w i) -> p (b h) w i", h=4, w=2)
                    kb.op("dve", lambda: V.scalar_tensor_tensor(out=NT[0][:], in0=P1v[:, :, 0, :], scalar=-1.0, in1=msk2[:, 0], op0=ALU.mult, op1=ALU.mult), [PB[0], PB[1], msk2.b], [NT[0].b])
                    kb.op("dve", lambda: V.scalar_tensor_tensor(out=N_[0][:], in0=hv(4), scalar=-1.0, in1=msk2[:, 2], op0=ALU.mult, op1=ALU.mult), [PB[4], msk2.b], [N_[0].b])
                    kb.op("dve", lambda: V.tensor_tensor(out=AakT[:], in0=P2v[:, :, 0, :], in1=msk2[:, 0], op=ALU.mult), [PB[2], PB[3], msk2.b], [AakT.b])
                    kb.op("dve", lambda: V.tensor_tensor(out=BabT[:], in0=P1v[:, :, 1, :], in1=msk2[:, 1], op=ALU.mult), [PB[0], PB[1], msk2.b], [BabT.b])
                    kb.op("dve", lambda: V.tensor_tensor(out=BakT[:], in0=P2v[:, :, 1, :], in1=msk2[:, 1], op=ALU.mult), [PB[2], PB[3], msk2.b], [BakT.b])
                    kb.op("pool", lambda: P.tensor_tensor(out=PTt[0][:], in0=NT[0][:], in1=msk2[:, 3], op=ALU.add), [NT[0].b, msk2.b], [PTt[0].b])
                    fns = []
                    for h, D_ in HD:
                        fns.append(lambda h=h, D_=D_: PEe.matmul(psum[D_, 5, H(h)], lhsT=qg[D_, h, j, 0:64], rhs=Zb[D_, h, :], start=True, stop=False))
                        fns.append(lambda h=h, D_=D_: PEe.matmul(psum[D_, 5, H(h)], lhsT=AakT[D_, h, :], rhs=vv[D_, j, H(h)], start=False, stop=True))
                    kb.pe(fns, bb(qg, vv) + [Zb.b, AakT.b], [PB[5]])
                    kb.op("act", lambda: A.mul(out=Wn[:], in_=hv(5), mul=-1.0), [PB[5]], [Wn.b])
                    for k in range(5):
                        fns = [lambda h=h, D_=D_: PEe.matmul(psum[D_, 6, H(h)], lhsT=NT[k][D_, h, :], rhs=N_[k][D_, h, :], start=True, stop=True) for h, D_ in HD]
                        kb.pe(fns, [NT[k].b, N_[k].b], [PB[6]])
                        kb.op("act", lambda: A.copy(out=N_[k + 1][:], in_=hv(6)), [PB[6]], [N_[k + 1].b])
                        if k < 4:
                            fns = [lambda h=h, D_=D_: PEe.matmul(psum[D_, 7, H(h)], lhsT=N_[k][D_, h, :], rhs=NT[k][D_, h, :], start=True, stop=True) for h, D_ in HD]
                            kb.pe(fns, [NT[k].b, N_[k].b], [PB[7]])
                            kb.op("act", lambda: A.copy(out=NT[k + 1][:], in_=hv(7)), [PB[7]], [NT[k + 1].b])
                        src, dst = PTt[k % 2], PTt[(k + 1) % 2]
                        fns = [lambda h=h, D_=D_: PEe.matmul(psum[D_, 0, H(h)], lhsT=N_[k + 1][D_, h, :], rhs=src[D_, h, :], start=True, stop=True) for h, D_ in HD]
                        kb.pe(fns, [N_[k + 1].b, src.b], [PB[0]])
                        kb.op("dve", lambda: V.tensor_tensor(out=dst[:], in0=hv(0), in1=src[:], op=ALU.add), [PB[0], src.b], [dst.b])
                    TTf = PTt[1]
                    fns = [lambda h=h, D_=D_: PEe.matmul(psum[D_, 1, H(h)], lhsT=TTf[D_, h, :], rhs=Wn[D_, h, :], start=True, stop=True) for h, D_ in HD]
                    kb.pe(fns, [TTf.b, Wn.b], [PB[1]])
                    kb.op("dve", lambda: V.tensor_copy(out=U[:], in_=hv(1)), [PB[1]], [U.b])
                    fns = []
                    for h, D_ in HD:
                        fns.append(lambda h=h, D_=D_: PEe.matmul(psum[D_, 2, H(h)], lhsT=qg[D_, h, j, 64:128], rhs=Zb[D_, h, :], start=True, stop=False))
                        fns.append(lambda h=h, D_=D_: PEe.matmul(psum[D_, 2, H(h)], lhsT=BabT[D_, h, :], rhs=U[D_, h, :], start=False, stop=False))
                        fns.append(lambda h=h, D_=D_: PEe.matmul(psum[D_, 2, H(h)], lhsT=BakT[D_, h, :], rhs=vv[D_, j, H(h)], start=False, stop=True))
                    kb.pe(fns, bb(qg, vv) + [Zb.b, BabT.b, U.b, BakT.b], [PB[2]])
                    ys = Yst[yi[0] % 2]
                    yi[0] += 1
                    kb.op("act", lambda: A.copy(out=ys[:], in_=psum[:, 2, :]), [PB[2]], [ys.b])
                    kb.dma("sp", ys.b.name + "f", YS[0][tokf:tokf + 64, :], ys[0:64, :], reads=[ys.b])
                    kb.dma("sp", ys.b.name + "b", YS[1][tokb:tokb + 64, :], ys[64:128, :], reads=[ys.b])
                    fns = []
                    for h, D_ in HD:
                        fns.append(lambda h=h, D_=D_: PEe.matmul(psum[D_, 3, H(h)], lhsT=ak[D_, j, H(h)], rhs=U[D_, h, :], start=True, stop=False))
                        fns.append(lambda h=h, D_=D_: PEe.matmul(psum[D_, 3, H(h)], lhsT=kk_[D_, j, H(h)], rhs=vv[D_, j, H(h)], start=False, stop=True))
                    kb.pe(fns, bb(ak, kk_, vv) + [U.b], [PB[3]])
                    kb.op("dve", lambda: V.tensor_tensor(out=Z[:], in0=hv(3), in1=Z[:], op=ALU.add), [PB[3], Z.b], [Z.b])
                    jb = nchk - 1 - j
                    kb.op("dve", lambda: V.tensor_tensor(out=Z[0:64], in0=Z[0:64], in1=gc[0:64, :, j:j + 1].to_broadcast([64, 8, 64]), op=ALU.mult), [Z.b, gc.b], [Z.b])
                    kb.op("pool", lambda: P.tensor_tensor(out=Z[64:128], in0=Z[64:128], in1=gc[64:128, :, jb:jb + 1].to_broadcast([64, 8, 64]), op=ALU.mult), [Z.b, gc.b2], [Z.b])
                    kb.op("act", lambda: A.copy(out=Zb[:], in_=Z[:]), [Z.b], [Zb.b])

                kb.op("dve", lambda: V.memset(Z[:], 0.0), [], [Z.b])
                kb.op("dve", lambda: V.memset(Zb[:], 0.0), [], [Zb.b])
                load(0, 0)
                for s in range(nB):
                    par = s % 2
                    if s + 1 < nB:
                        load(s + 1, 1 - par)
                    t0f, nb, _ = blocks[fo[s]]
                    t0b, _, _ = blocks[bo[s]]
                    nchk = nb // 64
                    for j in range(nchk):
                        chunk(par, j, nchk, t0f + j * 64, t0b + (nchk - 1 - j) * 64)
                kb.barrier()

        def phase_C3(l, s2):
            if True:
                bc = T(s2, nc, "bcg", [128, 2, 512], F32)
                kb.dma("sp", "bcg", bc[:].rearrange("p a n -> p (a n)"), BCd[:, l * 1024:(l + 1) * 1024], writes=[bc.b])
                def mk(nm, shape, dt):
                    return [T(s2, nc, "%s%d" % (nm, i), shape, dt) for i in range(2)]
                yf, yb = mk("yf", [128, 512], F32), mk("yb", [128, 512], F32)
                vz = mk("vz", [128, 1024], BF16)
                rk_ = mk("rk_", [128, 2, 4, 128], BF16)
                rkr = mk("rkr", [128, 4, 128], BF16)
                y = mk("y_", [128, 512], F32)
                ysq = mk("ysq", [128, 512], F32)
                st_ = mk("st_", [128, 6, 8], F32)
                sz = mk("sz", [128, 512], F32)
                bon = mk("bon", [128, 512], F32)
                yg = mk("yg", [128, 512], BF16)
                ygT = mk("ygT", [128, 4, 128], BF16)
                for tt in range(NTT):
                    p = tt % 2
                    ts_ = slice(tt * 128, (tt + 1) * 128)
                    kb.dma("sp", yf[p].b.name, yf[p][:], YS[0][ts_, :], writes=[yf[p].b])
                    kb.dma("sp", yb[p].b.name, yb[p][:], YS[1][ts_, :], writes=[yb[p].b])
                    kb.dma("sp", vz[p].b.name, vz[p][:], PT[ts_, 0:1024], writes=[vz[p].b])
                    kb.dma("sp", rk_[p].b.name, rk_[p][:].rearrange("p a c t -> p (a c) t"), PF[0:1024, ts_].rearrange("(c p) t -> p c t", p=128), writes=[rk_[p].b])
                    for c in range(4):
                        kb.op("dve", lambda: V.scalar_tensor_tensor(out=rkr[p][:, c, :], in0=rk_[p][:, 0, c, :], scalar=pp[:, l, 56 + c:57 + c], in1=rk_[p][:, 1, c, :],
                                                                     op0=ALU.mult, op1=ALU.mult), [rk_[p].b, pp.b], [rkr[p].b])
                    kb.pe([lambda c=c: PEe.matmul(psum[:, 6, 0:8], lhsT=rkr[p][:, c, :], rhs=sel_b[:, c * 8:(c + 1) * 8], start=(c == 0), stop=(c == 3)) for c in range(4)],
                          [rkr[p].b, konb.b], [PB[6]])
                    S_ = st_[p]
                    kb.op("act", lambda: A.copy(out=S_[:, 5, :], in_=psum[:, 6, 0:8]), [PB[6]], [S_.b])
                    kb.op("pool", lambda: P.tensor_tensor(out=y[p][:], in0=yf[p][:], in1=yb[p][:], op=ALU.add), [yf[p].b, yb[p].b], [y[p].b])
                    y3 = y[p][:].rearrange("p (h x) -> p h x", h=8)
                    kb.op("dve", lambda: V.tensor_reduce(out=S_[:, 0, :], in_=y3, axis=AX.X, op=ALU.add), [y[p].b], [S_.b])
                    kb.op("act", lambda: A.activation(out=ysq[p][:], in_=y[p][:], func=AF.Square), [y[p].b], [ysq[p].b])
                    kb.op("dve", lambda: V.tensor_reduce(out=S_[:, 1, :], in_=ysq[p][:].rearrange("p (h x) -> p h x", h=8), axis=AX.X, op=ALU.add), [ysq[p].b], [S_.b])
                    kb.op("dve", lambda: V.tensor_scalar(out=S_[:, 2, :], in0=S_[:, 0, :], scalar1=1.0 / 64, scalar2=None, op0=ALU.mult), [S_.b], [S_.b])
                    kb.op("dve", lambda: V.tensor_tensor(out=S_[:, 3, :], in0=S_[:, 2, :], in1=S_[:, 2, :], op=ALU.mult), [S_.b], [S_.b])
                    kb.op("dve", lambda: V.scalar_tensor_tensor(out=S_[:, 3, :], in0=S_[:, 1, :], scalar=1.0 / 64, in1=S_[:, 3, :], op0=ALU.mult, op1=ALU.subtract), [S_.b], [S_.b])
                    kb.op("act", lambda: A.activation(out=S_[:, 4, :], in_=S_[:, 3, :], func=AF.Sqrt, bias=GN_EPS, scale=1.0), [S_.b], [S_.b])
                    kb.op("dve", lambda: V.reciprocal(out=S_[:, 4, :], in_=S_[:, 4, :]), [S_.b], [S_.b])
                    kb.op("dve", lambda: V.tensor_tensor(out=y3, in0=y3, in1=S_[:, 2, :].unsqueeze(2).to_broadcast([128, 8, 64]), op=ALU.subtract), [y[p].b, S_.b], [y[p].b])
                    kb.op("dve", lambda: V.tensor_tensor(out=y3, in0=y3, in1=S_[:, 4, :].unsqueeze(2).to_broadcast([128, 8, 64]), op=ALU.mult), [y[p].b, S_.b], [y[p].b])
                    kb.op("pool", lambda: P.tensor_tensor(out=y[p][:], in0=y[p][:], in1=bc[:, 0, :], op=ALU.mult), [y[p].b, bc.b], [y[p].b])
                    kb.op("pool", lambda: P.tensor_tensor(out=y[p][:], in0=y[p][:], in1=bc[:, 1, :], op=ALU.add), [y[p].b, bc.b], [y[p].b])
                    kb.op("dve", lambda: V.tensor_tensor(out=bon[p][:].rearrange("p (h x) -> p h x", h=8), in0=vz[p][:, 0:512].rearrange("p (h x) -> p h x", h=8),
                                                         in1=S_[:, 5, :].unsqueeze(2).to_broadcast([128, 8, 64]), op=ALU.mult), [vz[p].b, S_.b], [bon[p].b])
                    kb.op("pool", lambda: P.tensor_tensor(out=y[p][:], in0=y[p][:], in1=bon[p][:], op=ALU.add), [y[p].b, bon[p].b], [y[p].b])
                    kb.op("act", lambda: A.activation(out=sz[p][:], in_=vz[p][:, 512:1024], func=AF.Silu), [vz[p].b], [sz[p].b])
                    kb.op("dve", lambda: V.tensor_tensor(out=yg[p][:], in0=y[p][:], in1=sz[p][:], op=ALU.mult), [y[p].b, sz[p].b], [yg[p].b])
                    kb.pe([lambda c=c: PEe.transpose(psbf[6][:, 512 + c * 128:512 + (c + 1) * 128], yg[p][:, c * 128:(c + 1) * 128], ident_b) for c in range(4)], [yg[p].b, konb.b], [PB[6]])
                    kb.op("act", lambda: A.copy(out=ygT[p][:].rearrange("p c t -> p (c t)"), in_=psbf[6][:, 512:1024]), [PB[6]], [ygT[p].b])
                    kb.dma("sp", ygT[p].b.name, YG[0:512, ts_].rearrange("(c p) t -> p c t", p=128), ygT[p][:], reads=[ygT[p].b])
                    yield

        def phase_D(l, s2):
            if True:
                gbz = T(s2, nc, "gbz", [128, 2, 4, 512], BF16)
                gcu = T(s2, nc, "gcu", [128, 2, 4, 514], BF16)
                gu = T(s2, nc, "gu", [128, 4, 514], F32)
                acc = T(s2, nc, "acc", [128, 4, 512], F32)
                szd = T(s2, nc, "szd", [128, 4, 512], F32)
                od = T(s2, nc, "od", [128, 4, 512], BF16)
                for bi, (t0, nb, col) in enumerate(blocks):
                    s0, s1 = (0, CTX) if col == 1 else (CTX, TT)
                    lo, hi = max(s0, t0 - 1), min(s1, t0 + nb + 1)
                    kb.op("pool", lambda: P.memset(gcu[:], 0.0), [], [gcu.b])
                    off = lo - (t0 - 1)
                    for j, r0 in enumerate([2816, 3328]):
                        kb.dma("sp", "gcu%d" % j, gcu[:, j, :, off:off + hi - lo], PF[r0:r0 + 512, lo:hi].rearrange("(c p) t -> p c t", p=128), writes=[gcu.b])
                    for j, r0 in enumerate([2304, 3840]):
                        kb.dma("sp", "gbz%d" % j, gbz[:, j, :, 0:nb], PF[r0:r0 + 512, t0:t0 + nb].rearrange("(c p) t -> p c t", p=128), writes=[gbz.b])
                    kb.op("dve", lambda: V.tensor_tensor(out=gu[:, :, 0:nb + 2], in0=gcu[:, 0, :, 0:nb + 2], in1=gcu[:, 1, :, 0:nb + 2], op=ALU.mult), [gcu.b], [gu.b])
                    for c in range(4):
                        cw = lambda j: pp[:, l, 60 + j * 4 + c:61 + j * 4 + c]
                        kb.op("dve", lambda: V.tensor_scalar(out=acc[:, c, 0:nb], in0=gu[:, c, 0:nb], scalar1=cw(0), scalar2=None, op0=ALU.mult), [gu.b, pp.b], [acc.b])
                        kb.op("dve", lambda: V.scalar_tensor_tensor(out=acc[:, c, 0:nb], in0=gu[:, c, 1:nb + 1], scalar=cw(1), in1=acc[:, c, 0:nb], op0=ALU.mult, op1=ALU.add), [gu.b, pp.b, acc.b], [acc.b])
                        kb.op("dve", lambda: V.scalar_tensor_tensor(out=acc[:, c, 0:nb], in0=gu[:, c, 2:nb + 2], scalar=cw(2), in1=acc[:, c, 0:nb], op0=ALU.mult, op1=ALU.add), [gu.b, pp.b, acc.b], [acc.b])
                        kb.op("dve", lambda: V.scalar_tensor_tensor(out=acc[:, c, 0:nb], in0=acc[:, c, 0:nb], scalar=pp[:, l, 72 + c:73 + c], in1=gbz[:, 0, c, 0:nb], op0=ALU.add, op1=ALU.mult),
                              [acc.b, pp.b, gbz.b], [acc.b])
                        yield
                    kb.op("act", lambda: A.activation(out=szd[:, :, 0:nb], in_=gbz[:, 1, :, 0:nb], func=AF.Silu), [gbz.b], [szd.b])
                    kb.op("pool", lambda: P.tensor_tensor(out=od[:, :, 0:nb], in0=acc[:, :, 0:nb], in1=szd[:, :, 0:nb], op=ALU.mult), [acc.b, szd.b], [od.b])
                    kb.dma("sp", "od", YG[512:1024, t0:t0 + nb].rearrange("(c p) t -> p c t", p=128), od[:, :, 0:nb], reads=[od.b])

        def phase_E1(l, s2):
            if True:
                pg = T(s2, nc, "pg", [128, 2, 128], BF16)
                qk = [T(s2, nc, "qk%d" % i, [128, 8, 512], BF16) for i in range(2)]
                cs_ = [T(s2, nc, "cs_%d" % i, [128, 2, 512], F32) for i in range(2)]
                sqe = [T(s2, nc, "sqe%d" % i, [128, 512], BF16) for i in range(2)]
                ri = [T(s2, nc, "ri%d" % i, [128, 512], F32) for i in range(2)]
                t1 = [T(s2, nc, "t1_%d" % i, [128, 512], F32) for i in range(2)]
                t2 = [T(s2, nc, "t2_%d" % i, [128, 512], F32) for i in range(2)]
                oq = [T(s2, nc, "oq%d" % i, [128, 512], BF16) for i in range(2)]
                for j in range(2):
                    kb.op("dve", lambda: V.tensor_scalar(out=pg[:, j, :], in0=perm_f, scalar1=pp[:, l, 76 + j:77 + j], scalar2=None, op0=ALU.mult), [kon.b, pp.b], [pg.b])
                for bi, (t0, nb, col) in enumerate(blocks):
                    S = slice(0, nb)
                    q_, c_ = qk[bi % 2], cs_[bi % 2]
                    kb.dma_group("sp", q_.b.name, [(q_[:, 0:4, S], PF[4352:4864, t0:t0 + nb].rearrange("(c p) t -> p c t", p=128)),
                                                   (q_[:, 4:8, S], PF[4864:5376, t0:t0 + nb].rearrange("(c p) t -> p c t", p=128))], writes=[q_.b])
                    kb.dma_group("sp", c_.b.name, [(c_[:, 0, S], ROPE[:, t0:t0 + nb]), (c_[:, 1, S], ROPE[:, TT + t0:TT + t0 + nb])], writes=[c_.b])
                    for j in range(8):
                        w = j // 4
                        p = j % 2
                        kb.op("act", lambda: A.activation(out=sqe[p][:, S], in_=q_[:, j, S], func=AF.Square), [q_.b], [sqe[p].b])
                        kb.pe([lambda: PEe.matmul(psum[:, 4 + p, S], lhsT=blk_b, rhs=sqe[p][:, S], start=True, stop=True)], [sqe[p].b, konb.b], [PB[4 + p]])
                        kb.pe([lambda: PEe.matmul(psum[:, 6 + p, S], lhsT=pg[:, w, :], rhs=q_[:, j, S], start=True, stop=True)], [pg.b, q_.b], [PB[6 + p]])
                        kb.op("act", lambda: A.activation(out=ri[p][:, S], in_=psum[:, 4 + p, S], func=AF.Sqrt, bias=RMS_EPS, scale=1.0 / 64), [PB[4 + p]], [ri[p].b])
                        kb.op("dve", lambda: V.reciprocal(out=ri[p][:, S], in_=ri[p][:, S]), [ri[p].b], [ri[p].b])
                        kb.op("dve", lambda: V.scalar_tensor_tensor(out=t1[p][:, S], in0=q_[:, j, S], scalar=pp[:, l, 76 + w:77 + w], in1=c_[:, 0, S], op0=ALU.mult, op1=ALU.mult),
                              [q_.b, pp.b, c_.b], [t1[p].b])
                        kb.op("dve", lambda: V.tensor_tensor(out=t2[p][:, S], in0=psum[:, 6 + p, S], in1=c_[:, 1, S], op=ALU.mult), [PB[6 + p], c_.b], [t2[p].b])
                        kb.op("pool", lambda: P.tensor_tensor(out=t1[p][:, S], in0=t1[p][:, S], in1=t2[p][:, S], op=ALU.add), [t1[p].b, t2[p].b], [t1[p].b])
                        kb.op("pool", lambda: P.tensor_tensor(out=oq[p][:, S], in0=t1[p][:, S], in1=ri[p][:, S], op=ALU.mult), [t1[p].b, ri[p].b], [oq[p].b])
                        dst = QRs if w == 0 else KRs
                        kb.dma("sp", oq[p].b.name, dst[(j % 4) * 128:(j % 4 + 1) * 128, t0:t0 + nb], oq[p][:, S], reads=[oq[p].b])
                        yield

        def phase_E2(l, last, s2):
            if True:
                QR = T(s2, nc, "QR", [128, 4, TT], BF16)
                KR = T(s2, nc, "KR", [128, 4, TT], BF16)
                VD = T(s2, nc, "VD", [128, NTT, 512], BF16)
                PTs = [T(s2, nc, "PTs%d" % i, [128, 2, 512], BF16) for i in range(3)]
                accD = [T(s2, nc, "accD%d" % i, [128, 512], F32) for i in range(2)]
                accP = [T(s2, nc, "accP%d" % i, [128, 512], F32) for i in range(2)]
                o0 = [T(s2, nc, "o0_%d" % i, [128, 512], F32) for i in range(2)]
                r0 = T(s2, nc, "r0", [128, 512], F32)
                r0a = T(s2, nc, "r0a", [128, 512], F32)
                sq2 = T(s2, nc, "sq2", [128, 512], BF16)
                rst = T(s2, nc, "rst", [128, 512], F32)
                zc = T(s2, nc, "zc", [128, 512], BF16)
                szc = T(s2, nc, "szc", [128, 512], F32)
                oc = T(s2, nc, "oc", [128, 512], BF16)
                kb.dma("sp", "VD", VD[:], PT[:, 1024:1536].rearrange("(t p) n -> p t n", p=128), writes=[VD.b])
                kb.dma("sp", "QR", QR[:], QRs.rearrange("(h p) t -> p h t", p=128), writes=[QR.b])
                kb.dma("sp", "KR", KR[:], KRs.rearrange("(h p) t -> p h t", p=128), writes=[KR.b])
                pending = []

                def flush(n):
                    for _ in range(min(n, len(pending))):
                        pending.pop(0)()
                n_ = 0
                bidx = 0
                for h in range(4):
                    for bi, (t0, nb, col) in enumerate(blocks):
                        if col == 1 and last:
                            continue
                        S = slice(0, nb)
                        kts = list(range(CTX // 128)) if col == 1 else list(range(NTT))
                        aD, aP = accD[bidx % 2], accP[bidx % 2]
                        bidx += 1

                        def smm(i):
                            kt = kts[i]
                            b0 = 2 * ((n_ + i) % 2)
                            kb.pe([lambda c=c: PEe.matmul(psum[:, b0 + c, S], lhsT=KR[c * 64:(c + 1) * 64, h, kt * 128:(kt + 1) * 128], rhs=QR[c * 64:(c + 1) * 64, h, t0:t0 + nb],
                                                          start=True, stop=True) for c in range(2)], [KR.b, QR.b], [PB[b0], PB[b0 + 1]])
                        smm(0)
                        for i, kt in enumerate(kts):
                            if i + 1 < len(kts):
                                smm(i + 1)
                            b0 = 2 * ((n_ + i) % 2)
                            pt = PTs[(n_ + i) % 3]
                            kb.op("act", lambda: A.activation(out=pt[:, :, S], in_=psum[:, b0:b0 + 2, S], func=AF.Exp, scale=0.125), [PB[b0], PB[b0 + 1]], [pt.b])
                            first, lastk = (i == 0), (i == len(kts) - 1)
                            kb.pe([lambda c=c: PEe.matmul(psum[:, 4 + c, S], lhsT=VD[:, kt, h * 128:(h + 1) * 128], rhs=pt[:, c, S], start=first, stop=lastk) for c in range(2)],
                                  [VD.b, pt.b], [PB[4], PB[5]])
                            kb.pe([lambda: PEe.matmul(psum[:, 7, S], lhsT=ones_b, rhs=pt[:, 0, S], start=first, stop=lastk)], [pt.b, konb.b], [PB[7]])
                            if i % 2 == 0:
                                if i == 0:
                                    kb.op("dve", lambda: V.tensor_copy(out=aD[:, S], in_=pt[:, 1, S]), [pt.b], [aD.b])
                                else:
                                    kb.op("dve", lambda: V.tensor_tensor(out=aD[:, S], in0=aD[:, S], in1=pt[:, 1, S], op=ALU.add), [pt.b, aD.b], [aD.b])
                            else:
                                if i == 1:
                                    kb.op("pool", lambda: P.tensor_copy(out=aP[:, S], in_=pt[:, 1, S]), [pt.b], [aP.b])
                                else:
                                    kb.op("pool", lambda: P.tensor_tensor(out=aP[:, S], in0=aP[:, S], in1=pt[:, 1, S], op=ALU.add), [pt.b, aP.b], [aP.b])
                            flush(1)
                            yield
                        n_ += len(kts)
                        flush(len(pending))
                        kb.op("act", lambda: A.copy(out=o0[0][:, S], in_=psum[:, 4, S]), [PB[4]], [o0[0].b])
                        kb.op("dve", lambda: V.tensor_copy(out=o0[1][:, S], in_=psum[:, 5, S]), [PB[5]], [o0[1].b])
                        kb.op("dve", lambda: V.reciprocal(out=r0a[:, S], in_=psum[:, 7, S]), [PB[7]], [r0a.b])

                        def mk_epilogue(h=h, t0=t0, nb=nb, S=S, aD=aD, aP=aP):
                            ops = []
                            ops.append(lambda: kb.dma("sp", "zc", zc[:, S], PF[5888 + h * 128:5888 + (h + 1) * 128, t0:t0 + nb], writes=[zc.b]))
                            ops.append(lambda: kb.op("dve", lambda: V.tensor_tensor(out=o0[0][:, S], in0=o0[0][:, S], in1=r0a[:, S], op=ALU.mult), [o0[0].b, r0a.b], [o0[0].b]))
                            ops.append(lambda: kb.pe([lambda: PEe.matmul(psum[:, 6, S], lhsT=ones_f, rhs=aD[:, S], start=True, stop=False),
                                                      lambda: PEe.matmul(psum[:, 6, S], lhsT=ones_f, rhs=aP[:, S], start=False, stop=True)], [aD.b, aP.b, kon.b], [PB[6]]))
                            ops.append(lambda: kb.op("dve", lambda: V.reciprocal(out=r0[:, S], in_=psum[:, 6, S]), [PB[6]], [r0.b]))
                            ops.append(lambda: kb.op("dve", lambda: V.tensor_tensor(out=o0[1][:, S], in0=o0[1][:, S], in1=r0[:, S], op=ALU.mult), [o0[1].b, r0.b], [o0[1].b]))
                            ops.append(lambda: kb.op("dve", lambda: V.scalar_tensor_tensor(out=o0[0][:, S], in0=o0[1][:, S], scalar=lamc[:, l, 0:1], in1=o0[0][:, S], op0=ALU.mult, op1=ALU.add),
                                                     [o0[0].b, o0[1].b, lamc.b], [o0[0].b]))
                            ops.append(lambda: kb.op("act", lambda: A.activation(out=sq2[:, S], in_=o0[0][:, S], func=AF.Square), [o0[0].b], [sq2.b]))
                            ops.append(lambda: kb.pe([lambda: PEe.matmul(psum[:, 6, S], lhsT=ones_b, rhs=sq2[:, S], start=True, stop=True)], [sq2.b, konb.b], [PB[6]]))
                            ops.append(lambda: kb.op("act", lambda: A.activation(out=rst[:, S], in_=psum[:, 6, S], func=AF.Sqrt, bias=RMS_EPS, scale=1.0 / 128), [PB[6]], [rst.b]))
                            ops.append(lambda: kb.op("dve", lambda: V.reciprocal(out=rst[:, S], in_=rst[:, S]), [rst.b], [rst.b]))
                            ops.append(lambda: kb.op("dve", lambda: V.scalar_tensor_tensor(out=o0[0][:, S], in0=o0[0][:, S], scalar=sgl[:, l:l + 1], in1=rst[:, S], op0=ALU.mult, op1=ALU.mult),
                                                     [o0[0].b, sgl.b, rst.b], [o0[0].b]))
                            ops.append(lambda: kb.op("act", lambda: A.activation(out=szc[:, S], in_=zc[:, S], func=AF.Silu), [zc.b], [szc.b]))
                            ops.append(lambda: kb.op("pool", lambda: P.tensor_tensor(out=oc[:, S], in0=o0[0][:, S], in1=szc[:, S], op=ALU.mult), [o0[0].b, szc.b], [oc.b]))
                            ops.append(lambda: kb.dma("sp", "oc", YG[1024 + h * 128:1024 + (h + 1) * 128, t0:t0 + nb], oc[:, S], reads=[oc.b]))
                            return ops
                        pending.extend(mk_epilogue())
                flush(len(pending))

        def phase_F(l, last):
            with ExitStack() as s2:
                wst = [T(s2, nc, "wst%d" % i, [128, 4, 1024], F32) for i in range(2)]
                wbr = T(s2, nc, "wbr", [128, 12, 1024], BF16)
                wou = T(s2, nc, "wou", [128, 8, 1024], BF16)
                for i in range(5):
                    w = wst[i % 2]
                    src = w_br[l][i * 512:(i + 1) * 512, :] if i < 3 else w_out[l][(i - 3) * 512:(i - 2) * 512, :]
                    kb.dma("sp", w.b.name, w[:], src.rearrange("(c p) n -> p c n", p=128), writes=[w.b])
                    dstt = wbr[:, i * 4:(i + 1) * 4, :] if i < 3 else wou[:, (i - 3) * 4:(i - 2) * 4, :]
                    dstb = wbr.b if i < 3 else wou.b
                    kb.op("pool", lambda: P.tensor_copy(out=dstt, in_=w[:]), [w.b], [dstb])
                ygb = T(s2, nc, "ygb", [128, 12, 512], BF16)
                gm = T(s2, nc, "gm", [128, 24, 512], BF16)
                xa = T(s2, nc, "xaf", [128, 8, 512], F32)
                xn = T(s2, nc, "xnf", [128, 8, 512], F32)
                mT = T(s2, nc, "mT", [128, 8, 512], BF16)
                sg = [T(s2, nc, "sgf%d" % i, [128, 512], F32) for i in range(3)]
                ma = [T(s2, nc, "ma%d" % i, [128, 512], F32) for i in range(3)]
                for bi, (t0, nb, col) in enumerate(blocks):
                    if col == 1 and last:
                        continue
                    S = slice(0, nb)
                    kb.dma("sp", "ygb", ygb[:, :, S], YG[:, t0:t0 + nb].rearrange("(c p) t -> p c t", p=128), writes=[ygb.b])
                    kb.dma("sp", "gm", gm[:, :, S], PF[6400:9472, t0:t0 + nb].rearrange("(c p) t -> p c t", p=128), writes=[gm.b])
                    kb.dma("sp", "xaf", xa[:, :, S], xTv[:, :, t0:t0 + nb], writes=[xa.b])
                    for dc in range(8):
                        for i in range(3):
                            kb.pe([lambda w=w: PEe.matmul(psum[:, 1 + i, S], lhsT=wbr[:, i * 4 + w, dc * 128:(dc + 1) * 128], rhs=ygb[:, i * 4 + w, S], start=(w == 0), stop=(w == 3))
                                   for w in range(4)], [wbr.b, ygb.b], [PB[1 + i]])
                            kb.op("act", lambda: A.activation(out=sg[i][:, S], in_=gm[:, i * 8 + dc, S], func=AF.Sigmoid), [gm.b], [sg[i].b])
                            kb.op("dve", lambda: V.tensor_tensor(out=ma[i][:, S], in0=psum[:, 1 + i, S], in1=sg[i][:, S], op=ALU.mult), [PB[1 + i], sg[i].b], [ma[i].b])
                        kb.op("pool", lambda: P.tensor_tensor(out=ma[0][:, S], in0=ma[0][:, S], in1=ma[1][:, S], op=ALU.add), [ma[0].b, ma[1].b], [ma[0].b])
                        kb.op("pool", lambda: P.tensor_tensor(out=mT[:, dc, S], in0=ma[0][:, S], in1=ma[2][:, S], op=ALU.add), [ma[0].b, ma[2].b], [mT.b])
                    for d2 in range(8):
                        bank = 4 + d2 % 2
                        kb.pe([lambda dc=dc: PEe.matmul(psum[:, bank, S], lhsT=wou[:, dc, d2 * 128:(d2 + 1) * 128], rhs=mT[:, dc, S], start=(dc == 0), stop=(dc == 7)) for dc in range(8)],
                              [wou.b, mT.b], [PB[bank]])
                        kb.op("dve", lambda: V.scalar_tensor_tensor(out=xn[:, d2, S], in0=psum[:, bank, S], scalar=modT[:, l, 16 + d2, col:col + 1], in1=xa[:, d2, S], op0=ALU.mult, op1=ALU.add),
                              [PB[bank], modT.b, xa.b], [xn.b])
                    kb.dma("sp", "xnf", xTv[:, :, t0:t0 + nb], xn[:, :, S], reads=[xn.b])
                kb.barrier()

        def drain(g):
            for _ in g:
                pass

        def layer(l):
            last = (l == DEPTH - 1)
            phase_AB(l)
            with ExitStack() as sg:
                kb.coop.run([(lambda: drain(phase_C1(l, sg)), 8), (lambda: drain(phase_E1(l, sg)), 3), (lambda: drain(phase_D(l, sg)), 1)])
                kb.barrier()
            phase_C2(l)
            with ExitStack() as sg:
                drain(phase_E2(l, last, sg))
                drain(phase_C3(l, sg))
                kb.barrier()
            phase_F(l, last)

        for l in range(DEPTH):
            layer(l)
        with ExitStack() as s2:
            xb_ = [T(s2, nc, "xo%d" % i, [128, 8, 512], F32) for i in range(2)]
            for bi, (t0, nb, col) in enumerate(blocks[1:]):
                x_ = xb_[bi % 2]
                kb.dma("sp", x_.b.name, x_[:, :, 0:nb], xTv[:, :, t0:t0 + nb], writes=[x_.b])
                kb.dma("sp", x_.b.name, yout.rearrange("(c p) t -> p c t", p=128)[:, :, t0 - CTX:t0 - CTX + nb], x_[:, :, 0:nb], reads=[x_.b])
            if dbg:
                for k_, shp in dbg.items():
                    src = {"PF": PF, "PX": PX, "PT": PT, "YG": YG, "YS0": YS[0], "YS1": YS[1], "xTs": xT, "AKT0": AKT[0], "GCs0": GCs[0], "GCs1": GCs[1]}[k_]
                    kb.dma("sp", "dbg", dbgo[k_], src, reads=[])
            kb.barrier()
    build.n_inst = kb.n_inst
    build.n_sem = len(kb.sems)
    return nc


def host_consts(SEQ, DEPTH):
    TT = CTX + SEQ
    kon = np.zeros((128, 1024), np.float32)
    kon[:, 0:128] = np.eye(128)
    kon[:, 128:256] = 1.0
    p = np.arange(128)
    kon[:, 256:384] = (p[:, None] // 64 == p[None, :] // 64)
    partner = np.where((p % 32) < 16, p + 16, p - 16)
    perm = np.zeros((128, 128), np.float32)
    perm[partner, p] = 1.0
    kon[:, 384:512] = perm
    sel = np.zeros((128, 4, 8), np.float32)
    for c in range(4):
        for q in range(128):
            sel[q, c, 2 * c + q // 64] = 1.0
    kon[:, 512:544] = sel.reshape(128, 32)
    i = np.arange(64)
    SU = (i[:, None] < i[None, :]); SL = (i[:, None] > i[None, :]); IU = (i[:, None] <= i[None, :]); IL = (i[:, None] >= i[None, :]); I = np.eye(64)
    rep = lambda ms: np.stack([np.broadcast_to(m[:, None, :], (64, 8, 64)) for m in ms], axis=1).astype(np.float32).reshape(64, 4 * 512)
    msk = np.concatenate([rep((SU, IU, SL, I)), rep((SL, IL, SU, I))], axis=0)
    n_freq = 16
    inv_freq = (10000.0 ** (-np.arange(n_freq, dtype=np.float32) / n_freq)).astype(np.float32)
    tok = np.arange(SEQ)
    row = (tok // 64).astype(np.float32)
    colp = (tok % 64).astype(np.float32)
    d = p % 64
    half = d // 32
    j = d % 16
    which = (d % 32) // 16
    pos = np.where(half[:, None] == 0, row[None, :], colp[None, :]).astype(np.float32)
    ang = (pos * inv_freq[j][:, None]).astype(np.float32)
    cosT = np.ones((128, TT), np.float32)
    sinT = np.zeros((128, TT), np.float32)
    cosT[:, CTX:] = np.cos(ang)
    sinT[:, CTX:] = np.sin(ang) * np.where(which == 0, -1.0, 1.0)[:, None]
    rope = np.concatenate([cosT, sinT], axis=1).astype(np.float32)
    ltri = np.ones((128, 512), np.float32)
    ltri[:, 0::64] = 0.0
    return dict(KON=kon, MSK=np.ascontiguousarray(msk), ROPE=np.ascontiguousarray(rope), LTRI=ltri)


def host_inputs(inp, SEQ, DEPTH, b):
    L = DEPTH
    f = lambda a: np.asarray(a, np.float32)
    xT = np.concatenate([f(inp["ctx"])[b], f(inp["x"])[b]], axis=0).T
    PPa = np.zeros((128, L, NPL), np.float32)
    colT = lambda v: v.reshape(-1, 128).T
    for l in range(L):
        PPa[:, l, 0:8] = colT(f(inp["norm_g"])[l])
        PPa[:, l, 8:32] = colT(f(inp["b_mod"])[l])
        PPa[:, l, 32:40] = colT(f(inp["decay_w0"])[l].reshape(-1))
        PPa[:, l, 40:48] = colT(f(inp["iclr_a0"])[l].reshape(-1))
        PPa[:, l, 48:52] = colT(f(inp["k_k"])[l])
        PPa[:, l, 52:56] = colT(f(inp["k_a"])[l])
        PPa[:, l, 56:60] = colT(f(inp["r_k"])[l].reshape(-1))
        PPa[:, l, 60:72] = colT(f(inp["conv_w"])[l].reshape(-1))
        PPa[:, l, 72:76] = colT(f(inp["conv_b"])[l])
        PPa[:, l, 76] = np.tile(f(inp["qk_norm_g"])[l, 0], 2)
        PPa[:, l, 77] = np.tile(f(inp["qk_norm_g"])[l, 1], 2)
        PPa[:, l, 78] = f(inp["subln_g"])[l]
    CC = np.stack([colT(f(inp["c"])[b]), colT(f(inp["c_ctx"]))], axis=2).reshape(128, 16)
    BC = np.stack([np.broadcast_to(f(inp["gn_g"])[:L, None, :], (L, 128, 512)), np.broadcast_to(f(inp["gn_b"])[:L, None, :], (L, 128, 512))], axis=2)
    BC = np.ascontiguousarray(BC.transpose(1, 0, 2, 3)).reshape(128, L * 1024)
    LAM = f(inp["lambda_qk"])[:L].reshape(1, L * 256)
    return dict(xT=np.ascontiguousarray(xT), PP=PPa.reshape(128, L * NPL), CC=np.ascontiguousarray(CC), BC=BC, LAM=np.ascontiguousarray(LAM),
                w_mod=f(inp["w_mod"])[:L], w_in=f(inp["w_in"])[:L], dup=f(inp["decay_up"])[:L].reshape(L, 128, 512), iup=f(inp["iclr_up"])[:L].reshape(L, 128, 512),
                w_br=f(inp["w_branch"])[:L].reshape(L, 1536, D), w_out=f(inp["w_out"])[:L])


def run(inp, SEQ, DEPTH, dbg=None):
    nc = build(SEQ, DEPTH, dbg)
    consts = host_consts(SEQ, DEPTH)
    nb = np.asarray(inp["x"]).shape[0]
    in_maps = []
    for b in range(nb):
        m = host_inputs(inp, SEQ, DEPTH, b)
        m.update(consts)
        in_maps.append(m)
    res = run_bass_kernel_spmd(nc, in_maps, core_ids=list(range(nb)))
    out = np.stack([np.asarray(r["yout"]).T for r in res.results], axis=0)
    return out.astype(np.float32), res


def kernel(**inputs):
    out, _ = run(inputs, 4096, 4)
    return out
```

```python
import math
import threading
import numpy as np
from contextlib import ExitStack
import concourse.bass as bass
import concourse.mybir as mybir
from concourse.bass_utils import run_bass_kernel_spmd

F32 = mybir.dt.float32
BF16 = mybir.dt.bfloat16
AF = mybir.ActivationFunctionType
ALU = mybir.AluOpType
AX = mybir.AxisListType

D = 1024
CTX = 256
NIN = 9472
NPL = 79
C0 = math.exp(-0.5)
GN_EPS = 64e-5
RMS_EPS = 1e-6


class Buf:
    __slots__ = ("name", "w", "r")

    def __init__(self, name):
        self.name = name
        self.w = None
        self.r = []


class Coop:
    def __init__(self):
        self.cur = None

    def run(self, specs):
        n = len(specs)
        self.evs = [threading.Event() for _ in range(n)]
        self.alive = [True] * n
        self.quota = [q for _, q in specs]
        self.used = [0] * n
        self.done_ev = threading.Event()
        self.err = None

        def wrap(i, fn):
            self.evs[i].wait()
            self.evs[i].clear()
            self.cur = i
            try:
                fn()
            except BaseException as e:
                self.err = e
            self.alive[i] = False
            self._pass(i)
        ths = [threading.Thread(target=wrap, args=(i, f)) for i, (f, _) in enumerate(specs)]
        for t in ths:
            t.start()
        self.evs[0].set()
        self.done_ev.wait()
        for t in ths:
            t.join()
        self.cur = None
        if self.err is not None:
            raise self.err

    def _pass(self, i):
        n = len(self.alive)
        for k in range(1, n + 1):
            j = (i + k) % n
            if self.alive[j]:
                if j == i:
                    return True
                self.evs[j].set()
                return False
        self.done_ev.set()
        return False

    def switch(self):
        i = self.cur
        if i is None:
            return
        self.used[i] += 1
        if self.used[i] < self.quota[i]:
            return
        self.used[i] = 0
        if not self._pass(i):
            self.evs[i].wait()
            self.evs[i].clear()
            self.cur = i


class KB:
    def __init__(self, nc, stack):
        self.nc = nc
        self.stack = stack
        self.eng = {"pe": nc.tensor, "dve": nc.vector, "act": nc.scalar, "pool": nc.gpsimd, "sp": nc.sync}
        self.sems = {}
        self.cnt = {}
        self.seen = {e: {} for e in self.eng}
        for e in self.eng:
            self.sems[e] = stack.enter_context(nc.semaphore("s_" + e))
            self.cnt[e] = 0
        self.n_inst = 0
        self.dma_names = {}
        self.dma_pool = []
        self.coop = Coop()

    def dma_sem(self, name):
        if name not in self.dma_names:
            i = len(self.dma_names)
            if i >= len(self.dma_pool):
                key = "d_%d" % i
                self.sems[key] = self.stack.enter_context(self.nc.semaphore(key))
                self.cnt[key] = 0
                self.dma_pool.append(key)
            self.dma_names[name] = self.dma_pool[i]
        return self.dma_names[name]

    def phase_reset(self):
        self.dma_names = {}

    def _wait(self, e, ev):
        if ev is None:
            return
        key, val = ev
        if key == "pe" and e == "pe":
            return
        if self.seen[e].get(key, 0) >= val:
            return
        self.eng[e].wait_ge(self.sems[key], val)
        self.seen[e][key] = val
        self.n_inst += 1

    def deps(self, e, reads, writes):
        for b in reads:
            self._wait(e, b.w)
        for b in writes:
            self._wait(e, b.w)
            for ev in b.r:
                self._wait(e, ev)

    def done(self, ev, reads, writes):
        for b in reads:
            b.r.append(ev)
            if len(b.r) > 16:
                d = {}
                for k, v in b.r:
                    d[k] = max(d.get(k, 0), v)
                b.r = list(d.items())
        for b in writes:
            b.w = ev
            b.r = []

    def op(self, e, fn, reads=(), writes=()):
        self.deps(e, reads, writes)
        ins = fn()
        self.cnt[e] += 1
        ins.then_inc(self.sems[e], 1)
        ev = (e, self.cnt[e])
        self.done(ev, reads, writes)
        self.n_inst += 1
        self.coop.switch()
        return ev

    def pe(self, fns, reads=(), writes=()):
        self.deps("pe", reads, writes)
        ins = None
        for fn in fns:
            ins = fn()
            self.n_inst += 1
        self.cnt["pe"] += 1
        ins.then_inc(self.sems["pe"], 1)
        ev = ("pe", self.cnt["pe"])
        self.done(ev, reads, writes)
        self.coop.switch()
        return ev

    def dma(self, q, semname, out, in_, reads=(), writes=()):
        key = self.dma_sem(semname)
        self.deps(q, reads, writes)
        ins = self.eng[q].dma_start(out=out, in_=in_)
        self.cnt[key] += 16
        ins.then_inc(self.sems[key], 16)
        ev = (key, self.cnt[key])
        self.done(ev, reads, writes)
        self.n_inst += 1
        self.coop.switch()
        return ev

    def dma_group(self, q, semname, pairs, reads=(), writes=()):
        key = self.dma_sem(semname)
        self.deps(q, reads, writes)
        for out, in_ in pairs:
            ins = self.eng[q].dma_start(out=out, in_=in_)
            self.cnt[key] += 16
            ins.then_inc(self.sems[key], 16)
            self.n_inst += 1
        ev = (key, self.cnt[key])
        self.done(ev, reads, writes)
        self.coop.switch()
        return ev

    def barrier(self):
        for e in self.eng:
            for k in self.cnt:
                if self.cnt[k] > 0:
                    self._wait(e, (k, self.cnt[k]))
        self.phase_reset()


class T:
    uid = [0]

    def __init__(self, st, nc, name, shape, dt, nbuf=1):
        T.uid[0] += 1
        self.t = st.enter_context(nc.sbuf_tensor("%s_u%d" % (name, T.uid[0]), shape, dt))
        self.b = Buf(name)

    def __getitem__(self, k):
        return self.t[k]


SEGS = [(0, 512, "F"), (512, 512, "F"), (1024, 512, "T"), (1536, 256, "X"), (1792, 512, "T"),
        (2304, 512, "F"), (2816, 512, "F"), (3328, 512, "F"), (3840, 512, "F"),
        (4352, 512, "F"), (4864, 512, "F"), (5376, 512, "T"), (5888, 512, "F")] + \
       [(6400 + 512 * i, 512, "F") for i in range(6)]
TCOL = {1024: 0, 1792: 512, 5376: 1024}


def build(SEQ, DEPTH, dbg=None):
    TT = CTX + SEQ
    NCH = TT // 64
    blocks = [(0, CTX, 1)] + [(CTX + 512 * i, 512, 0) for i in range(SEQ // 512)]
    nc = bass.Bass("TRN2", target_bir_lowering=False)
    dram = lambda n, s, d, k="Internal": nc.dram_tensor(n, s, d, kind=k).ap()
    xT_in = dram("xT", [D, TT], F32, "ExternalInput")
    PP = dram("PP", [128, DEPTH * NPL], F32, "ExternalInput")
    CCd = dram("CC", [128, 16], F32, "ExternalInput")
    BCd = dram("BC", [128, DEPTH * 2 * 512], F32, "ExternalInput")
    LAMd = dram("LAM", [1, DEPTH * 256], F32, "ExternalInput")
    KON = dram("KON", [128, 1024], F32, "ExternalInput")
    MSK = dram("MSK", [128, 4 * 512], F32, "ExternalInput")
    ROPE = dram("ROPE", [128, 2 * TT], F32, "ExternalInput")
    LTRI = dram("LTRI", [128, 512], F32, "ExternalInput")
    w_mod = dram("w_mod", [DEPTH, D, 3 * D], F32, "ExternalInput")
    w_in = dram("w_in", [DEPTH, D, NIN], F32, "ExternalInput")
    dup = dram("dup", [DEPTH, 128, 512], F32, "ExternalInput")
    iup = dram("iup", [DEPTH, 128, 512], F32, "ExternalInput")
    w_br = dram("w_br", [DEPTH, 1536, D], F32, "ExternalInput")
    w_out = dram("w_out", [DEPTH, D, D], F32, "ExternalInput")
    yout = dram("yout", [D, SEQ], F32, "ExternalOutput")
    xT = dram("xTs", [D, TT], F32)
    PF = dram("PF", [NIN, TT], BF16)
    PX = dram("PX", [256, TT], F32)
    PT = dram("PT", [TT, 1536], BF16)
    QGs = [dram("QGs%d" % d, [512, NCH, 2, 64], BF16) for d in range(2)]
    AKT = [dram("AKT%d" % d, [512, TT], BF16) for d in range(2)]
    KIT = [dram("KIT%d" % d, [512, TT], BF16) for d in range(2)]
    AKK = [dram("AKK%d" % d, [TT, 512], BF16) for d in range(2)]
    KIK = [dram("KIK%d" % d, [TT, 512], BF16) for d in range(2)]
    GCs = [dram("GCs%d" % d, [512, NCH], F32) for d in range(2)]
    YS = [dram("YS%d" % d, [TT, 512], F32) for d in range(2)]
    YG = dram("YG", [1536, TT], BF16)
    QRs = dram("QRs", [512, TT], BF16)
    KRs = dram("KRs", [512, TT], BF16)
    dbgo = {}
    if dbg:
        for n, shp in dbg.items():
            dbgo[n] = dram("dbg_" + n, shp, F32, "ExternalOutput")

    with ExitStack() as st:
        kb = KB(nc, st)
        V, A, P, PEe = nc.vector, nc.scalar, nc.gpsimd, nc.tensor
        psum = st.enter_context(nc.psum_tensor("psum", [128, 8, 512], F32))
        PB = [Buf("ps%d" % i) for i in range(8)]
        psbf = [psum[:, i, :].bitcast(BF16) for i in range(8)]

        kon = T(st, nc, "kon", [128, 1024], F32)
        konb = T(st, nc, "konb", [128, 1024], BF16)
        msk2 = T(st, nc, "msk2", [128, 4, 8, 64], F32)
        pp = T(st, nc, "pp", [128, DEPTH, NPL], F32)
        cc = T(st, nc, "cc", [128, 8, 2], F32)
        ltri = T(st, nc, "ltri", [128, 512], F32)
        modT = T(st, nc, "modT", [128, DEPTH, 24, 2], F32)
        gsc = T(st, nc, "gsc", [128, DEPTH, 8, 2], F32)
        omka = T(st, nc, "omka", [128, DEPTH, 4], F32)
        lamc = T(st, nc, "lamc", [128, DEPTH, 2], F32)
        sgl = T(st, nc, "sgl", [128, DEPTH], F32)
        kb.dma("sp", "c0", kon[:], KON[:, :], writes=[kon.b])
        kb.dma("sp", "c1", msk2[:].rearrange("p a h x -> p (a h x)"), MSK[:, :], writes=[msk2.b])
        kb.dma("sp", "c2", pp[:].rearrange("p l n -> p (l n)"), PP[:, :], writes=[pp.b])
        kb.dma("sp", "c3", cc[:].rearrange("p a b -> p (a b)"), CCd[:, :], writes=[cc.b])
        kb.dma("sp", "c4", ltri[:], LTRI[:, :], writes=[ltri.b])
        kb.op("dve", lambda: V.tensor_copy(out=konb[:], in_=kon[:]), [kon.b], [konb.b])
        ident_f, ones_f = kon[:, 0:128], kon[:, 128:256]
        ident_b, ones_b, blk_b, perm_f = konb[:, 0:128], konb[:, 128:256], konb[:, 256:384], kon[:, 384:512]
        sel_b = konb[:, 512:544]

        with ExitStack() as s2:
            sc = T(s2, nc, "sc", [128, 8, 2], F32)
            wm = [T(s2, nc, "wm%d" % i, [128, 8, 512], F32) for i in range(2)]
            lamt = T(s2, nc, "lamt", [1, DEPTH, 4, 64], F32)
            lam2 = T(s2, nc, "lam2", [1, DEPTH, 2], F32)
            lam3 = T(s2, nc, "lam3", [1, DEPTH, 2], F32)
            kb.op("act", lambda: A.activation(out=sc[:], in_=cc[:], func=AF.Silu), [cc.b], [sc.b])
            i = 0
            for l in range(DEPTH):
                for nb6 in range(6):
                    w = wm[i % 2]
                    i += 1
                    kb.dma("sp", w.b.name, w[:], w_mod[l].rearrange("(c p) n -> p c n", p=128)[:, :, nb6 * 512:(nb6 + 1) * 512], writes=[w.b])
                    fns = []
                    for n4 in range(4):
                        for c in range(8):
                            fns.append(lambda n4=n4, c=c, w=w: PEe.matmul(psum[:, 0, n4 * 2:n4 * 2 + 2], lhsT=w[:, c, n4 * 128:(n4 + 1) * 128], rhs=sc[:, c, :], start=(c == 0), stop=(c == 7)))
                    kb.pe(fns, [w.b, sc.b], [PB[0]])
                    kb.op("dve", lambda l=l, nb6=nb6: V.tensor_tensor(out=modT[:, l, nb6 * 4:(nb6 + 1) * 4, :], in0=psum[:, 0, 0:8].rearrange("p (a b) -> p a b", b=2),
                                                                   in1=pp[:, l, 8 + nb6 * 4:8 + (nb6 + 1) * 4].unsqueeze(2).to_broadcast([128, 4, 2]), op=ALU.add), [PB[0], pp.b], [modT.b])
                kb.op("dve", lambda l=l: V.tensor_scalar(out=gsc[:, l], in0=modT[:, l, 8:16, :], scalar1=1.0, scalar2=None, op0=ALU.add), [modT.b], [gsc.b])
                kb.op("dve", lambda l=l: V.tensor_tensor(out=gsc[:, l], in0=gsc[:, l], in1=pp[:, l, 0:8].unsqueeze(2).to_broadcast([128, 8, 2]), op=ALU.mult), [gsc.b, pp.b], [gsc.b])
                kb.op("dve", lambda l=l: V.tensor_scalar(out=omka[:, l, :], in0=pp[:, l, 52:56], scalar1=-1.0, scalar2=1.0, op0=ALU.mult, op1=ALU.add), [pp.b], [omka.b])
            kb.dma("sp", "c5", lamt[:].rearrange("p l a x -> p (l a x)"), LAMd[:, :], writes=[lamt.b])
            lv = lamt[:].rearrange("p l (a b) x -> p l a b x", b=2)
            kb.op("dve", lambda: V.tensor_tensor(out=lv[:, :, :, 0, :], in0=lv[:, :, :, 0, :], in1=lv[:, :, :, 1, :], op=ALU.mult), [lamt.b], [lamt.b])
            kb.op("dve", lambda: V.tensor_reduce(out=lam2[:], in_=lv[:, :, :, 0, :], axis=AX.X, op=ALU.add), [lamt.b], [lam2.b])
            kb.op("act", lambda: A.activation(out=lam3[:], in_=lam2[:], func=AF.Exp), [lam2.b], [lam3.b])
            kb.op("dve", lambda: V.tensor_tensor(out=lam2[:, :, 0], in0=lam3[:, :, 1], in1=lam3[:, :, 0], op=ALU.subtract), [lam3.b], [lam2.b])
            for l in range(DEPTH):
                li = 0.8 - 0.6 * math.exp(-0.3 * l)
                kb.op("dve", lambda l=l, li=li: V.tensor_scalar(out=lam2[:, l, 0:1], in0=lam2[:, l, 0:1], scalar1=-li, scalar2=None, op0=ALU.add), [lam2.b], [lam2.b])
                kb.op("dve", lambda l=l, li=li: V.tensor_scalar(out=sgl[:, l:l + 1], in0=pp[:, l, 78:79], scalar1=(1.0 - li), scalar2=None, op0=ALU.mult), [pp.b], [sgl.b])
            kb.pe([lambda: PEe.matmul(psum[:, 0, 0:2 * DEPTH], lhsT=ones_f[0:1, :], rhs=lam2[:].rearrange("p l a -> p (l a)"), start=True, stop=True)], [lam2.b, kon.b], [PB[0]])
            kb.op("dve", lambda: V.tensor_copy(out=lamc[:].rearrange("p l a -> p (l a)"), in_=psum[:, 0, 0:2 * DEPTH]), [PB[0]], [lamc.b])
            kb.barrier()

        with ExitStack() as s2:
            xb_ = [T(s2, nc, "xcp%d" % i, [128, 8, 512], F32) for i in range(2)]
            for bi, (t0, nb, col) in enumerate(blocks):
                x_ = xb_[bi % 2]
                kb.dma("sp", x_.b.name, x_[:, :, 0:nb], xT_in.rearrange("(c p) t -> p c t", p=128)[:, :, t0:t0 + nb], writes=[x_.b])
                kb.dma("sp", x_.b.name, xT.rearrange("(c p) t -> p c t", p=128)[:, :, t0:t0 + nb], x_[:, :, 0:nb], reads=[x_.b])
            kb.barrier()

        xTv = xT.rearrange("(c p) t -> p c t", p=128)
        NTT = TT // 128

        def blk_of(tt):
            t = tt * 128
            for bi, (t0, nb, col) in enumerate(blocks):
                if t0 <= t < t0 + nb:
                    return bi

        evac_rr = [0]

        def evac(out, in_, rb, wb, scale=None):
            evac_rr[0] += 1
            if evac_rr[0] % 2 == 0:
                if scale is None:
                    return kb.op("act", lambda: A.copy(out=out, in_=in_), rb, wb)
                return kb.op("act", lambda: A.mul(out=out, in_=in_, mul=scale), rb, wb)
            if scale is None:
                return kb.op("dve", lambda: V.tensor_copy(out=out, in_=in_), rb, wb)
            return kb.op("dve", lambda: V.tensor_scalar(out=out, in0=in_, scalar1=scale, scalar2=None, op0=ALU.mult), rb, wb)

        def run_threads(specs):
            th = [[g, 0, float(tot)] for g, tot in specs]
            while th:
                t_ = min(th, key=lambda x: x[1] / x[2])
                try:
                    next(t_[0])
                    t_[1] += 1
                except StopIteration:
                    th.remove(t_)

        def phase_AB(l):
            with ExitStack() as s2:
                hT = T(s2, nc, "hT", [128, 8, TT], BF16)
                hTb = [Buf("hT%d" % i) for i in range(len(blocks))]
                xa = T(s2, nc, "xa", [128, 8, 512], F32)
                sq = T(s2, nc, "sq", [128, 8, 512], F32)
                rs = T(s2, nc, "rs", [128, 512], F32)
                tmp = [T(s2, nc, "tmpa%d" % i, [128, 512], F32) for i in range(2)]
                wf = [T(s2, nc, "wf%d" % i, [128, 8, 512], F32) for i in range(2)]
                wb = [T(s2, nc, "wb%d" % i, [128, 8, 512], BF16) for i in range(2)]
                ost = [T(s2, nc, "ost%d" % i, [128, 512], BF16) for i in range(4)]
                osx = [T(s2, nc, "osx%d" % i, [128, 512], F32) for i in range(2)]
                w_inv = w_in[l].rearrange("(c p) n -> p c n", p=128)

                def loadw(si):
                    c0, wd, kind = SEGS[si]
                    f, b = wf[si % 2], wb[si % 2]
                    kb.dma("sp", f.b.name, f[:, :, 0:wd], w_inv[:, :, c0:c0 + wd], writes=[f.b])
                    kb.op("pool", lambda: P.tensor_copy(out=b[:, :, 0:wd], in_=f[:, :, 0:wd]), [f.b], [b.b])

                loadw(0)
                for bi, (t0, nb, col) in enumerate(blocks):
                    kb.dma("sp", "xa", xa[:, :, 0:nb], xTv[:, :, t0:t0 + nb], writes=[xa.b])
                    kb.op("act", lambda: A.activation(out=sq[:, :, 0:nb], in_=xa[:, :, 0:nb], func=AF.Square), [xa.b], [sq.b])
                    kb.pe([lambda c=c: PEe.matmul(psum[:, 0, 0:nb], lhsT=ones_f, rhs=sq[:, c, 0:nb], start=(c == 0), stop=(c == 7)) for c in range(8)],
                          [sq.b, kon.b], [PB[0]])
                    kb.op("act", lambda: A.activation(out=rs[:, 0:nb], in_=psum[:, 0, 0:nb], func=AF.Sqrt, bias=RMS_EPS, scale=1.0 / D), [PB[0]], [rs.b])
                    kb.op("dve", lambda: V.reciprocal(out=rs[:, 0:nb], in_=rs[:, 0:nb]), [rs.b], [rs.b])
                    for c in range(8):
                        tm = tmp[c % 2]
                        kb.op("dve", lambda: V.scalar_tensor_tensor(out=tm[:, 0:nb], in0=xa[:, c, 0:nb], scalar=gsc[:, l, c, col:col + 1], in1=rs[:, 0:nb],
                                                                    op0=ALU.mult, op1=ALU.mult), [xa.b, rs.b, gsc.b], [tm.b])
                        kb.op("act", lambda: A.activation(out=hT[:, c, t0:t0 + nb], in_=tm[:, 0:nb], func=AF.Identity, bias=modT[:, l, c, col:col + 1], scale=1.0),
                              [tm.b, modT.b], [hTb[bi]])
                k = 0
                pk = 0
                for si, (c0, wd, kind) in enumerate(SEGS):
                    if si + 1 < len(SEGS):
                        loadw(si + 1)
                    b = wb[si % 2]
                    if kind in "FX":
                        for bi, (t0, nb, col) in enumerate(blocks):
                            for n4 in range(wd // 128):
                                bank = 1 + pk % 6
                                pk += 1
                                kb.pe([lambda c=c: PEe.matmul(psum[:, bank, 0:nb], lhsT=b[:, c, n4 * 128:(n4 + 1) * 128], rhs=hT[:, c, t0:t0 + nb], start=(c == 0), stop=(c == 7))
                                       for c in range(8)], [b.b, hTb[bi]], [PB[bank]])
                                if kind == "F":
                                    o = ost[k % 4]
                                    k += 1
                                    evac(o[:, 0:nb], psum[:, bank, 0:nb], [PB[bank]], [o.b])
                                    kb.dma("sp", o.b.name, PF[c0 + n4 * 128:c0 + (n4 + 1) * 128, t0:t0 + nb], o[:, 0:nb], reads=[o.b])
                                else:
                                    o = osx[k % 2]
                                    k += 1
                                    evac(o[:, 0:nb], psum[:, bank, 0:nb], [PB[bank]], [o.b])
                                    kb.dma("sp", o.b.name, PX[n4 * 128:(n4 + 1) * 128, t0:t0 + nb], o[:, 0:nb], reads=[o.b])
                    else:
                        for tt in range(NTT):
                            bank = 1 + pk % 6
                            pk += 1
                            kb.pe([lambda c=c: PEe.matmul(psum[:, bank, 0:512], lhsT=hT[:, c, tt * 128:(tt + 1) * 128], rhs=b[:, c, 0:512], start=(c == 0), stop=(c == 7))
                                   for c in range(8)], [b.b, hTb[blk_of(tt)]], [PB[bank]])
                            o = ost[k % 4]
                            k += 1
                            evac(o[:, :], psum[:, bank, :], [PB[bank]], [o.b])
                            kb.dma("sp", o.b.name, PT[tt * 128:(tt + 1) * 128, TCOL[c0]:TCOL[c0] + 512], o[:, :], reads=[o.b])
                kb.barrier()

        def phase_C1(l, s2):
            if True:
                rT = T(s2, nc, "rT", [128, 4, 512], BF16)
                kT = T(s2, nc, "kT", [128, 4, 512], BF16)
                lw = T(s2, nc, "lw", [128, 512], F32)
                la = T(s2, nc, "la", [128, 512], F32)
                tl = T(s2, nc, "tl", [128, 512], BF16)
                lab = T(s2, nc, "lab", [128, 512], BF16)
                upf = T(s2, nc, "upf", [128, 2, 512], F32)
                upb = T(s2, nc, "upb", [128, 2, 512], BF16)
                names = ["kkraw", "rk", "kk", "sg", "a_", "akk", "t_", "kd", "csf", "cs", "cse", "Ep", "Em", "Epv", "tmpb"]
                f = {n: T(s2, nc, "c1_" + n, [128, 512], F32) for n in names}
                sqk = T(s2, nc, "sqk", [128, 512], BF16)
                ob = {n: [T(s2, nc, "c1o_%s%d" % (n, i), [128, 512], BF16) for i in range(2)] for n in ["kkg", "rg", "aki", "ki"]}
                gcs = [T(s2, nc, "gcs%d" % i, [128, 8], F32) for i in range(2)]
                tk = [T(s2, nc, "tk%d" % i, [128, 2, 4, 128], BF16) for i in range(2)]
                kb.dma("sp", "upf", upf[:, 0, :], dup[l], writes=[upf.b])
                kb.dma("sp", "upf", upf[:, 1, :], iup[l], writes=[upf.b])
                kb.op("dve", lambda: V.tensor_copy(out=upb[:], in_=upf[:]), [upf.b], [upb.b])
                it = 0
                for bi, (t0, nb, col) in enumerate(blocks):
                    nchk = nb // 64
                    ch0 = t0 // 64
                    ntt = nb // 128
                    kb.dma("sp", "rT", rT[:, :, 0:nb], PF[0:512, t0:t0 + nb].rearrange("(c p) t -> p c t", p=128), writes=[rT.b])
                    kb.dma("sp", "kT", kT[:, :, 0:nb], PF[512:1024, t0:t0 + nb].rearrange("(c p) t -> p c t", p=128), writes=[kT.b])
                    kb.dma("sp", "lw", lw[:, 0:nb], PX[0:128, t0:t0 + nb], writes=[lw.b])
                    kb.dma("sp", "la", la[:, 0:nb], PX[128:256, t0:t0 + nb], writes=[la.b])
                    kb.op("act", lambda: A.activation(out=tl[:, 0:nb], in_=lw[:, 0:nb], func=AF.Tanh), [lw.b], [tl.b])
                    kb.op("dve", lambda: V.tensor_copy(out=lab[:, 0:nb], in_=la[:, 0:nb]), [la.b], [lab.b])
                    for cc_ in range(4):
                        S = slice(0, nb)
                        kb.op("dve", lambda: V.tensor_scalar(out=f["kkraw"][:, S], in0=kT[:, cc_, S], scalar1=pp[:, l, 48 + cc_:49 + cc_], scalar2=None, op0=ALU.mult),
                              [kT.b, pp.b], [f["kkraw"].b])
                        kb.op("act", lambda: A.activation(out=sqk[:, S], in_=f["kkraw"][:, S], func=AF.Square), [f["kkraw"].b], [sqk.b])
                        kb.pe([lambda: PEe.matmul(psum[:, 0, S], lhsT=blk_b, rhs=sqk[:, S], start=True, stop=True)], [sqk.b, konb.b], [PB[0]])
                        kb.op("act", lambda: A.activation(out=f["rk"][:, S], in_=psum[:, 0, S], func=AF.Sqrt, bias=1e-12, scale=1.0), [PB[0]], [f["rk"].b])
                        kb.op("dve", lambda: V.reciprocal(out=f["rk"][:, S], in_=f["rk"][:, S]), [f["rk"].b], [f["rk"].b])
                        kb.op("dve", lambda: V.tensor_tensor(out=f["kk"][:, S], in0=f["kkraw"][:, S], in1=f["rk"][:, S], op=ALU.mult), [f["kkraw"].b, f["rk"].b], [f["kk"].b])
                        for d in range(2):
                            R = slice(d * 64, (d + 1) * 64)
                            C = slice(cc_ * 128, (cc_ + 1) * 128)
                            o = {n: ob[n][it % 2] for n in ob}
                            g_ = gcs[it % 2]
                            tk_ = tk[it % 2]
                            it += 1
                            kb.pe([lambda: PEe.matmul(psum[:, 1, S], lhsT=upb[R, 0, C], rhs=tl[R, S], start=True, stop=True)], [upb.b, tl.b], [PB[1]])
                            kb.op("act", lambda: A.activation(out=f["sg"][:, S], in_=psum[:, 1, S], func=AF.Sigmoid, bias=pp[:, l, 32 + d * 4 + cc_:33 + d * 4 + cc_], scale=1.0),
                                  [PB[1], pp.b], [f["sg"].b])
                            kb.pe([lambda: PEe.matmul(psum[:, 2, S], lhsT=upb[R, 1, C], rhs=lab[R, S], start=True, stop=True)], [upb.b, lab.b], [PB[2]])
                            kb.op("act", lambda: A.activation(out=f["a_"][:, S], in_=psum[:, 2, S], func=AF.Sigmoid, bias=pp[:, l, 40 + d * 4 + cc_:41 + d * 4 + cc_], scale=1.0),
                                  [PB[2], pp.b], [f["a_"].b])
                            kb.op("pool", lambda: P.tensor_tensor(out=f["akk"][:, S], in0=f["a_"][:, S], in1=f["kk"][:, S], op=ALU.mult), [f["a_"].b, f["kk"].b], [f["akk"].b])
                            kb.op("dve", lambda: V.tensor_scalar(out=f["t_"][:, S], in0=f["a_"][:, S], scalar1=pp[:, l, 52 + cc_:53 + cc_], scalar2=omka[:, l, cc_:cc_ + 1],
                                                                 op0=ALU.mult, op1=ALU.add), [f["a_"].b, pp.b, omka.b], [f["t_"].b])
                            kb.op("pool", lambda: P.tensor_tensor(out=f["kd"][:, S], in0=f["t_"][:, S], in1=kT[:, cc_, S], op=ALU.mult), [f["t_"].b, kT.b], [f["kd"].b])
                            kb.op("dve", lambda: V.tensor_tensor_scan(out=f["csf"][:, S], data0=ltri[:, S], data1=f["sg"][:, S], initial=0.0, op0=ALU.mult, op1=ALU.add),
                                  [ltri.b, f["sg"].b], [f["csf"].b])
                            if d == 0:
                                cs = f["csf"]
                            else:
                                cs = f["cs"]
                                kb.op("dve", lambda: V.tensor_tensor(out=f["tmpb"][:, S], in0=f["sg"][:, S], in1=f["csf"][:, S], op=ALU.subtract), [f["sg"].b, f["csf"].b], [f["tmpb"].b])
                                kb.op("dve", lambda: V.tensor_tensor(out=cs[:, S].rearrange("p (c x) -> p c x", x=64), in0=f["tmpb"][:, S].rearrange("p (c x) -> p c x", x=64),
                                                                     in1=f["csf"][:, S].rearrange("p (c x) -> p c x", x=64)[:, :, 63:64].to_broadcast([128, nchk, 64]), op=ALU.add),
                                      [f["tmpb"].b, f["csf"].b], [cs.b])
                            kb.op("pool", lambda: P.tensor_tensor(out=f["cse"][:, S], in0=cs[:, S], in1=f["sg"][:, S], op=ALU.subtract), [cs.b, f["sg"].b], [f["cse"].b])
                            kb.op("act", lambda: A.activation(out=f["Ep"][:, S], in_=cs[:, S], func=AF.Exp, scale=-C0), [cs.b], [f["Ep"].b])
                            kb.op("act", lambda: A.activation(out=f["Em"][:, S], in_=cs[:, S], func=AF.Exp, scale=C0), [cs.b], [f["Em"].b])
                            kb.op("act", lambda: A.activation(out=f["Epv"][:, S], in_=f["cse"][:, S], func=AF.Exp, scale=-C0), [f["cse"].b], [f["Epv"].b])
                            kb.op("dve", lambda: V.tensor_tensor(out=o["rg"][:, S], in0=rT[:, cc_, S], in1=f["Ep"][:, S], op=ALU.mult), [rT.b, f["Ep"].b], [o["rg"].b])
                            kb.op("pool", lambda: P.tensor_tensor(out=o["kkg"][:, S], in0=f["kk"][:, S], in1=f["Epv"][:, S], op=ALU.mult), [f["kk"].b, f["Epv"].b], [o["kkg"].b])
                            kb.op("dve", lambda: V.tensor_tensor(out=o["aki"][:, S], in0=f["akk"][:, S], in1=f["Em"][:, S], op=ALU.mult), [f["akk"].b, f["Em"].b], [o["aki"].b])
                            kb.op("pool", lambda: P.tensor_tensor(out=o["ki"][:, S], in0=f["kd"][:, S], in1=f["Em"][:, S], op=ALU.mult), [f["kd"].b, f["Em"].b], [o["ki"].b])
                            e_ = 63 if d == 0 else 0
                            kb.op("dve", lambda: V.tensor_copy(out=g_[:, 0:nchk], in_=f["Ep"][:, S].rearrange("p (c x) -> p c x", x=64)[:, :, e_]), [f["Ep"].b], [g_.b])
                            kb.dma("sp", o["kkg"].b.name, QGs[d][C, ch0:ch0 + nchk, 0, :], o["kkg"][:, S].rearrange("p (c x) -> p c x", x=64), reads=[o["kkg"].b])
                            kb.dma("sp", o["rg"].b.name, QGs[d][C, ch0:ch0 + nchk, 1, :], o["rg"][:, S].rearrange("p (c x) -> p c x", x=64), reads=[o["rg"].b])
                            kb.dma("sp", o["aki"].b.name, AKT[d][C, t0:t0 + nb], o["aki"][:, S], reads=[o["aki"].b])
                            kb.dma("sp", o["ki"].b.name, KIT[d][C, t0:t0 + nb], o["ki"][:, S], reads=[o["ki"].b])
                            kb.dma("sp", g_.b.name, GCs[d][C, ch0:ch0 + nchk], g_[:, 0:nchk], reads=[g_.b])
                            fns = []
                            for qi, nm in enumerate(["aki", "ki"]):
                                for tt in range(ntt):
                                    fns.append(lambda qi=qi, nm=nm, tt=tt: PEe.transpose(psbf[3][:, (qi * 4 + tt) * 128:(qi * 4 + tt + 1) * 128], o[nm][:, tt * 128:(tt + 1) * 128], ident_b))
                            kb.pe(fns, [o["aki"].b, o["ki"].b, konb.b], [PB[3]])
                            kb.op("act", lambda: A.copy(out=tk_[:].rearrange("p q t c -> p (q t c)"), in_=psbf[3][:, :]), [PB[3]], [tk_.b])
                            kb.dma("sp", tk_.b.name + "a", AKK[d][t0:t0 + nb, C].rearrange("(t p) c -> p t c", p=128), tk_[:, 0, 0:ntt, :], reads=[tk_.b])
                            kb.dma("sp", tk_.b.name + "b", KIK[d][t0:t0 + nb, C].rearrange("(t p) c -> p t c", p=128), tk_[:, 1, 0:ntt, :], reads=[tk_.b])
                            yield

        def phase_C2(l):
            with ExitStack() as s2:
                def mk(nm, shape, dt):
                    ts = [T(s2, nc, "%s%d" % (nm, i), shape, dt) for i in range(2)]
                    for t_ in ts:
                        t_.b2 = Buf(t_.b.name + "x")
                    return ts
                QG = mk("QG", [128, 8, 8, 128], BF16)
                AT = mk("AT", [128, 8, 512], BF16)
                KT_ = mk("KT", [128, 8, 512], BF16)
                AK = mk("AK", [128, 8, 512], BF16)
                KK_ = mk("KK", [128, 8, 512], BF16)
                VV = mk("VV", [128, 8, 512], BF16)
                GC = mk("GC", [128, 8, 8], F32)
                Z = T(s2, nc, "Z", [128, 8, 64], F32)
                Zb = T(s2, nc, "Zb", [128, 8, 64], BF16)
                NT = [T(s2, nc, "NT%d" % i, [128, 8, 64], BF16) for i in range(5)]
                N_ = [T(s2, nc, "N%d" % i, [128, 8, 64], BF16) for i in range(6)]
                PTt = [T(s2, nc, "PTt%d" % i, [128, 8, 64], BF16) for i in range(2)]
                BabT = T(s2, nc, "BabT", [128, 8, 64], BF16)
                AakT = T(s2, nc, "AakT", [128, 8, 64], BF16)
                BakT = T(s2, nc, "BakT", [128, 8, 64], BF16)
                Wn = T(s2, nc, "Wn", [128, 8, 64], BF16)
                U = T(s2, nc, "U", [128, 8, 64], BF16)
                Yst = [T(s2, nc, "Yst%d" % i, [128, 512], F32) for i in range(2)]
                yi = [0]
                DS = [slice(0, 64), slice(64, 128)]
                nB = len(blocks)
                fo = list(range(nB))
                bo = [0] + list(range(nB - 1, 0, -1))

                def hv(bank):
                    return psum[:, bank, :].rearrange("p (h i) -> p h i", h=8)

                def load(s, par):
                    for d in range(2):
                        t0, nb, col = blocks[fo[s] if d == 0 else bo[s]]
                        nchk, ch0 = nb // 64, t0 // 64
                        if d == 0:
                            csl = slice(ch0, ch0 + nchk)
                        else:
                            csl = slice(ch0 + nchk - 1, (ch0 - 1) if ch0 > 0 else None, -1)
                        D_ = DS[d]
                        bsel = (lambda t_: t_.b) if d == 0 else (lambda t_: t_.b2)
                        sfx = "f" if d == 0 else "b"
                        def ld(t_, dst, src):
                            kb.dma("sp", t_.b.name + sfx, dst, src, writes=[bsel(t_)])

                        def ldh(t_, dstf, srcf):
                            if d == 0:
                                ld(t_, dstf(slice(None)), srcf(slice(None)))
                            else:
                                kb.dma_group("sp", t_.b.name + sfx, [(dstf(h), srcf(h)) for h in range(8)], writes=[bsel(t_)])
                        qsrc = QGs[d].rearrange("(h k) c w i -> k h c (w i)", k=64)
                        ldh(QG[par], lambda h: QG[par][D_, h, 0:nchk, :], lambda h: qsrc[:, h, csl, :])
                        asrc = AKT[d].rearrange("(h k) (c x) -> k h c x", k=64, x=64)
                        ldh(AT[par], lambda h: AT[par][D_, h, 0:nb].rearrange("p (c x) -> p c x", x=64) if not isinstance(h, slice) else AT[par][D_, :, 0:nb].rearrange("p h (c x) -> p h c x", x=64),
                            lambda h: asrc[:, h, csl, :])
                        ksrc = KIT[d].rearrange("(h k) (c x) -> k h c x", k=64, x=64)
                        ldh(KT_[par], lambda h: KT_[par][D_, h, 0:nb].rearrange("p (c x) -> p c x", x=64) if not isinstance(h, slice) else KT_[par][D_, :, 0:nb].rearrange("p h (c x) -> p h c x", x=64),
                            lambda h: ksrc[:, h, csl, :])
                        ld(AK[par], AK[par][D_, 0:nchk, :], AKK[d].rearrange("(c p) n -> p c n", p=64)[:, csl, :])
                        ld(KK_[par], KK_[par][D_, 0:nchk, :], KIK[d].rearrange("(c p) n -> p c n", p=64)[:, csl, :])
                        ld(VV[par], VV[par][D_, 0:nchk, :], PT[:, 0:512].rearrange("(c p) n -> p c n", p=64)[:, csl, :])
                        ld(GC[par], GC[par][D_, :, 0:nchk], GCs[d].rearrange("(h k) c -> k h c", k=64)[:, :, ch0:ch0 + nchk])

                def chunk(par, j, nchk, tokf, tokb):
                    qg, at, kt, ak, kk_, vv, gc = QG[par], AT[par], KT_[par], AK[par], KK_[par], VV[par], GC[par]
                    bb = lambda *ts: [x for t_ in ts for x in (t_.b, t_.b2)]
                    cs = slice(j * 64, j * 64 + 64)
                    H = lambda h: slice(h * 64, (h + 1) * 64)
                    HD = [(h, D_) for h in range(8) for D_ in DS]
                    fns = []
                    for h, D_ in HD:
                        fns.append(lambda h=h, D_=D_: PEe.matmul(psum[D_, h // 4, (h % 4) * 128:(h % 4) * 128 + 128], lhsT=at[D_, h, cs], rhs=qg[D_, h, j, :], start=True, stop=True))
                    for h, D_ in HD:
                        fns.append(lambda h=h, D_=D_: PEe.matmul(psum[D_, 2 + h // 4, (h % 4) * 128:(h % 4) * 128 + 128], lhsT=kt[D_, h, cs], rhs=qg[D_, h, j, :], start=True, stop=True))
                    for h, D_ in HD:
                        fns.append(lambda h=h, D_=D_: PEe.matmul(psum[D_, 4, H(h)], lhsT=qg[D_, h, j, 0:64], rhs=at[D_, h, cs], start=True, stop=True))
                    kb.pe(fns, bb(qg, at, kt), [PB[0], PB[1], PB[2], PB[3], PB[4]])
                    P1v = psum[:, 0:2, :].rearrange("p b (h w i) -> p (b h) w i", h=4, w=2)
                    P2v = psum[:, 2:4, :].rearrange("p b (h w i) -> p (b h) w i", h=4, w=2)
                    kb.op("dve", lambda: V.scalar_tensor_tensor(out=NT[0][:], in0=P1v[:, :, 0, :], scalar=-1.0, in1=msk2[:, 0], op0=ALU.mult, op1=ALU.mult), [PB[0], PB[1], msk2.b], [NT[0].b])
                    kb.op("dve", lambda: V.scalar_tensor_tensor(out=N_[0][:], in0=hv(4), scalar=-1.0, in1=msk2[:, 2], op0=ALU.mult, op1=ALU.mult), [PB[4], msk2.b], [N_[0].b])
                    kb.op("dve", lambda: V.tensor_tensor(out=AakT[:], in0=P2v[:, :, 0, :], in1=msk2[:, 0], op=ALU.mult), [PB[2], PB[3], msk2.b], [AakT.b])
                    kb.op("dve", lambda: V.tensor_tensor(out=BabT[:], in0=P1v[:, :, 1, :], in1=msk2[:, 1], op=ALU.mult), [PB[0], PB[1], msk2.b], [BabT.b])
                    kb.op("dve", lambda: V.tensor_tensor(out=BakT[:], in0=P2v[:, :, 1, :], in1=msk2[:, 1], op=ALU.mult), [PB[2], PB[3], msk2.b], [BakT.b])
                    kb.op("pool", lambda: P.tensor_tensor(out=PTt[0][:], in0=NT[0][:], in1=msk2[:, 3], op=ALU.add), [NT[0].b, msk2.b], [PTt[0].b])
                    fns = []
                    for h, D_ in HD:
                        fns.append(lambda h=h, D_=D_: PEe.matmul(psum[D_, 5, H(h)], lhsT=qg[D_, h, j, 0:64], rhs=Zb[D_, h, :], start=True, stop=False))
                        fns.append(lambda h=h, D_=D_: PEe.matmul(psum[D_, 5, H(h)], lhsT=AakT[D_, h, :], rhs=vv[D_, j, H(h)], start=False, stop=True))
                    kb.pe(fns, bb(qg, vv) + [Zb.b, AakT.b], [PB[5]])
                    kb.op("act", lambda: A.mul(out=Wn[:], in_=hv(5), mul=-1.0), [PB[5]], [Wn.b])
                    for k in range(5):
                        fns = [lambda h=h, D_=D_: PEe.matmul(psum[D_, 6, H(h)], lhsT=NT[k][D_, h, :], rhs=N_[k][D_, h, :], start=True, stop=True) for h, D_ in HD]
                        kb.pe(fns, [NT[k].b, N_[k].b], [PB[6]])
                        kb.op("act", lambda: A.copy(out=N_[k + 1][:], in_=hv(6)), [PB[6]], [N_[k + 1].b])
                        if k < 4:
                            fns = [lambda h=h, D_=D_: PEe.matmul(psum[D_, 7, H(h)], lhsT=N_[k][D_, h, :], rhs=NT[k][D_, h, :], start=True, stop=True) for h, D_ in HD]
                            kb.pe(fns, [NT[k].b, N_[k].b], [PB[7]])
                            kb.op("act", lambda: A.copy(out=NT[k + 1][:], in_=hv(7)), [PB[7]], [NT[k + 1].b])
                        src, dst = PTt[k % 2], PTt[(k + 1) % 2]
                        fns = [lambda h=h, D_=D_: PEe.matmul(psum[D_, 0, H(h)], lhsT=N_[k + 1][D_, h, :], rhs=src[D_, h, :], start=True, stop=True) for h, D_ in HD]
                        kb.pe(fns, [N_[k + 1].b, src.b], [PB[0]])
                        kb.op("dve", lambda: V.tensor_tensor(out=dst[:], in0=hv(0), in1=src[:], op=ALU.add), [PB[0], src.b], [dst.b])
                    TTf = PTt[1]
                    fns = [lambda h=h, D_=D_: PEe.matmul(psum[D_, 1, H(h)], lhsT=TTf[D_, h, :], rhs=Wn[D_, h, :], start=True, stop=True) for h, D_ in HD]
                    kb.pe(fns, [TTf.b, Wn.b], [PB[1]])
                    kb.op("dve", lambda: V.tensor_copy(out=U[:], in_=hv(1)), [PB[1]], [U.b])
                    fns = []
                    for h, D_ in HD:
                        fns.append(lambda h=h, D_=D_: PEe.matmul(psum[D_, 2, H(h)], lhsT=qg[D_, h, j, 64:128], rhs=Zb[D_, h, :], start=True, stop=False))
                        fns.append(lambda h=h, D_=D_: PEe.matmul(psum[D_, 2, H(h)], lhsT=BabT[D_, h, :], rhs=U[D_, h, :], start=False, stop=False))
                        fns.append(lambda h=h, D_=D_: PEe.matmul(psum[D_, 2, H(h)], lhsT=BakT[D_, h, :], rhs=vv[D_, j, H(h)], start=False, stop=True))
                    kb.pe(fns, bb(qg, vv) + [Zb.b, BabT.b, U.b, BakT.b], [PB[2]])
                    ys = Yst[yi[0] % 2]
                    yi[0] += 1
                    kb.op("act", lambda: A.copy(out=ys[:], in_=psum[:, 2, :]), [PB[2]], [ys.b])
                    kb.dma("sp", ys.b.name + "f", YS[0][tokf:tokf + 64, :], ys[0:64, :], reads=[ys.b])
                    kb.dma("sp", ys.b.name + "b", YS[1][tokb:tokb + 64, :], ys[64:128, :], reads=[ys.b])
                    fns = []
                    for h, D_ in HD:
                        fns.append(lambda h=h, D_=D_: PEe.matmul(psum[D_, 3, H(h)], lhsT=ak[D_, j, H(h)], rhs=U[D_, h, :], start=True, stop=False))
                        fns.append(lambda h=h, D_=D_: PEe.matmul(psum[D_, 3, H(h)], lhsT=kk_[D_, j, H(h)], rhs=vv[D_, j, H(h)], start=False, stop=True))
                    kb.pe(fns, bb(ak, kk_, vv) + [U.b], [PB[3]])
                    kb.op("dve", lambda: V.tensor_tensor(out=Z[:], in0=hv(3), in1=Z[:], op=ALU.add), [PB[3], Z.b], [Z.b])
                    jb = nchk - 1 - j
                    kb.op("dve", lambda: V.tensor_tensor(out=Z[0:64], in0=Z[0:64], in1=gc[0:64, :, j:j + 1].to_broadcast([64, 8, 64]), op=ALU.mult), [Z.b, gc.b], [Z.b])
                    kb.op("pool", lambda: P.tensor_tensor(out=Z[64:128], in0=Z[64:128], in1=gc[64:128, :, jb:jb + 1].to_broadcast([64, 8, 64]), op=ALU.mult), [Z.b, gc.b2], [Z.b])
                    kb.op("act", lambda: A.copy(out=Zb[:], in_=Z[:]), [Z.b], [Zb.b])

                kb.op("dve", lambda: V.memset(Z[:], 0.0), [], [Z.b])
                kb.op("dve", lambda: V.memset(Zb[:], 0.0), [], [Zb.b])
                load(0, 0)
                for s in range(nB):
                    par = s % 2
                    if s + 1 < nB:
                        load(s + 1, 1 - par)
                    t0f, nb, _ = blocks[fo[s]]
                    t0b, _, _ = blocks[bo[s]]
                    nchk = nb // 64
                    for j in range(nchk):
                        chunk(par, j, nchk, t0f + j * 64, t0b + (nchk - 1 - j) * 64)
                kb.barrier()

        def phase_C3(l, s2):
            if True:
                bc = T(s2, nc, "bcg", [128, 2, 512], F32)
                kb.dma("sp", "bcg", bc[:].rearrange("p a n -> p (a n)"), BCd[:, l * 1024:(l + 1) * 1024], writes=[bc.b])
                def mk(nm, shape, dt):
                    return [T(s2, nc, "%s%d" % (nm, i), shape, dt) for i in range(2)]
                yf, yb = mk("yf", [128, 512], F32), mk("yb", [128, 512], F32)
                vz = mk("vz", [128, 1024], BF16)
                rk_ = mk("rk_", [128, 2, 4, 128], BF16)
                rkr = mk("rkr", [128, 4, 128], BF16)
                y = mk("y_", [128, 512], F32)
                ysq = mk("ysq", [128, 512], F32)
                st_ = mk("st_", [128, 6, 8], F32)
                sz = mk("sz", [128, 512], F32)
                bon = mk("bon", [128, 512], F32)
                yg = mk("yg", [128, 512], BF16)
                ygT = mk("ygT", [128, 4, 128], BF16)
                for tt in range(NTT):
                    p = tt % 2
                    ts_ = slice(tt * 128, (tt + 1) * 128)
                    kb.dma("sp", yf[p].b.name, yf[p][:], YS[0][ts_, :], writes=[yf[p].b])
                    kb.dma("sp", yb[p].b.name, yb[p][:], YS[1][ts_, :], writes=[yb[p].b])
                    kb.dma("sp", vz[p].b.name, vz[p][:], PT[ts_, 0:1024], writes=[vz[p].b])
                    kb.dma("sp", rk_[p].b.name, rk_[p][:].rearrange("p a c t -> p (a c) t"), PF[0:1024, ts_].rearrange("(c p) t -> p c t", p=128), writes=[rk_[p].b])
                    for c in range(4):
                        kb.op("dve", lambda: V.scalar_tensor_tensor(out=rkr[p][:, c, :], in0=rk_[p][:, 0, c, :], scalar=pp[:, l, 56 + c:57 + c], in1=rk_[p][:, 1, c, :],
                                                                     op0=ALU.mult, op1=ALU.mult), [rk_[p].b, pp.b], [rkr[p].b])
                    kb.pe([lambda c=c: PEe.matmul(psum[:, 6, 0:8], lhsT=rkr[p][:, c, :], rhs=sel_b[:, c * 8:(c + 1) * 8], start=(c == 0), stop=(c == 3)) for c in range(4)],
                          [rkr[p].b, konb.b], [PB[6]])
                    S_ = st_[p]
                    kb.op("act", lambda: A.copy(out=S_[:, 5, :], in_=psum[:, 6, 0:8]), [PB[6]], [S_.b])
                    kb.op("pool", lambda: P.tensor_tensor(out=y[p][:], in0=yf[p][:], in1=yb[p][:], op=ALU.add), [yf[p].b, yb[p].b], [y[p].b])
                    y3 = y[p][:].rearrange("p (h x) -> p h x", h=8)
                    kb.op("dve", lambda: V.tensor_reduce(out=S_[:, 0, :], in_=y3, axis=AX.X, op=ALU.add), [y[p].b], [S_.b])
                    kb.op("act", lambda: A.activation(out=ysq[p][:], in_=y[p][:], func=AF.Square), [y[p].b], [ysq[p].b])
                    kb.op("dve", lambda: V.tensor_reduce(out=S_[:, 1, :], in_=ysq[p][:].rearrange("p (h x) -> p h x", h=8), axis=AX.X, op=ALU.add), [ysq[p].b], [S_.b])
                    kb.op("dve", lambda: V.tensor_scalar(out=S_[:, 2, :], in0=S_[:, 0, :], scalar1=1.0 / 64, scalar2=None, op0=ALU.mult), [S_.b], [S_.b])
                    kb.op("dve", lambda: V.tensor_tensor(out=S_[:, 3, :], in0=S_[:, 2, :], in1=S_[:, 2, :], op=ALU.mult), [S_.b], [S_.b])
                    kb.op("dve", lambda: V.scalar_tensor_tensor(out=S_[:, 3, :], in0=S_[:, 1, :], scalar=1.0 / 64, in1=S_[:, 3, :], op0=ALU.mult, op1=ALU.subtract), [S_.b], [S_.b])
                    kb.op("act", lambda: A.activation(out=S_[:, 4, :], in_=S_[:, 3, :], func=AF.Sqrt, bias=GN_EPS, scale=1.0), [S_.b], [S_.b])
                    kb.op("dve", lambda: V.reciprocal(out=S_[:, 4, :], in_=S_[:, 4, :]), [S_.b], [S_.b])
                    kb.op("dve", lambda: V.tensor_tensor(out=y3, in0=y3, in1=S_[:, 2, :].unsqueeze(2).to_broadcast([128, 8, 64]), op=ALU.subtract), [y[p].b, S_.b], [y[p].b])
                    kb.op("dve", lambda: V.tensor_tensor(out=y3, in0=y3, in1=S_[:, 4, :].unsqueeze(2).to_broadcast([128, 8, 64]), op=ALU.mult), [y[p].b, S_.b], [y[p].b])
                    kb.op("pool", lambda: P.tensor_tensor(out=y[p][:], in0=y[p][:], in1=bc[:, 0, :], op=ALU.mult), [y[p].b, bc.b], [y[p].b])
                    kb.op("pool", lambda: P.tensor_tensor(out=y[p][:], in0=y[p][:], in1=bc[:, 1, :], op=ALU.add), [y[p].b, bc.b], [y[p].b])
                    kb.op("dve", lambda: V.tensor_tensor(out=bon[p][:].rearrange("p (h x) -> p h x", h=8), in0=vz[p][:, 0:512].rearrange("p (h x) -> p h x", h=8),
                                                         in1=S_[:, 5, :].unsqueeze(2).to_broadcast([128, 8, 64]), op=ALU.mult), [vz[p].b, S_.b], [bon[p].b])
                    kb.op("pool", lambda: P.tensor_tensor(out=y[p][:], in0=y[p][:], in1=bon[p][:], op=ALU.add), [y[p].b, bon[p].b], [y[p].b])
                    kb.op("act", lambda: A.activation(out=sz[p][:], in_=vz[p][:, 512:1024], func=AF.Silu), [vz[p].b], [sz[p].b])
                    kb.op("dve", lambda: V.tensor_tensor(out=yg[p][:], in0=y[p][:], in1=sz[p][:], op=ALU.mult), [y[p].b, sz[p].b], [yg[p].b])
                    kb.pe([lambda c=c: PEe.transpose(psbf[6][:, 512 + c * 128:512 + (c + 1) * 128], yg[p][:, c * 128:(c + 1) * 128], ident_b) for c in range(4)], [yg[p].b, konb.b], [PB[6]])
                    kb.op("act", lambda: A.copy(out=ygT[p][:].rearrange("p c t -> p (c t)"), in_=psbf[6][:, 512:1024]), [PB[6]], [ygT[p].b])
                    kb.dma("sp", ygT[p].b.name, YG[0:512, ts_].rearrange("(c p) t -> p c t", p=128), ygT[p][:], reads=[ygT[p].b])
                    yield

        def phase_D(l, s2):
            if True:
                gbz = T(s2, nc, "gbz", [128, 2, 4, 512], BF16)
                gcu = T(s2, nc, "gcu", [128, 2, 4, 514], BF16)
                gu = T(s2, nc, "gu", [128, 4, 514], F32)
                acc = T(s2, nc, "acc", [128, 4, 512], F32)
                szd = T(s2, nc, "szd", [128, 4, 512], F32)
                od = T(s2, nc, "od", [128, 4, 512], BF16)
                for bi, (t0, nb, col) in enumerate(blocks):
                    s0, s1 = (0, CTX) if col == 1 else (CTX, TT)
                    lo, hi = max(s0, t0 - 1), min(s1, t0 + nb + 1)
                    kb.op("pool", lambda: P.memset(gcu[:], 0.0), [], [gcu.b])
                    off = lo - (t0 - 1)
                    for j, r0 in enumerate([2816, 3328]):
                        kb.dma("sp", "gcu%d" % j, gcu[:, j, :, off:off + hi - lo], PF[r0:r0 + 512, lo:hi].rearrange("(c p) t -> p c t", p=128), writes=[gcu.b])
                    for j, r0 in enumerate([2304, 3840]):
                        kb.dma("sp", "gbz%d" % j, gbz[:, j, :, 0:nb], PF[r0:r0 + 512, t0:t0 + nb].rearrange("(c p) t -> p c t", p=128), writes=[gbz.b])
                    kb.op("dve", lambda: V.tensor_tensor(out=gu[:, :, 0:nb + 2], in0=gcu[:, 0, :, 0:nb + 2], in1=gcu[:, 1, :, 0:nb + 2], op=ALU.mult), [gcu.b], [gu.b])
                    for c in range(4):
                        cw = lambda j: pp[:, l, 60 + j * 4 + c:61 + j * 4 + c]
                        kb.op("dve", lambda: V.tensor_scalar(out=acc[:, c, 0:nb], in0=gu[:, c, 0:nb], scalar1=cw(0), scalar2=None, op0=ALU.mult), [gu.b, pp.b], [acc.b])
                        kb.op("dve", lambda: V.scalar_tensor_tensor(out=acc[:, c, 0:nb], in0=gu[:, c, 1:nb + 1], scalar=cw(1), in1=acc[:, c, 0:nb], op0=ALU.mult, op1=ALU.add), [gu.b, pp.b, acc.b], [acc.b])
                        kb.op("dve", lambda: V.scalar_tensor_tensor(out=acc[:, c, 0:nb], in0=gu[:, c, 2:nb + 2], scalar=cw(2), in1=acc[:, c, 0:nb], op0=ALU.mult, op1=ALU.add), [gu.b, pp.b, acc.b], [acc.b])
                        kb.op("dve", lambda: V.scalar_tensor_tensor(out=acc[:, c, 0:nb], in0=acc[:, c, 0:nb], scalar=pp[:, l, 72 + c:73 + c], in1=gbz[:, 0, c, 0:nb], op0=ALU.add, op1=ALU.mult),
                              [acc.b, pp.b, gbz.b], [acc.b])
                        yield
                    kb.op("act", lambda: A.activation(out=szd[:, :, 0:nb], in_=gbz[:, 1, :, 0:nb], func=AF.Silu), [gbz.b], [szd.b])
                    kb.op("pool", lambda: P.tensor_tensor(out=od[:, :, 0:nb], in0=acc[:, :, 0:nb], in1=szd[:, :, 0:nb], op=ALU.mult), [acc.b, szd.b], [od.b])
                    kb.dma("sp", "od", YG[512:1024, t0:t0 + nb].rearrange("(c p) t -> p c t", p=128), od[:, :, 0:nb], reads=[od.b])

        def phase_E1(l, s2):
            if True:
                pg = T(s2, nc, "pg", [128, 2, 128], BF16)
                qk = [T(s2, nc, "qk%d" % i, [128, 8, 512], BF16) for i in range(2)]
                cs_ = [T(s2, nc, "cs_%d" % i, [128, 2, 512], F32) for i in range(2)]
                sqe = [T(s2, nc, "sqe%d" % i, [128, 512], BF16) for i in range(2)]
                ri = [T(s2, nc, "ri%d" % i, [128, 512], F32) for i in range(2)]
                t1 = [T(s2, nc, "t1_%d" % i, [128, 512], F32) for i in range(2)]
                t2 = [T(s2, nc, "t2_%d" % i, [128, 512], F32) for i in range(2)]
                oq = [T(s2, nc, "oq%d" % i, [128, 512], BF16) for i in range(2)]
                for j in range(2):
                    kb.op("dve", lambda: V.tensor_scalar(out=pg[:, j, :], in0=perm_f, scalar1=pp[:, l, 76 + j:77 + j], scalar2=None, op0=ALU.mult), [kon.b, pp.b], [pg.b])
                for bi, (t0, nb, col) in enumerate(blocks):
                    S = slice(0, nb)
                    q_, c_ = qk[bi % 2], cs_[bi % 2]
                    kb.dma_group("sp", q_.b.name, [(q_[:, 0:4, S], PF[4352:4864, t0:t0 + nb].rearrange("(c p) t -> p c t", p=128)),
                                                   (q_[:, 4:8, S], PF[4864:5376, t0:t0 + nb].rearrange("(c p) t -> p c t", p=128))], writes=[q_.b])
                    kb.dma_group("sp", c_.b.name, [(c_[:, 0, S], ROPE[:, t0:t0 + nb]), (c_[:, 1, S], ROPE[:, TT + t0:TT + t0 + nb])], writes=[c_.b])
                    for j in range(8):
                        w = j // 4
                        p = j % 2
                        kb.op("act", lambda: A.activation(out=sqe[p][:, S], in_=q_[:, j, S], func=AF.Square), [q_.b], [sqe[p].b])
                        kb.pe([lambda: PEe.matmul(psum[:, 4 + p, S], lhsT=blk_b, rhs=sqe[p][:, S], start=True, stop=True)], [sqe[p].b, konb.b], [PB[4 + p]])
                        kb.pe([lambda: PEe.matmul(psum[:, 6 + p, S], lhsT=pg[:, w, :], rhs=q_[:, j, S], start=True, stop=True)], [pg.b, q_.b], [PB[6 + p]])
                        kb.op("act", lambda: A.activation(out=ri[p][:, S], in_=psum[:, 4 + p, S], func=AF.Sqrt, bias=RMS_EPS, scale=1.0 / 64), [PB[4 + p]], [ri[p].b])
                        kb.op("dve", lambda: V.reciprocal(out=ri[p][:, S], in_=ri[p][:, S]), [ri[p].b], [ri[p].b])
                        kb.op("dve", lambda: V.scalar_tensor_tensor(out=t1[p][:, S], in0=q_[:, j, S], scalar=pp[:, l, 76 + w:77 + w], in1=c_[:, 0, S], op0=ALU.mult, op1=ALU.mult),
                              [q_.b, pp.b, c_.b], [t1[p].b])
                        kb.op("dve", lambda: V.tensor_tensor(out=t2[p][:, S], in0=psum[:, 6 + p, S], in1=c_[:, 1, S], op=ALU.mult), [PB[6 + p], c_.b], [t2[p].b])
                        kb.op("pool", lambda: P.tensor_tensor(out=t1[p][:, S], in0=t1[p][:, S], in1=t2[p][:, S], op=ALU.add), [t1[p].b, t2[p].b], [t1[p].b])
                        kb.op("pool", lambda: P.tensor_tensor(out=oq[p][:, S], in0=t1[p][:, S], in1=ri[p][:, S], op=ALU.mult), [t1[p].b, ri[p].b], [oq[p].b])
                        dst = QRs if w == 0 else KRs
                        kb.dma("sp", oq[p].b.name, dst[(j % 4) * 128:(j % 4 + 1) * 128, t0:t0 + nb], oq[p][:, S], reads=[oq[p].b])
                        yield

        def phase_E2(l, last, s2):
            if True:
                QR = T(s2, nc, "QR", [128, 4, TT], BF16)
                KR = T(s2, nc, "KR", [128, 4, TT], BF16)
                VD = T(s2, nc, "VD", [128, NTT, 512], BF16)
                PTs = [[T(s2, nc, "PTs%d_%d" % (i, c), [128, 512], BF16) for c in range(2)] for i in range(3)]
                accD = [T(s2, nc, "accD%d" % i, [128, 512], F32) for i in range(2)]
                accP = [T(s2, nc, "accP%d" % i, [128, 512], F32) for i in range(2)]
                o0 = [T(s2, nc, "o0_%d" % i, [128, 512], F32) for i in range(2)]
                r0 = T(s2, nc, "r0", [128, 512], F32)
                r0a = T(s2, nc, "r0a", [128, 512], F32)
                sq2 = T(s2, nc, "sq2", [128, 512], BF16)
                rst = T(s2, nc, "rst", [128, 512], F32)
                zc = T(s2, nc, "zc", [128, 512], BF16)
                szc = T(s2, nc, "szc", [128, 512], F32)
                oc = T(s2, nc, "oc", [128, 512], BF16)
                kb.dma("sp", "VD", VD[:], PT[:, 1024:1536].rearrange("(t p) n -> p t n", p=128), writes=[VD.b])
                kb.dma("sp", "QR", QR[:], QRs.rearrange("(h p) t -> p h t", p=128), writes=[QR.b])
                kb.dma("sp", "KR", KR[:], KRs.rearrange("(h p) t -> p h t", p=128), writes=[KR.b])
                pending = []

                def flush(n):
                    for _ in range(min(n, len(pending))):
                        pending.pop(0)()
                n_ = 0
                bidx = 0
                for h in range(4):
                    for bi, (t0, nb, col) in enumerate(blocks):
                        if col == 1 and last:
                            continue
                        S = slice(0, nb)
                        kts = list(range(CTX // 128)) if col == 1 else list(range(NTT))
                        aD, aP = accD[bidx % 2], accP[bidx % 2]
                        bidx += 1

                        def smm(i):
                            kt = kts[i]
                            b0 = 2 * ((n_ + i) % 2)
                            kb.pe([lambda c=c: PEe.matmul(psum[:, b0 + c, S], lhsT=KR[c * 64:(c + 1) * 64, h, kt * 128:(kt + 1) * 128], rhs=QR[c * 64:(c + 1) * 64, h, t0:t0 + nb],
                                                          start=True, stop=True) for c in range(2)], [KR.b, QR.b], [PB[b0], PB[b0 + 1]])
                        smm(0)
                        for i, kt in enumerate(kts):
                            if i + 1 < len(kts):
                                smm(i + 1)
                            b0 = 2 * ((n_ + i) % 2)
                            pt = PTs[(n_ + i) % 3]
                            first, lastk = (i == 0), (i == len(kts) - 1)
                            for c in range(2):
                                kb.op("act", lambda: A.activation(out=pt[c][:, S], in_=psum[:, b0 + c, S], func=AF.Exp, scale=0.125), [PB[b0 + c]], [pt[c].b])
                                fns = [lambda: PEe.matmul(psum[:, 4 + c, S], lhsT=VD[:, kt, h * 128:(h + 1) * 128], rhs=pt[c][:, S], start=first, stop=lastk)]
                                wr = [PB[4 + c]]
                                if c == 0:
                                    fns.append(lambda: PEe.matmul(psum[:, 7, S], lhsT=ones_b, rhs=pt[0][:, S], start=first, stop=lastk))
                                    wr.append(PB[7])
                                kb.pe(fns, [VD.b, pt[c].b, konb.b], wr)
                            if i % 2 == 0:
                                if i == 0:
                                    kb.op("dve", lambda: V.tensor_copy(out=aD[:, S], in_=pt[1][:, S]), [pt[1].b], [aD.b])
                                else:
                                    kb.op("dve", lambda: V.tensor_tensor(out=aD[:, S], in0=aD[:, S], in1=pt[1][:, S], op=ALU.add), [pt[1].b, aD.b], [aD.b])
                            else:
                                if i == 1:
                                    kb.op("pool", lambda: P.tensor_copy(out=aP[:, S], in_=pt[1][:, S]), [pt[1].b], [aP.b])
                                else:
                                    kb.op("pool", lambda: P.tensor_tensor(out=aP[:, S], in0=aP[:, S], in1=pt[1][:, S], op=ALU.add), [pt[1].b, aP.b], [aP.b])
                            flush(1)
                            yield
                        n_ += len(kts)
                        flush(len(pending))
                        kb.op("act", lambda: A.copy(out=o0[0][:, S], in_=psum[:, 4, S]), [PB[4]], [o0[0].b])
                        kb.op("dve", lambda: V.tensor_copy(out=o0[1][:, S], in_=psum[:, 5, S]), [PB[5]], [o0[1].b])
                        kb.op("dve", lambda: V.reciprocal(out=r0a[:, S], in_=psum[:, 7, S]), [PB[7]], [r0a.b])

                        def mk_epilogue(h=h, t0=t0, nb=nb, S=S, aD=aD, aP=aP):
                            ops = []
                            ops.append(lambda: kb.dma("sp", "zc", zc[:, S], PF[5888 + h * 128:5888 + (h + 1) * 128, t0:t0 + nb], writes=[zc.b]))
                            ops.append(lambda: kb.op("dve", lambda: V.tensor_tensor(out=o0[0][:, S], in0=o0[0][:, S], in1=r0a[:, S], op=ALU.mult), [o0[0].b, r0a.b], [o0[0].b]))
                            ops.append(lambda: kb.pe([lambda: PEe.matmul(psum[:, 6, S], lhsT=ones_f, rhs=aD[:, S], start=True, stop=False),
                                                      lambda: PEe.matmul(psum[:, 6, S], lhsT=ones_f, rhs=aP[:, S], start=False, stop=True)], [aD.b, aP.b, kon.b], [PB[6]]))
                            ops.append(lambda: kb.op("dve", lambda: V.reciprocal(out=r0[:, S], in_=psum[:, 6, S]), [PB[6]], [r0.b]))
                            ops.append(lambda: kb.op("dve", lambda: V.tensor_tensor(out=o0[1][:, S], in0=o0[1][:, S], in1=r0[:, S], op=ALU.mult), [o0[1].b, r0.b], [o0[1].b]))
                            ops.append(lambda: kb.op("dve", lambda: V.scalar_tensor_tensor(out=o0[0][:, S], in0=o0[1][:, S], scalar=lamc[:, l, 0:1], in1=o0[0][:, S], op0=ALU.mult, op1=ALU.add),
                                                     [o0[0].b, o0[1].b, lamc.b], [o0[0].b]))
                            ops.append(lambda: kb.op("act", lambda: A.activation(out=sq2[:, S], in_=o0[0][:, S], func=AF.Square), [o0[0].b], [sq2.b]))
                            ops.append(lambda: kb.pe([lambda: PEe.matmul(psum[:, 6, S], lhsT=ones_b, rhs=sq2[:, S], start=True, stop=True)], [sq2.b, konb.b], [PB[6]]))
                            ops.append(lambda: kb.op("act", lambda: A.activation(out=rst[:, S], in_=psum[:, 6, S], func=AF.Sqrt, bias=RMS_EPS, scale=1.0 / 128), [PB[6]], [rst.b]))
                            ops.append(lambda: kb.op("dve", lambda: V.reciprocal(out=rst[:, S], in_=rst[:, S]), [rst.b], [rst.b]))
                            ops.append(lambda: kb.op("dve", lambda: V.scalar_tensor_tensor(out=o0[0][:, S], in0=o0[0][:, S], scalar=sgl[:, l:l + 1], in1=rst[:, S], op0=ALU.mult, op1=ALU.mult),
                                                     [o0[0].b, sgl.b, rst.b], [o0[0].b]))
                            ops.append(lambda: kb.op("act", lambda: A.activation(out=szc[:, S], in_=zc[:, S], func=AF.Silu), [zc.b], [szc.b]))
                            ops.append(lambda: kb.op("pool", lambda: P.tensor_tensor(out=oc[:, S], in0=o0[0][:, S], in1=szc[:, S], op=ALU.mult), [o0[0].b, szc.b], [oc.b]))
                            ops.append(lambda: kb.dma("sp", "oc", YG[1024 + h * 128:1024 + (h + 1) * 128, t0:t0 + nb], oc[:, S], reads=[oc.b]))
                            return ops
                        pending.extend(mk_epilogue())
                flush(len(pending))

        def phase_F(l, last):
            with ExitStack() as s2:
                wst = [T(s2, nc, "wst%d" % i, [128, 4, 1024], F32) for i in range(2)]
                wbr = T(s2, nc, "wbr", [128, 12, 1024], BF16)
                wou = T(s2, nc, "wou", [128, 8, 1024], BF16)
                for i in range(5):
                    w = wst[i % 2]
                    src = w_br[l][i * 512:(i + 1) * 512, :] if i < 3 else w_out[l][(i - 3) * 512:(i - 2) * 512, :]
                    kb.dma("sp", w.b.name, w[:], src.rearrange("(c p) n -> p c n", p=128), writes=[w.b])
                    dstt = wbr[:, i * 4:(i + 1) * 4, :] if i < 3 else wou[:, (i - 3) * 4:(i - 2) * 4, :]
                    dstb = wbr.b if i < 3 else wou.b
                    kb.op("pool", lambda: P.tensor_copy(out=dstt, in_=w[:]), [w.b], [dstb])
                ygb = T(s2, nc, "ygb", [128, 12, 512], BF16)
                gm = T(s2, nc, "gm", [128, 24, 512], BF16)
                xa = T(s2, nc, "xaf", [128, 8, 512], F32)
                xn = T(s2, nc, "xnf", [128, 8, 512], F32)
                mT = T(s2, nc, "mT", [128, 8, 512], BF16)
                sg = [T(s2, nc, "sgf%d" % i, [128, 512], F32) for i in range(3)]
                ma = [T(s2, nc, "ma%d" % i, [128, 512], F32) for i in range(3)]
                for bi, (t0, nb, col) in enumerate(blocks):
                    if col == 1 and last:
                        continue
                    S = slice(0, nb)
                    kb.dma("sp", "ygb", ygb[:, :, S], YG[:, t0:t0 + nb].rearrange("(c p) t -> p c t", p=128), writes=[ygb.b])
                    kb.dma("sp", "gm", gm[:, :, S], PF[6400:9472, t0:t0 + nb].rearrange("(c p) t -> p c t", p=128), writes=[gm.b])
                    kb.dma("sp", "xaf", xa[:, :, S], xTv[:, :, t0:t0 + nb], writes=[xa.b])
                    for dc in range(8):
                        for i in range(3):
                            kb.pe([lambda w=w: PEe.matmul(psum[:, 1 + i, S], lhsT=wbr[:, i * 4 + w, dc * 128:(dc + 1) * 128], rhs=ygb[:, i * 4 + w, S], start=(w == 0), stop=(w == 3))
                                   for w in range(4)], [wbr.b, ygb.b], [PB[1 + i]])
                            kb.op("act", lambda: A.activation(out=sg[i][:, S], in_=gm[:, i * 8 + dc, S], func=AF.Sigmoid), [gm.b], [sg[i].b])
                            kb.op("dve", lambda: V.tensor_tensor(out=ma[i][:, S], in0=psum[:, 1 + i, S], in1=sg[i][:, S], op=ALU.mult), [PB[1 + i], sg[i].b], [ma[i].b])
                        kb.op("pool", lambda: P.tensor_tensor(out=ma[0][:, S], in0=ma[0][:, S], in1=ma[1][:, S], op=ALU.add), [ma[0].b, ma[1].b], [ma[0].b])
                        kb.op("pool", lambda: P.tensor_tensor(out=mT[:, dc, S], in0=ma[0][:, S], in1=ma[2][:, S], op=ALU.add), [ma[0].b, ma[2].b], [mT.b])
                    for d2 in range(8):
                        bank = 4 + d2 % 2
                        kb.pe([lambda dc=dc: PEe.matmul(psum[:, bank, S], lhsT=wou[:, dc, d2 * 128:(d2 + 1) * 128], rhs=mT[:, dc, S], start=(dc == 0), stop=(dc == 7)) for dc in range(8)],
                              [wou.b, mT.b], [PB[bank]])
                        kb.op("dve", lambda: V.scalar_tensor_tensor(out=xn[:, d2, S], in0=psum[:, bank, S], scalar=modT[:, l, 16 + d2, col:col + 1], in1=xa[:, d2, S], op0=ALU.mult, op1=ALU.add),
                              [PB[bank], modT.b, xa.b], [xn.b])
                    kb.dma("sp", "xnf", xTv[:, :, t0:t0 + nb], xn[:, :, S], reads=[xn.b])
                kb.barrier()

        def drain(g):
            for _ in g:
                pass

        def layer(l):
            last = (l == DEPTH - 1)
            phase_AB(l)
            with ExitStack() as sg:
                kb.coop.run([(lambda: drain(phase_C1(l, sg)), 8), (lambda: drain(phase_E1(l, sg)), 3), (lambda: drain(phase_D(l, sg)), 1)])
                kb.barrier()
            phase_C2(l)
            with ExitStack() as sg:
                drain(phase_E2(l, last, sg))
                drain(phase_C3(l, sg))
                kb.barrier()
            phase_F(l, last)

        for l in range(DEPTH):
            layer(l)
        with ExitStack() as s2:
            xb_ = [T(s2, nc, "xo%d" % i, [128, 8, 512], F32) for i in range(2)]
            for bi, (t0, nb, col) in enumerate(blocks[1:]):
                x_ = xb_[bi % 2]
                kb.dma("sp", x_.b.name, x_[:, :, 0:nb], xTv[:, :, t0:t0 + nb], writes=[x_.b])
                kb.dma("sp", x_.b.name, yout.rearrange("(c p) t -> p c t", p=128)[:, :, t0 - CTX:t0 - CTX + nb], x_[:, :, 0:nb], reads=[x_.b])
            if dbg:
                for k_, shp in dbg.items():
                    src = {"PF": PF, "PX": PX, "PT": PT, "YG": YG, "YS0": YS[0], "YS1": YS[1], "xTs": xT, "AKT0": AKT[0], "GCs0": GCs[0], "GCs1": GCs[1]}[k_]
                    kb.dma("sp", "dbg", dbgo[k_], src, reads=[])
            kb.barrier()
    build.n_inst = kb.n_inst
    build.n_sem = len(kb.sems)
    return nc


def host_consts(SEQ, DEPTH):
    TT = CTX + SEQ
    kon = np.zeros((128, 1024), np.float32)
    kon[:, 0:128] = np.eye(128)
    kon[:, 128:256] = 1.0
    p = np.arange(128)
    kon[:, 256:384] = (p[:, None] // 64 == p[None, :] // 64)
    partner = np.where((p % 32) < 16, p + 16, p - 16)
    perm = np.zeros((128, 128), np.float32)
    perm[partner, p] = 1.0
    kon[:, 384:512] = perm
    sel = np.zeros((128, 4, 8), np.float32)
    for c in range(4):
        for q in range(128):
            sel[q, c, 2 * c + q // 64] = 1.0
    kon[:, 512:544] = sel.reshape(128, 32)
    i = np.arange(64)
    SU = (i[:, None] < i[None, :]); SL = (i[:, None] > i[None, :]); IU = (i[:, None] <= i[None, :]); IL = (i[:, None] >= i[None, :]); I = np.eye(64)
    rep = lambda ms: np.stack([np.broadcast_to(m[:, None, :], (64, 8, 64)) for m in ms], axis=1).astype(np.float32).reshape(64, 4 * 512)
    msk = np.concatenate([rep((SU, IU, SL, I)), rep((SL, IL, SU, I))], axis=0)
    n_freq = 16
    inv_freq = (10000.0 ** (-np.arange(n_freq, dtype=np.float32) / n_freq)).astype(np.float32)
    tok = np.arange(SEQ)
    row = (tok // 64).astype(np.float32)
    colp = (tok % 64).astype(np.float32)
    d = p % 64
    half = d // 32
    j = d % 16
    which = (d % 32) // 16
    pos = np.where(half[:, None] == 0, row[None, :], colp[None, :]).astype(np.float32)
    ang = (pos * inv_freq[j][:, None]).astype(np.float32)
    cosT = np.ones((128, TT), np.float32)
    sinT = np.zeros((128, TT), np.float32)
    cosT[:, CTX:] = np.cos(ang)
    sinT[:, CTX:] = np.sin(ang) * np.where(which == 0, -1.0, 1.0)[:, None]
    rope = np.concatenate([cosT, sinT], axis=1).astype(np.float32)
    ltri = np.ones((128, 512), np.float32)
    ltri[:, 0::64] = 0.0
    return dict(KON=kon, MSK=np.ascontiguousarray(msk), ROPE=np.ascontiguousarray(rope), LTRI=ltri)


def host_inputs(inp, SEQ, DEPTH, b):
    L = DEPTH
    f = lambda a: np.asarray(a, np.float32)
    xT = np.concatenate([f(inp["ctx"])[b], f(inp["x"])[b]], axis=0).T
    PPa = np.zeros((128, L, NPL), np.float32)
    colT = lambda v: v.reshape(-1, 128).T
    for l in range(L):
        PPa[:, l, 0:8] = colT(f(inp["norm_g"])[l])
        PPa[:, l, 8:32] = colT(f(inp["b_mod"])[l])
        PPa[:, l, 32:40] = colT(f(inp["decay_w0"])[l].reshape(-1))
        PPa[:, l, 40:48] = colT(f(inp["iclr_a0"])[l].reshape(-1))
        PPa[:, l, 48:52] = colT(f(inp["k_k"])[l])
        PPa[:, l, 52:56] = colT(f(inp["k_a"])[l])
        PPa[:, l, 56:60] = colT(f(inp["r_k"])[l].reshape(-1))
        PPa[:, l, 60:72] = colT(f(inp["conv_w"])[l].reshape(-1))
        PPa[:, l, 72:76] = colT(f(inp["conv_b"])[l])
        PPa[:, l, 76] = np.tile(f(inp["qk_norm_g"])[l, 0], 2)
        PPa[:, l, 77] = np.tile(f(inp["qk_norm_g"])[l, 1], 2)
        PPa[:, l, 78] = f(inp["subln_g"])[l]
    CC = np.stack([colT(f(inp["c"])[b]), colT(f(inp["c_ctx"]))], axis=2).reshape(128, 16)
    BC = np.stack([np.broadcast_to(f(inp["gn_g"])[:L, None, :], (L, 128, 512)), np.broadcast_to(f(inp["gn_b"])[:L, None, :], (L, 128, 512))], axis=2)
    BC = np.ascontiguousarray(BC.transpose(1, 0, 2, 3)).reshape(128, L * 1024)
    LAM = f(inp["lambda_qk"])[:L].reshape(1, L * 256)
    return dict(xT=np.ascontiguousarray(xT), PP=PPa.reshape(128, L * NPL), CC=np.ascontiguousarray(CC), BC=BC, LAM=np.ascontiguousarray(LAM),
                w_mod=f(inp["w_mod"])[:L], w_in=f(inp["w_in"])[:L], dup=f(inp["decay_up"])[:L].reshape(L, 128, 512), iup=f(inp["iclr_up"])[:L].reshape(L, 128, 512),
                w_br=f(inp["w_branch"])[:L].reshape(L, 1536, D), w_out=f(inp["w_out"])[:L])


def run(inp, SEQ, DEPTH, dbg=None):
    nc = build(SEQ, DEPTH, dbg)
    consts = host_consts(SEQ, DEPTH)
    nb = np.asarray(inp["x"]).shape[0]
    in_maps = []
    for b in range(nb):
        m = host_inputs(inp, SEQ, DEPTH, b)
        m.update(consts)
        in_maps.append(m)
    res = run_bass_kernel_spmd(nc, in_maps, core_ids=list(range(nb)))
    out = np.stack([np.asarray(r["yout"]).T for r in res.results], axis=0)
    return out.astype(np.float32), res


def kernel(**inputs):
    out, _ = run(inputs, 4096, 4)
    return out
```

```python
import math
import threading
import numpy as np
from contextlib import ExitStack
import concourse.bass as bass
import concourse.mybir as mybir
from concourse.bass_utils import run_bass_kernel_spmd

F32 = mybir.dt.float32
BF16 = mybir.dt.bfloat16
AF = mybir.ActivationFunctionType
ALU = mybir.AluOpType
AX = mybir.AxisListType

D = 1024
CTX = 256
NIN = 9472
NPL = 79
C0 = math.exp(-0.5)
GN_EPS = 64e-5
RMS_EPS = 1e-6


class Buf:
    __slots__ = ("name", "w", "r")

    def __init__(self, name):
        self.name = name
        self.w = None
        self.r = []


class Coop:
    def __init__(self):
        self.cur = None

    def run(self, specs):
        n = len(specs)
        self.evs = [threading.Event() for _ in range(n)]
        self.alive = [True] * n
        self.quota = [q for _, q in specs]
        self.used = [0] * n
        self.done_ev = threading.Event()
        self.err = None

        def wrap(i, fn):
            self.evs[i].wait()
            self.evs[i].clear()
            self.cur = i
            try:
                fn()
            except BaseException as e:
                self.err = e
            self.alive[i] = False
            self._pass(i)
        ths = [threading.Thread(target=wrap, args=(i, f)) for i, (f, _) in enumerate(specs)]
        for t in ths:
            t.start()
        self.evs[0].set()
        self.done_ev.wait()
        for t in ths:
            t.join()
        self.cur = None
        if self.err is not None:
            raise self.err

    def _pass(self, i):
        n = len(self.alive)
        for k in range(1, n + 1):
            j = (i + k) % n
            if self.alive[j]:
                if j == i:
                    return True
                self.evs[j].set()
                return False
        self.done_ev.set()
        return False

    def switch(self):
        i = self.cur
        if i is None:
            return
        self.used[i] += 1
        if self.used[i] < self.quota[i]:
            return
        self.used[i] = 0
        if not self._pass(i):
            self.evs[i].wait()
            self.evs[i].clear()
            self.cur = i


class KB:
    def __init__(self, nc, stack):
        self.nc = nc
        self.stack = stack
        self.eng = {"pe": nc.tensor, "dve": nc.vector, "act": nc.scalar, "pool": nc.gpsimd, "sp": nc.sync}
        self.sems = {}
        self.cnt = {}
        self.seen = {e: {} for e in self.eng}
        for e in self.eng:
            self.sems[e] = stack.enter_context(nc.semaphore("s_" + e))
            self.cnt[e] = 0
        self.n_inst = 0
        self.dma_names = {}
        self.dma_pool = []
        self.coop = Coop()

    def dma_sem(self, name):
        if name not in self.dma_names:
            i = len(self.dma_names)
            if i >= len(self.dma_pool):
                key = "d_%d" % i
                self.sems[key] = self.stack.enter_context(self.nc.semaphore(key))
                self.cnt[key] = 0
                self.dma_pool.append(key)
            self.dma_names[name] = self.dma_pool[i]
        return self.dma_names[name]

    def phase_reset(self):
        self.dma_names = {}

    def _wait(self, e, ev):
        if ev is None:
            return
        key, val = ev
        if key == "pe" and e == "pe":
            return
        if self.seen[e].get(key, 0) >= val:
            return
        self.eng[e].wait_ge(self.sems[key], val)
        self.seen[e][key] = val
        self.n_inst += 1

    def deps(self, e, reads, writes):
        for b in reads:
            self._wait(e, b.w)
        for b in writes:
            self._wait(e, b.w)
            for ev in b.r:
                self._wait(e, ev)

    def done(self, ev, reads, writes):
        for b in reads:
            b.r.append(ev)
            if len(b.r) > 16:
                d = {}
                for k, v in b.r:
                    d[k] = max(d.get(k, 0), v)
                b.r = list(d.items())
        for b in writes:
            b.w = ev
            b.r = []

    def op(self, e, fn, reads=(), writes=()):
        self.deps(e, reads, writes)
        ins = fn()
        self.cnt[e] += 1
        ins.then_inc(self.sems[e], 1)
        ev = (e, self.cnt[e])
        self.done(ev, reads, writes)
        self.n_inst += 1
        self.coop.switch()
        return ev

    def pe(self, fns, reads=(), writes=()):
        self.deps("pe", reads, writes)
        ins = None
        for fn in fns:
            ins = fn()
            self.n_inst += 1
        self.cnt["pe"] += 1
        ins.then_inc(self.sems["pe"], 1)
        ev = ("pe", self.cnt["pe"])
        self.done(ev, reads, writes)
        self.coop.switch()
        return ev

    def dma(self, q, semname, out, in_, reads=(), writes=()):
        key = self.dma_sem(semname)
        self.deps(q, reads, writes)
        ins = self.eng[q].dma_start(out=out, in_=in_)
        self.cnt[key] += 16
        ins.then_inc(self.sems[key], 16)
        ev = (key, self.cnt[key])
        self.done(ev, reads, writes)
        self.n_inst += 1
        self.coop.switch()
        return ev

    def dma_group(self, q, semname, pairs, reads=(), writes=()):
        key = self.dma_sem(semname)
        self.deps(q, reads, writes)
        for out, in_ in pairs:
            ins = self.eng[q].dma_start(out=out, in_=in_)
            self.cnt[key] += 16
            ins.then_inc(self.sems[key], 16)
            self.n_inst += 1
        ev = (key, self.cnt[key])
        self.done(ev, reads, writes)
        self.coop.switch()
        return ev

    def barrier(self):
        for e in self.eng:
            for k in self.cnt:
                if self.cnt[k] > 0:
                    self._wait(e, (k, self.cnt[k]))
        self.phase_reset()


class T:
    uid = [0]

    def __init__(self, st, nc, name, shape, dt, nbuf=1):
        T.uid[0] += 1
        self.t = st.enter_context(nc.sbuf_tensor("%s_u%d" % (name, T.uid[0]), shape, dt))
        self.b = Buf(name)

    def __getitem__(self, k):
        return self.t[k]


SEGS = [(0, 512, "F"), (512, 512, "F"), (1024, 512, "T"), (1536, 256, "X"), (1792, 512, "T"),
        (2304, 512, "F"), (2816, 512, "F"), (3328, 512, "F"), (3840, 512, "F"),
        (4352, 512, "F"), (4864, 512, "F"), (5376, 512, "T"), (5888, 512, "F")] + \
       [(6400 + 512 * i, 512, "F") for i in range(6)]
TCOL = {1024: 0, 1792: 512, 5376: 1024}


def build(SEQ, DEPTH, dbg=None):
    TT = CTX + SEQ
    NCH = TT // 64
    blocks = [(0, CTX, 1)] + [(CTX + 512 * i, 512, 0) for i in range(SEQ // 512)]
    nc = bass.Bass("TRN2", target_bir_lowering=False)
    dram = lambda n, s, d, k="Internal": nc.dram_tensor(n, s, d, kind=k).ap()
    xT_in = dram("xT", [D, TT], F32, "ExternalInput")
    PP = dram("PP", [128, DEPTH * NPL], F32, "ExternalInput")
    CCd = dram("CC", [128, 16], F32, "ExternalInput")
    BCd = dram("BC", [128, DEPTH * 2 * 512], F32, "ExternalInput")
    LAMd = dram("LAM", [1, DEPTH * 256], F32, "ExternalInput")
    KON = dram("KON", [128, 1024], F32, "ExternalInput")
    MSK = dram("MSK", [128, 4 * 512], F32, "ExternalInput")
    ROPE = dram("ROPE", [128, 2 * TT], F32, "ExternalInput")
    LTRI = dram("LTRI", [128, 512], F32, "ExternalInput")
    w_mod = dram("w_mod", [DEPTH, D, 3 * D], F32, "ExternalInput")
    w_in = dram("w_in", [DEPTH, D, NIN], F32, "ExternalInput")
    dup = dram("dup", [DEPTH, 128, 512], F32, "ExternalInput")
    iup = dram("iup", [DEPTH, 128, 512], F32, "ExternalInput")
    w_br = dram("w_br", [DEPTH, 1536, D], F32, "ExternalInput")
    w_out = dram("w_out", [DEPTH, D, D], F32, "ExternalInput")
    yout = dram("yout", [D, SEQ], F32, "ExternalOutput")
    xT = dram("xTs", [D, TT], F32)
    PF = dram("PF", [NIN, TT], BF16)
    PX = dram("PX", [256, TT], F32)
    PT = dram("PT", [TT, 1536], BF16)
    QGs = [dram("QGs%d" % d, [512, NCH, 2, 64], BF16) for d in range(2)]
    AKT = [dram("AKT%d" % d, [512, TT], BF16) for d in range(2)]
    KIT = [dram("KIT%d" % d, [512, TT], BF16) for d in range(2)]
    AKK = [dram("AKK%d" % d, [TT, 512], BF16) for d in range(2)]
    KIK = [dram("KIK%d" % d, [TT, 512], BF16) for d in range(2)]
    GCs = [dram("GCs%d" % d, [512, NCH], F32) for d in range(2)]
    YS = [dram("YS%d" % d, [TT, 512], F32) for d in range(2)]
    YG = dram("YG", [1536, TT], BF16)
    QRs = dram("QRs", [512, TT], BF16)
    KRs = dram("KRs", [512, TT], BF16)
    dbgo = {}
    if dbg:
        for n, shp in dbg.items():
            dbgo[n] = dram("dbg_" + n, shp, F32, "ExternalOutput")

    with ExitStack() as st:
        kb = KB(nc, st)
        V, A, P, PEe = nc.vector, nc.scalar, nc.gpsimd, nc.tensor
        psum = st.enter_context(nc.psum_tensor("psum", [128, 8, 512], F32))
        PB = [Buf("ps%d" % i) for i in range(8)]
        psbf = [psum[:, i, :].bitcast(BF16) for i in range(8)]

        kon = T(st, nc, "kon", [128, 1024], F32)
        konb = T(st, nc, "konb", [128, 1024], BF16)
        msk2 = T(st, nc, "msk2", [128, 4, 8, 64], F32)
        pp = T(st, nc, "pp", [128, DEPTH, NPL], F32)
        cc = T(st, nc, "cc", [128, 8, 2], F32)
        ltri = T(st, nc, "ltri", [128, 512], F32)
        modT = T(st, nc, "modT", [128, DEPTH, 24, 2], F32)
        gsc = T(st, nc, "gsc", [128, DEPTH, 8, 2], F32)
        omka = T(st, nc, "omka", [128, DEPTH, 4], F32)
        lamc = T(st, nc, "lamc", [128, DEPTH, 2], F32)
        sgl = T(st, nc, "sgl", [128, DEPTH], F32)
        kb.dma("sp", "c0", kon[:], KON[:, :], writes=[kon.b])
        kb.dma("sp", "c1", msk2[:].rearrange("p a h x -> p (a h x)"), MSK[:, :], writes=[msk2.b])
        kb.dma("sp", "c2", pp[:].rearrange("p l n -> p (l n)"), PP[:, :], writes=[pp.b])
        kb.dma("sp", "c3", cc[:].rearrange("p a b -> p (a b)"), CCd[:, :], writes=[cc.b])
        kb.dma("sp", "c4", ltri[:], LTRI[:, :], writes=[ltri.b])
        kb.op("dve", lambda: V.tensor_copy(out=konb[:], in_=kon[:]), [kon.b], [konb.b])
        epsc = T(st, nc, "epsc", [128, 2], F32)
        kb.op("pool", lambda: P.memset(epsc[:, 0:1], RMS_EPS), [], [epsc.b])
        kb.op("pool", lambda: P.memset(epsc[:, 1:2], 1e-12), [], [epsc.b])
        ident_f, ones_f = kon[:, 0:128], kon[:, 128:256]
        ident_b, ones_b, blk_b, perm_f = konb[:, 0:128], konb[:, 128:256], konb[:, 256:384], kon[:, 384:512]
        sel_b = konb[:, 512:544]

        with ExitStack() as s2:
            sc = T(s2, nc, "sc", [128, 8, 2], F32)
            wm = [T(s2, nc, "wm%d" % i, [128, 8, 512], F32) for i in range(2)]
            lamt = T(s2, nc, "lamt", [1, DEPTH, 4, 64], F32)
            lam2 = T(s2, nc, "lam2", [1, DEPTH, 2], F32)
            lam3 = T(s2, nc, "lam3", [1, DEPTH, 2], F32)
            kb.op("act", lambda: A.activation(out=sc[:], in_=cc[:], func=AF.Silu), [cc.b], [sc.b])
            i = 0
            for l in range(DEPTH):
                for nb6 in range(6):
                    w = wm[i % 2]
                    i += 1
                    kb.dma("sp", w.b.name, w[:], w_mod[l].rearrange("(c p) n -> p c n", p=128)[:, :, nb6 * 512:(nb6 + 1) * 512], writes=[w.b])
                    fns = []
                    for n4 in range(4):
                        for c in range(8):
                            fns.append(lambda n4=n4, c=c, w=w: PEe.matmul(psum[:, 0, n4 * 2:n4 * 2 + 2], lhsT=w[:, c, n4 * 128:(n4 + 1) * 128], rhs=sc[:, c, :], start=(c == 0), stop=(c == 7)))
                    kb.pe(fns, [w.b, sc.b], [PB[0]])
                    kb.op("dve", lambda l=l, nb6=nb6: V.tensor_tensor(out=modT[:, l, nb6 * 4:(nb6 + 1) * 4, :], in0=psum[:, 0, 0:8].rearrange("p (a b) -> p a b", b=2),
                                                                   in1=pp[:, l, 8 + nb6 * 4:8 + (nb6 + 1) * 4].unsqueeze(2).to_broadcast([128, 4, 2]), op=ALU.add), [PB[0], pp.b], [modT.b])
                kb.op("dve", lambda l=l: V.tensor_scalar(out=gsc[:, l], in0=modT[:, l, 8:16, :], scalar1=1.0, scalar2=None, op0=ALU.add), [modT.b], [gsc.b])
                kb.op("dve", lambda l=l: V.tensor_tensor(out=gsc[:, l], in0=gsc[:, l], in1=pp[:, l, 0:8].unsqueeze(2).to_broadcast([128, 8, 2]), op=ALU.mult), [gsc.b, pp.b], [gsc.b])
                kb.op("dve", lambda l=l: V.tensor_scalar(out=omka[:, l, :], in0=pp[:, l, 52:56], scalar1=-1.0, scalar2=1.0, op0=ALU.mult, op1=ALU.add), [pp.b], [omka.b])
            kb.dma("sp", "c5", lamt[:].rearrange("p l a x -> p (l a x)"), LAMd[:, :], writes=[lamt.b])
            lv = lamt[:].rearrange("p l (a b) x -> p l a b x", b=2)
            kb.op("dve", lambda: V.tensor_tensor(out=lv[:, :, :, 0, :], in0=lv[:, :, :, 0, :], in1=lv[:, :, :, 1, :], op=ALU.mult), [lamt.b], [lamt.b])
            kb.op("dve", lambda: V.tensor_reduce(out=lam2[:], in_=lv[:, :, :, 0, :], axis=AX.X, op=ALU.add), [lamt.b], [lam2.b])
            kb.op("act", lambda: A.activation(out=lam3[:], in_=lam2[:], func=AF.Exp), [lam2.b], [lam3.b])
            kb.op("dve", lambda: V.tensor_tensor(out=lam2[:, :, 0], in0=lam3[:, :, 1], in1=lam3[:, :, 0], op=ALU.subtract), [lam3.b], [lam2.b])
            for l in range(DEPTH):
                li = 0.8 - 0.6 * math.exp(-0.3 * l)
                kb.op("dve", lambda l=l, li=li: V.tensor_scalar(out=lam2[:, l, 0:1], in0=lam2[:, l, 0:1], scalar1=-li, scalar2=None, op0=ALU.add), [lam2.b], [lam2.b])
                kb.op("dve", lambda l=l, li=li: V.tensor_scalar(out=sgl[:, l:l + 1], in0=pp[:, l, 78:79], scalar1=(1.0 - li), scalar2=None, op0=ALU.mult), [pp.b], [sgl.b])
            kb.pe([lambda: PEe.matmul(psum[:, 0, 0:2 * DEPTH], lhsT=ones_f[0:1, :], rhs=lam2[:].rearrange("p l a -> p (l a)"), start=True, stop=True)], [lam2.b, kon.b], [PB[0]])
            kb.op("dve", lambda: V.tensor_copy(out=lamc[:].rearrange("p l a -> p (l a)"), in_=psum[:, 0, 0:2 * DEPTH]), [PB[0]], [lamc.b])
            kb.barrier()

        with ExitStack() as s2:
            xb_ = [T(s2, nc, "xcp%d" % i, [128, 8, 512], F32) for i in range(2)]
            for bi, (t0, nb, col) in enumerate(blocks):
                x_ = xb_[bi % 2]
                kb.dma("sp", x_.b.name, x_[:, :, 0:nb], xT_in.rearrange("(c p) t -> p c t", p=128)[:, :, t0:t0 + nb], writes=[x_.b])
                kb.dma("sp", x_.b.name, xT.rearrange("(c p) t -> p c t", p=128)[:, :, t0:t0 + nb], x_[:, :, 0:nb], reads=[x_.b])
            kb.barrier()

        xTv = xT.rearrange("(c p) t -> p c t", p=128)
        NTT = TT // 128

        def blk_of(tt):
            t = tt * 128
            for bi, (t0, nb, col) in enumerate(blocks):
                if t0 <= t < t0 + nb:
                    return bi

        evac_rr = [0]

        def evac(out, in_, rb, wb, scale=None):
            evac_rr[0] += 1
            if evac_rr[0] % 2 == 0:
                if scale is None:
                    return kb.op("act", lambda: A.copy(out=out, in_=in_), rb, wb)
                return kb.op("act", lambda: A.mul(out=out, in_=in_, mul=scale), rb, wb)
            if scale is None:
                return kb.op("dve", lambda: V.tensor_copy(out=out, in_=in_), rb, wb)
            return kb.op("dve", lambda: V.tensor_scalar(out=out, in0=in_, scalar1=scale, scalar2=None, op0=ALU.mult), rb, wb)

        def run_threads(specs):
            th = [[g, 0, float(tot)] for g, tot in specs]
            while th:
                t_ = min(th, key=lambda x: x[1] / x[2])
                try:
                    next(t_[0])
                    t_[1] += 1
                except StopIteration:
                    th.remove(t_)

        def phase_AB(l):
            with ExitStack() as s2:
                hT = T(s2, nc, "hT", [128, 8, TT], BF16)
                hTb = [Buf("hT%d" % i) for i in range(len(blocks))]
                xa = T(s2, nc, "xa", [128, 8, 512], F32)
                sq = T(s2, nc, "sq", [128, 8, 512], F32)
                rs = T(s2, nc, "rs", [128, 512], F32)
                tmp = [T(s2, nc, "tmpa%d" % i, [128, 512], F32) for i in range(2)]
                wf = [T(s2, nc, "wf%d" % i, [128, 8, 512], F32) for i in range(2)]
                wb = [T(s2, nc, "wb%d" % i, [128, 8, 512], BF16) for i in range(2)]
                ost = [T(s2, nc, "ost%d" % i, [128, 512], BF16) for i in range(4)]
                osx = [T(s2, nc, "osx%d" % i, [128, 512], F32) for i in range(2)]
                w_inv = w_in[l].rearrange("(c p) n -> p c n", p=128)

                def loadw(si):
                    c0, wd, kind = SEGS[si]
                    f, b = wf[si % 2], wb[si % 2]
                    kb.dma("sp", f.b.name, f[:, :, 0:wd], w_inv[:, :, c0:c0 + wd], writes=[f.b])
                    kb.op("pool", lambda: P.tensor_copy(out=b[:, :, 0:wd], in_=f[:, :, 0:wd]), [f.b], [b.b])

                loadw(0)
                for bi, (t0, nb, col) in enumerate(blocks):
                    kb.dma("sp", "xa", xa[:, :, 0:nb], xTv[:, :, t0:t0 + nb], writes=[xa.b])
                    kb.op("act", lambda: A.activation(out=sq[:, :, 0:nb], in_=xa[:, :, 0:nb], func=AF.Square), [xa.b], [sq.b])
                    kb.pe([lambda c=c: PEe.matmul(psum[:, 0, 0:nb], lhsT=ones_f, rhs=sq[:, c, 0:nb], start=(c == 0), stop=(c == 7)) for c in range(8)],
                          [sq.b, kon.b], [PB[0]])
                    kb.op("act", lambda: A.activation(out=rs[:, 0:nb], in_=psum[:, 0, 0:nb], func=AF.Ln, bias=epsc[:, 0:1], scale=1.0 / D), [PB[0], epsc.b], [rs.b])
                    kb.op("act", lambda: A.activation(out=rs[:, 0:nb], in_=rs[:, 0:nb], func=AF.Exp, scale=-0.5), [rs.b], [rs.b])
                    for c in range(8):
                        tm = tmp[c % 2]
                        kb.op("dve", lambda: V.scalar_tensor_tensor(out=tm[:, 0:nb], in0=xa[:, c, 0:nb], scalar=gsc[:, l, c, col:col + 1], in1=rs[:, 0:nb],
                                                                    op0=ALU.mult, op1=ALU.mult), [xa.b, rs.b, gsc.b], [tm.b])
                        kb.op("act", lambda: A.activation(out=hT[:, c, t0:t0 + nb], in_=tm[:, 0:nb], func=AF.Identity, bias=modT[:, l, c, col:col + 1], scale=1.0),
                              [tm.b, modT.b], [hTb[bi]])
                k = 0
                pk = 0
                for si, (c0, wd, kind) in enumerate(SEGS):
                    if si + 1 < len(SEGS):
                        loadw(si + 1)
                    b = wb[si % 2]
                    if kind in "FX":
                        for bi, (t0, nb, col) in enumerate(blocks):
                            for n4 in range(wd // 128):
                                bank = 1 + pk % 6
                                pk += 1
                                kb.pe([lambda c=c: PEe.matmul(psum[:, bank, 0:nb], lhsT=b[:, c, n4 * 128:(n4 + 1) * 128], rhs=hT[:, c, t0:t0 + nb], start=(c == 0), stop=(c == 7))
                                       for c in range(8)], [b.b, hTb[bi]], [PB[bank]])
                                if kind == "F":
                                    o = ost[k % 4]
                                    k += 1
                                    evac(o[:, 0:nb], psum[:, bank, 0:nb], [PB[bank]], [o.b])
                                    kb.dma("sp", o.b.name, PF[c0 + n4 * 128:c0 + (n4 + 1) * 128, t0:t0 + nb], o[:, 0:nb], reads=[o.b])
                                else:
                                    o = osx[k % 2]
                                    k += 1
                                    evac(o[:, 0:nb], psum[:, bank, 0:nb], [PB[bank]], [o.b])
                                    kb.dma("sp", o.b.name, PX[n4 * 128:(n4 + 1) * 128, t0:t0 + nb], o[:, 0:nb], reads=[o.b])
                    else:
                        for tt in range(NTT):
                            bank = 1 + pk % 6
                            pk += 1
                            kb.pe([lambda c=c: PEe.matmul(psum[:, bank, 0:512], lhsT=hT[:, c, tt * 128:(tt + 1) * 128], rhs=b[:, c, 0:512], start=(c == 0), stop=(c == 7))
                                   for c in range(8)], [b.b, hTb[blk_of(tt)]], [PB[bank]])
                            o = ost[k % 4]
                            k += 1
                            evac(o[:, :], psum[:, bank, :], [PB[bank]], [o.b])
                            kb.dma("sp", o.b.name, PT[tt * 128:(tt + 1) * 128, TCOL[c0]:TCOL[c0] + 512], o[:, :], reads=[o.b])
                kb.barrier()

        def phase_C1(l, s2):
            if True:
                rT = T(s2, nc, "rT", [128, 4, 512], BF16)
                kT = T(s2, nc, "kT", [128, 4, 512], BF16)
                lw = T(s2, nc, "lw", [128, 512], F32)
                la = T(s2, nc, "la", [128, 512], F32)
                tl = T(s2, nc, "tl", [128, 512], BF16)
                lab = T(s2, nc, "lab", [128, 512], BF16)
                upf = T(s2, nc, "upf", [128, 2, 512], F32)
                upb = T(s2, nc, "upb", [128, 2, 512], BF16)
                names = ["kkraw", "rk", "kk", "sg", "a_", "akk", "t_", "kd", "csf", "cs", "cse", "Ep", "Em", "Epv", "tmpb"]
                f = {n: T(s2, nc, "c1_" + n, [128, 512], F32) for n in names}
                sqk = T(s2, nc, "sqk", [128, 512], BF16)
                ob = {n: [T(s2, nc, "c1o_%s%d" % (n, i), [128, 512], BF16) for i in range(2)] for n in ["kkg", "rg", "aki", "ki"]}
                gcs = [T(s2, nc, "gcs%d" % i, [128, 8], F32) for i in range(2)]
                tk = [T(s2, nc, "tk%d" % i, [128, 2, 4, 128], BF16) for i in range(2)]
                kb.dma("sp", "upf", upf[:, 0, :], dup[l], writes=[upf.b])
                kb.dma("sp", "upf", upf[:, 1, :], iup[l], writes=[upf.b])
                kb.op("dve", lambda: V.tensor_copy(out=upb[:], in_=upf[:]), [upf.b], [upb.b])
                it = 0
                for bi, (t0, nb, col) in enumerate(blocks):
                    nchk = nb // 64
                    ch0 = t0 // 64
                    ntt = nb // 128
                    kb.dma("sp", "rT", rT[:, :, 0:nb], PF[0:512, t0:t0 + nb].rearrange("(c p) t -> p c t", p=128), writes=[rT.b])
                    kb.dma("sp", "kT", kT[:, :, 0:nb], PF[512:1024, t0:t0 + nb].rearrange("(c p) t -> p c t", p=128), writes=[kT.b])
                    kb.dma("sp", "lw", lw[:, 0:nb], PX[0:128, t0:t0 + nb], writes=[lw.b])
                    kb.dma("sp", "la", la[:, 0:nb], PX[128:256, t0:t0 + nb], writes=[la.b])
                    kb.op("act", lambda: A.activation(out=tl[:, 0:nb], in_=lw[:, 0:nb], func=AF.Tanh), [lw.b], [tl.b])
                    kb.op("dve", lambda: V.tensor_copy(out=lab[:, 0:nb], in_=la[:, 0:nb]), [la.b], [lab.b])
                    for cc_ in range(4):
                        S = slice(0, nb)
                        kb.op("dve", lambda: V.tensor_scalar(out=f["kkraw"][:, S], in0=kT[:, cc_, S], scalar1=pp[:, l, 48 + cc_:49 + cc_], scalar2=None, op0=ALU.mult),
                              [kT.b, pp.b], [f["kkraw"].b])
                        kb.op("act", lambda: A.activation(out=sqk[:, S], in_=f["kkraw"][:, S], func=AF.Square), [f["kkraw"].b], [sqk.b])
                        kb.pe([lambda: PEe.matmul(psum[:, 0, S], lhsT=blk_b, rhs=sqk[:, S], start=True, stop=True)], [sqk.b, konb.b], [PB[0]])
                        kb.op("act", lambda: A.activation(out=f["rk"][:, S], in_=psum[:, 0, S], func=AF.Ln, bias=epsc[:, 1:2], scale=1.0), [PB[0], epsc.b], [f["rk"].b])
                        kb.op("act", lambda: A.activation(out=f["rk"][:, S], in_=f["rk"][:, S], func=AF.Exp, scale=-0.5), [f["rk"].b], [f["rk"].b])
                        kb.op("dve", lambda: V.tensor_tensor(out=f["kk"][:, S], in0=f["kkraw"][:, S], in1=f["rk"][:, S], op=ALU.mult), [f["kkraw"].b, f["rk"].b], [f["kk"].b])
                        for d in range(2):
                            R = slice(d * 64, (d + 1) * 64)
                            C = slice(cc_ * 128, (cc_ + 1) * 128)
                            o = {n: ob[n][it % 2] for n in ob}
                            g_ = gcs[it % 2]
                            tk_ = tk[it % 2]
                            it += 1
                            kb.pe([lambda: PEe.matmul(psum[:, 1, S], lhsT=upb[R, 0, C], rhs=tl[R, S], start=True, stop=True)], [upb.b, tl.b], [PB[1]])
                            kb.op("act", lambda: A.activation(out=f["sg"][:, S], in_=psum[:, 1, S], func=AF.Sigmoid, bias=pp[:, l, 32 + d * 4 + cc_:33 + d * 4 + cc_], scale=1.0),
                                  [PB[1], pp.b], [f["sg"].b])
                            kb.pe([lambda: PEe.matmul(psum[:, 2, S], lhsT=upb[R, 1, C], rhs=lab[R, S], start=True, stop=True)], [upb.b, lab.b], [PB[2]])
                            kb.op("act", lambda: A.activation(out=f["a_"][:, S], in_=psum[:, 2, S], func=AF.Sigmoid, bias=pp[:, l, 40 + d * 4 + cc_:41 + d * 4 + cc_], scale=1.0),
                                  [PB[2], pp.b], [f["a_"].b])
                            kb.op("pool", lambda: P.tensor_tensor(out=f["akk"][:, S], in0=f["a_"][:, S], in1=f["kk"][:, S], op=ALU.mult), [f["a_"].b, f["kk"].b], [f["akk"].b])
                            kb.op("dve", lambda: V.tensor_scalar(out=f["t_"][:, S], in0=f["a_"][:, S], scalar1=pp[:, l, 52 + cc_:53 + cc_], scalar2=omka[:, l, cc_:cc_ + 1],
                                                                 op0=ALU.mult, op1=ALU.add), [f["a_"].b, pp.b, omka.b], [f["t_"].b])
                            kb.op("pool", lambda: P.tensor_tensor(out=f["kd"][:, S], in0=f["t_"][:, S], in1=kT[:, cc_, S], op=ALU.mult), [f["t_"].b, kT.b], [f["kd"].b])
                            kb.op("dve", lambda: V.tensor_tensor_scan(out=f["csf"][:, S], data0=ltri[:, S], data1=f["sg"][:, S], initial=0.0, op0=ALU.mult, op1=ALU.add),
                                  [ltri.b, f["sg"].b], [f["csf"].b])
                            if d == 0:
                                cs = f["csf"]
                            else:
                                cs = f["cs"]
                                kb.op("dve", lambda: V.tensor_tensor(out=f["tmpb"][:, S], in0=f["sg"][:, S], in1=f["csf"][:, S], op=ALU.subtract), [f["sg"].b, f["csf"].b], [f["tmpb"].b])
                                kb.op("dve", lambda: V.tensor_tensor(out=cs[:, S].rearrange("p (c x) -> p c x", x=64), in0=f["tmpb"][:, S].rearrange("p (c x) -> p c x", x=64),
                                                                     in1=f["csf"][:, S].rearrange("p (c x) -> p c x", x=64)[:, :, 63:64].to_broadcast([128, nchk, 64]), op=ALU.add),
                                      [f["tmpb"].b, f["csf"].b], [cs.b])
                            kb.op("pool", lambda: P.tensor_tensor(out=f["cse"][:, S], in0=cs[:, S], in1=f["sg"][:, S], op=ALU.subtract), [cs.b, f["sg"].b], [f["cse"].b])
                            kb.op("act", lambda: A.activation(out=f["Ep"][:, S], in_=cs[:, S], func=AF.Exp, scale=-C0), [cs.b], [f["Ep"].b])
                            kb.op("act", lambda: A.activation(out=f["Em"][:, S], in_=cs[:, S], func=AF.Exp, scale=C0), [cs.b], [f["Em"].b])
                            kb.op("act", lambda: A.activation(out=f["Epv"][:, S], in_=f["cse"][:, S], func=AF.Exp, scale=-C0), [f["cse"].b], [f["Epv"].b])
                            kb.op("dve", lambda: V.tensor_tensor(out=o["rg"][:, S], in0=rT[:, cc_, S], in1=f["Ep"][:, S], op=ALU.mult), [rT.b, f["Ep"].b], [o["rg"].b])
                            kb.op("pool", lambda: P.tensor_tensor(out=o["kkg"][:, S], in0=f["kk"][:, S], in1=f["Epv"][:, S], op=ALU.mult), [f["kk"].b, f["Epv"].b], [o["kkg"].b])
                            kb.op("dve", lambda: V.tensor_tensor(out=o["aki"][:, S], in0=f["akk"][:, S], in1=f["Em"][:, S], op=ALU.mult), [f["akk"].b, f["Em"].b], [o["aki"].b])
                            kb.op("pool", lambda: P.tensor_tensor(out=o["ki"][:, S], in0=f["kd"][:, S], in1=f["Em"][:, S], op=ALU.mult), [f["kd"].b, f["Em"].b], [o["ki"].b])
                            e_ = 63 if d == 0 else 0
                            kb.op("dve", lambda: V.tensor_copy(out=g_[:, 0:nchk], in_=f["Ep"][:, S].rearrange("p (c x) -> p c x", x=64)[:, :, e_]), [f["Ep"].b], [g_.b])
                            kb.dma("sp", o["kkg"].b.name, QGs[d][C, ch0:ch0 + nchk, 0, :], o["kkg"][:, S].rearrange("p (c x) -> p c x", x=64), reads=[o["kkg"].b])
                            kb.dma("sp", o["rg"].b.name, QGs[d][C, ch0:ch0 + nchk, 1, :], o["rg"][:, S].rearrange("p (c x) -> p c x", x=64), reads=[o["rg"].b])
                            kb.dma("sp", o["aki"].b.name, AKT[d][C, t0:t0 + nb], o["aki"][:, S], reads=[o["aki"].b])
                            kb.dma("sp", o["ki"].b.name, KIT[d][C, t0:t0 + nb], o["ki"][:, S], reads=[o["ki"].b])
                            kb.dma("sp", g_.b.name, GCs[d][C, ch0:ch0 + nchk], g_[:, 0:nchk], reads=[g_.b])
                            fns = []
                            for qi, nm in enumerate(["aki", "ki"]):
                                for tt in range(ntt):
                                    fns.append(lambda qi=qi, nm=nm, tt=tt: PEe.transpose(psbf[3][:, (qi * 4 + tt) * 128:(qi * 4 + tt + 1) * 128], o[nm][:, tt * 128:(tt + 1) * 128], ident_b))
                            kb.pe(fns, [o["aki"].b, o["ki"].b, konb.b], [PB[3]])
                            kb.op("act", lambda: A.copy(out=tk_[:].rearrange("p q t c -> p (q t c)"), in_=psbf[3][:, :]), [PB[3]], [tk_.b])
                            kb.dma("sp", tk_.b.name + "a", AKK[d][t0:t0 + nb, C].rearrange("(t p) c -> p t c", p=128), tk_[:, 0, 0:ntt, :], reads=[tk_.b])
                            kb.dma("sp", tk_.b.name + "b", KIK[d][t0:t0 + nb, C].rearrange("(t p) c -> p t c", p=128), tk_[:, 1, 0:ntt, :], reads=[tk_.b])
                            yield

        def phase_C2(l):
            with ExitStack() as s2:
                def mk(nm, shape, dt):
                    ts = [T(s2, nc, "%s%d" % (nm, i), shape, dt) for i in range(2)]
                    for t_ in ts:
                        t_.b2 = Buf(t_.b.name + "x")
                    return ts
                QG = mk("QG", [128, 8, 8, 128], BF16)
                AT = mk("AT", [128, 8, 512], BF16)
                KT_ = mk("KT", [128, 8, 512], BF16)
                AK = mk("AK", [128, 8, 512], BF16)
                KK_ = mk("KK", [128, 8, 512], BF16)
                VV = mk("VV", [128, 8, 512], BF16)
                GC = mk("GC", [128, 8, 8], F32)
                Z = T(s2, nc, "Z", [128, 8, 64], F32)
                Zb = T(s2, nc, "Zb", [128, 8, 64], BF16)
                NT = [T(s2, nc, "NT%d" % i, [128, 8, 64], BF16) for i in range(5)]
                N_ = [T(s2, nc, "N%d" % i, [128, 8, 64], BF16) for i in range(6)]
                PTt = [T(s2, nc, "PTt%d" % i, [128, 8, 64], BF16) for i in range(2)]
                BabT = T(s2, nc, "BabT", [128, 8, 64], BF16)
                AakT = T(s2, nc, "AakT", [128, 8, 64], BF16)
                BakT = T(s2, nc, "BakT", [128, 8, 64], BF16)
                Wn = T(s2, nc, "Wn", [128, 8, 64], BF16)
                U = T(s2, nc, "U", [128, 8, 64], BF16)
                Yst = [T(s2, nc, "Yst%d" % i, [128, 512], F32) for i in range(2)]
                yi = [0]
                DS = [slice(0, 64), slice(64, 128)]
                nB = len(blocks)
                fo = list(range(nB))
                bo = [0] + list(range(nB - 1, 0, -1))

                def hv(bank):
                    return psum[:, bank, :].rearrange("p (h i) -> p h i", h=8)

                def load(s, par):
                    for d in range(2):
                        t0, nb, col = blocks[fo[s] if d == 0 else bo[s]]
                        nchk, ch0 = nb // 64, t0 // 64
                        if d == 0:
                            csl = slice(ch0, ch0 + nchk)
                        else:
                            csl = slice(ch0 + nchk - 1, (ch0 - 1) if ch0 > 0 else None, -1)
                        D_ = DS[d]
                        bsel = (lambda t_: t_.b) if d == 0 else (lambda t_: t_.b2)
                        sfx = "f" if d == 0 else "b"
                        def ld(t_, dst, src):
                            kb.dma("sp", t_.b.name + sfx, dst, src, writes=[bsel(t_)])

                        def ldh(t_, dstf, srcf):
                            if d == 0:
                                ld(t_, dstf(slice(None)), srcf(slice(None)))
                            else:
                                kb.dma_group("sp", t_.b.name + sfx, [(dstf(h), srcf(h)) for h in range(8)], writes=[bsel(t_)])
                        qsrc = QGs[d].rearrange("(h k) c w i -> k h c (w i)", k=64)
                        ldh(QG[par], lambda h: QG[par][D_, h, 0:nchk, :], lambda h: qsrc[:, h, csl, :])
                        asrc = AKT[d].rearrange("(h k) (c x) -> k h c x", k=64, x=64)
                        ldh(AT[par], lambda h: AT[par][D_, h, 0:nb].rearrange("p (c x) -> p c x", x=64) if not isinstance(h, slice) else AT[par][D_, :, 0:nb].rearrange("p h (c x) -> p h c x", x=64),
                            lambda h: asrc[:, h, csl, :])
                        ksrc = KIT[d].rearrange("(h k) (c x) -> k h c x", k=64, x=64)
                        ldh(KT_[par], lambda h: KT_[par][D_, h, 0:nb].rearrange("p (c x) -> p c x", x=64) if not isinstance(h, slice) else KT_[par][D_, :, 0:nb].rearrange("p h (c x) -> p h c x", x=64),
                            lambda h: ksrc[:, h, csl, :])
                        ld(AK[par], AK[par][D_, 0:nchk, :], AKK[d].rearrange("(c p) n -> p c n", p=64)[:, csl, :])
                        ld(KK_[par], KK_[par][D_, 0:nchk, :], KIK[d].rearrange("(c p) n -> p c n", p=64)[:, csl, :])
                        ld(VV[par], VV[par][D_, 0:nchk, :], PT[:, 0:512].rearrange("(c p) n -> p c n", p=64)[:, csl, :])
                        ld(GC[par], GC[par][D_, :, 0:nchk], GCs[d].rearrange("(h k) c -> k h c", k=64)[:, :, ch0:ch0 + nchk])

                def chunk(par, j, nchk, tokf, tokb):
                    qg, at, kt, ak, kk_, vv, gc = QG[par], AT[par], KT_[par], AK[par], KK_[par], VV[par], GC[par]
                    bb = lambda *ts: [x for t_ in ts for x in (t_.b, t_.b2)]
                    cs = slice(j * 64, j * 64 + 64)
                    H = lambda h: slice(h * 64, (h + 1) * 64)
                    HD = [(h, D_) for h in range(8) for D_ in DS]
                    fns = []
                    for h, D_ in HD:
                        fns.append(lambda h=h, D_=D_: PEe.matmul(psum[D_, h // 4, (h % 4) * 128:(h % 4) * 128 + 128], lhsT=at[D_, h, cs], rhs=qg[D_, h, j, :], start=True, stop=True))
                    for h, D_ in HD:
                        fns.append(lambda h=h, D_=D_: PEe.matmul(psum[D_, 2 + h // 4, (h % 4) * 128:(h % 4) * 128 + 128], lhsT=kt[D_, h, cs], rhs=qg[D_, h, j, :], start=True, stop=True))
                    for h, D_ in HD:
                        fns.append(lambda h=h, D_=D_: PEe.matmul(psum[D_, 4, H(h)], lhsT=qg[D_, h, j, 0:64], rhs=at[D_, h, cs], start=True, stop=True))
                    kb.pe(fns, bb(qg, at, kt), [PB[0], PB[1], PB[2], PB[3], PB[4]])
                    P1v = psum[:, 0:2, :].rearrange("p b (h w i) -> p (b h) w i", h=4, w=2)
                    P2v = psum[:, 2:4, :].rearrange("p b (h w i) -> p (b h) w i", h=4, w=2)
                    kb.op("dve", lambda: V.scalar_tensor_tensor(out=NT[0][:], in0=P1v[:, :, 0, :], scalar=-1.0, in1=msk2[:, 0], op0=ALU.mult, op1=ALU.mult), [PB[0], PB[1], msk2.b], [NT[0].b])
                    kb.op("dve", lambda: V.scalar_tensor_tensor(out=N_[0][:], in0=hv(4), scalar=-1.0, in1=msk2[:, 2], op0=ALU.mult, op1=ALU.mult), [PB[4], msk2.b], [N_[0].b])
                    kb.op("dve", lambda: V.tensor_tensor(out=AakT[:], in0=P2v[:, :, 0, :], in1=msk2[:, 0], op=ALU.mult), [PB[2], PB[3], msk2.b], [AakT.b])
                    kb.op("dve", lambda: V.tensor_tensor(out=BabT[:], in0=P1v[:, :, 1, :], in1=msk2[:, 1], op=ALU.mult), [PB[0], PB[1], msk2.b], [BabT.b])
                    kb.op("dve", lambda: V.tensor_tensor(out=BakT[:], in0=P2v[:, :, 1, :], in1=msk2[:, 1], op=ALU.mult), [PB[2], PB[3], msk2.b], [BakT.b])
                    kb.op("pool", lambda: P.tensor_tensor(out=PTt[0][:], in0=NT[0][:], in1=msk2[:, 3], op=ALU.add), [NT[0].b, msk2.b], [PTt[0].b])
                    fns = []
                    for h, D_ in HD:
                        fns.append(lambda h=h, D_=D_: PEe.matmul(psum[D_, 5, H(h)], lhsT=qg[D_, h, j, 0:64], rhs=Zb[D_, h, :], start=True, stop=False))
                        fns.append(lambda h=h, D_=D_: PEe.matmul(psum[D_, 5, H(h)], lhsT=AakT[D_, h, :], rhs=vv[D_, j, H(h)], start=False, stop=True))
                    kb.pe(fns, bb(qg, vv) + [Zb.b, AakT.b], [PB[5]])
                    kb.op("act", lambda: A.mul(out=Wn[:], in_=hv(5), mul=-1.0), [PB[5]], [Wn.b])
                    for k in range(5):
                        fns = [lambda h=h, D_=D_: PEe.matmul(psum[D_, 6, H(h)], lhsT=NT[k][D_, h, :], rhs=N_[k][D_, h, :], start=True, stop=True) for h, D_ in HD]
                        kb.pe(fns, [NT[k].b, N_[k].b], [PB[6]])
                        kb.op("act", lambda: A.copy(out=N_[k + 1][:], in_=hv(6)), [PB[6]], [N_[k + 1].b])
                        if k < 4:
                            fns = [lambda h=h, D_=D_: PEe.matmul(psum[D_, 7, H(h)], lhsT=N_[k][D_, h, :], rhs=NT[k][D_, h, :], start=True, stop=True) for h, D_ in HD]
                            kb.pe(fns, [NT[k].b, N_[k].b], [PB[7]])
                            kb.op("act", lambda: A.copy(out=NT[k + 1][:], in_=hv(7)), [PB[7]], [NT[k + 1].b])
                        src, dst = PTt[k % 2], PTt[(k + 1) % 2]
                        fns = [lambda h=h, D_=D_: PEe.matmul(psum[D_, 0, H(h)], lhsT=N_[k + 1][D_, h, :], rhs=src[D_, h, :], start=True, stop=True) for h, D_ in HD]
                        kb.pe(fns, [N_[k + 1].b, src.b], [PB[0]])
                        kb.op("dve", lambda: V.tensor_tensor(out=dst[:], in0=hv(0), in1=src[:], op=ALU.add), [PB[0], src.b], [dst.b])
                    TTf = PTt[1]
                    fns = [lambda h=h, D_=D_: PEe.matmul(psum[D_, 1, H(h)], lhsT=TTf[D_, h, :], rhs=Wn[D_, h, :], start=True, stop=True) for h, D_ in HD]
                    kb.pe(fns, [TTf.b, Wn.b], [PB[1]])
                    kb.op("dve", lambda: V.tensor_copy(out=U[:], in_=hv(1)), [PB[1]], [U.b])
                    fns = []
                    for h, D_ in HD:
                        fns.append(lambda h=h, D_=D_: PEe.matmul(psum[D_, 2, H(h)], lhsT=qg[D_, h, j, 64:128], rhs=Zb[D_, h, :], start=True, stop=False))
                        fns.append(lambda h=h, D_=D_: PEe.matmul(psum[D_, 2, H(h)], lhsT=BabT[D_, h, :], rhs=U[D_, h, :], start=False, stop=False))
                        fns.append(lambda h=h, D_=D_: PEe.matmul(psum[D_, 2, H(h)], lhsT=BakT[D_, h, :], rhs=vv[D_, j, H(h)], start=False, stop=True))
                    kb.pe(fns, bb(qg, vv) + [Zb.b, BabT.b, U.b, BakT.b], [PB[2]])
                    ys = Yst[yi[0] % 2]
                    yi[0] += 1
                    kb.op("act", lambda: A.copy(out=ys[:], in_=psum[:, 2, :]), [PB[2]], [ys.b])
                    kb.dma("sp", ys.b.name + "f", YS[0][tokf:tokf + 64, :], ys[0:64, :], reads=[ys.b])
                    kb.dma("sp", ys.b.name + "b", YS[1][tokb:tokb + 64, :], ys[64:128, :], reads=[ys.b])
                    fns = []
                    for h, D_ in HD:
                        fns.append(lambda h=h, D_=D_: PEe.matmul(psum[D_, 3, H(h)], lhsT=ak[D_, j, H(h)], rhs=U[D_, h, :], start=True, stop=False))
                        fns.append(lambda h=h, D_=D_: PEe.matmul(psum[D_, 3, H(h)], lhsT=kk_[D_, j, H(h)], rhs=vv[D_, j, H(h)], start=False, stop=True))
                    kb.pe(fns, bb(ak, kk_, vv) + [U.b], [PB[3]])
                    kb.op("dve", lambda: V.tensor_tensor(out=Z[:], in0=hv(3), in1=Z[:], op=ALU.add), [PB[3], Z.b], [Z.b])
                    jb = nchk - 1 - j
                    kb.op("dve", lambda: V.tensor_tensor(out=Z[0:64], in0=Z[0:64], in1=gc[0:64, :, j:j + 1].to_broadcast([64, 8, 64]), op=ALU.mult), [Z.b, gc.b], [Z.b])
                    kb.op("pool", lambda: P.tensor_tensor(out=Z[64:128], in0=Z[64:128], in1=gc[64:128, :, jb:jb + 1].to_broadcast([64, 8, 64]), op=ALU.mult), [Z.b, gc.b2], [Z.b])
                    kb.op("act", lambda: A.copy(out=Zb[:], in_=Z[:]), [Z.b], [Zb.b])

                kb.op("dve", lambda: V.memset(Z[:], 0.0), [], [Z.b])
                kb.op("dve", lambda: V.memset(Zb[:], 0.0), [], [Zb.b])
                load(0, 0)
                for s in range(nB):
                    par = s % 2
                    if s + 1 < nB:
                        load(s + 1, 1 - par)
                    t0f, nb, _ = blocks[fo[s]]
                    t0b, _, _ = blocks[bo[s]]
                    nchk = nb // 64
                    for j in range(nchk):
                        chunk(par, j, nchk, t0f + j * 64, t0b + (nchk - 1 - j) * 64)
                kb.barrier()

        def phase_C3(l, s2):
            if True:
                bc = T(s2, nc, "bcg", [128, 2, 512], F32)
                kb.dma("sp", "bcg", bc[:].rearrange("p a n -> p (a n)"), BCd[:, l * 1024:(l + 1) * 1024], writes=[bc.b])
                def mk(nm, shape, dt):
                    return [T(s2, nc, "%s%d" % (nm, i), shape, dt) for i in range(2)]
                yf, yb = mk("yf", [128, 512], F32), mk("yb", [128, 512], F32)
                vz = mk("vz", [128, 1024], BF16)
                rk_ = mk("rk_", [128, 2, 4, 128], BF16)
                rkr = mk("rkr", [128, 4, 128], BF16)
                y = mk("y_", [128, 512], F32)
                ysq = mk("ysq", [128, 512], F32)
                st_ = mk("st_", [128, 6, 8], F32)
                sz = mk("sz", [128, 512], F32)
                bon = mk("bon", [128, 512], F32)
                yg = mk("yg", [128, 512], BF16)
                ygT = mk("ygT", [128, 4, 128], BF16)
                for tt in range(NTT):
                    p = tt % 2
                    ts_ = slice(tt * 128, (tt + 1) * 128)
                    kb.dma("sp", yf[p].b.name, yf[p][:], YS[0][ts_, :], writes=[yf[p].b])
                    kb.dma("sp", yb[p].b.name, yb[p][:], YS[1][ts_, :], writes=[yb[p].b])
                    kb.dma("sp", vz[p].b.name, vz[p][:], PT[ts_, 0:1024], writes=[vz[p].b])
                    kb.dma("sp", rk_[p].b.name, rk_[p][:].rearrange("p a c t -> p (a c) t"), PF[0:1024, ts_].rearrange("(c p) t -> p c t", p=128), writes=[rk_[p].b])
                    for c in range(4):
                        kb.op("dve", lambda: V.scalar_tensor_tensor(out=rkr[p][:, c, :], in0=rk_[p][:, 0, c, :], scalar=pp[:, l, 56 + c:57 + c], in1=rk_[p][:, 1, c, :],
                                                                     op0=ALU.mult, op1=ALU.mult), [rk_[p].b, pp.b], [rkr[p].b])
                    kb.pe([lambda c=c: PEe.matmul(psum[:, 6, 0:8], lhsT=rkr[p][:, c, :], rhs=sel_b[:, c * 8:(c + 1) * 8], start=(c == 0), stop=(c == 3)) for c in range(4)],
                          [rkr[p].b, konb.b], [PB[6]])
                    S_ = st_[p]
                    kb.op("act", lambda: A.copy(out=S_[:, 5, :], in_=psum[:, 6, 0:8]), [PB[6]], [S_.b])
                    kb.op("pool", lambda: P.tensor_tensor(out=y[p][:], in0=yf[p][:], in1=yb[p][:], op=ALU.add), [yf[p].b, yb[p].b], [y[p].b])
                    y3 = y[p][:].rearrange("p (h x) -> p h x", h=8)
                    kb.op("dve", lambda: V.tensor_reduce(out=S_[:, 0, :], in_=y3, axis=AX.X, op=ALU.add), [y[p].b], [S_.b])
                    kb.op("act", lambda: A.activation(out=ysq[p][:], in_=y[p][:], func=AF.Square), [y[p].b], [ysq[p].b])
                    kb.op("dve", lambda: V.tensor_reduce(out=S_[:, 1, :], in_=ysq[p][:].rearrange("p (h x) -> p h x", h=8), axis=AX.X, op=ALU.add), [ysq[p].b], [S_.b])
                    kb.op("dve", lambda: V.tensor_scalar(out=S_[:, 2, :], in0=S_[:, 0, :], scalar1=1.0 / 64, scalar2=None, op0=ALU.mult), [S_.b], [S_.b])
                    kb.op("dve", lambda: V.tensor_tensor(out=S_[:, 3, :], in0=S_[:, 2, :], in1=S_[:, 2, :], op=ALU.mult), [S_.b], [S_.b])
                    kb.op("dve", lambda: V.scalar_tensor_tensor(out=S_[:, 3, :], in0=S_[:, 1, :], scalar=1.0 / 64, in1=S_[:, 3, :], op0=ALU.mult, op1=ALU.subtract), [S_.b], [S_.b])
                    kb.op("act", lambda: A.activation(out=S_[:, 4, :], in_=S_[:, 3, :], func=AF.Sqrt, bias=GN_EPS, scale=1.0), [S_.b], [S_.b])
                    kb.op("dve", lambda: V.reciprocal(out=S_[:, 4, :], in_=S_[:, 4, :]), [S_.b], [S_.b])
                    kb.op("dve", lambda: V.tensor_tensor(out=y3, in0=y3, in1=S_[:, 2, :].unsqueeze(2).to_broadcast([128, 8, 64]), op=ALU.subtract), [y[p].b, S_.b], [y[p].b])
                    kb.op("dve", lambda: V.tensor_tensor(out=y3, in0=y3, in1=S_[:, 4, :].unsqueeze(2).to_broadcast([128, 8, 64]), op=ALU.mult), [y[p].b, S_.b], [y[p].b])
                    kb.op("pool", lambda: P.tensor_tensor(out=y[p][:], in0=y[p][:], in1=bc[:, 0, :], op=ALU.mult), [y[p].b, bc.b], [y[p].b])
                    kb.op("pool", lambda: P.tensor_tensor(out=y[p][:], in0=y[p][:], in1=bc[:, 1, :], op=ALU.add), [y[p].b, bc.b], [y[p].b])
                    kb.op("dve", lambda: V.tensor_tensor(out=bon[p][:].rearrange("p (h x) -> p h x", h=8), in0=vz[p][:, 0:512].rearrange("p (h x) -> p h x", h=8),
                                                         in1=S_[:, 5, :].unsqueeze(2).to_broadcast([128, 8, 64]), op=ALU.mult), [vz[p].b, S_.b], [bon[p].b])
                    kb.op("pool", lambda: P.tensor_tensor(out=y[p][:], in0=y[p][:], in1=bon[p][:], op=ALU.add), [y[p].b, bon[p].b], [y[p].b])
                    kb.op("act", lambda: A.activation(out=sz[p][:], in_=vz[p][:, 512:1024], func=AF.Silu), [vz[p].b], [sz[p].b])
                    kb.op("dve", lambda: V.tensor_tensor(out=yg[p][:], in0=y[p][:], in1=sz[p][:], op=ALU.mult), [y[p].b, sz[p].b], [yg[p].b])
                    kb.pe([lambda c=c: PEe.transpose(psbf[6][:, 512 + c * 128:512 + (c + 1) * 128], yg[p][:, c * 128:(c + 1) * 128], ident_b) for c in range(4)], [yg[p].b, konb.b], [PB[6]])
                    kb.op("act", lambda: A.copy(out=ygT[p][:].rearrange("p c t -> p (c t)"), in_=psbf[6][:, 512:1024]), [PB[6]], [ygT[p].b])
                    kb.dma("sp", ygT[p].b.name, YG[0:512, ts_].rearrange("(c p) t -> p c t", p=128), ygT[p][:], reads=[ygT[p].b])
                    yield

        def phase_D(l, s2):
            if True:
                gbz = T(s2, nc, "gbz", [128, 2, 4, 512], BF16)
                gcu = T(s2, nc, "gcu", [128, 2, 4, 514], BF16)
                gu = T(s2, nc, "gu", [128, 4, 514], F32)
                acc = T(s2, nc, "acc", [128, 4, 512], F32)
                szd = T(s2, nc, "szd", [128, 4, 512], F32)
                od = T(s2, nc, "od", [128, 4, 512], BF16)
                for bi, (t0, nb, col) in enumerate(blocks):
                    s0, s1 = (0, CTX) if col == 1 else (CTX, TT)
                    lo, hi = max(s0, t0 - 1), min(s1, t0 + nb + 1)
                    kb.op("pool", lambda: P.memset(gcu[:], 0.0), [], [gcu.b])
                    off = lo - (t0 - 1)
                    for j, r0 in enumerate([2816, 3328]):
                        kb.dma("sp", "gcu%d" % j, gcu[:, j, :, off:off + hi - lo], PF[r0:r0 + 512, lo:hi].rearrange("(c p) t -> p c t", p=128), writes=[gcu.b])
                    for j, r0 in enumerate([2304, 3840]):
                        kb.dma("sp", "gbz%d" % j, gbz[:, j, :, 0:nb], PF[r0:r0 + 512, t0:t0 + nb].rearrange("(c p) t -> p c t", p=128), writes=[gbz.b])
                    kb.op("dve", lambda: V.tensor_tensor(out=gu[:, :, 0:nb + 2], in0=gcu[:, 0, :, 0:nb + 2], in1=gcu[:, 1, :, 0:nb + 2], op=ALU.mult), [gcu.b], [gu.b])
                    for c in range(4):
                        cw = lambda j: pp[:, l, 60 + j * 4 + c:61 + j * 4 + c]
                        kb.op("dve", lambda: V.tensor_scalar(out=acc[:, c, 0:nb], in0=gu[:, c, 0:nb], scalar1=cw(0), scalar2=None, op0=ALU.mult), [gu.b, pp.b], [acc.b])
                        kb.op("dve", lambda: V.scalar_tensor_tensor(out=acc[:, c, 0:nb], in0=gu[:, c, 1:nb + 1], scalar=cw(1), in1=acc[:, c, 0:nb], op0=ALU.mult, op1=ALU.add), [gu.b, pp.b, acc.b], [acc.b])
                        kb.op("dve", lambda: V.scalar_tensor_tensor(out=acc[:, c, 0:nb], in0=gu[:, c, 2:nb + 2], scalar=cw(2), in1=acc[:, c, 0:nb], op0=ALU.mult, op1=ALU.add), [gu.b, pp.b, acc.b], [acc.b])
                        kb.op("dve", lambda: V.scalar_tensor_tensor(out=acc[:, c, 0:nb], in0=acc[:, c, 0:nb], scalar=pp[:, l, 72 + c:73 + c], in1=gbz[:, 0, c, 0:nb], op0=ALU.add, op1=ALU.mult),
                              [acc.b, pp.b, gbz.b], [acc.b])
                        yield
                    kb.op("act", lambda: A.activation(out=szd[:, :, 0:nb], in_=gbz[:, 1, :, 0:nb], func=AF.Silu), [gbz.b], [szd.b])
                    kb.op("pool", lambda: P.tensor_tensor(out=od[:, :, 0:nb], in0=acc[:, :, 0:nb], in1=szd[:, :, 0:nb], op=ALU.mult), [acc.b, szd.b], [od.b])
                    kb.dma("sp", "od", YG[512:1024, t0:t0 + nb].rearrange("(c p) t -> p c t", p=128), od[:, :, 0:nb], reads=[od.b])

        def phase_E1(l, s2):
            if True:
                pg = T(s2, nc, "pg", [128, 2, 128], BF16)
                qk = [T(s2, nc, "qk%d" % i, [128, 8, 512], BF16) for i in range(2)]
                cs_ = [T(s2, nc, "cs_%d" % i, [128, 2, 512], F32) for i in range(2)]
                sqe = [T(s2, nc, "sqe%d" % i, [128, 512], BF16) for i in range(2)]
                ri = [T(s2, nc, "ri%d" % i, [128, 512], F32) for i in range(2)]
                t1 = [T(s2, nc, "t1_%d" % i, [128, 512], F32) for i in range(2)]
                t2 = [T(s2, nc, "t2_%d" % i, [128, 512], F32) for i in range(2)]
                oq = [T(s2, nc, "oq%d" % i, [128, 512], BF16) for i in range(2)]
                for j in range(2):
                    kb.op("dve", lambda: V.tensor_scalar(out=pg[:, j, :], in0=perm_f, scalar1=pp[:, l, 76 + j:77 + j], scalar2=None, op0=ALU.mult), [kon.b, pp.b], [pg.b])
                for bi, (t0, nb, col) in enumerate(blocks):
                    S = slice(0, nb)
                    q_, c_ = qk[bi % 2], cs_[bi % 2]
                    kb.dma_group("sp", q_.b.name, [(q_[:, 0:4, S], PF[4352:4864, t0:t0 + nb].rearrange("(c p) t -> p c t", p=128)),
                                                   (q_[:, 4:8, S], PF[4864:5376, t0:t0 + nb].rearrange("(c p) t -> p c t", p=128))], writes=[q_.b])
                    kb.dma_group("sp", c_.b.name, [(c_[:, 0, S], ROPE[:, t0:t0 + nb]), (c_[:, 1, S], ROPE[:, TT + t0:TT + t0 + nb])], writes=[c_.b])
                    for j in range(8):
                        w = j // 4
                        p = j % 2
                        kb.op("act", lambda: A.activation(out=sqe[p][:, S], in_=q_[:, j, S], func=AF.Square), [q_.b], [sqe[p].b])
                        kb.pe([lambda: PEe.matmul(psum[:, 4 + p, S], lhsT=blk_b, rhs=sqe[p][:, S], start=True, stop=True)], [sqe[p].b, konb.b], [PB[4 + p]])
                        kb.pe([lambda: PEe.matmul(psum[:, 6 + p, S], lhsT=pg[:, w, :], rhs=q_[:, j, S], start=True, stop=True)], [pg.b, q_.b], [PB[6 + p]])
                        kb.op("act", lambda: A.activation(out=ri[p][:, S], in_=psum[:, 4 + p, S], func=AF.Ln, bias=epsc[:, 0:1], scale=1.0 / 64), [PB[4 + p], epsc.b], [ri[p].b])
                        kb.op("act", lambda: A.activation(out=ri[p][:, S], in_=ri[p][:, S], func=AF.Exp, scale=-0.5), [ri[p].b], [ri[p].b])
                        kb.op("dve", lambda: V.scalar_tensor_tensor(out=t1[p][:, S], in0=q_[:, j, S], scalar=pp[:, l, 76 + w:77 + w], in1=c_[:, 0, S], op0=ALU.mult, op1=ALU.mult),
                              [q_.b, pp.b, c_.b], [t1[p].b])
                        kb.op("dve", lambda: V.tensor_tensor(out=t2[p][:, S], in0=psum[:, 6 + p, S], in1=c_[:, 1, S], op=ALU.mult), [PB[6 + p], c_.b], [t2[p].b])
                        kb.op("pool", lambda: P.tensor_tensor(out=t1[p][:, S], in0=t1[p][:, S], in1=t2[p][:, S], op=ALU.add), [t1[p].b, t2[p].b], [t1[p].b])
                        kb.op("pool", lambda: P.tensor_tensor(out=oq[p][:, S], in0=t1[p][:, S], in1=ri[p][:, S], op=ALU.mult), [t1[p].b, ri[p].b], [oq[p].b])
                        dst = QRs if w == 0 else KRs
                        kb.dma("sp", oq[p].b.name, dst[(j % 4) * 128:(j % 4 + 1) * 128, t0:t0 + nb], oq[p][:, S], reads=[oq[p].b])
                        yield

        def phase_E2(l, last, s2):
            if True:
                QR = T(s2, nc, "QR", [128, 4, TT], BF16)
                KR = T(s2, nc, "KR", [128, 4, TT], BF16)
                VD = T(s2, nc, "VD", [128, NTT, 512], BF16)
                PTs = [[T(s2, nc, "PTs%d_%d" % (i, c), [128, 512], BF16) for c in range(2)] for i in range(3)]
                accD = [T(s2, nc, "accD%d" % i, [128, 512], F32) for i in range(2)]
                accP = [T(s2, nc, "accP%d" % i, [128, 512], F32) for i in range(2)]
                o0 = [T(s2, nc, "o0_%d" % i, [128, 512], F32) for i in range(2)]
                r0 = T(s2, nc, "r0", [128, 512], F32)
                r0a = T(s2, nc, "r0a", [128, 512], F32)
                sq2 = T(s2, nc, "sq2", [128, 512], BF16)
                rst = T(s2, nc, "rst", [128, 512], F32)
                zc = T(s2, nc, "zc", [128, 512], BF16)
                szc = T(s2, nc, "szc", [128, 512], F32)
                oc = T(s2, nc, "oc", [128, 512], BF16)
                kb.dma("sp", "VD", VD[:], PT[:, 1024:1536].rearrange("(t p) n -> p t n", p=128), writes=[VD.b])
                kb.dma("sp", "QR", QR[:], QRs.rearrange("(h p) t -> p h t", p=128), writes=[QR.b])
                kb.dma("sp", "KR", KR[:], KRs.rearrange("(h p) t -> p h t", p=128), writes=[KR.b])
                pending = []

                def flush(n):
                    for _ in range(min(n, len(pending))):
                        pending.pop(0)()
                n_ = 0
                bidx = 0
                for h in range(4):
                    for bi, (t0, nb, col) in enumerate(blocks):
                        if col == 1 and last:
                            continue
                        S = slice(0, nb)
                        kts = list(range(CTX // 128)) if col == 1 else list(range(NTT))
                        aD, aP = accD[bidx % 2], accP[bidx % 2]
                        bidx += 1

                        def smm(i):
                            kt = kts[i]
                            b0 = 2 * ((n_ + i) % 2)
                            kb.pe([lambda c=c: PEe.matmul(psum[:, b0 + c, S], lhsT=KR[c * 64:(c + 1) * 64, h, kt * 128:(kt + 1) * 128], rhs=QR[c * 64:(c + 1) * 64, h, t0:t0 + nb],
                                                          start=True, stop=True) for c in range(2)], [KR.b, QR.b], [PB[b0], PB[b0 + 1]])
                        smm(0)
                        for i, kt in enumerate(kts):
                            if i + 1 < len(kts):
                                smm(i + 1)
                            b0 = 2 * ((n_ + i) % 2)
                            pt = PTs[(n_ + i) % 3]
                            first, lastk = (i == 0), (i == len(kts) - 1)
                            for c in range(2):
                                kb.op("act", lambda: A.activation(out=pt[c][:, S], in_=psum[:, b0 + c, S], func=AF.Exp, scale=0.125), [PB[b0 + c]], [pt[c].b])
                                fns = [lambda: PEe.matmul(psum[:, 4 + c, S], lhsT=VD[:, kt, h * 128:(h + 1) * 128], rhs=pt[c][:, S], start=first, stop=lastk)]
                                wr = [PB[4 + c]]
                                if c == 0:
                                    fns.append(lambda: PEe.matmul(psum[:, 7, S], lhsT=ones_b, rhs=pt[0][:, S], start=first, stop=lastk))
                                    wr.append(PB[7])
                                kb.pe(fns, [VD.b, pt[c].b, konb.b], wr)
                            if i % 2 == 0:
                                if i == 0:
                                    kb.op("dve", lambda: V.tensor_copy(out=aD[:, S], in_=pt[1][:, S]), [pt[1].b], [aD.b])
                                else:
                                    kb.op("dve", lambda: V.tensor_tensor(out=aD[:, S], in0=aD[:, S], in1=pt[1][:, S], op=ALU.add), [pt[1].b, aD.b], [aD.b])
                            else:
                                if i == 1:
                                    kb.op("pool", lambda: P.tensor_copy(out=aP[:, S], in_=pt[1][:, S]), [pt[1].b], [aP.b])
                                else:
                                    kb.op("pool", lambda: P.tensor_tensor(out=aP[:, S], in0=aP[:, S], in1=pt[1][:, S], op=ALU.add), [pt[1].b, aP.b], [aP.b])
                            flush(1)
                            yield
                        n_ += len(kts)
                        flush(len(pending))
                        kb.op("act", lambda: A.copy(out=o0[0][:, S], in_=psum[:, 4, S]), [PB[4]], [o0[0].b])
                        kb.op("dve", lambda: V.tensor_copy(out=o0[1][:, S], in_=psum[:, 5, S]), [PB[5]], [o0[1].b])
                        kb.op("act", lambda: A.copy(out=r0a[:, S], in_=psum[:, 7, S]), [PB[7]], [r0a.b])

                        def mk_epilogue(h=h, t0=t0, nb=nb, S=S, aD=aD, aP=aP):
                            ops = []
                            ops.append(lambda: kb.dma("sp", "zc", zc[:, S], PF[5888 + h * 128:5888 + (h + 1) * 128, t0:t0 + nb], writes=[zc.b]))
                            ops.append(lambda: kb.op("dve", lambda: V.reciprocal(out=r0a[:, S], in_=r0a[:, S]), [r0a.b], [r0a.b]))
                            ops.append(lambda: kb.op("dve", lambda: V.tensor_tensor(out=o0[0][:, S], in0=o0[0][:, S], in1=r0a[:, S], op=ALU.mult), [o0[0].b, r0a.b], [o0[0].b]))
                            ops.append(lambda: kb.pe([lambda: PEe.matmul(psum[:, 6, S], lhsT=ones_f, rhs=aD[:, S], start=True, stop=False),
                                                      lambda: PEe.matmul(psum[:, 6, S], lhsT=ones_f, rhs=aP[:, S], start=False, stop=True)], [aD.b, aP.b, kon.b], [PB[6]]))
                            ops.append(lambda: kb.op("dve", lambda: V.reciprocal(out=r0[:, S], in_=psum[:, 6, S]), [PB[6]], [r0.b]))
                            ops.append(lambda: kb.op("dve", lambda: V.tensor_tensor(out=o0[1][:, S], in0=o0[1][:, S], in1=r0[:, S], op=ALU.mult), [o0[1].b, r0.b], [o0[1].b]))
                            ops.append(lambda: kb.op("dve", lambda: V.scalar_tensor_tensor(out=o0[0][:, S], in0=o0[1][:, S], scalar=lamc[:, l, 0:1], in1=o0[0][:, S], op0=ALU.mult, op1=ALU.add),
                                                     [o0[0].b, o0[1].b, lamc.b], [o0[0].b]))
                            ops.append(lambda: kb.op("act", lambda: A.activation(out=sq2[:, S], in_=o0[0][:, S], func=AF.Square), [o0[0].b], [sq2.b]))
                            ops.append(lambda: kb.pe([lambda: PEe.matmul(psum[:, 6, S], lhsT=ones_b, rhs=sq2[:, S], start=True, stop=True)], [sq2.b, konb.b], [PB[6]]))
                            ops.append(lambda: kb.op("act", lambda: A.activation(out=rst[:, S], in_=psum[:, 6, S], func=AF.Ln, bias=epsc[:, 0:1], scale=1.0 / 128), [PB[6], epsc.b], [rst.b]))
                            ops.append(lambda: kb.op("act", lambda: A.activation(out=rst[:, S], in_=rst[:, S], func=AF.Exp, scale=-0.5), [rst.b], [rst.b]))
                            ops.append(lambda: kb.op("dve", lambda: V.scalar_tensor_tensor(out=o0[0][:, S], in0=o0[0][:, S], scalar=sgl[:, l:l + 1], in1=rst[:, S], op0=ALU.mult, op1=ALU.mult),
                                                     [o0[0].b, sgl.b, rst.b], [o0[0].b]))
                            ops.append(lambda: kb.op("act", lambda: A.activation(out=szc[:, S], in_=zc[:, S], func=AF.Silu), [zc.b], [szc.b]))
                            ops.append(lambda: kb.op("pool", lambda: P.tensor_tensor(out=oc[:, S], in0=o0[0][:, S], in1=szc[:, S], op=ALU.mult), [o0[0].b, szc.b], [oc.b]))
                            ops.append(lambda: kb.dma("sp", "oc", YG[1024 + h * 128:1024 + (h + 1) * 128, t0:t0 + nb], oc[:, S], reads=[oc.b]))
                            return ops
                        pending.extend(mk_epilogue())
                flush(len(pending))

        def phase_F(l, last):
            with ExitStack() as s2:
                wst = [T(s2, nc, "wst%d" % i, [128, 4, 1024], F32) for i in range(2)]
                wbr = T(s2, nc, "wbr", [128, 12, 1024], BF16)
                wou = T(s2, nc, "wou", [128, 8, 1024], BF16)
                for i in range(5):
                    w = wst[i % 2]
                    src = w_br[l][i * 512:(i + 1) * 512, :] if i < 3 else w_out[l][(i - 3) * 512:(i - 2) * 512, :]
                    kb.dma("sp", w.b.name, w[:], src.rearrange("(c p) n -> p c n", p=128), writes=[w.b])
                    dstt = wbr[:, i * 4:(i + 1) * 4, :] if i < 3 else wou[:, (i - 3) * 4:(i - 2) * 4, :]
                    dstb = wbr.b if i < 3 else wou.b
                    kb.op("pool", lambda: P.tensor_copy(out=dstt, in_=w[:]), [w.b], [dstb])
                ygb = T(s2, nc, "ygb", [128, 12, 512], BF16)
                gm = T(s2, nc, "gm", [128, 24, 512], BF16)
                xa = T(s2, nc, "xaf", [128, 8, 512], F32)
                xn = T(s2, nc, "xnf", [128, 8, 512], F32)
                mT = T(s2, nc, "mT", [128, 8, 512], BF16)
                sg = [T(s2, nc, "sgf%d" % i, [128, 512], F32) for i in range(3)]
                ma = [T(s2, nc, "ma%d" % i, [128, 512], F32) for i in range(3)]
                for bi, (t0, nb, col) in enumerate(blocks):
                    if col == 1 and last:
                        continue
                    S = slice(0, nb)
                    kb.dma("sp", "ygb", ygb[:, :, S], YG[:, t0:t0 + nb].rearrange("(c p) t -> p c t", p=128), writes=[ygb.b])
                    kb.dma("sp", "gm", gm[:, :, S], PF[6400:9472, t0:t0 + nb].rearrange("(c p) t -> p c t", p=128), writes=[gm.b])
                    kb.dma("sp", "xaf", xa[:, :, S], xTv[:, :, t0:t0 + nb], writes=[xa.b])
                    for dc in range(8):
                        for i in range(3):
                            kb.pe([lambda w=w: PEe.matmul(psum[:, 1 + i, S], lhsT=wbr[:, i * 4 + w, dc * 128:(dc + 1) * 128], rhs=ygb[:, i * 4 + w, S], start=(w == 0), stop=(w == 3))
                                   for w in range(4)], [wbr.b, ygb.b], [PB[1 + i]])
                            kb.op("act", lambda: A.activation(out=sg[i][:, S], in_=gm[:, i * 8 + dc, S], func=AF.Sigmoid), [gm.b], [sg[i].b])
                            kb.op("dve", lambda: V.tensor_tensor(out=ma[i][:, S], in0=psum[:, 1 + i, S], in1=sg[i][:, S], op=ALU.mult), [PB[1 + i], sg[i].b], [ma[i].b])
                        kb.op("pool", lambda: P.tensor_tensor(out=ma[0][:, S], in0=ma[0][:, S], in1=ma[1][:, S], op=ALU.add), [ma[0].b, ma[1].b], [ma[0].b])
                        kb.op("pool", lambda: P.tensor_tensor(out=mT[:, dc, S], in0=ma[0][:, S], in1=ma[2][:, S], op=ALU.add), [ma[0].b, ma[2].b], [mT.b])
                    for d2 in range(8):
                        bank = 4 + d2 % 2
                        kb.pe([lambda dc=dc: PEe.matmul(psum[:, bank, S], lhsT=wou[:, dc, d2 * 128:(d2 + 1) * 128], rhs=mT[:, dc, S], start=(dc == 0), stop=(dc == 7)) for dc in range(8)],
                              [wou.b, mT.b], [PB[bank]])
                        kb.op("dve", lambda: V.scalar_tensor_tensor(out=xn[:, d2, S], in0=psum[:, bank, S], scalar=modT[:, l, 16 + d2, col:col + 1], in1=xa[:, d2, S], op0=ALU.mult, op1=ALU.add),
                              [PB[bank], modT.b, xa.b], [xn.b])
                    kb.dma("sp", "xnf", xTv[:, :, t0:t0 + nb], xn[:, :, S], reads=[xn.b])
                kb.barrier()

        def drain(g):
            for _ in g:
                pass

        def layer(l):
            last = (l == DEPTH - 1)
            phase_AB(l)
            with ExitStack() as sg:
                kb.coop.run([(lambda: drain(phase_C1(l, sg)), 8), (lambda: drain(phase_E1(l, sg)), 3), (lambda: drain(phase_D(l, sg)), 1)])
                kb.barrier()
            phase_C2(l)
            with ExitStack() as sg:
                drain(phase_E2(l, last, sg))
                drain(phase_C3(l, sg))
                kb.barrier()
            phase_F(l, last)

        for l in range(DEPTH):
            layer(l)
        with ExitStack() as s2:
            xb_ = [T(s2, nc, "xo%d" % i, [128, 8, 512], F32) for i in range(2)]
            for bi, (t0, nb, col) in enumerate(blocks[1:]):
                x_ = xb_[bi % 2]
                kb.dma("sp", x_.b.name, x_[:, :, 0:nb], xTv[:, :, t0:t0 + nb], writes=[x_.b])
                kb.dma("sp", x_.b.name, yout.rearrange("(c p) t -> p c t", p=128)[:, :, t0 - CTX:t0 - CTX + nb], x_[:, :, 0:nb], reads=[x_.b])
            if dbg:
                for k_, shp in dbg.items():
                    src = {"PF": PF, "PX": PX, "PT": PT, "YG": YG, "YS0": YS[0], "YS1": YS[1], "xTs": xT, "AKT0": AKT[0], "GCs0": GCs[0], "GCs1": GCs[1]}[k_]
                    kb.dma("sp", "dbg", dbgo[k_], src, reads=[])
            kb.barrier()
    build.n_inst = kb.n_inst
    build.n_sem = len(kb.sems)
    return nc


def host_consts(SEQ, DEPTH):
    TT = CTX + SEQ
    kon = np.zeros((128, 1024), np.float32)
    kon[:, 0:128] = np.eye(128)
    kon[:, 128:256] = 1.0
    p = np.arange(128)
    kon[:, 256:384] = (p[:, None] // 64 == p[None, :] // 64)
    partner = np.where((p % 32) < 16, p + 16, p - 16)
    perm = np.zeros((128, 128), np.float32)
    perm[partner, p] = 1.0
    kon[:, 384:512] = perm
    sel = np.zeros((128, 4, 8), np.float32)
    for c in range(4):
        for q in range(128):
            sel[q, c, 2 * c + q // 64] = 1.0
    kon[:, 512:544] = sel.reshape(128, 32)
    i = np.arange(64)
    SU = (i[:, None] < i[None, :]); SL = (i[:, None] > i[None, :]); IU = (i[:, None] <= i[None, :]); IL = (i[:, None] >= i[None, :]); I = np.eye(64)
    rep = lambda ms: np.stack([np.broadcast_to(m[:, None, :], (64, 8, 64)) for m in ms], axis=1).astype(np.float32).reshape(64, 4 * 512)
    msk = np.concatenate([rep((SU, IU, SL, I)), rep((SL, IL, SU, I))], axis=0)
    n_freq = 16
    inv_freq = (10000.0 ** (-np.arange(n_freq, dtype=np.float32) / n_freq)).astype(np.float32)
    tok = np.arange(SEQ)
    row = (tok // 64).astype(np.float32)
    colp = (tok % 64).astype(np.float32)
    d = p % 64
    half = d // 32
    j = d % 16
    which = (d % 32) // 16
    pos = np.where(half[:, None] == 0, row[None, :], colp[None, :]).astype(np.float32)
    ang = (pos * inv_freq[j][:, None]).astype(np.float32)
    cosT = np.ones((128, TT), np.float32)
    sinT = np.zeros((128, TT), np.float32)
    cosT[:, CTX:] = np.cos(ang)
    sinT[:, CTX:] = np.sin(ang) * np.where(which == 0, -1.0, 1.0)[:, None]
    rope = np.concatenate([cosT, sinT], axis=1).astype(np.float32)
    ltri = np.ones((128, 512), np.float32)
    ltri[:, 0::64] = 0.0
    return dict(KON=kon, MSK=np.ascontiguousarray(msk), ROPE=np.ascontiguousarray(rope), LTRI=ltri)


def host_inputs(inp, SEQ, DEPTH, b):
    L = DEPTH
    f = lambda a: np.asarray(a, np.float32)
    xT = np.concatenate([f(inp["ctx"])[b], f(inp["x"])[b]], axis=0).T
    PPa = np.zeros((128, L, NPL), np.float32)
    colT = lambda v: v.reshape(-1, 128).T
    for l in range(L):
        PPa[:, l, 0:8] = colT(f(inp["norm_g"])[l])
        PPa[:, l, 8:32] = colT(f(inp["b_mod"])[l])
        PPa[:, l, 32:40] = colT(f(inp["decay_w0"])[l].reshape(-1))
        PPa[:, l, 40:48] = colT(f(inp["iclr_a0"])[l].reshape(-1))
        PPa[:, l, 48:52] = colT(f(inp["k_k"])[l])
        PPa[:, l, 52:56] = colT(f(inp["k_a"])[l])
        PPa[:, l, 56:60] = colT(f(inp["r_k"])[l].reshape(-1))
        PPa[:, l, 60:72] = colT(f(inp["conv_w"])[l].reshape(-1))
        PPa[:, l, 72:76] = colT(f(inp["conv_b"])[l])
        PPa[:, l, 76] = np.tile(f(inp["qk_norm_g"])[l, 0], 2)
        PPa[:, l, 77] = np.tile(f(inp["qk_norm_g"])[l, 1], 2)
        PPa[:, l, 78] = f(inp["subln_g"])[l]
    CC = np.stack([colT(f(inp["c"])[b]), colT(f(inp["c_ctx"]))], axis=2).reshape(128, 16)
    BC = np.stack([np.broadcast_to(f(inp["gn_g"])[:L, None, :], (L, 128, 512)), np.broadcast_to(f(inp["gn_b"])[:L, None, :], (L, 128, 512))], axis=2)
    BC = np.ascontiguousarray(BC.transpose(1, 0, 2, 3)).reshape(128, L * 1024)
    LAM = f(inp["lambda_qk"])[:L].reshape(1, L * 256)
    return dict(xT=np.ascontiguousarray(xT), PP=PPa.reshape(128, L * NPL), CC=np.ascontiguousarray(CC), BC=BC, LAM=np.ascontiguousarray(LAM),
                w_mod=f(inp["w_mod"])[:L], w_in=f(inp["w_in"])[:L], dup=f(inp["decay_up"])[:L].reshape(L, 128, 512), iup=f(inp["iclr_up"])[:L].reshape(L, 128, 512),
                w_br=f(inp["w_branch"])[:L].reshape(L, 1536, D), w_out=f(inp["w_out"])[:L])


def run(inp, SEQ, DEPTH, dbg=None):
    nc = build(SEQ, DEPTH, dbg)
    consts = host_consts(SEQ, DEPTH)
    nb = np.asarray(inp["x"]).shape[0]
    in_maps = []
    for b in range(nb):
        m = host_inputs(inp, SEQ, DEPTH, b)
        m.update(consts)
        in_maps.append(m)
    res = run_bass_kernel_spmd(nc, in_maps, core_ids=list(range(nb)))
    out = np.stack([np.asarray(r["yout"]).T for r in res.results], axis=0)
    return out.astype(np.float32), res


def kernel(**inputs):
    out, _ = run(inputs, 4096, 4)
    return out
```

```python
import math
import threading
import numpy as np
from contextlib import ExitStack
import concourse.bass as bass
import concourse.mybir as mybir
from concourse.bass_utils import run_bass_kernel_spmd

F32 = mybir.dt.float32
BF16 = mybir.dt.bfloat16
AF = mybir.ActivationFunctionType
ALU = mybir.AluOpType
AX = mybir.AxisListType

D = 1024
CTX = 256
NIN = 9472
NPL = 79
C0 = math.exp(-0.5)
GN_EPS = 64e-5
RMS_EPS = 1e-6


class Buf:
    __slots__ = ("name", "w", "r")

    def __init__(self, name):
        self.name = name
        self.w = None
        self.r = []


class Coop:
    def __init__(self):
        self.cur = None

    def run(self, specs):
        n = len(specs)
        self.evs = [threading.Event() for _ in range(n)]
        self.alive = [True] * n
        self.quota = [q for _, q in specs]
        self.used = [0] * n
        self.done_ev = threading.Event()
        self.err = None

        def wrap(i, fn):
            self.evs[i].wait()
            self.evs[i].clear()
            self.cur = i
            try:
                fn()
            except BaseException as e:
                self.err = e
            self.alive[i] = False
            self._pass(i)
        ths = [threading.Thread(target=wrap, args=(i, f)) for i, (f, _) in enumerate(specs)]
        for t in ths:
            t.start()
        self.evs[0].set()
        self.done_ev.wait()
        for t in ths:
            t.join()
        self.cur = None
        if self.err is not None:
            raise self.err

    def _pass(self, i):
        n = len(self.alive)
        for k in range(1, n + 1):
            j = (i + k) % n
            if self.alive[j]:
                if j == i:
                    return True
                self.evs[j].set()
                return False
        self.done_ev.set()
        return False

    def switch(self):
        i = self.cur
        if i is None:
            return
        self.used[i] += 1
        if self.used[i] < self.quota[i]:
            return
        self.used[i] = 0
        if not self._pass(i):
            self.evs[i].wait()
            self.evs[i].clear()
            self.cur = i


class KB:
    def __init__(self, nc, stack):
        self.nc = nc
        self.stack = stack
        self.eng = {"pe": nc.tensor, "dve": nc.vector, "act": nc.scalar, "pool": nc.gpsimd, "sp": nc.sync}
        self.sems = {}
        self.cnt = {}
        self.seen = {e: {} for e in self.eng}
        for e in self.eng:
            self.sems[e] = stack.enter_context(nc.semaphore("s_" + e))
            self.cnt[e] = 0
        self.n_inst = 0
        self.dma_names = {}
        self.dma_pool = []
        self.coop = Coop()

    def dma_sem(self, name):
        if name not in self.dma_names:
            i = len(self.dma_names)
            if i >= len(self.dma_pool):
                key = "d_%d" % i
                self.sems[key] = self.stack.enter_context(self.nc.semaphore(key))
                self.cnt[key] = 0
                self.dma_pool.append(key)
            self.dma_names[name] = self.dma_pool[i]
        return self.dma_names[name]

    def phase_reset(self):
        self.dma_names = {}

    def _wait(self, e, ev):
        if ev is None:
            return
        key, val = ev
        if key == "pe" and e == "pe":
            return
        if self.seen[e].get(key, 0) >= val:
            return
        self.eng[e].wait_ge(self.sems[key], val)
        self.seen[e][key] = val
        self.n_inst += 1

    def deps(self, e, reads, writes):
        for b in reads:
            self._wait(e, b.w)
        for b in writes:
            self._wait(e, b.w)
            for ev in b.r:
                self._wait(e, ev)

    def done(self, ev, reads, writes):
        for b in reads:
            b.r.append(ev)
            if len(b.r) > 16:
                d = {}
                for k, v in b.r:
                    d[k] = max(d.get(k, 0), v)
                b.r = list(d.items())
        for b in writes:
            b.w = ev
            b.r = []

    def op(self, e, fn, reads=(), writes=()):
        self.deps(e, reads, writes)
        ins = fn()
        self.cnt[e] += 1
        ins.then_inc(self.sems[e], 1)
        ev = (e, self.cnt[e])
        self.done(ev, reads, writes)
        self.n_inst += 1
        self.coop.switch()
        return ev

    def pe(self, fns, reads=(), writes=()):
        self.deps("pe", reads, writes)
        ins = None
        for fn in fns:
            ins = fn()
            self.n_inst += 1
        self.cnt["pe"] += 1
        ins.then_inc(self.sems["pe"], 1)
        ev = ("pe", self.cnt["pe"])
        self.done(ev, reads, writes)
        self.coop.switch()
        return ev

    def dma(self, q, semname, out, in_, reads=(), writes=()):
        key = self.dma_sem(semname)
        self.deps(q, reads, writes)
        ins = self.eng[q].dma_start(out=out, in_=in_)
        self.cnt[key] += 16
        ins.then_inc(self.sems[key], 16)
        ev = (key, self.cnt[key])
        self.done(ev, reads, writes)
        self.n_inst += 1
        self.coop.switch()
        return ev

    def dma_group(self, q, semname, pairs, reads=(), writes=()):
        key = self.dma_sem(semname)
        self.deps(q, reads, writes)
        for out, in_ in pairs:
            ins = self.eng[q].dma_start(out=out, in_=in_)
            self.cnt[key] += 16
            ins.then_inc(self.sems[key], 16)
            self.n_inst += 1
        ev = (key, self.cnt[key])
        self.done(ev, reads, writes)
        self.coop.switch()
        return ev

    def barrier(self):
        for e in self.eng:
            for k in self.cnt:
                if self.cnt[k] > 0:
                    self._wait(e, (k, self.cnt[k]))
        self.phase_reset()


class T:
    uid = [0]

    def __init__(self, st, nc, name, shape, dt, nbuf=1):
        T.uid[0] += 1
        self.t = st.enter_context(nc.sbuf_tensor("%s_u%d" % (name, T.uid[0]), shape, dt))
        self.b = Buf(name)

    def __getitem__(self, k):
        return self.t[k]


SEGS = [(0, 512, "F"), (512, 512, "F"), (1024, 512, "T"), (1536, 256, "X"), (1792, 512, "T"),
        (2304, 512, "F"), (2816, 512, "F"), (3328, 512, "F"), (3840, 512, "F"),
        (4352, 512, "F"), (4864, 512, "F"), (5376, 512, "T"), (5888, 512, "F")] + \
       [(6400 + 512 * i, 512, "F") for i in range(6)]
TCOL = {1024: 0, 1792: 512, 5376: 1024}


def build(SEQ, DEPTH, dbg=None):
    TT = CTX + SEQ
    NCH = TT // 64
    blocks = [(0, CTX, 1)] + [(CTX + 512 * i, 512, 0) for i in range(SEQ // 512)]
    nc = bass.Bass("TRN2", target_bir_lowering=False)
    dram = lambda n, s, d, k="Internal": nc.dram_tensor(n, s, d, kind=k).ap()
    xT_in = dram("xT", [D, TT], F32, "ExternalInput")
    PP = dram("PP", [128, DEPTH * NPL], F32, "ExternalInput")
    CCd = dram("CC", [128, 16], F32, "ExternalInput")
    BCd = dram("BC", [128, DEPTH * 2 * 512], F32, "ExternalInput")
    LAMd = dram("LAM", [1, DEPTH * 256], F32, "ExternalInput")
    KON = dram("KON", [128, 1024], F32, "ExternalInput")
    MSK = dram("MSK", [128, 4 * 512], F32, "ExternalInput")
    ROPE = dram("ROPE", [128, 2 * TT], F32, "ExternalInput")
    LTRI = dram("LTRI", [128, 512], F32, "ExternalInput")
    w_mod = dram("w_mod", [DEPTH, D, 3 * D], F32, "ExternalInput")
    w_in = dram("w_in", [DEPTH, D, NIN], F32, "ExternalInput")
    dup = dram("dup", [DEPTH, 128, 512], F32, "ExternalInput")
    iup = dram("iup", [DEPTH, 128, 512], F32, "ExternalInput")
    w_br = dram("w_br", [DEPTH, 1536, D], F32, "ExternalInput")
    w_out = dram("w_out", [DEPTH, D, D], F32, "ExternalInput")
    yout = dram("yout", [D, SEQ], F32, "ExternalOutput")
    xT = dram("xTs", [D, TT], F32)
    PF = dram("PF", [NIN, TT], BF16)
    PX = dram("PX", [256, TT], F32)
    PT = dram("PT", [TT, 1536], BF16)
    QGs = [dram("QGs%d" % d, [512, NCH, 2, 64], BF16) for d in range(2)]
    AKT = [dram("AKT%d" % d, [512, TT], BF16) for d in range(2)]
    KIT = [dram("KIT%d" % d, [512, TT], BF16) for d in range(2)]
    AKK = [dram("AKK%d" % d, [TT, 512], BF16) for d in range(2)]
    KIK = [dram("KIK%d" % d, [TT, 512], BF16) for d in range(2)]
    GCs = [dram("GCs%d" % d, [512, NCH], F32) for d in range(2)]
    YS = [dram("YS%d" % d, [TT, 512], F32) for d in range(2)]
    YG = dram("YG", [1536, TT], BF16)
    QRs = dram("QRs", [512, TT], BF16)
    KRs = dram("KRs", [512, TT], BF16)
    dbgo = {}
    if dbg:
        for n, shp in dbg.items():
            dbgo[n] = dram("dbg_" + n, shp, F32, "ExternalOutput")

    with ExitStack() as st:
        kb = KB(nc, st)
        V, A, P, PEe = nc.vector, nc.scalar, nc.gpsimd, nc.tensor
        psum = st.enter_context(nc.psum_tensor("psum", [128, 8, 512], F32))
        PB = [Buf("ps%d" % i) for i in range(8)]
        psbf = [psum[:, i, :].bitcast(BF16) for i in range(8)]

        kon = T(st, nc, "kon", [128, 1024], F32)
        konb = T(st, nc, "konb", [128, 1024], BF16)
        msk2 = T(st, nc, "msk2", [128, 4, 8, 64], F32)
        pp = T(st, nc, "pp", [128, DEPTH, NPL], F32)
        cc = T(st, nc, "cc", [128, 8, 2], F32)
        ltri = T(st, nc, "ltri", [128, 512], F32)
        modT = T(st, nc, "modT", [128, DEPTH, 24, 2], F32)
        gsc = T(st, nc, "gsc", [128, DEPTH, 8, 2], F32)
        omka = T(st, nc, "omka", [128, DEPTH, 4], F32)
        lamc = T(st, nc, "lamc", [128, DEPTH, 2], F32)
        sgl = T(st, nc, "sgl", [128, DEPTH], F32)
        kb.dma("sp", "c0", kon[:], KON[:, :], writes=[kon.b])
        kb.dma("sp", "c1", msk2[:].rearrange("p a h x -> p (a h x)"), MSK[:, :], writes=[msk2.b])
        kb.dma("sp", "c2", pp[:].rearrange("p l n -> p (l n)"), PP[:, :], writes=[pp.b])
        kb.dma("sp", "c3", cc[:].rearrange("p a b -> p (a b)"), CCd[:, :], writes=[cc.b])
        kb.dma("sp", "c4", ltri[:], LTRI[:, :], writes=[ltri.b])
        kb.op("dve", lambda: V.tensor_copy(out=konb[:], in_=kon[:]), [kon.b], [konb.b])
        epsc = T(st, nc, "epsc", [128, 2], F32)
        kb.op("pool", lambda: P.memset(epsc[:, 0:1], RMS_EPS), [], [epsc.b])
        kb.op("pool", lambda: P.memset(epsc[:, 1:2], 1e-12), [], [epsc.b])
        ident_f, ones_f = kon[:, 0:128], kon[:, 128:256]
        ident_b, ones_b, blk_b, perm_f = konb[:, 0:128], konb[:, 128:256], konb[:, 256:384], kon[:, 384:512]
        sel_b = konb[:, 512:544]

        with ExitStack() as s2:
            sc = T(s2, nc, "sc", [128, 8, 2], F32)
            wm = [T(s2, nc, "wm%d" % i, [128, 8, 512], F32) for i in range(2)]
            lamt = T(s2, nc, "lamt", [1, DEPTH, 4, 64], F32)
            lam2 = T(s2, nc, "lam2", [1, DEPTH, 2], F32)
            lam3 = T(s2, nc, "lam3", [1, DEPTH, 2], F32)
            kb.op("act", lambda: A.activation(out=sc[:], in_=cc[:], func=AF.Silu), [cc.b], [sc.b])
            i = 0
            for l in range(DEPTH):
                for nb6 in range(6):
                    w = wm[i % 2]
                    i += 1
                    kb.dma("sp", w.b.name, w[:], w_mod[l].rearrange("(c p) n -> p c n", p=128)[:, :, nb6 * 512:(nb6 + 1) * 512], writes=[w.b])
                    fns = []
                    for n4 in range(4):
                        for c in range(8):
                            fns.append(lambda n4=n4, c=c, w=w: PEe.matmul(psum[:, 0, n4 * 2:n4 * 2 + 2], lhsT=w[:, c, n4 * 128:(n4 + 1) * 128], rhs=sc[:, c, :], start=(c == 0), stop=(c == 7)))
                    kb.pe(fns, [w.b, sc.b], [PB[0]])
                    kb.op("dve", lambda l=l, nb6=nb6: V.tensor_tensor(out=modT[:, l, nb6 * 4:(nb6 + 1) * 4, :], in0=psum[:, 0, 0:8].rearrange("p (a b) -> p a b", b=2),
                                                                   in1=pp[:, l, 8 + nb6 * 4:8 + (nb6 + 1) * 4].unsqueeze(2).to_broadcast([128, 4, 2]), op=ALU.add), [PB[0], pp.b], [modT.b])
                kb.op("dve", lambda l=l: V.tensor_scalar(out=gsc[:, l], in0=modT[:, l, 8:16, :], scalar1=1.0, scalar2=None, op0=ALU.add), [modT.b], [gsc.b])
                kb.op("dve", lambda l=l: V.tensor_tensor(out=gsc[:, l], in0=gsc[:, l], in1=pp[:, l, 0:8].unsqueeze(2).to_broadcast([128, 8, 2]), op=ALU.mult), [gsc.b, pp.b], [gsc.b])
                kb.op("dve", lambda l=l: V.tensor_scalar(out=omka[:, l, :], in0=pp[:, l, 52:56], scalar1=-1.0, scalar2=1.0, op0=ALU.mult, op1=ALU.add), [pp.b], [omka.b])
            kb.dma("sp", "c5", lamt[:].rearrange("p l a x -> p (l a x)"), LAMd[:, :], writes=[lamt.b])
            lv = lamt[:].rearrange("p l (a b) x -> p l a b x", b=2)
            kb.op("dve", lambda: V.tensor_tensor(out=lv[:, :, :, 0, :], in0=lv[:, :, :, 0, :], in1=lv[:, :, :, 1, :], op=ALU.mult), [lamt.b], [lamt.b])
            kb.op("dve", lambda: V.tensor_reduce(out=lam2[:], in_=lv[:, :, :, 0, :], axis=AX.X, op=ALU.add), [lamt.b], [lam2.b])
            kb.op("act", lambda: A.activation(out=lam3[:], in_=lam2[:], func=AF.Exp), [lam2.b], [lam3.b])
            kb.op("dve", lambda: V.tensor_tensor(out=lam2[:, :, 0], in0=lam3[:, :, 1], in1=lam3[:, :, 0], op=ALU.subtract), [lam3.b], [lam2.b])
            for l in range(DEPTH):
                li = 0.8 - 0.6 * math.exp(-0.3 * l)
                kb.op("dve", lambda l=l, li=li: V.tensor_scalar(out=lam2[:, l, 0:1], in0=lam2[:, l, 0:1], scalar1=-li, scalar2=None, op0=ALU.add), [lam2.b], [lam2.b])
                kb.op("dve", lambda l=l, li=li: V.tensor_scalar(out=sgl[:, l:l + 1], in0=pp[:, l, 78:79], scalar1=(1.0 - li), scalar2=None, op0=ALU.mult), [pp.b], [sgl.b])
            kb.pe([lambda: PEe.matmul(psum[:, 0, 0:2 * DEPTH], lhsT=ones_f[0:1, :], rhs=lam2[:].rearrange("p l a -> p (l a)"), start=True, stop=True)], [lam2.b, kon.b], [PB[0]])
            kb.op("dve", lambda: V.tensor_copy(out=lamc[:].rearrange("p l a -> p (l a)"), in_=psum[:, 0, 0:2 * DEPTH]), [PB[0]], [lamc.b])
            kb.barrier()

        with ExitStack() as s2:
            xb_ = [T(s2, nc, "xcp%d" % i, [128, 8, 512], F32) for i in range(2)]
            for bi, (t0, nb, col) in enumerate(blocks):
                x_ = xb_[bi % 2]
                kb.dma("sp", x_.b.name, x_[:, :, 0:nb], xT_in.rearrange("(c p) t -> p c t", p=128)[:, :, t0:t0 + nb], writes=[x_.b])
                kb.dma("sp", x_.b.name, xT.rearrange("(c p) t -> p c t", p=128)[:, :, t0:t0 + nb], x_[:, :, 0:nb], reads=[x_.b])
            kb.barrier()

        xTv = xT.rearrange("(c p) t -> p c t", p=128)
        NTT = TT // 128

        def blk_of(tt):
            t = tt * 128
            for bi, (t0, nb, col) in enumerate(blocks):
                if t0 <= t < t0 + nb:
                    return bi

        evac_rr = [0]

        def evac(out, in_, rb, wb, scale=None):
            evac_rr[0] += 1
            if evac_rr[0] % 2 == 0:
                if scale is None:
                    return kb.op("act", lambda: A.copy(out=out, in_=in_), rb, wb)
                return kb.op("act", lambda: A.mul(out=out, in_=in_, mul=scale), rb, wb)
            if scale is None:
                return kb.op("dve", lambda: V.tensor_copy(out=out, in_=in_), rb, wb)
            return kb.op("dve", lambda: V.tensor_scalar(out=out, in0=in_, scalar1=scale, scalar2=None, op0=ALU.mult), rb, wb)

        def run_threads(specs):
            th = [[g, 0, float(tot)] for g, tot in specs]
            while th:
                t_ = min(th, key=lambda x: x[1] / x[2])
                try:
                    next(t_[0])
                    t_[1] += 1
                except StopIteration:
                    th.remove(t_)

        def phase_AB(l):
            with ExitStack() as s2:
                hT = T(s2, nc, "hT", [128, 8, TT], BF16)
                hTb = [Buf("hT%d" % i) for i in range(len(blocks))]
                xa = T(s2, nc, "xa", [128, 8, 512], F32)
                sq = T(s2, nc, "sq", [128, 8, 512], F32)
                rs = T(s2, nc, "rs", [128, 512], F32)
                tmp = [T(s2, nc, "tmpa%d" % i, [128, 512], F32) for i in range(2)]
                wf = [T(s2, nc, "wf%d" % i, [128, 8, 512], F32) for i in range(2)]
                wb = [T(s2, nc, "wb%d" % i, [128, 8, 512], BF16) for i in range(2)]
                ost = [T(s2, nc, "ost%d" % i, [128, 512], BF16) for i in range(4)]
                osx = [T(s2, nc, "osx%d" % i, [128, 512], F32) for i in range(2)]
                w_inv = w_in[l].rearrange("(c p) n -> p c n", p=128)

                def loadw(si):
                    c0, wd, kind = SEGS[si]
                    f, b = wf[si % 2], wb[si % 2]
                    kb.dma("sp", f.b.name, f[:, :, 0:wd], w_inv[:, :, c0:c0 + wd], writes=[f.b])
                    kb.op("pool", lambda: P.tensor_copy(out=b[:, :, 0:wd], in_=f[:, :, 0:wd]), [f.b], [b.b])

                loadw(0)
                for bi, (t0, nb, col) in enumerate(blocks):
                    kb.dma("sp", "xa", xa[:, :, 0:nb], xTv[:, :, t0:t0 + nb], writes=[xa.b])
                    kb.op("act", lambda: A.activation(out=sq[:, :, 0:nb], in_=xa[:, :, 0:nb], func=AF.Square), [xa.b], [sq.b])
                    kb.pe([lambda c=c: PEe.matmul(psum[:, 0, 0:nb], lhsT=ones_f, rhs=sq[:, c, 0:nb], start=(c == 0), stop=(c == 7)) for c in range(8)],
                          [sq.b, kon.b], [PB[0]])
                    kb.op("act", lambda: A.activation(out=rs[:, 0:nb], in_=psum[:, 0, 0:nb], func=AF.Ln, bias=epsc[:, 0:1], scale=1.0 / D), [PB[0], epsc.b], [rs.b])
                    kb.op("act", lambda: A.activation(out=rs[:, 0:nb], in_=rs[:, 0:nb], func=AF.Exp, scale=-0.5), [rs.b], [rs.b])
                    for c in range(8):
                        tm = tmp[c % 2]
                        kb.op("dve", lambda: V.scalar_tensor_tensor(out=tm[:, 0:nb], in0=xa[:, c, 0:nb], scalar=gsc[:, l, c, col:col + 1], in1=rs[:, 0:nb],
                                                                    op0=ALU.mult, op1=ALU.mult), [xa.b, rs.b, gsc.b], [tm.b])
                        kb.op("act", lambda: A.activation(out=hT[:, c, t0:t0 + nb], in_=tm[:, 0:nb], func=AF.Identity, bias=modT[:, l, c, col:col + 1], scale=1.0),
                              [tm.b, modT.b], [hTb[bi]])
                k = 0
                pk = 0
                for si, (c0, wd, kind) in enumerate(SEGS):
                    if si + 1 < len(SEGS):
                        loadw(si + 1)
                    b = wb[si % 2]
                    if kind in "FX":
                        for bi, (t0, nb, col) in enumerate(blocks):
                            for n4 in range(wd // 128):
                                bank = 1 + pk % 6
                                pk += 1
                                kb.pe([lambda c=c: PEe.matmul(psum[:, bank, 0:nb], lhsT=b[:, c, n4 * 128:(n4 + 1) * 128], rhs=hT[:, c, t0:t0 + nb], start=(c == 0), stop=(c == 7))
                                       for c in range(8)], [b.b, hTb[bi]], [PB[bank]])
                                if kind == "F":
                                    o = ost[k % 4]
                                    k += 1
                                    evac(o[:, 0:nb], psum[:, bank, 0:nb], [PB[bank]], [o.b])
                                    kb.dma("sp", o.b.name, PF[c0 + n4 * 128:c0 + (n4 + 1) * 128, t0:t0 + nb], o[:, 0:nb], reads=[o.b])
                                else:
                                    o = osx[k % 2]
                                    k += 1
                                    evac(o[:, 0:nb], psum[:, bank, 0:nb], [PB[bank]], [o.b])
                                    kb.dma("sp", o.b.name, PX[n4 * 128:(n4 + 1) * 128, t0:t0 + nb], o[:, 0:nb], reads=[o.b])
                    else:
                        for tt in range(NTT):
                            bank = 1 + pk % 6
                            pk += 1
                            kb.pe([lambda c=c: PEe.matmul(psum[:, bank, 0:512], lhsT=hT[:, c, tt * 128:(tt + 1) * 128], rhs=b[:, c, 0:512], start=(c == 0), stop=(c == 7))
                                   for c in range(8)], [b.b, hTb[blk_of(tt)]], [PB[bank]])
                            o = ost[k % 4]
                            k += 1
                            evac(o[:, :], psum[:, bank, :], [PB[bank]], [o.b])
                            kb.dma("sp", o.b.name, PT[tt * 128:(tt + 1) * 128, TCOL[c0]:TCOL[c0] + 512], o[:, :], reads=[o.b])
                kb.barrier()

        def phase_C1(l, s2):
            if True:
                rT = T(s2, nc, "rT", [128, 4, 512], BF16)
                kT = T(s2, nc, "kT", [128, 4, 512], BF16)
                lw = T(s2, nc, "lw", [128, 512], F32)
                la = T(s2, nc, "la", [128, 512], F32)
                tl = T(s2, nc, "tl", [128, 512], BF16)
                lab = T(s2, nc, "lab", [128, 512], BF16)
                upf = T(s2, nc, "upf", [128, 2, 512], F32)
                upb = T(s2, nc, "upb", [128, 2, 512], BF16)
                names = ["kkraw", "rk", "kk", "sg", "a_", "akk", "t_", "kd", "csf", "cs", "cse", "Ep", "Em", "Epv", "tmpb"]
                f = {n: T(s2, nc, "c1_" + n, [128, 512], F32) for n in names}
                sqk = T(s2, nc, "sqk", [128, 512], BF16)
                ob = {n: [T(s2, nc, "c1o_%s%d" % (n, i), [128, 512], BF16) for i in range(2)] for n in ["kkg", "rg", "aki", "ki"]}
                gcs = [T(s2, nc, "gcs%d" % i, [128, 8], F32) for i in range(2)]
                tk = [T(s2, nc, "tk%d" % i, [128, 2, 4, 128], BF16) for i in range(2)]
                kb.dma("sp", "upf", upf[:, 0, :], dup[l], writes=[upf.b])
                kb.dma("sp", "upf", upf[:, 1, :], iup[l], writes=[upf.b])
                kb.op("dve", lambda: V.tensor_copy(out=upb[:], in_=upf[:]), [upf.b], [upb.b])
                it = 0
                for bi, (t0, nb, col) in enumerate(blocks):
                    nchk = nb // 64
                    ch0 = t0 // 64
                    ntt = nb // 128
                    kb.dma("sp", "rT", rT[:, :, 0:nb], PF[0:512, t0:t0 + nb].rearrange("(c p) t -> p c t", p=128), writes=[rT.b])
                    kb.dma("sp", "kT", kT[:, :, 0:nb], PF[512:1024, t0:t0 + nb].rearrange("(c p) t -> p c t", p=128), writes=[kT.b])
                    kb.dma("sp", "lw", lw[:, 0:nb], PX[0:128, t0:t0 + nb], writes=[lw.b])
                    kb.dma("sp", "la", la[:, 0:nb], PX[128:256, t0:t0 + nb], writes=[la.b])
                    kb.op("act", lambda: A.activation(out=tl[:, 0:nb], in_=lw[:, 0:nb], func=AF.Tanh), [lw.b], [tl.b])
                    kb.op("dve", lambda: V.tensor_copy(out=lab[:, 0:nb], in_=la[:, 0:nb]), [la.b], [lab.b])
                    for cc_ in range(4):
                        S = slice(0, nb)
                        kb.op("dve", lambda: V.tensor_scalar(out=f["kkraw"][:, S], in0=kT[:, cc_, S], scalar1=pp[:, l, 48 + cc_:49 + cc_], scalar2=None, op0=ALU.mult),
                              [kT.b, pp.b], [f["kkraw"].b])
                        kb.op("act", lambda: A.activation(out=sqk[:, S], in_=f["kkraw"][:, S], func=AF.Square), [f["kkraw"].b], [sqk.b])
                        kb.pe([lambda: PEe.matmul(psum[:, 0, S], lhsT=blk_b, rhs=sqk[:, S], start=True, stop=True)], [sqk.b, konb.b], [PB[0]])
                        kb.op("act", lambda: A.activation(out=f["rk"][:, S], in_=psum[:, 0, S], func=AF.Ln, bias=epsc[:, 1:2], scale=1.0), [PB[0], epsc.b], [f["rk"].b])
                        kb.op("act", lambda: A.activation(out=f["rk"][:, S], in_=f["rk"][:, S], func=AF.Exp, scale=-0.5), [f["rk"].b], [f["rk"].b])
                        kb.op("dve", lambda: V.tensor_tensor(out=f["kk"][:, S], in0=f["kkraw"][:, S], in1=f["rk"][:, S], op=ALU.mult), [f["kkraw"].b, f["rk"].b], [f["kk"].b])
                        for d in range(2):
                            R = slice(d * 64, (d + 1) * 64)
                            C = slice(cc_ * 128, (cc_ + 1) * 128)
                            o = {n: ob[n][it % 2] for n in ob}
                            g_ = gcs[it % 2]
                            tk_ = tk[it % 2]
                            it += 1
                            kb.pe([lambda: PEe.matmul(psum[:, 1, S], lhsT=upb[R, 0, C], rhs=tl[R, S], start=True, stop=True)], [upb.b, tl.b], [PB[1]])
                            kb.op("act", lambda: A.activation(out=f["sg"][:, S], in_=psum[:, 1, S], func=AF.Sigmoid, bias=pp[:, l, 32 + d * 4 + cc_:33 + d * 4 + cc_], scale=1.0),
                                  [PB[1], pp.b], [f["sg"].b])
                            kb.pe([lambda: PEe.matmul(psum[:, 2, S], lhsT=upb[R, 1, C], rhs=lab[R, S], start=True, stop=True)], [upb.b, lab.b], [PB[2]])
                            kb.op("act", lambda: A.activation(out=f["a_"][:, S], in_=psum[:, 2, S], func=AF.Sigmoid, bias=pp[:, l, 40 + d * 4 + cc_:41 + d * 4 + cc_], scale=1.0),
                                  [PB[2], pp.b], [f["a_"].b])
                            kb.op("pool", lambda: P.tensor_tensor(out=f["akk"][:, S], in0=f["a_"][:, S], in1=f["kk"][:, S], op=ALU.mult), [f["a_"].b, f["kk"].b], [f["akk"].b])
                            kb.op("dve", lambda: V.tensor_scalar(out=f["t_"][:, S], in0=f["a_"][:, S], scalar1=pp[:, l, 52 + cc_:53 + cc_], scalar2=omka[:, l, cc_:cc_ + 1],
                                                                 op0=ALU.mult, op1=ALU.add), [f["a_"].b, pp.b, omka.b], [f["t_"].b])
                            kb.op("pool", lambda: P.tensor_tensor(out=f["kd"][:, S], in0=f["t_"][:, S], in1=kT[:, cc_, S], op=ALU.mult), [f["t_"].b, kT.b], [f["kd"].b])
                            kb.op("dve", lambda: V.tensor_tensor_scan(out=f["csf"][:, S], data0=ltri[:, S], data1=f["sg"][:, S], initial=0.0, op0=ALU.mult, op1=ALU.add),
                                  [ltri.b, f["sg"].b], [f["csf"].b])
                            if d == 0:
                                cs = f["csf"]
                            else:
                                cs = f["cs"]
                                kb.op("dve", lambda: V.tensor_tensor(out=f["tmpb"][:, S], in0=f["sg"][:, S], in1=f["csf"][:, S], op=ALU.subtract), [f["sg"].b, f["csf"].b], [f["tmpb"].b])
                                kb.op("dve", lambda: V.tensor_tensor(out=cs[:, S].rearrange("p (c x) -> p c x", x=64), in0=f["tmpb"][:, S].rearrange("p (c x) -> p c x", x=64),
                                                                     in1=f["csf"][:, S].rearrange("p (c x) -> p c x", x=64)[:, :, 63:64].to_broadcast([128, nchk, 64]), op=ALU.add),
                                      [f["tmpb"].b, f["csf"].b], [cs.b])
                            kb.op("pool", lambda: P.tensor_tensor(out=f["cse"][:, S], in0=cs[:, S], in1=f["sg"][:, S], op=ALU.subtract), [cs.b, f["sg"].b], [f["cse"].b])
                            kb.op("act", lambda: A.activation(out=f["Ep"][:, S], in_=cs[:, S], func=AF.Exp, scale=-C0), [cs.b], [f["Ep"].b])
                            kb.op("act", lambda: A.activation(out=f["Em"][:, S], in_=cs[:, S], func=AF.Exp, scale=C0), [cs.b], [f["Em"].b])
                            kb.op("act", lambda: A.activation(out=f["Epv"][:, S], in_=f["cse"][:, S], func=AF.Exp, scale=-C0), [f["cse"].b], [f["Epv"].b])
                            kb.op("dve", lambda: V.tensor_tensor(out=o["rg"][:, S], in0=rT[:, cc_, S], in1=f["Ep"][:, S], op=ALU.mult), [rT.b, f["Ep"].b], [o["rg"].b])
                            kb.op("pool", lambda: P.tensor_tensor(out=o["kkg"][:, S], in0=f["kk"][:, S], in1=f["Epv"][:, S], op=ALU.mult), [f["kk"].b, f["Epv"].b], [o["kkg"].b])
                            kb.op("dve", lambda: V.tensor_tensor(out=o["aki"][:, S], in0=f["akk"][:, S], in1=f["Em"][:, S], op=ALU.mult), [f["akk"].b, f["Em"].b], [o["aki"].b])
                            kb.op("pool", lambda: P.tensor_tensor(out=o["ki"][:, S], in0=f["kd"][:, S], in1=f["Em"][:, S], op=ALU.mult), [f["kd"].b, f["Em"].b], [o["ki"].b])
                            e_ = 63 if d == 0 else 0
                            kb.op("dve", lambda: V.tensor_copy(out=g_[:, 0:nchk], in_=f["Ep"][:, S].rearrange("p (c x) -> p c x", x=64)[:, :, e_]), [f["Ep"].b], [g_.b])
                            kb.dma("sp", o["kkg"].b.name, QGs[d][C, ch0:ch0 + nchk, 0, :], o["kkg"][:, S].rearrange("p (c x) -> p c x", x=64), reads=[o["kkg"].b])
                            kb.dma("sp", o["rg"].b.name, QGs[d][C, ch0:ch0 + nchk, 1, :], o["rg"][:, S].rearrange("p (c x) -> p c x", x=64), reads=[o["rg"].b])
                            kb.dma("sp", o["aki"].b.name, AKT[d][C, t0:t0 + nb], o["aki"][:, S], reads=[o["aki"].b])
                            kb.dma("sp", o["ki"].b.name, KIT[d][C, t0:t0 + nb], o["ki"][:, S], reads=[o["ki"].b])
                            kb.dma("sp", g_.b.name, GCs[d][C, ch0:ch0 + nchk], g_[:, 0:nchk], reads=[g_.b])
                            fns = []
                            for qi, nm in enumerate(["aki", "ki"]):
                                for tt in range(ntt):
                                    fns.append(lambda qi=qi, nm=nm, tt=tt: PEe.transpose(psbf[3][:, (qi * 4 + tt) * 128:(qi * 4 + tt + 1) * 128], o[nm][:, tt * 128:(tt + 1) * 128], ident_b))
                            kb.pe(fns, [o["aki"].b, o["ki"].b, konb.b], [PB[3]])
                            kb.op("act", lambda: A.copy(out=tk_[:].rearrange("p q t c -> p (q t c)"), in_=psbf[3][:, :]), [PB[3]], [tk_.b])
                            kb.dma("sp", tk_.b.name + "a", AKK[d][t0:t0 + nb, C].rearrange("(t p) c -> p t c", p=128), tk_[:, 0, 0:ntt, :], reads=[tk_.b])
                            kb.dma("sp", tk_.b.name + "b", KIK[d][t0:t0 + nb, C].rearrange("(t p) c -> p t c", p=128), tk_[:, 1, 0:ntt, :], reads=[tk_.b])
                            yield

        def phase_C2(l):
            with ExitStack() as s2:
                def mk(nm, shape, dt):
                    ts = [T(s2, nc, "%s%d" % (nm, i), shape, dt) for i in range(2)]
                    for t_ in ts:
                        t_.b2 = Buf(t_.b.name + "x")
                    return ts
                QG = mk("QG", [128, 8, 8, 128], BF16)
                AT = mk("AT", [128, 8, 512], BF16)
                KT_ = mk("KT", [128, 8, 512], BF16)
                AK = mk("AK", [128, 8, 512], BF16)
                KK_ = mk("KK", [128, 8, 512], BF16)
                VV = mk("VV", [128, 8, 512], BF16)
                GC = mk("GC", [128, 8, 8], F32)
                Z = T(s2, nc, "Z", [128, 8, 64], F32)
                Zb = T(s2, nc, "Zb", [128, 8, 64], BF16)
                NT = [T(s2, nc, "NT%d" % i, [128, 8, 64], BF16) for i in range(5)]
                N_ = [T(s2, nc, "N%d" % i, [128, 8, 64], BF16) for i in range(6)]
                PTt = [T(s2, nc, "PTt%d" % i, [128, 8, 64], BF16) for i in range(2)]
                BabT = T(s2, nc, "BabT", [128, 8, 64], BF16)
                AakT = T(s2, nc, "AakT", [128, 8, 64], BF16)
                BakT = T(s2, nc, "BakT", [128, 8, 64], BF16)
                Wn = T(s2, nc, "Wn", [128, 8, 64], BF16)
                U = T(s2, nc, "U", [128, 8, 64], BF16)
                Yst = [T(s2, nc, "Yst%d" % i, [128, 512], F32) for i in range(2)]
                yi = [0]
                DS = [slice(0, 64), slice(64, 128)]
                nB = len(blocks)
                fo = list(range(nB))
                bo = [0] + list(range(nB - 1, 0, -1))

                def hv(bank):
                    return psum[:, bank, :].rearrange("p (h i) -> p h i", h=8)

                def load(s, par):
                    for d in range(2):
                        t0, nb, col = blocks[fo[s] if d == 0 else bo[s]]
                        nchk, ch0 = nb // 64, t0 // 64
                        if d == 0:
                            csl = slice(ch0, ch0 + nchk)
                        else:
                            csl = slice(ch0 + nchk - 1, (ch0 - 1) if ch0 > 0 else None, -1)
                        D_ = DS[d]
                        bsel = (lambda t_: t_.b) if d == 0 else (lambda t_: t_.b2)
                        sfx = "f" if d == 0 else "b"
                        def ld(t_, dst, src):
                            kb.dma("sp", t_.b.name + sfx, dst, src, writes=[bsel(t_)])

                        def ldh(t_, dstf, srcf):
                            if d == 0:
                                ld(t_, dstf(slice(None)), srcf(slice(None)))
                            else:
                                kb.dma_group("sp", t_.b.name + sfx, [(dstf(h), srcf(h)) for h in range(8)], writes=[bsel(t_)])
                        qsrc = QGs[d].rearrange("(h k) c w i -> k h c (w i)", k=64)
                        ldh(QG[par], lambda h: QG[par][D_, h, 0:nchk, :], lambda h: qsrc[:, h, csl, :])
                        asrc = AKT[d].rearrange("(h k) (c x) -> k h c x", k=64, x=64)
                        ldh(AT[par], lambda h: AT[par][D_, h, 0:nb].rearrange("p (c x) -> p c x", x=64) if not isinstance(h, slice) else AT[par][D_, :, 0:nb].rearrange("p h (c x) -> p h c x", x=64),
                            lambda h: asrc[:, h, csl, :])
                        ksrc = KIT[d].rearrange("(h k) (c x) -> k h c x", k=64, x=64)
                        ldh(KT_[par], lambda h: KT_[par][D_, h, 0:nb].rearrange("p (c x) -> p c x", x=64) if not isinstance(h, slice) else KT_[par][D_, :, 0:nb].rearrange("p h (c x) -> p h c x", x=64),
                            lambda h: ksrc[:, h, csl, :])
                        ld(AK[par], AK[par][D_, 0:nchk, :], AKK[d].rearrange("(c p) n -> p c n", p=64)[:, csl, :])
                        ld(KK_[par], KK_[par][D_, 0:nchk, :], KIK[d].rearrange("(c p) n -> p c n", p=64)[:, csl, :])
                        ld(VV[par], VV[par][D_, 0:nchk, :], PT[:, 0:512].rearrange("(c p) n -> p c n", p=64)[:, csl, :])
                        ld(GC[par], GC[par][D_, :, 0:nchk], GCs[d].rearrange("(h k) c -> k h c", k=64)[:, :, ch0:ch0 + nchk])

                def chunk(par, j, nchk, tokf, tokb):
                    qg, at, kt, ak, kk_, vv, gc = QG[par], AT[par], KT_[par], AK[par], KK_[par], VV[par], GC[par]
                    bb = lambda *ts: [x for t_ in ts for x in (t_.b, t_.b2)]
                    cs = slice(j * 64, j * 64 + 64)
                    H = lambda h: slice(h * 64, (h + 1) * 64)
                    HD = [(h, D_) for h in range(8) for D_ in DS]
                    fns = []
                    for h, D_ in HD:
                        fns.append(lambda h=h, D_=D_: PEe.matmul(psum[D_, h // 4, (h % 4) * 128:(h % 4) * 128 + 128], lhsT=at[D_, h, cs], rhs=qg[D_, h, j, :], start=True, stop=True))
                    for h, D_ in HD:
                        fns.append(lambda h=h, D_=D_: PEe.matmul(psum[D_, 2 + h // 4, (h % 4) * 128:(h % 4) * 128 + 128], lhsT=kt[D_, h, cs], rhs=qg[D_, h, j, :], start=True, stop=True))
                    for h, D_ in HD:
                        fns.append(lambda h=h, D_=D_: PEe.matmul(psum[D_, 4, H(h)], lhsT=qg[D_, h, j, 0:64], rhs=at[D_, h, cs], start=True, stop=True))
                    kb.pe(fns, bb(qg, at, kt), [PB[0], PB[1], PB[2], PB[3], PB[4]])
                    P1v = psum[:, 0:2, :].rearrange("p b (h w i) -> p (b h) w i", h=4, w=2)
                    P2v = psum[:, 2:4, :].rearrange("p b (h w i) -> p (b h) w i", h=4, w=2)
                    kb.op("dve", lambda: V.scalar_tensor_tensor(out=NT[0][:], in0=P1v[:, :, 0, :], scalar=-1.0, in1=msk2[:, 0], op0=ALU.mult, op1=ALU.mult), [PB[0], PB[1], msk2.b], [NT[0].b])
                    kb.op("dve", lambda: V.scalar_tensor_tensor(out=N_[0][:], in0=hv(4), scalar=-1.0, in1=msk2[:, 2], op0=ALU.mult, op1=ALU.mult), [PB[4], msk2.b], [N_[0].b])
                    kb.op("dve", lambda: V.tensor_tensor(out=AakT[:], in0=P2v[:, :, 0, :], in1=msk2[:, 0], op=ALU.mult), [PB[2], PB[3], msk2.b], [AakT.b])
                    kb.op("dve", lambda: V.tensor_tensor(out=BabT[:], in0=P1v[:, :, 1, :], in1=msk2[:, 1], op=ALU.mult), [PB[0], PB[1], msk2.b], [BabT.b])
                    kb.op("dve", lambda: V.tensor_tensor(out=BakT[:], in0=P2v[:, :, 1, :], in1=msk2[:, 1], op=ALU.mult), [PB[2], PB[3], msk2.b], [BakT.b])
                    kb.op("pool", lambda: P.tensor_tensor(out=PTt[0][:], in0=NT[0][:], in1=msk2[:, 3], op=ALU.add), [NT[0].b, msk2.b], [PTt[0].b])
                    fns = []
                    for h, D_ in HD:
                        fns.append(lambda h=h, D_=D_: PEe.matmul(psum[D_, 5, H(h)], lhsT=qg[D_, h, j, 0:64], rhs=Zb[D_, h, :], start=True, stop=False))
                        fns.append(lambda h=h, D_=D_: PEe.matmul(psum[D_, 5, H(h)], lhsT=AakT[D_, h, :], rhs=vv[D_, j, H(h)], start=False, stop=True))
                    kb.pe(fns, bb(qg, vv) + [Zb.b, AakT.b], [PB[5]])
                    kb.op("act", lambda: A.mul(out=Wn[:], in_=hv(5), mul=-1.0), [PB[5]], [Wn.b])
                    for k in range(5):
                        fns = [lambda h=h, D_=D_: PEe.matmul(psum[D_, 6, H(h)], lhsT=NT[k][D_, h, :], rhs=N_[k][D_, h, :], start=True, stop=True) for h, D_ in HD]
                        kb.pe(fns, [NT[k].b, N_[k].b], [PB[6]])
                        kb.op("act", lambda: A.copy(out=N_[k + 1][:], in_=hv(6)), [PB[6]], [N_[k + 1].b])
                        if k < 4:
                            fns = [lambda h=h, D_=D_: PEe.matmul(psum[D_, 7, H(h)], lhsT=N_[k][D_, h, :], rhs=NT[k][D_, h, :], start=True, stop=True) for h, D_ in HD]
                            kb.pe(fns, [NT[k].b, N_[k].b], [PB[7]])
                            kb.op("act", lambda: A.copy(out=NT[k + 1][:], in_=hv(7)), [PB[7]], [NT[k + 1].b])
                        src, dst = PTt[k % 2], PTt[(k + 1) % 2]
                        fns = [lambda h=h, D_=D_: PEe.matmul(psum[D_, 0, H(h)], lhsT=N_[k + 1][D_, h, :], rhs=src[D_, h, :], start=True, stop=True) for h, D_ in HD]
                        kb.pe(fns, [N_[k + 1].b, src.b], [PB[0]])
                        kb.op("dve", lambda: V.tensor_tensor(out=dst[:], in0=hv(0), in1=src[:], op=ALU.add), [PB[0], src.b], [dst.b])
                    TTf = PTt[1]
                    fns = [lambda h=h, D_=D_: PEe.matmul(psum[D_, 1, H(h)], lhsT=TTf[D_, h, :], rhs=Wn[D_, h, :], start=True, stop=True) for h, D_ in HD]
                    kb.pe(fns, [TTf.b, Wn.b], [PB[1]])
                    kb.op("dve", lambda: V.tensor_copy(out=U[:], in_=hv(1)), [PB[1]], [U.b])
                    fns = []
                    for h, D_ in HD:
                        fns.append(lambda h=h, D_=D_: PEe.matmul(psum[D_, 2, H(h)], lhsT=qg[D_, h, j, 64:128], rhs=Zb[D_, h, :], start=True, stop=False))
                        fns.append(lambda h=h, D_=D_: PEe.matmul(psum[D_, 2, H(h)], lhsT=BabT[D_, h, :], rhs=U[D_, h, :], start=False, stop=False))
                        fns.append(lambda h=h, D_=D_: PEe.matmul(psum[D_, 2, H(h)], lhsT=BakT[D_, h, :], rhs=vv[D_, j, H(h)], start=False, stop=True))
                    kb.pe(fns, bb(qg, vv) + [Zb.b, BabT.b, U.b, BakT.b], [PB[2]])
                    ys = Yst[yi[0] % 2]
                    yi[0] += 1
                    kb.op("act", lambda: A.copy(out=ys[:], in_=psum[:, 2, :]), [PB[2]], [ys.b])
                    kb.dma("sp", ys.b.name + "f", YS[0][tokf:tokf + 64, :], ys[0:64, :], reads=[ys.b])
                    kb.dma("sp", ys.b.name + "b", YS[1][tokb:tokb + 64, :], ys[64:128, :], reads=[ys.b])
                    fns = []
                    for h, D_ in HD:
                        fns.append(lambda h=h, D_=D_: PEe.matmul(psum[D_, 3, H(h)], lhsT=ak[D_, j, H(h)], rhs=U[D_, h, :], start=True, stop=False))
                        fns.append(lambda h=h, D_=D_: PEe.matmul(psum[D_, 3, H(h)], lhsT=kk_[D_, j, H(h)], rhs=vv[D_, j, H(h)], start=False, stop=True))
                    kb.pe(fns, bb(ak, kk_, vv) + [U.b], [PB[3]])
                    kb.op("dve", lambda: V.tensor_tensor(out=Z[:], in0=hv(3), in1=Z[:], op=ALU.add), [PB[3], Z.b], [Z.b])
                    jb = nchk - 1 - j
                    kb.op("dve", lambda: V.tensor_tensor(out=Z[0:64], in0=Z[0:64], in1=gc[0:64, :, j:j + 1].to_broadcast([64, 8, 64]), op=ALU.mult), [Z.b, gc.b], [Z.b])
                    kb.op("pool", lambda: P.tensor_tensor(out=Z[64:128], in0=Z[64:128], in1=gc[64:128, :, jb:jb + 1].to_broadcast([64, 8, 64]), op=ALU.mult), [Z.b, gc.b2], [Z.b])
                    kb.op("act", lambda: A.copy(out=Zb[:], in_=Z[:]), [Z.b], [Zb.b])

                kb.op("dve", lambda: V.memset(Z[:], 0.0), [], [Z.b])
                kb.op("dve", lambda: V.memset(Zb[:], 0.0), [], [Zb.b])
                load(0, 0)
                for s in range(nB):
                    par = s % 2
                    if s + 1 < nB:
                        load(s + 1, 1 - par)
                    t0f, nb, _ = blocks[fo[s]]
                    t0b, _, _ = blocks[bo[s]]
                    nchk = nb // 64
                    for j in range(nchk):
                        chunk(par, j, nchk, t0f + j * 64, t0b + (nchk - 1 - j) * 64)
                kb.barrier()

        def phase_C3(l, s2, part=0, nparts=1):
            if True:
                bk = 6 + part
                bc = T(s2, nc, "bcg", [128, 2, 512], F32)
                kb.dma("sp", "bcg%d" % part, bc[:].rearrange("p a n -> p (a n)"), BCd[:, l * 1024:(l + 1) * 1024], writes=[bc.b])
                def mk(nm, shape, dt):
                    return [T(s2, nc, "%s%d_%d" % (nm, i, part), shape, dt) for i in range(2)]
                yf, yb = mk("yf", [128, 512], F32), mk("yb", [128, 512], F32)
                vz = mk("vz", [128, 1024], BF16)
                rk_ = mk("rk_", [128, 2, 4, 128], BF16)
                rkr = mk("rkr", [128, 4, 128], BF16)
                y = mk("y_", [128, 512], F32)
                ysq = mk("ysq", [128, 512], F32)
                st_ = mk("st_", [128, 6, 8], F32)
                sz = mk("sz", [128, 512], F32)
                bon = mk("bon", [128, 512], F32)
                yg = mk("yg", [128, 512], BF16)
                ygT = mk("ygT", [128, 4, 128], BF16)
                for ti, tt in enumerate(range(part, NTT, nparts)):
                    p = ti % 2
                    ts_ = slice(tt * 128, (tt + 1) * 128)
                    kb.dma("sp", yf[p].b.name, yf[p][:], YS[0][ts_, :], writes=[yf[p].b])
                    kb.dma("sp", yb[p].b.name, yb[p][:], YS[1][ts_, :], writes=[yb[p].b])
                    kb.dma("sp", vz[p].b.name, vz[p][:], PT[ts_, 0:1024], writes=[vz[p].b])
                    kb.dma("sp", rk_[p].b.name, rk_[p][:].rearrange("p a c t -> p (a c) t"), PF[0:1024, ts_].rearrange("(c p) t -> p c t", p=128), writes=[rk_[p].b])
                    for c in range(4):
                        kb.op("dve", lambda: V.scalar_tensor_tensor(out=rkr[p][:, c, :], in0=rk_[p][:, 0, c, :], scalar=pp[:, l, 56 + c:57 + c], in1=rk_[p][:, 1, c, :],
                                                                     op0=ALU.mult, op1=ALU.mult), [rk_[p].b, pp.b], [rkr[p].b])
                    kb.pe([lambda c=c: PEe.matmul(psum[:, bk, 0:8], lhsT=rkr[p][:, c, :], rhs=sel_b[:, c * 8:(c + 1) * 8], start=(c == 0), stop=(c == 3)) for c in range(4)],
                          [rkr[p].b, konb.b], [PB[bk]])
                    S_ = st_[p]
                    kb.op("act", lambda: A.copy(out=S_[:, 5, :], in_=psum[:, bk, 0:8]), [PB[bk]], [S_.b])
                    kb.op("pool", lambda: P.tensor_tensor(out=y[p][:], in0=yf[p][:], in1=yb[p][:], op=ALU.add), [yf[p].b, yb[p].b], [y[p].b])
                    y3 = y[p][:].rearrange("p (h x) -> p h x", h=8)
                    kb.op("dve", lambda: V.tensor_reduce(out=S_[:, 0, :], in_=y3, axis=AX.X, op=ALU.add), [y[p].b], [S_.b])
                    kb.op("act", lambda: A.activation(out=ysq[p][:], in_=y[p][:], func=AF.Square), [y[p].b], [ysq[p].b])
                    kb.op("dve", lambda: V.tensor_reduce(out=S_[:, 1, :], in_=ysq[p][:].rearrange("p (h x) -> p h x", h=8), axis=AX.X, op=ALU.add), [ysq[p].b], [S_.b])
                    kb.op("dve", lambda: V.tensor_scalar(out=S_[:, 2, :], in0=S_[:, 0, :], scalar1=1.0 / 64, scalar2=None, op0=ALU.mult), [S_.b], [S_.b])
                    kb.op("dve", lambda: V.tensor_tensor(out=S_[:, 3, :], in0=S_[:, 2, :], in1=S_[:, 2, :], op=ALU.mult), [S_.b], [S_.b])
                    kb.op("dve", lambda: V.scalar_tensor_tensor(out=S_[:, 3, :], in0=S_[:, 1, :], scalar=1.0 / 64, in1=S_[:, 3, :], op0=ALU.mult, op1=ALU.subtract), [S_.b], [S_.b])
                    kb.op("act", lambda: A.activation(out=S_[:, 4, :], in_=S_[:, 3, :], func=AF.Sqrt, bias=GN_EPS, scale=1.0), [S_.b], [S_.b])
                    kb.op("dve", lambda: V.reciprocal(out=S_[:, 4, :], in_=S_[:, 4, :]), [S_.b], [S_.b])
                    kb.op("dve", lambda: V.tensor_tensor(out=y3, in0=y3, in1=S_[:, 2, :].unsqueeze(2).to_broadcast([128, 8, 64]), op=ALU.subtract), [y[p].b, S_.b], [y[p].b])
                    kb.op("dve", lambda: V.tensor_tensor(out=y3, in0=y3, in1=S_[:, 4, :].unsqueeze(2).to_broadcast([128, 8, 64]), op=ALU.mult), [y[p].b, S_.b], [y[p].b])
                    kb.op("pool", lambda: P.tensor_tensor(out=y[p][:], in0=y[p][:], in1=bc[:, 0, :], op=ALU.mult), [y[p].b, bc.b], [y[p].b])
                    kb.op("pool", lambda: P.tensor_tensor(out=y[p][:], in0=y[p][:], in1=bc[:, 1, :], op=ALU.add), [y[p].b, bc.b], [y[p].b])
                    kb.op("dve", lambda: V.tensor_tensor(out=bon[p][:].rearrange("p (h x) -> p h x", h=8), in0=vz[p][:, 0:512].rearrange("p (h x) -> p h x", h=8),
                                                         in1=S_[:, 5, :].unsqueeze(2).to_broadcast([128, 8, 64]), op=ALU.mult), [vz[p].b, S_.b], [bon[p].b])
                    kb.op("pool", lambda: P.tensor_tensor(out=y[p][:], in0=y[p][:], in1=bon[p][:], op=ALU.add), [y[p].b, bon[p].b], [y[p].b])
                    kb.op("act", lambda: A.activation(out=sz[p][:], in_=vz[p][:, 512:1024], func=AF.Silu), [vz[p].b], [sz[p].b])
                    kb.op("dve", lambda: V.tensor_tensor(out=yg[p][:], in0=y[p][:], in1=sz[p][:], op=ALU.mult), [y[p].b, sz[p].b], [yg[p].b])
                    kb.pe([lambda c=c: PEe.transpose(psbf[bk][:, 512 + c * 128:512 + (c + 1) * 128], yg[p][:, c * 128:(c + 1) * 128], ident_b) for c in range(4)], [yg[p].b, konb.b], [PB[bk]])
                    kb.op("act", lambda: A.copy(out=ygT[p][:].rearrange("p c t -> p (c t)"), in_=psbf[bk][:, 512:1024]), [PB[bk]], [ygT[p].b])
                    kb.dma("sp", ygT[p].b.name, YG[0:512, ts_].rearrange("(c p) t -> p c t", p=128), ygT[p][:], reads=[ygT[p].b])
                    yield

        def phase_D(l, s2):
            if True:
                gbz = T(s2, nc, "gbz", [128, 2, 4, 512], BF16)
                gcu = T(s2, nc, "gcu", [128, 2, 4, 514], BF16)
                gu = T(s2, nc, "gu", [128, 4, 514], F32)
                acc = T(s2, nc, "acc", [128, 4, 512], F32)
                szd = T(s2, nc, "szd", [128, 4, 512], F32)
                od = T(s2, nc, "od", [128, 4, 512], BF16)
                for bi, (t0, nb, col) in enumerate(blocks):
                    s0, s1 = (0, CTX) if col == 1 else (CTX, TT)
                    lo, hi = max(s0, t0 - 1), min(s1, t0 + nb + 1)
                    kb.op("pool", lambda: P.memset(gcu[:], 0.0), [], [gcu.b])
                    off = lo - (t0 - 1)
                    for j, r0 in enumerate([2816, 3328]):
                        kb.dma("sp", "gcu%d" % j, gcu[:, j, :, off:off + hi - lo], PF[r0:r0 + 512, lo:hi].rearrange("(c p) t -> p c t", p=128), writes=[gcu.b])
                    for j, r0 in enumerate([2304, 3840]):
                        kb.dma("sp", "gbz%d" % j, gbz[:, j, :, 0:nb], PF[r0:r0 + 512, t0:t0 + nb].rearrange("(c p) t -> p c t", p=128), writes=[gbz.b])
                    kb.op("dve", lambda: V.tensor_tensor(out=gu[:, :, 0:nb + 2], in0=gcu[:, 0, :, 0:nb + 2], in1=gcu[:, 1, :, 0:nb + 2], op=ALU.mult), [gcu.b], [gu.b])
                    for c in range(4):
                        cw = lambda j: pp[:, l, 60 + j * 4 + c:61 + j * 4 + c]
                        kb.op("dve", lambda: V.tensor_scalar(out=acc[:, c, 0:nb], in0=gu[:, c, 0:nb], scalar1=cw(0), scalar2=None, op0=ALU.mult), [gu.b, pp.b], [acc.b])
                        kb.op("dve", lambda: V.scalar_tensor_tensor(out=acc[:, c, 0:nb], in0=gu[:, c, 1:nb + 1], scalar=cw(1), in1=acc[:, c, 0:nb], op0=ALU.mult, op1=ALU.add), [gu.b, pp.b, acc.b], [acc.b])
                        kb.op("dve", lambda: V.scalar_tensor_tensor(out=acc[:, c, 0:nb], in0=gu[:, c, 2:nb + 2], scalar=cw(2), in1=acc[:, c, 0:nb], op0=ALU.mult, op1=ALU.add), [gu.b, pp.b, acc.b], [acc.b])
                        kb.op("dve", lambda: V.scalar_tensor_tensor(out=acc[:, c, 0:nb], in0=acc[:, c, 0:nb], scalar=pp[:, l, 72 + c:73 + c], in1=gbz[:, 0, c, 0:nb], op0=ALU.add, op1=ALU.mult),
                              [acc.b, pp.b, gbz.b], [acc.b])
                        yield
                    kb.op("act", lambda: A.activation(out=szd[:, :, 0:nb], in_=gbz[:, 1, :, 0:nb], func=AF.Silu), [gbz.b], [szd.b])
                    kb.op("pool", lambda: P.tensor_tensor(out=od[:, :, 0:nb], in0=acc[:, :, 0:nb], in1=szd[:, :, 0:nb], op=ALU.mult), [acc.b, szd.b], [od.b])
                    kb.dma("sp", "od", YG[512:1024, t0:t0 + nb].rearrange("(c p) t -> p c t", p=128), od[:, :, 0:nb], reads=[od.b])

        def phase_E1(l, s2):
            if True:
                pg = T(s2, nc, "pg", [128, 2, 128], BF16)
                qk = [T(s2, nc, "qk%d" % i, [128, 8, 512], BF16) for i in range(2)]
                cs_ = [T(s2, nc, "cs_%d" % i, [128, 2, 512], F32) for i in range(2)]
                sqe = [T(s2, nc, "sqe%d" % i, [128, 512], BF16) for i in range(2)]
                ri = [T(s2, nc, "ri%d" % i, [128, 512], F32) for i in range(2)]
                t1 = [T(s2, nc, "t1_%d" % i, [128, 512], F32) for i in range(2)]
                t2 = [T(s2, nc, "t2_%d" % i, [128, 512], F32) for i in range(2)]
                oq = [T(s2, nc, "oq%d" % i, [128, 512], BF16) for i in range(2)]
                for j in range(2):
                    kb.op("dve", lambda: V.tensor_scalar(out=pg[:, j, :], in0=perm_f, scalar1=pp[:, l, 76 + j:77 + j], scalar2=None, op0=ALU.mult), [kon.b, pp.b], [pg.b])
                for bi, (t0, nb, col) in enumerate(blocks):
                    S = slice(0, nb)
                    q_, c_ = qk[bi % 2], cs_[bi % 2]
                    kb.dma_group("sp", q_.b.name, [(q_[:, 0:4, S], PF[4352:4864, t0:t0 + nb].rearrange("(c p) t -> p c t", p=128)),
                                                   (q_[:, 4:8, S], PF[4864:5376, t0:t0 + nb].rearrange("(c p) t -> p c t", p=128))], writes=[q_.b])
                    kb.dma_group("sp", c_.b.name, [(c_[:, 0, S], ROPE[:, t0:t0 + nb]), (c_[:, 1, S], ROPE[:, TT + t0:TT + t0 + nb])], writes=[c_.b])
                    for j in range(8):
                        w = j // 4
                        p = j % 2
                        kb.op("act", lambda: A.activation(out=sqe[p][:, S], in_=q_[:, j, S], func=AF.Square), [q_.b], [sqe[p].b])
                        kb.pe([lambda: PEe.matmul(psum[:, 4 + p, S], lhsT=blk_b, rhs=sqe[p][:, S], start=True, stop=True)], [sqe[p].b, konb.b], [PB[4 + p]])
                        kb.pe([lambda: PEe.matmul(psum[:, 6 + p, S], lhsT=pg[:, w, :], rhs=q_[:, j, S], start=True, stop=True)], [pg.b, q_.b], [PB[6 + p]])
                        kb.op("act", lambda: A.activation(out=ri[p][:, S], in_=psum[:, 4 + p, S], func=AF.Ln, bias=epsc[:, 0:1], scale=1.0 / 64), [PB[4 + p], epsc.b], [ri[p].b])
                        kb.op("act", lambda: A.activation(out=ri[p][:, S], in_=ri[p][:, S], func=AF.Exp, scale=-0.5), [ri[p].b], [ri[p].b])
                        kb.op("dve", lambda: V.scalar_tensor_tensor(out=t1[p][:, S], in0=q_[:, j, S], scalar=pp[:, l, 76 + w:77 + w], in1=c_[:, 0, S], op0=ALU.mult, op1=ALU.mult),
                              [q_.b, pp.b, c_.b], [t1[p].b])
                        kb.op("dve", lambda: V.tensor_tensor(out=t2[p][:, S], in0=psum[:, 6 + p, S], in1=c_[:, 1, S], op=ALU.mult), [PB[6 + p], c_.b], [t2[p].b])
                        kb.op("pool", lambda: P.tensor_tensor(out=t1[p][:, S], in0=t1[p][:, S], in1=t2[p][:, S], op=ALU.add), [t1[p].b, t2[p].b], [t1[p].b])
                        kb.op("pool", lambda: P.tensor_tensor(out=oq[p][:, S], in0=t1[p][:, S], in1=ri[p][:, S], op=ALU.mult), [t1[p].b, ri[p].b], [oq[p].b])
                        dst = QRs if w == 0 else KRs
                        kb.dma("sp", oq[p].b.name, dst[(j % 4) * 128:(j % 4 + 1) * 128, t0:t0 + nb], oq[p][:, S], reads=[oq[p].b])
                        yield

        def phase_E2(l, last, s2):
            if True:
                QR = T(s2, nc, "QR", [128, 4, TT], BF16)
                KR = T(s2, nc, "KR", [128, 4, TT], BF16)
                VD = T(s2, nc, "VD", [128, NTT, 512], BF16)
                PTs = [[T(s2, nc, "PTs%d_%d" % (i, c), [128, 512], BF16) for c in range(2)] for i in range(3)]
                accD = [T(s2, nc, "accD%d" % i, [128, 512], F32) for i in range(2)]
                accP = [T(s2, nc, "accP%d" % i, [128, 512], F32) for i in range(2)]
                o0 = [T(s2, nc, "o0_%d" % i, [128, 512], F32) for i in range(2)]
                r0 = T(s2, nc, "r0", [128, 512], F32)
                r0a = T(s2, nc, "r0a", [128, 512], F32)
                sq2 = T(s2, nc, "sq2", [128, 512], BF16)
                rst = T(s2, nc, "rst", [128, 512], F32)
                zc = T(s2, nc, "zc", [128, 512], BF16)
                szc = T(s2, nc, "szc", [128, 512], F32)
                oc = T(s2, nc, "oc", [128, 512], BF16)
                kb.dma("sp", "VD", VD[:], PT[:, 1024:1536].rearrange("(t p) n -> p t n", p=128), writes=[VD.b])
                kb.dma("sp", "QR", QR[:], QRs.rearrange("(h p) t -> p h t", p=128), writes=[QR.b])
                kb.dma("sp", "KR", KR[:], KRs.rearrange("(h p) t -> p h t", p=128), writes=[KR.b])
                pending = []

                def flush(n):
                    for _ in range(min(n, len(pending))):
                        pending.pop(0)()
                n_ = 0
                bidx = 0
                for h in range(4):
                    for bi, (t0, nb, col) in enumerate(blocks):
                        if col == 1 and last:
                            continue
                        S = slice(0, nb)
                        kts = list(range(CTX // 128)) if col == 1 else list(range(NTT))
                        aD, aP = accD[bidx % 2], accP[bidx % 2]
                        bidx += 1

                        def smm(i):
                            kt = kts[i]
                            b0 = 2 * ((n_ + i) % 2)
                            kb.pe([lambda c=c: PEe.matmul(psum[:, b0 + c, S], lhsT=KR[c * 64:(c + 1) * 64, h, kt * 128:(kt + 1) * 128], rhs=QR[c * 64:(c + 1) * 64, h, t0:t0 + nb],
                                                          start=True, stop=True) for c in range(2)], [KR.b, QR.b], [PB[b0], PB[b0 + 1]])
                        smm(0)
                        for i, kt in enumerate(kts):
                            if i + 1 < len(kts):
                                smm(i + 1)
                            b0 = 2 * ((n_ + i) % 2)
                            pt = PTs[(n_ + i) % 3]
                            first, lastk = (i == 0), (i == len(kts) - 1)
                            for c in range(2):
                                kb.op("act", lambda: A.activation(out=pt[c][:, S], in_=psum[:, b0 + c, S], func=AF.Exp, scale=0.125), [PB[b0 + c]], [pt[c].b])
                                fns = [lambda: PEe.matmul(psum[:, 4 + c, S], lhsT=VD[:, kt, h * 128:(h + 1) * 128], rhs=pt[c][:, S], start=first, stop=lastk)]
                                wr = [PB[4 + c]]
                                if c == 0:
                                    fns.append(lambda: PEe.matmul(psum[:, 7, S], lhsT=ones_b, rhs=pt[0][:, S], start=first, stop=lastk))
                                    wr.append(PB[7])
                                kb.pe(fns, [VD.b, pt[c].b, konb.b], wr)
                            if i % 2 == 0:
                                if i == 0:
                                    kb.op("dve", lambda: V.tensor_copy(out=aD[:, S], in_=pt[1][:, S]), [pt[1].b], [aD.b])
                                else:
                                    kb.op("dve", lambda: V.tensor_tensor(out=aD[:, S], in0=aD[:, S], in1=pt[1][:, S], op=ALU.add), [pt[1].b, aD.b], [aD.b])
                            else:
                                if i == 1:
                                    kb.op("pool", lambda: P.tensor_copy(out=aP[:, S], in_=pt[1][:, S]), [pt[1].b], [aP.b])
                                else:
                                    kb.op("pool", lambda: P.tensor_tensor(out=aP[:, S], in0=aP[:, S], in1=pt[1][:, S], op=ALU.add), [pt[1].b, aP.b], [aP.b])
                            flush(1)
                            yield
                        n_ += len(kts)
                        flush(len(pending))
                        kb.op("act", lambda: A.copy(out=o0[0][:, S], in_=psum[:, 4, S]), [PB[4]], [o0[0].b])
                        kb.op("dve", lambda: V.tensor_copy(out=o0[1][:, S], in_=psum[:, 5, S]), [PB[5]], [o0[1].b])
                        kb.op("act", lambda: A.copy(out=r0a[:, S], in_=psum[:, 7, S]), [PB[7]], [r0a.b])

                        def mk_epilogue(h=h, t0=t0, nb=nb, S=S, aD=aD, aP=aP):
                            ops = []
                            ops.append(lambda: kb.dma("sp", "zc", zc[:, S], PF[5888 + h * 128:5888 + (h + 1) * 128, t0:t0 + nb], writes=[zc.b]))
                            ops.append(lambda: kb.op("dve", lambda: V.reciprocal(out=r0a[:, S], in_=r0a[:, S]), [r0a.b], [r0a.b]))
                            ops.append(lambda: kb.op("dve", lambda: V.tensor_tensor(out=o0[0][:, S], in0=o0[0][:, S], in1=r0a[:, S], op=ALU.mult), [o0[0].b, r0a.b], [o0[0].b]))
                            ops.append(lambda: kb.pe([lambda: PEe.matmul(psum[:, 6, S], lhsT=ones_f, rhs=aD[:, S], start=True, stop=False),
                                                      lambda: PEe.matmul(psum[:, 6, S], lhsT=ones_f, rhs=aP[:, S], start=False, stop=True)], [aD.b, aP.b, kon.b], [PB[6]]))
                            ops.append(lambda: kb.op("dve", lambda: V.reciprocal(out=r0[:, S], in_=psum[:, 6, S]), [PB[6]], [r0.b]))
                            ops.append(lambda: kb.op("dve", lambda: V.tensor_tensor(out=o0[1][:, S], in0=o0[1][:, S], in1=r0[:, S], op=ALU.mult), [o0[1].b, r0.b], [o0[1].b]))
                            ops.append(lambda: kb.op("dve", lambda: V.scalar_tensor_tensor(out=o0[0][:, S], in0=o0[1][:, S], scalar=lamc[:, l, 0:1], in1=o0[0][:, S], op0=ALU.mult, op1=ALU.add),
                                                     [o0[0].b, o0[1].b, lamc.b], [o0[0].b]))
                            ops.append(lambda: kb.op("act", lambda: A.activation(out=sq2[:, S], in_=o0[0][:, S], func=AF.Square), [o0[0].b], [sq2.b]))
                            ops.append(lambda: kb.pe([lambda: PEe.matmul(psum[:, 6, S], lhsT=ones_b, rhs=sq2[:, S], start=True, stop=True)], [sq2.b, konb.b], [PB[6]]))
                            ops.append(lambda: kb.op("act", lambda: A.activation(out=rst[:, S], in_=psum[:, 6, S], func=AF.Ln, bias=epsc[:, 0:1], scale=1.0 / 128), [PB[6], epsc.b], [rst.b]))
                            ops.append(lambda: kb.op("act", lambda: A.activation(out=rst[:, S], in_=rst[:, S], func=AF.Exp, scale=-0.5), [rst.b], [rst.b]))
                            ops.append(lambda: kb.op("dve", lambda: V.scalar_tensor_tensor(out=o0[0][:, S], in0=o0[0][:, S], scalar=sgl[:, l:l + 1], in1=rst[:, S], op0=ALU.mult, op1=ALU.mult),
                                                     [o0[0].b, sgl.b, rst.b], [o0[0].b]))
                            ops.append(lambda: kb.op("act", lambda: A.activation(out=szc[:, S], in_=zc[:, S], func=AF.Silu), [zc.b], [szc.b]))
                            ops.append(lambda: kb.op("pool", lambda: P.tensor_tensor(out=oc[:, S], in0=o0[0][:, S], in1=szc[:, S], op=ALU.mult), [o0[0].b, szc.b], [oc.b]))
                            ops.append(lambda: kb.dma("sp", "oc", YG[1024 + h * 128:1024 + (h + 1) * 128, t0:t0 + nb], oc[:, S], reads=[oc.b]))
                            return ops
                        pending.extend(mk_epilogue())
                flush(len(pending))

        def phase_F(l, last):
            with ExitStack() as s2:
                wbr = T(s2, nc, "wbr", [128, 12, 1024], BF16)
                wou = T(s2, nc, "wou", [128, 8, 1024], BF16)
                ygb = [T(s2, nc, "ygb%d" % i, [128, 12, 512], BF16) for i in range(2)]
                gm = [T(s2, nc, "gm%d" % i, [128, 24, 512], BF16) for i in range(2)]
                xa = [T(s2, nc, "xaf%d" % i, [128, 8, 512], F32) for i in range(2)]
                xn = T(s2, nc, "xnf", [128, 8, 512], F32)
                mT = T(s2, nc, "mT", [128, 8, 512], BF16)
                sg = [T(s2, nc, "sgf%d" % i, [128, 512], F32) for i in range(3)]
                ma = [T(s2, nc, "ma%d" % i, [128, 512], F32) for i in range(3)]
                stg = [xa[0], xa[1], xn]
                for i in range(5):
                    w = stg[i % 3]
                    wv = w[:].rearrange("p c n -> p (c n)").rearrange("p (a m) -> p a m", a=4)
                    src = w_br[l][i * 512:(i + 1) * 512, :] if i < 3 else w_out[l][(i - 3) * 512:(i - 2) * 512, :]
                    kb.dma("sp", w.b.name, wv, src.rearrange("(c p) n -> p c n", p=128), writes=[w.b])
                    dstt = wbr[:, i * 4:(i + 1) * 4, :] if i < 3 else wou[:, (i - 3) * 4:(i - 2) * 4, :]
                    dstb = wbr.b if i < 3 else wou.b
                    kb.op("pool", lambda: P.tensor_copy(out=dstt, in_=wv), [w.b], [dstb])
                todo = [(bi, blk) for bi, blk in enumerate(blocks) if not (blk[2] == 1 and last)]

                def loads(k):
                    bi, (t0, nb, col) = todo[k]
                    p = k % 2
                    S = slice(0, nb)
                    kb.dma("sp", ygb[p].b.name, ygb[p][:, :, S], YG[:, t0:t0 + nb].rearrange("(c p) t -> p c t", p=128), writes=[ygb[p].b])
                    kb.dma("sp", gm[p].b.name, gm[p][:, :, S], PF[6400:9472, t0:t0 + nb].rearrange("(c p) t -> p c t", p=128), writes=[gm[p].b])
                    kb.dma("sp", xa[p].b.name, xa[p][:, :, S], xTv[:, :, t0:t0 + nb], writes=[xa[p].b])
                loads(0)
                for k, (bi, (t0, nb, col)) in enumerate(todo):
                    if k + 1 < len(todo):
                        loads(k + 1)
                    p = k % 2
                    S = slice(0, nb)
                    for dc in range(8):
                        for i in range(3):
                            kb.pe([lambda w=w: PEe.matmul(psum[:, 1 + i, S], lhsT=wbr[:, i * 4 + w, dc * 128:(dc + 1) * 128], rhs=ygb[p][:, i * 4 + w, S], start=(w == 0), stop=(w == 3))
                                   for w in range(4)], [wbr.b, ygb[p].b], [PB[1 + i]])
                            kb.op("act", lambda: A.activation(out=sg[i][:, S], in_=gm[p][:, i * 8 + dc, S], func=AF.Sigmoid), [gm[p].b], [sg[i].b])
                            kb.op("dve", lambda: V.tensor_tensor(out=ma[i][:, S], in0=psum[:, 1 + i, S], in1=sg[i][:, S], op=ALU.mult), [PB[1 + i], sg[i].b], [ma[i].b])
                        kb.op("pool", lambda: P.tensor_tensor(out=ma[0][:, S], in0=ma[0][:, S], in1=ma[1][:, S], op=ALU.add), [ma[0].b, ma[1].b], [ma[0].b])
                        kb.op("pool", lambda: P.tensor_tensor(out=mT[:, dc, S], in0=ma[0][:, S], in1=ma[2][:, S], op=ALU.add), [ma[0].b, ma[2].b], [mT.b])
                    for d2 in range(8):
                        bank = 4 + d2 % 2
                        kb.pe([lambda dc=dc: PEe.matmul(psum[:, bank, S], lhsT=wou[:, dc, d2 * 128:(d2 + 1) * 128], rhs=mT[:, dc, S], start=(dc == 0), stop=(dc == 7)) for dc in range(8)],
                              [wou.b, mT.b], [PB[bank]])
                        kb.op("dve", lambda: V.scalar_tensor_tensor(out=xn[:, d2, S], in0=psum[:, bank, S], scalar=modT[:, l, 16 + d2, col:col + 1], in1=xa[p][:, d2, S], op0=ALU.mult, op1=ALU.add),
                              [PB[bank], modT.b, xa[p].b], [xn.b])
                    kb.dma("sp", "xnf", xTv[:, :, t0:t0 + nb], xn[:, :, S], reads=[xn.b])
                kb.barrier()

        def drain(g):
            for _ in g:
                pass

        def layer(l):
            last = (l == DEPTH - 1)
            phase_AB(l)
            with ExitStack() as sg:
                kb.coop.run([(lambda: drain(phase_C1(l, sg)), 8), (lambda: drain(phase_E1(l, sg)), 3), (lambda: drain(phase_D(l, sg)), 1)])
                kb.barrier()
            phase_C2(l)
            with ExitStack() as sg:
                drain(phase_E2(l, last, sg))
                kb.barrier()
            with ExitStack() as sg:
                kb.coop.run([(lambda: drain(phase_C3(l, sg, 0, 2)), 1), (lambda: drain(phase_C3(l, sg, 1, 2)), 1)])
                kb.barrier()
            phase_F(l, last)

        for l in range(DEPTH):
            layer(l)
        with ExitStack() as s2:
            xb_ = [T(s2, nc, "xo%d" % i, [128, 8, 512], F32) for i in range(2)]
            for bi, (t0, nb, col) in enumerate(blocks[1:]):
                x_ = xb_[bi % 2]
                kb.dma("sp", x_.b.name, x_[:, :, 0:nb], xTv[:, :, t0:t0 + nb], writes=[x_.b])
                kb.dma("sp", x_.b.name, yout.rearrange("(c p) t -> p c t", p=128)[:, :, t0 - CTX:t0 - CTX + nb], x_[:, :, 0:nb], reads=[x_.b])
            if dbg:
                for k_, shp in dbg.items():
                    src = {"PF": PF, "PX": PX, "PT": PT, "YG": YG, "YS0": YS[0], "YS1": YS[1], "xTs": xT, "AKT0": AKT[0], "GCs0": GCs[0], "GCs1": GCs[1]}[k_]
                    kb.dma("sp", "dbg", dbgo[k_], src, reads=[])
            kb.barrier()
    build.n_inst = kb.n_inst
    build.n_sem = len(kb.sems)
    return nc


def host_consts(SEQ, DEPTH):
    TT = CTX + SEQ
    kon = np.zeros((128, 1024), np.float32)
    kon[:, 0:128] = np.eye(128)
    kon[:, 128:256] = 1.0
    p = np.arange(128)
    kon[:, 256:384] = (p[:, None] // 64 == p[None, :] // 64)
    partner = np.where((p % 32) < 16, p + 16, p - 16)
    perm = np.zeros((128, 128), np.float32)
    perm[partner, p] = 1.0
    kon[:, 384:512] = perm
    sel = np.zeros((128, 4, 8), np.float32)
    for c in range(4):
        for q in range(128):
            sel[q, c, 2 * c + q // 64] = 1.0
    kon[:, 512:544] = sel.reshape(128, 32)
    i = np.arange(64)
    SU = (i[:, None] < i[None, :]); SL = (i[:, None] > i[None, :]); IU = (i[:, None] <= i[None, :]); IL = (i[:, None] >= i[None, :]); I = np.eye(64)
    rep = lambda ms: np.stack([np.broadcast_to(m[:, None, :], (64, 8, 64)) for m in ms], axis=1).astype(np.float32).reshape(64, 4 * 512)
    msk = np.concatenate([rep((SU, IU, SL, I)), rep((SL, IL, SU, I))], axis=0)
    n_freq = 16
    inv_freq = (10000.0 ** (-np.arange(n_freq, dtype=np.float32) / n_freq)).astype(np.float32)
    tok = np.arange(SEQ)
    row = (tok // 64).astype(np.float32)
    colp = (tok % 64).astype(np.float32)
    d = p % 64
    half = d // 32
    j = d % 16
    which = (d % 32) // 16
    pos = np.where(half[:, None] == 0, row[None, :], colp[None, :]).astype(np.float32)
    ang = (pos * inv_freq[j][:, None]).astype(np.float32)
    cosT = np.ones((128, TT), np.float32)
    sinT = np.zeros((128, TT), np.float32)
    cosT[:, CTX:] = np.cos(ang)
    sinT[:, CTX:] = np.sin(ang) * np.where(which == 0, -1.0, 1.0)[:, None]
    rope = np.concatenate([cosT, sinT], axis=1).astype(np.float32)
    ltri = np.ones((128, 512), np.float32)
    ltri[:, 0::64] = 0.0
    return dict(KON=kon, MSK=np.ascontiguousarray(msk), ROPE=np.ascontiguousarray(rope), LTRI=ltri)


def host_inputs(inp, SEQ, DEPTH, b):
    L = DEPTH
    f = lambda a: np.asarray(a, np.float32)
    xT = np.concatenate([f(inp["ctx"])[b], f(inp["x"])[b]], axis=0).T
    PPa = np.zeros((128, L, NPL), np.float32)
    colT = lambda v: v.reshape(-1, 128).T
    for l in range(L):
        PPa[:, l, 0:8] = colT(f(inp["norm_g"])[l])
        PPa[:, l, 8:32] = colT(f(inp["b_mod"])[l])
        PPa[:, l, 32:40] = colT(f(inp["decay_w0"])[l].reshape(-1))
        PPa[:, l, 40:48] = colT(f(inp["iclr_a0"])[l].reshape(-1))
        PPa[:, l, 48:52] = colT(f(inp["k_k"])[l])
        PPa[:, l, 52:56] = colT(f(inp["k_a"])[l])
        PPa[:, l, 56:60] = colT(f(inp["r_k"])[l].reshape(-1))
        PPa[:, l, 60:72] = colT(f(inp["conv_w"])[l].reshape(-1))
        PPa[:, l, 72:76] = colT(f(inp["conv_b"])[l])
        PPa[:, l, 76] = np.tile(f(inp["qk_norm_g"])[l, 0], 2)
        PPa[:, l, 77] = np.tile(f(inp["qk_norm_g"])[l, 1], 2)
        PPa[:, l, 78] = f(inp["subln_g"])[l]
    CC = np.stack([colT(f(inp["c"])[b]), colT(f(inp["c_ctx"]))], axis=2).reshape(128, 16)
    BC = np.stack([np.broadcast_to(f(inp["gn_g"])[:L, None, :], (L, 128, 512)), np.broadcast_to(f(inp["gn_b"])[:L, None, :], (L, 128, 512))], axis=2)
    BC = np.ascontiguousarray(BC.transpose(1, 0, 2, 3)).reshape(128, L * 1024)
    LAM = f(inp["lambda_qk"])[:L].reshape(1, L * 256)
    return dict(xT=np.ascontiguousarray(xT), PP=PPa.reshape(128, L * NPL), CC=np.ascontiguousarray(CC), BC=BC, LAM=np.ascontiguousarray(LAM),
                w_mod=f(inp["w_mod"])[:L], w_in=f(inp["w_in"])[:L], dup=f(inp["decay_up"])[:L].reshape(L, 128, 512), iup=f(inp["iclr_up"])[:L].reshape(L, 128, 512),
                w_br=f(inp["w_branch"])[:L].reshape(L, 1536, D), w_out=f(inp["w_out"])[:L])


def run(inp, SEQ, DEPTH, dbg=None):
    nc = build(SEQ, DEPTH, dbg)
    consts = host_consts(SEQ, DEPTH)
    nb = np.asarray(inp["x"]).shape[0]
    in_maps = []
    for b in range(nb):
        m = host_inputs(inp, SEQ, DEPTH, b)
        m.update(consts)
        in_maps.append(m)
    res = run_bass_kernel_spmd(nc, in_maps, core_ids=list(range(nb)))
    out = np.stack([np.asarray(r["yout"]).T for r in res.results], axis=0)
    return out.astype(np.float32), res


def kernel(**inputs):
    out, _ = run(inputs, 4096, 4)
    return out
```

```python
import math
import threading
import numpy as np
from contextlib import ExitStack
import concourse.bass as bass
import concourse.mybir as mybir
from concourse.bass_utils import run_bass_kernel_spmd

F32 = mybir.dt.float32
BF16 = mybir.dt.bfloat16
AF = mybir.ActivationFunctionType
ALU = mybir.AluOpType
AX = mybir.AxisListType

D = 1024
CTX = 256
NIN = 9472
NPL = 79
C0 = math.exp(-0.5)
GN_EPS = 64e-5
RMS_EPS = 1e-6


class Buf:
    __slots__ = ("name", "w", "r")

    def __init__(self, name):
        self.name = name
        self.w = None
        self.r = []


class Coop:
    def __init__(self):
        self.cur = None

    def run(self, specs):
        n = len(specs)
        self.evs = [threading.Event() for _ in range(n)]
        self.alive = [True] * n
        self.quota = [q for _, q in specs]
        self.used = [0] * n
        self.done_ev = threading.Event()
        self.err = None

        def wrap(i, fn):
            self.evs[i].wait()
            self.evs[i].clear()
            self.cur = i
            try:
                fn()
            except BaseException as e:
                self.err = e
            self.alive[i] = False
            self._pass(i)
        ths = [threading.Thread(target=wrap, args=(i, f)) for i, (f, _) in enumerate(specs)]
        for t in ths:
            t.start()
        self.evs[0].set()
        self.done_ev.wait()
        for t in ths:
            t.join()
        self.cur = None
        if self.err is not None:
            raise self.err

    def _pass(self, i):
        n = len(self.alive)
        for k in range(1, n + 1):
            j = (i + k) % n
            if self.alive[j]:
                if j == i:
                    return True
                self.evs[j].set()
                return False
        self.done_ev.set()
        return False

    def switch(self):
        i = self.cur
        if i is None:
            return
        self.used[i] += 1
        if self.used[i] < self.quota[i]:
            return
        self.used[i] = 0
        if not self._pass(i):
            self.evs[i].wait()
            self.evs[i].clear()
            self.cur = i


class KB:
    def __init__(self, nc, stack):
        self.nc = nc
        self.stack = stack
        self.eng = {"pe": nc.tensor, "dve": nc.vector, "act": nc.scalar, "pool": nc.gpsimd, "sp": nc.sync}
        self.sems = {}
        self.cnt = {}
        self.seen = {e: {} for e in self.eng}
        for e in self.eng:
            self.sems[e] = stack.enter_context(nc.semaphore("s_" + e))
            self.cnt[e] = 0
        self.n_inst = 0
        self.dma_names = {}
        self.dma_pool = []
        self.coop = Coop()

    def dma_sem(self, name):
        if name not in self.dma_names:
            i = len(self.dma_names)
            if i >= len(self.dma_pool):
                key = "d_%d" % i
                self.sems[key] = self.stack.enter_context(self.nc.semaphore(key))
                self.cnt[key] = 0
                self.dma_pool.append(key)
            self.dma_names[name] = self.dma_pool[i]
        return self.dma_names[name]

    def phase_reset(self):
        self.dma_names = {}

    def _wait(self, e, ev):
        if ev is None:
            return
        key, val = ev
        if key == "pe" and e == "pe":
            return
        if self.seen[e].get(key, 0) >= val:
            return
        self.eng[e].wait_ge(self.sems[key], val)
        self.seen[e][key] = val
        self.n_inst += 1

    def deps(self, e, reads, writes):
        for b in reads:
            self._wait(e, b.w)
        for b in writes:
            self._wait(e, b.w)
            for ev in b.r:
                self._wait(e, ev)

    def done(self, ev, reads, writes):
        for b in reads:
            b.r.append(ev)
            if len(b.r) > 16:
                d = {}
                for k, v in b.r:
                    d[k] = max(d.get(k, 0), v)
                b.r = list(d.items())
        for b in writes:
            b.w = ev
            b.r = []

    def op(self, e, fn, reads=(), writes=()):
        self.deps(e, reads, writes)
        ins = fn()
        self.cnt[e] += 1
        ins.then_inc(self.sems[e], 1)
        ev = (e, self.cnt[e])
        self.done(ev, reads, writes)
        self.n_inst += 1
        self.coop.switch()
        return ev

    def pe(self, fns, reads=(), writes=()):
        self.deps("pe", reads, writes)
        ins = None
        for fn in fns:
            ins = fn()
            self.n_inst += 1
        self.cnt["pe"] += 1
        ins.then_inc(self.sems["pe"], 1)
        ev = ("pe", self.cnt["pe"])
        self.done(ev, reads, writes)
        self.coop.switch()
        return ev

    def dma(self, q, semname, out, in_, reads=(), writes=()):
        key = self.dma_sem(semname)
        self.deps(q, reads, writes)
        ins = self.eng[q].dma_start(out=out, in_=in_)
        self.cnt[key] += 16
        ins.then_inc(self.sems[key], 16)
        ev = (key, self.cnt[key])
        self.done(ev, reads, writes)
        self.n_inst += 1
        self.coop.switch()
        return ev

    def dma_group(self, q, semname, pairs, reads=(), writes=()):
        key = self.dma_sem(semname)
        self.deps(q, reads, writes)
        for out, in_ in pairs:
            ins = self.eng[q].dma_start(out=out, in_=in_)
            self.cnt[key] += 16
            ins.then_inc(self.sems[key], 16)
            self.n_inst += 1
        ev = (key, self.cnt[key])
        self.done(ev, reads, writes)
        self.coop.switch()
        return ev

    def barrier(self):
        for e in self.eng:
            for k in self.cnt:
                if self.cnt[k] > 0:
                    self._wait(e, (k, self.cnt[k]))
        self.phase_reset()


class T:
    uid = [0]

    def __init__(self, st, nc, name, shape, dt, nbuf=1):
        T.uid[0] += 1
        self.t = st.enter_context(nc.sbuf_tensor("%s_u%d" % (name, T.uid[0]), shape, dt))
        self.b = Buf(name)

    def __getitem__(self, k):
        return self.t[k]


SEGS = [(0, 512, "F"), (512, 512, "F"), (1024, 512, "T"), (1536, 256, "X"), (1792, 512, "T"),
        (2304, 512, "F"), (2816, 512, "F"), (3328, 512, "F"), (3840, 512, "F"),
        (4352, 512, "F"), (4864, 512, "F"), (5376, 512, "T"), (5888, 512, "F")] + \
       [(6400 + 512 * i, 512, "F") for i in range(6)]
TCOL = {1024: 0, 1792: 512, 5376: 1024}


def build(SEQ, DEPTH, dbg=None):
    TT = CTX + SEQ
    NCH = TT // 64
    blocks = [(0, CTX, 1)] + [(CTX + 512 * i, 512, 0) for i in range(SEQ // 512)]
    nc = bass.Bass("TRN2", target_bir_lowering=False)
    dram = lambda n, s, d, k="Internal": nc.dram_tensor(n, s, d, kind=k).ap()
    xT_in = dram("xT", [D, TT], F32, "ExternalInput")
    PP = dram("PP", [128, DEPTH * NPL], F32, "ExternalInput")
    CCd = dram("CC", [128, 16], F32, "ExternalInput")
    BCd = dram("BC", [128, DEPTH * 2 * 512], F32, "ExternalInput")
    LAMd = dram("LAM", [1, DEPTH * 256], F32, "ExternalInput")
    KON = dram("KON", [128, 1024], F32, "ExternalInput")
    MSK = dram("MSK", [128, 4 * 512], F32, "ExternalInput")
    ROPE = dram("ROPE", [128, 2 * TT], F32, "ExternalInput")
    LTRI = dram("LTRI", [128, 512], F32, "ExternalInput")
    w_mod = dram("w_mod", [DEPTH, D, 3 * D], F32, "ExternalInput")
    w_in = dram("w_in", [DEPTH, D, NIN], F32, "ExternalInput")
    dup = dram("dup", [DEPTH, 128, 512], F32, "ExternalInput")
    iup = dram("iup", [DEPTH, 128, 512], F32, "ExternalInput")
    w_br = dram("w_br", [DEPTH, 1536, D], F32, "ExternalInput")
    w_out = dram("w_out", [DEPTH, D, D], F32, "ExternalInput")
    yout = dram("yout", [D, SEQ], F32, "ExternalOutput")
    xT = dram("xTs", [D, TT], F32)
    PF = dram("PF", [NIN, TT], BF16)
    PX = dram("PX", [256, TT], F32)
    PT = dram("PT", [TT, 1536], BF16)
    QGs = [dram("QGs%d" % d, [512, NCH, 2, 64], BF16) for d in range(2)]
    AKT = [dram("AKT%d" % d, [512, TT], BF16) for d in range(2)]
    KIT = [dram("KIT%d" % d, [512, TT], BF16) for d in range(2)]
    AKK = [dram("AKK%d" % d, [TT, 512], BF16) for d in range(2)]
    KIK = [dram("KIK%d" % d, [TT, 512], BF16) for d in range(2)]
    GCs = [dram("GCs%d" % d, [512, NCH], F32) for d in range(2)]
    YS = [dram("YS%d" % d, [TT, 512], F32) for d in range(2)]
    YG = dram("YG", [1536, TT], BF16)
    QRs = dram("QRs", [512, TT], BF16)
    KRs = dram("KRs", [512, TT], BF16)
    dbgo = {}
    if dbg:
        for n, shp in dbg.items():
            dbgo[n] = dram("dbg_" + n, shp, F32, "ExternalOutput")

    with ExitStack() as st:
        kb = KB(nc, st)
        V, A, P, PEe = nc.vector, nc.scalar, nc.gpsimd, nc.tensor
        psum = st.enter_context(nc.psum_tensor("psum", [128, 8, 512], F32))
        PB = [Buf("ps%d" % i) for i in range(8)]
        psbf = [psum[:, i, :].bitcast(BF16) for i in range(8)]

        kon = T(st, nc, "kon", [128, 1024], F32)
        konb = T(st, nc, "konb", [128, 1024], BF16)
        msk2 = T(st, nc, "msk2", [128, 4, 8, 64], F32)
        pp = T(st, nc, "pp", [128, DEPTH, NPL], F32)
        cc = T(st, nc, "cc", [128, 8, 2], F32)
        ltri = T(st, nc, "ltri", [128, 512], F32)
        modT = T(st, nc, "modT", [128, DEPTH, 24, 2], F32)
        gsc = T(st, nc, "gsc", [128, DEPTH, 8, 2], F32)
        omka = T(st, nc, "omka", [128, DEPTH, 4], F32)
        lamc = T(st, nc, "lamc", [128, DEPTH, 2], F32)
        sgl = T(st, nc, "sgl", [128, DEPTH], F32)
        kb.dma("sp", "c0", kon[:], KON[:, :], writes=[kon.b])
        kb.dma("sp", "c1", msk2[:].rearrange("p a h x -> p (a h x)"), MSK[:, :], writes=[msk2.b])
        kb.dma("sp", "c2", pp[:].rearrange("p l n -> p (l n)"), PP[:, :], writes=[pp.b])
        kb.dma("sp", "c3", cc[:].rearrange("p a b -> p (a b)"), CCd[:, :], writes=[cc.b])
        kb.dma("sp", "c4", ltri[:], LTRI[:, :], writes=[ltri.b])
        kb.op("dve", lambda: V.tensor_copy(out=konb[:], in_=kon[:]), [kon.b], [konb.b])
        epsc = T(st, nc, "epsc", [128, 2], F32)
        kb.op("pool", lambda: P.memset(epsc[:, 0:1], RMS_EPS), [], [epsc.b])
        kb.op("pool", lambda: P.memset(epsc[:, 1:2], 1e-12), [], [epsc.b])
        ident_f, ones_f = kon[:, 0:128], kon[:, 128:256]
        ident_b, ones_b, blk_b, perm_f = konb[:, 0:128], konb[:, 128:256], konb[:, 256:384], kon[:, 384:512]
        sel_b = konb[:, 512:544]

        with ExitStack() as s2:
            sc = T(s2, nc, "sc", [128, 8, 2], F32)
            wm = [T(s2, nc, "wm%d" % i, [128, 8, 512], F32) for i in range(2)]
            lamt = T(s2, nc, "lamt", [1, DEPTH, 4, 64], F32)
            lam2 = T(s2, nc, "lam2", [1, DEPTH, 2], F32)
            lam3 = T(s2, nc, "lam3", [1, DEPTH, 2], F32)
            kb.op("act", lambda: A.activation(out=sc[:], in_=cc[:], func=AF.Silu), [cc.b], [sc.b])
            i = 0
            for l in range(DEPTH):
                for nb6 in range(6):
                    w = wm[i % 2]
                    i += 1
                    kb.dma("sp", w.b.name, w[:], w_mod[l].rearrange("(c p) n -> p c n", p=128)[:, :, nb6 * 512:(nb6 + 1) * 512], writes=[w.b])
                    fns = []
                    for n4 in range(4):
                        for c in range(8):
                            fns.append(lambda n4=n4, c=c, w=w: PEe.matmul(psum[:, 0, n4 * 2:n4 * 2 + 2], lhsT=w[:, c, n4 * 128:(n4 + 1) * 128], rhs=sc[:, c, :], start=(c == 0), stop=(c == 7)))
                    kb.pe(fns, [w.b, sc.b], [PB[0]])
                    kb.op("dve", lambda l=l, nb6=nb6: V.tensor_tensor(out=modT[:, l, nb6 * 4:(nb6 + 1) * 4, :], in0=psum[:, 0, 0:8].rearrange("p (a b) -> p a b", b=2),
                                                                   in1=pp[:, l, 8 + nb6 * 4:8 + (nb6 + 1) * 4].unsqueeze(2).to_broadcast([128, 4, 2]), op=ALU.add), [PB[0], pp.b], [modT.b])
                kb.op("dve", lambda l=l: V.tensor_scalar(out=gsc[:, l], in0=modT[:, l, 8:16, :], scalar1=1.0, scalar2=None, op0=ALU.add), [modT.b], [gsc.b])
                kb.op("dve", lambda l=l: V.tensor_tensor(out=gsc[:, l], in0=gsc[:, l], in1=pp[:, l, 0:8].unsqueeze(2).to_broadcast([128, 8, 2]), op=ALU.mult), [gsc.b, pp.b], [gsc.b])
                kb.op("dve", lambda l=l: V.tensor_scalar(out=omka[:, l, :], in0=pp[:, l, 52:56], scalar1=-1.0, scalar2=1.0, op0=ALU.mult, op1=ALU.add), [pp.b], [omka.b])
            kb.dma("sp", "c5", lamt[:].rearrange("p l a x -> p (l a x)"), LAMd[:, :], writes=[lamt.b])
            lv = lamt[:].rearrange("p l (a b) x -> p l a b x", b=2)
            kb.op("dve", lambda: V.tensor_tensor(out=lv[:, :, :, 0, :], in0=lv[:, :, :, 0, :], in1=lv[:, :, :, 1, :], op=ALU.mult), [lamt.b], [lamt.b])
            kb.op("dve", lambda: V.tensor_reduce(out=lam2[:], in_=lv[:, :, :, 0, :], axis=AX.X, op=ALU.add), [lamt.b], [lam2.b])
            kb.op("act", lambda: A.activation(out=lam3[:], in_=lam2[:], func=AF.Exp), [lam2.b], [lam3.b])
            kb.op("dve", lambda: V.tensor_tensor(out=lam2[:, :, 0], in0=lam3[:, :, 1], in1=lam3[:, :, 0], op=ALU.subtract), [lam3.b], [lam2.b])
            for l in range(DEPTH):
                li = 0.8 - 0.6 * math.exp(-0.3 * l)
                kb.op("dve", lambda l=l, li=li: V.tensor_scalar(out=lam2[:, l, 0:1], in0=lam2[:, l, 0:1], scalar1=-li, scalar2=None, op0=ALU.add), [lam2.b], [lam2.b])
                kb.op("dve", lambda l=l, li=li: V.tensor_scalar(out=sgl[:, l:l + 1], in0=pp[:, l, 78:79], scalar1=(1.0 - li), scalar2=None, op0=ALU.mult), [pp.b], [sgl.b])
            kb.pe([lambda: PEe.matmul(psum[:, 0, 0:2 * DEPTH], lhsT=ones_f[0:1, :], rhs=lam2[:].rearrange("p l a -> p (l a)"), start=True, stop=True)], [lam2.b, kon.b], [PB[0]])
            kb.op("dve", lambda: V.tensor_copy(out=lamc[:].rearrange("p l a -> p (l a)"), in_=psum[:, 0, 0:2 * DEPTH]), [PB[0]], [lamc.b])
            kb.barrier()

        with ExitStack() as s2:
            xb_ = [T(s2, nc, "xcp%d" % i, [128, 8, 512], F32) for i in range(2)]
            for bi, (t0, nb, col) in enumerate(blocks):
                x_ = xb_[bi % 2]
                kb.dma("sp", x_.b.name, x_[:, :, 0:nb], xT_in.rearrange("(c p) t -> p c t", p=128)[:, :, t0:t0 + nb], writes=[x_.b])
                kb.dma("sp", x_.b.name, xT.rearrange("(c p) t -> p c t", p=128)[:, :, t0:t0 + nb], x_[:, :, 0:nb], reads=[x_.b])
            kb.barrier()

        xTv = xT.rearrange("(c p) t -> p c t", p=128)
        NTT = TT // 128

        def blk_of(tt):
            t = tt * 128
            for bi, (t0, nb, col) in enumerate(blocks):
                if t0 <= t < t0 + nb:
                    return bi

        evac_rr = [0]

        def evac(out, in_, rb, wb, scale=None):
            evac_rr[0] += 1
            if evac_rr[0] % 2 == 0:
                if scale is None:
                    return kb.op("act", lambda: A.copy(out=out, in_=in_), rb, wb)
                return kb.op("act", lambda: A.mul(out=out, in_=in_, mul=scale), rb, wb)
            if scale is None:
                return kb.op("dve", lambda: V.tensor_copy(out=out, in_=in_), rb, wb)
            return kb.op("dve", lambda: V.tensor_scalar(out=out, in0=in_, scalar1=scale, scalar2=None, op0=ALU.mult), rb, wb)

        def run_threads(specs):
            th = [[g, 0, float(tot)] for g, tot in specs]
            while th:
                t_ = min(th, key=lambda x: x[1] / x[2])
                try:
                    next(t_[0])
                    t_[1] += 1
                except StopIteration:
                    th.remove(t_)

        def phase_AB(l):
            with ExitStack() as s2:
                hT = T(s2, nc, "hT", [128, 8, TT], BF16)
                hTb = [Buf("hT%d" % i) for i in range(len(blocks))]
                xa_ = [T(s2, nc, "xa%d" % i, [128, 8, 512], F32) for i in range(2)]
                sq_ = [T(s2, nc, "sq%d" % i, [128, 8, 512], BF16) for i in range(2)]
                rs_ = [T(s2, nc, "rs%d" % i, [128, 512], F32) for i in range(2)]
                tmp = [T(s2, nc, "tmpa%d" % i, [128, 512], F32) for i in range(2)]
                wf = [T(s2, nc, "wf%d" % i, [128, 8, 512], F32) for i in range(2)]
                wb = [T(s2, nc, "wb%d" % i, [128, 8, 512], BF16) for i in range(2)]
                ost = [T(s2, nc, "ost%d" % i, [128, 512], BF16) for i in range(4)]
                osx = [T(s2, nc, "osx%d" % i, [128, 512], F32) for i in range(2)]
                w_inv = w_in[l].rearrange("(c p) n -> p c n", p=128)

                def loadw(si):
                    c0, wd, kind = SEGS[si]
                    f, b = wf[si % 2], wb[si % 2]
                    kb.dma("sp", f.b.name, f[:, :, 0:wd], w_inv[:, :, c0:c0 + wd], writes=[f.b])
                    kb.op("pool", lambda: P.tensor_copy(out=b[:, :, 0:wd], in_=f[:, :, 0:wd]), [f.b], [b.b])

                loadw(0)

                def loadx(bi):
                    t0, nb, col = blocks[bi]
                    kb.dma("sp", xa_[bi % 2].b.name, xa_[bi % 2][:, :, 0:nb], xTv[:, :, t0:t0 + nb], writes=[xa_[bi % 2].b])
                loadx(0)
                for bi, (t0, nb, col) in enumerate(blocks):
                    if bi + 1 < len(blocks):
                        loadx(bi + 1)
                    xa, sq, rs = xa_[bi % 2], sq_[bi % 2], rs_[bi % 2]
                    kb.op("act", lambda: A.activation(out=sq[:, :, 0:nb], in_=xa[:, :, 0:nb], func=AF.Square), [xa.b], [sq.b])
                    kb.pe([lambda c=c: PEe.matmul(psum[:, bi % 2, 0:nb], lhsT=ones_b, rhs=sq[:, c, 0:nb], start=(c == 0), stop=(c == 7)) for c in range(8)],
                          [sq.b, konb.b], [PB[bi % 2]])
                    kb.op("act", lambda: A.activation(out=rs[:, 0:nb], in_=psum[:, bi % 2, 0:nb], func=AF.Ln, bias=epsc[:, 0:1], scale=1.0 / D), [PB[bi % 2], epsc.b], [rs.b])
                    kb.op("act", lambda: A.activation(out=rs[:, 0:nb], in_=rs[:, 0:nb], func=AF.Exp, scale=-0.5), [rs.b], [rs.b])
                    for c in range(8):
                        tm = tmp[c % 2]
                        kb.op("dve", lambda: V.scalar_tensor_tensor(out=tm[:, 0:nb], in0=xa[:, c, 0:nb], scalar=gsc[:, l, c, col:col + 1], in1=rs[:, 0:nb],
                                                                    op0=ALU.mult, op1=ALU.mult), [xa.b, rs.b, gsc.b], [tm.b])
                        kb.op("act", lambda: A.activation(out=hT[:, c, t0:t0 + nb], in_=tm[:, 0:nb], func=AF.Identity, bias=modT[:, l, c, col:col + 1], scale=1.0),
                              [tm.b, modT.b], [hTb[bi]])
                k = 0
                pk = 0
                for si, (c0, wd, kind) in enumerate(SEGS):
                    if si + 1 < len(SEGS):
                        loadw(si + 1)
                    b = wb[si % 2]
                    if kind in "FX":
                        for bi, (t0, nb, col) in enumerate(blocks):
                            for n4 in range(wd // 128):
                                bank = 2 + pk % 6
                                pk += 1
                                kb.pe([lambda c=c: PEe.matmul(psum[:, bank, 0:nb], lhsT=b[:, c, n4 * 128:(n4 + 1) * 128], rhs=hT[:, c, t0:t0 + nb], start=(c == 0), stop=(c == 7))
                                       for c in range(8)], [b.b, hTb[bi]], [PB[bank]])
                                if kind == "F":
                                    o = ost[k % 4]
                                    k += 1
                                    evac(o[:, 0:nb], psum[:, bank, 0:nb], [PB[bank]], [o.b])
                                    kb.dma("sp", o.b.name, PF[c0 + n4 * 128:c0 + (n4 + 1) * 128, t0:t0 + nb], o[:, 0:nb], reads=[o.b])
                                else:
                                    o = osx[k % 2]
                                    k += 1
                                    evac(o[:, 0:nb], psum[:, bank, 0:nb], [PB[bank]], [o.b])
                                    kb.dma("sp", o.b.name, PX[n4 * 128:(n4 + 1) * 128, t0:t0 + nb], o[:, 0:nb], reads=[o.b])
                    else:
                        for tt in range(NTT):
                            bank = 2 + pk % 6
                            pk += 1
                            kb.pe([lambda c=c: PEe.matmul(psum[:, bank, 0:512], lhsT=hT[:, c, tt * 128:(tt + 1) * 128], rhs=b[:, c, 0:512], start=(c == 0), stop=(c == 7))
                                   for c in range(8)], [b.b, hTb[blk_of(tt)]], [PB[bank]])
                            o = ost[k % 4]
                            k += 1
                            evac(o[:, :], psum[:, bank, :], [PB[bank]], [o.b])
                            kb.dma("sp", o.b.name, PT[tt * 128:(tt + 1) * 128, TCOL[c0]:TCOL[c0] + 512], o[:, :], reads=[o.b])
                kb.barrier()

        def phase_C1(l, s2):
            if True:
                rT = T(s2, nc, "rT", [128, 4, 512], BF16)
                kT = T(s2, nc, "kT", [128, 4, 512], BF16)
                lw = T(s2, nc, "lw", [128, 512], F32)
                la = T(s2, nc, "la", [128, 512], F32)
                tl = T(s2, nc, "tl", [128, 512], BF16)
                lab = T(s2, nc, "lab", [128, 512], BF16)
                upf = T(s2, nc, "upf", [128, 2, 512], F32)
                upb = T(s2, nc, "upb", [128, 2, 512], BF16)
                names = ["kkraw", "rk", "kk", "sg", "a_", "akk", "t_", "kd", "csf", "cs", "cse", "Ep", "Em", "Epv", "tmpb"]
                f = {n: T(s2, nc, "c1_" + n, [128, 512], F32) for n in names}
                sqk = T(s2, nc, "sqk", [128, 512], BF16)
                ob = {n: [T(s2, nc, "c1o_%s%d" % (n, i), [128, 512], BF16) for i in range(2)] for n in ["kkg", "rg", "aki", "ki"]}
                gcs = [T(s2, nc, "gcs%d" % i, [128, 8], F32) for i in range(2)]
                tk = [T(s2, nc, "tk%d" % i, [128, 2, 4, 128], BF16) for i in range(2)]
                kb.dma("sp", "upf", upf[:, 0, :], dup[l], writes=[upf.b])
                kb.dma("sp", "upf", upf[:, 1, :], iup[l], writes=[upf.b])
                kb.op("dve", lambda: V.tensor_copy(out=upb[:], in_=upf[:]), [upf.b], [upb.b])
                it = 0
                for bi, (t0, nb, col) in enumerate(blocks):
                    nchk = nb // 64
                    ch0 = t0 // 64
                    ntt = nb // 128
                    kb.dma("sp", "rT", rT[:, :, 0:nb], PF[0:512, t0:t0 + nb].rearrange("(c p) t -> p c t", p=128), writes=[rT.b])
                    kb.dma("sp", "kT", kT[:, :, 0:nb], PF[512:1024, t0:t0 + nb].rearrange("(c p) t -> p c t", p=128), writes=[kT.b])
                    kb.dma("sp", "lw", lw[:, 0:nb], PX[0:128, t0:t0 + nb], writes=[lw.b])
                    kb.dma("sp", "la", la[:, 0:nb], PX[128:256, t0:t0 + nb], writes=[la.b])
                    kb.op("act", lambda: A.activation(out=tl[:, 0:nb], in_=lw[:, 0:nb], func=AF.Tanh), [lw.b], [tl.b])
                    kb.op("dve", lambda: V.tensor_copy(out=lab[:, 0:nb], in_=la[:, 0:nb]), [la.b], [lab.b])
                    for cc_ in range(4):
                        S = slice(0, nb)
                        kb.op("dve", lambda: V.tensor_scalar(out=f["kkraw"][:, S], in0=kT[:, cc_, S], scalar1=pp[:, l, 48 + cc_:49 + cc_], scalar2=None, op0=ALU.mult),
                              [kT.b, pp.b], [f["kkraw"].b])
                        kb.op("act", lambda: A.activation(out=sqk[:, S], in_=f["kkraw"][:, S], func=AF.Square), [f["kkraw"].b], [sqk.b])
                        kb.pe([lambda: PEe.matmul(psum[:, 0, S], lhsT=blk_b, rhs=sqk[:, S], start=True, stop=True)], [sqk.b, konb.b], [PB[0]])
                        kb.op("act", lambda: A.activation(out=f["rk"][:, S], in_=psum[:, 0, S], func=AF.Ln, bias=epsc[:, 1:2], scale=1.0), [PB[0], epsc.b], [f["rk"].b])
                        kb.op("act", lambda: A.activation(out=f["rk"][:, S], in_=f["rk"][:, S], func=AF.Exp, scale=-0.5), [f["rk"].b], [f["rk"].b])
                        kb.op("dve", lambda: V.tensor_tensor(out=f["kk"][:, S], in0=f["kkraw"][:, S], in1=f["rk"][:, S], op=ALU.mult), [f["kkraw"].b, f["rk"].b], [f["kk"].b])
                        for d in range(2):
                            R = slice(d * 64, (d + 1) * 64)
                            C = slice(cc_ * 128, (cc_ + 1) * 128)
                            o = {n: ob[n][it % 2] for n in ob}
                            g_ = gcs[it % 2]
                            tk_ = tk[it % 2]
                            it += 1
                            kb.pe([lambda: PEe.matmul(psum[:, 1, S], lhsT=upb[R, 0, C], rhs=tl[R, S], start=True, stop=True)], [upb.b, tl.b], [PB[1]])
                            kb.op("act", lambda: A.activation(out=f["sg"][:, S], in_=psum[:, 1, S], func=AF.Sigmoid, bias=pp[:, l, 32 + d * 4 + cc_:33 + d * 4 + cc_], scale=1.0),
                                  [PB[1], pp.b], [f["sg"].b])
                            kb.pe([lambda: PEe.matmul(psum[:, 2, S], lhsT=upb[R, 1, C], rhs=lab[R, S], start=True, stop=True)], [upb.b, lab.b], [PB[2]])
                            kb.op("act", lambda: A.activation(out=f["a_"][:, S], in_=psum[:, 2, S], func=AF.Sigmoid, bias=pp[:, l, 40 + d * 4 + cc_:41 + d * 4 + cc_], scale=1.0),
                                  [PB[2], pp.b], [f["a_"].b])
                            kb.op("pool", lambda: P.tensor_tensor(out=f["akk"][:, S], in0=f["a_"][:, S], in1=f["kk"][:, S], op=ALU.mult), [f["a_"].b, f["kk"].b], [f["akk"].b])
                            kb.op("dve", lambda: V.tensor_scalar(out=f["t_"][:, S], in0=f["a_"][:, S], scalar1=pp[:, l, 52 + cc_:53 + cc_], scalar2=omka[:, l, cc_:cc_ + 1],
                                                                 op0=ALU.mult, op1=ALU.add), [f["a_"].b, pp.b, omka.b], [f["t_"].b])
                            kb.op("pool", lambda: P.tensor_tensor(out=f["kd"][:, S], in0=f["t_"][:, S], in1=kT[:, cc_, S], op=ALU.mult), [f["t_"].b, kT.b], [f["kd"].b])
                            kb.op("dve", lambda: V.tensor_tensor_scan(out=f["csf"][:, S], data0=ltri[:, S], data1=f["sg"][:, S], initial=0.0, op0=ALU.mult, op1=ALU.add),
                                  [ltri.b, f["sg"].b], [f["csf"].b])
                            if d == 0:
                                cs = f["csf"]
                            else:
                                cs = f["cs"]
                                kb.op("dve", lambda: V.tensor_tensor(out=f["tmpb"][:, S], in0=f["sg"][:, S], in1=f["csf"][:, S], op=ALU.subtract), [f["sg"].b, f["csf"].b], [f["tmpb"].b])
                                kb.op("dve", lambda: V.tensor_tensor(out=cs[:, S].rearrange("p (c x) -> p c x", x=64), in0=f["tmpb"][:, S].rearrange("p (c x) -> p c x", x=64),
                                                                     in1=f["csf"][:, S].rearrange("p (c x) -> p c x", x=64)[:, :, 63:64].to_broadcast([128, nchk, 64]), op=ALU.add),
                                      [f["tmpb"].b, f["csf"].b], [cs.b])
                            kb.op("pool", lambda: P.tensor_tensor(out=f["cse"][:, S], in0=cs[:, S], in1=f["sg"][:, S], op=ALU.subtract), [cs.b, f["sg"].b], [f["cse"].b])
                            kb.op("act", lambda: A.activation(out=f["Ep"][:, S], in_=cs[:, S], func=AF.Exp, scale=-C0), [cs.b], [f["Ep"].b])
                            kb.op("act", lambda: A.activation(out=f["Em"][:, S], in_=cs[:, S], func=AF.Exp, scale=C0), [cs.b], [f["Em"].b])
                            kb.op("act", lambda: A.activation(out=f["Epv"][:, S], in_=f["cse"][:, S], func=AF.Exp, scale=-C0), [f["cse"].b], [f["Epv"].b])
                            kb.op("dve", lambda: V.tensor_tensor(out=o["rg"][:, S], in0=rT[:, cc_, S], in1=f["Ep"][:, S], op=ALU.mult), [rT.b, f["Ep"].b], [o["rg"].b])
                            kb.op("pool", lambda: P.tensor_tensor(out=o["kkg"][:, S], in0=f["kk"][:, S], in1=f["Epv"][:, S], op=ALU.mult), [f["kk"].b, f["Epv"].b], [o["kkg"].b])
                            kb.op("dve", lambda: V.tensor_tensor(out=o["aki"][:, S], in0=f["akk"][:, S], in1=f["Em"][:, S], op=ALU.mult), [f["akk"].b, f["Em"].b], [o["aki"].b])
                            kb.op("pool", lambda: P.tensor_tensor(out=o["ki"][:, S], in0=f["kd"][:, S], in1=f["Em"][:, S], op=ALU.mult), [f["kd"].b, f["Em"].b], [o["ki"].b])
                            e_ = 63 if d == 0 else 0
                            kb.op("dve", lambda: V.tensor_copy(out=g_[:, 0:nchk], in_=f["Ep"][:, S].rearrange("p (c x) -> p c x", x=64)[:, :, e_]), [f["Ep"].b], [g_.b])
                            kb.dma("sp", o["kkg"].b.name, QGs[d][C, ch0:ch0 + nchk, 0, :], o["kkg"][:, S].rearrange("p (c x) -> p c x", x=64), reads=[o["kkg"].b])
                            kb.dma("sp", o["rg"].b.name, QGs[d][C, ch0:ch0 + nchk, 1, :], o["rg"][:, S].rearrange("p (c x) -> p c x", x=64), reads=[o["rg"].b])
                            kb.dma("sp", o["aki"].b.name, AKT[d][C, t0:t0 + nb], o["aki"][:, S], reads=[o["aki"].b])
                            kb.dma("sp", o["ki"].b.name, KIT[d][C, t0:t0 + nb], o["ki"][:, S], reads=[o["ki"].b])
                            kb.dma("sp", g_.b.name, GCs[d][C, ch0:ch0 + nchk], g_[:, 0:nchk], reads=[g_.b])
                            fns = []
                            for qi, nm in enumerate(["aki", "ki"]):
                                for tt in range(ntt):
                                    fns.append(lambda qi=qi, nm=nm, tt=tt: PEe.transpose(psbf[3][:, (qi * 4 + tt) * 128:(qi * 4 + tt + 1) * 128], o[nm][:, tt * 128:(tt + 1) * 128], ident_b))
                            kb.pe(fns, [o["aki"].b, o["ki"].b, konb.b], [PB[3]])
                            kb.op("act", lambda: A.copy(out=tk_[:].rearrange("p q t c -> p (q t c)"), in_=psbf[3][:, :]), [PB[3]], [tk_.b])
                            kb.dma("sp", tk_.b.name + "a", AKK[d][t0:t0 + nb, C].rearrange("(t p) c -> p t c", p=128), tk_[:, 0, 0:ntt, :], reads=[tk_.b])
                            kb.dma("sp", tk_.b.name + "b", KIK[d][t0:t0 + nb, C].rearrange("(t p) c -> p t c", p=128), tk_[:, 1, 0:ntt, :], reads=[tk_.b])
                            yield

        def phase_C2(l):
            with ExitStack() as s2:
                def mk(nm, shape, dt):
                    ts = [T(s2, nc, "%s%d" % (nm, i), shape, dt) for i in range(2)]
                    for t_ in ts:
                        t_.b2 = Buf(t_.b.name + "x")
                    return ts
                QG = mk("QG", [128, 8, 8, 128], BF16)
                AT = mk("AT", [128, 8, 512], BF16)
                KT_ = mk("KT", [128, 8, 512], BF16)
                AK = mk("AK", [128, 8, 512], BF16)
                KK_ = mk("KK", [128, 8, 512], BF16)
                VV = mk("VV", [128, 8, 512], BF16)
                GC = mk("GC", [128, 8, 8], F32)
                Z = T(s2, nc, "Z", [128, 8, 64], F32)
                Zb = T(s2, nc, "Zb", [128, 8, 64], BF16)
                NT = [T(s2, nc, "NT%d" % i, [128, 8, 64], BF16) for i in range(5)]
                N_ = [T(s2, nc, "N%d" % i, [128, 8, 64], BF16) for i in range(6)]
                PTt = [T(s2, nc, "PTt%d" % i, [128, 8, 64], BF16) for i in range(2)]
                BabT = T(s2, nc, "BabT", [128, 8, 64], BF16)
                AakT = T(s2, nc, "AakT", [128, 8, 64], BF16)
                BakT = T(s2, nc, "BakT", [128, 8, 64], BF16)
                Wn = T(s2, nc, "Wn", [128, 8, 64], BF16)
                U = T(s2, nc, "U", [128, 8, 64], BF16)
                Yst = [T(s2, nc, "Yst%d" % i, [128, 512], F32) for i in range(2)]
                yi = [0]
                DS = [slice(0, 64), slice(64, 128)]
                nB = len(blocks)
                fo = list(range(nB))
                bo = [0] + list(range(nB - 1, 0, -1))

                def hv(bank):
                    return psum[:, bank, :].rearrange("p (h i) -> p h i", h=8)

                def load(s, par):
                    for d in range(2):
                        t0, nb, col = blocks[fo[s] if d == 0 else bo[s]]
                        nchk, ch0 = nb // 64, t0 // 64
                        if d == 0:
                            csl = slice(ch0, ch0 + nchk)
                        else:
                            csl = slice(ch0 + nchk - 1, (ch0 - 1) if ch0 > 0 else None, -1)
                        D_ = DS[d]
                        bsel = (lambda t_: t_.b) if d == 0 else (lambda t_: t_.b2)
                        sfx = "f" if d == 0 else "b"
                        def ld(t_, dst, src):
                            kb.dma("sp", t_.b.name + sfx, dst, src, writes=[bsel(t_)])

                        def ldh(t_, dstf, srcf):
                            if d == 0:
                                ld(t_, dstf(slice(None)), srcf(slice(None)))
                            else:
                                kb.dma_group("sp", t_.b.name + sfx, [(dstf(h), srcf(h)) for h in range(8)], writes=[bsel(t_)])
                        qsrc = QGs[d].rearrange("(h k) c w i -> k h c (w i)", k=64)
                        ldh(QG[par], lambda h: QG[par][D_, h, 0:nchk, :], lambda h: qsrc[:, h, csl, :])
                        asrc = AKT[d].rearrange("(h k) (c x) -> k h c x", k=64, x=64)
                        ldh(AT[par], lambda h: AT[par][D_, h, 0:nb].rearrange("p (c x) -> p c x", x=64) if not isinstance(h, slice) else AT[par][D_, :, 0:nb].rearrange("p h (c x) -> p h c x", x=64),
                            lambda h: asrc[:, h, csl, :])
                        ksrc = KIT[d].rearrange("(h k) (c x) -> k h c x", k=64, x=64)
                        ldh(KT_[par], lambda h: KT_[par][D_, h, 0:nb].rearrange("p (c x) -> p c x", x=64) if not isinstance(h, slice) else KT_[par][D_, :, 0:nb].rearrange("p h (c x) -> p h c x", x=64),
                            lambda h: ksrc[:, h, csl, :])
                        ld(AK[par], AK[par][D_, 0:nchk, :], AKK[d].rearrange("(c p) n -> p c n", p=64)[:, csl, :])
                        ld(KK_[par], KK_[par][D_, 0:nchk, :], KIK[d].rearrange("(c p) n -> p c n", p=64)[:, csl, :])
                        ld(VV[par], VV[par][D_, 0:nchk, :], PT[:, 0:512].rearrange("(c p) n -> p c n", p=64)[:, csl, :])
                        ld(GC[par], GC[par][D_, :, 0:nchk], GCs[d].rearrange("(h k) c -> k h c", k=64)[:, :, ch0:ch0 + nchk])

                def chunk(par, j, nchk, tokf, tokb):
                    qg, at, kt, ak, kk_, vv, gc = QG[par], AT[par], KT_[par], AK[par], KK_[par], VV[par], GC[par]
                    bb = lambda *ts: [x for t_ in ts for x in (t_.b, t_.b2)]
                    cs = slice(j * 64, j * 64 + 64)
                    H = lambda h: slice(h * 64, (h + 1) * 64)
                    HD = [(h, D_) for h in range(8) for D_ in DS]
                    fns = []
                    for h, D_ in HD:
                        fns.append(lambda h=h, D_=D_: PEe.matmul(psum[D_, h // 4, (h % 4) * 128:(h % 4) * 128 + 128], lhsT=at[D_, h, cs], rhs=qg[D_, h, j, :], start=True, stop=True))
                    for h, D_ in HD:
                        fns.append(lambda h=h, D_=D_: PEe.matmul(psum[D_, 2 + h // 4, (h % 4) * 128:(h % 4) * 128 + 128], lhsT=kt[D_, h, cs], rhs=qg[D_, h, j, :], start=True, stop=True))
                    for h, D_ in HD:
                        fns.append(lambda h=h, D_=D_: PEe.matmul(psum[D_, 4, H(h)], lhsT=qg[D_, h, j, 0:64], rhs=at[D_, h, cs], start=True, stop=True))
                    kb.pe(fns, bb(qg, at, kt), [PB[0], PB[1], PB[2], PB[3], PB[4]])
                    P1v = psum[:, 0:2, :].rearrange("p b (h w i) -> p (b h) w i", h=4, w=2)
                    P2v = psum[:, 2:4, :].rearrange("p b (h w i) -> p (b h) w i", h=4, w=2)
                    kb.op("dve", lambda: V.scalar_tensor_tensor(out=NT[0][:], in0=P1v[:, :, 0, :], scalar=-1.0, in1=msk2[:, 0], op0=ALU.mult, op1=ALU.mult), [PB[0], PB[1], msk2.b], [NT[0].b])
                    kb.op("dve", lambda: V.scalar_tensor_tensor(out=N_[0][:], in0=hv(4), scalar=-1.0, in1=msk2[:, 2], op0=ALU.mult, op1=ALU.mult), [PB[4], msk2.b], [N_[0].b])
                    kb.op("dve", lambda: V.tensor_tensor(out=AakT[:], in0=P2v[:, :, 0, :], in1=msk2[:, 0], op=ALU.mult), [PB[2], PB[3], msk2.b], [AakT.b])
                    kb.op("dve", lambda: V.tensor_tensor(out=BabT[:], in0=P1v[:, :, 1, :], in1=msk2[:, 1], op=ALU.mult), [PB[0], PB[1], msk2.b], [BabT.b])
                    kb.op("dve", lambda: V.tensor_tensor(out=BakT[:], in0=P2v[:, :, 1, :], in1=msk2[:, 1], op=ALU.mult), [PB[2], PB[3], msk2.b], [BakT.b])
                    kb.op("pool", lambda: P.tensor_tensor(out=PTt[0][:], in0=NT[0][:], in1=msk2[:, 3], op=ALU.add), [NT[0].b, msk2.b], [PTt[0].b])
                    fns = []
                    for h, D_ in HD:
                        fns.append(lambda h=h, D_=D_: PEe.matmul(psum[D_, 5, H(h)], lhsT=qg[D_, h, j, 0:64], rhs=Zb[D_, h, :], start=True, stop=False))
                        fns.append(lambda h=h, D_=D_: PEe.matmul(psum[D_, 5, H(h)], lhsT=AakT[D_, h, :], rhs=vv[D_, j, H(h)], start=False, stop=True))
                    kb.pe(fns, bb(qg, vv) + [Zb.b, AakT.b], [PB[5]])
                    kb.op("act", lambda: A.mul(out=Wn[:], in_=hv(5), mul=-1.0), [PB[5]], [Wn.b])
                    for k in range(5):
                        fns = [lambda h=h, D_=D_: PEe.matmul(psum[D_, 6, H(h)], lhsT=NT[k][D_, h, :], rhs=N_[k][D_, h, :], start=True, stop=True) for h, D_ in HD]
                        kb.pe(fns, [NT[k].b, N_[k].b], [PB[6]])
                        kb.op("act", lambda: A.copy(out=N_[k + 1][:], in_=hv(6)), [PB[6]], [N_[k + 1].b])
                        if k < 4:
                            fns = [lambda h=h, D_=D_: PEe.matmul(psum[D_, 7, H(h)], lhsT=N_[k][D_, h, :], rhs=NT[k][D_, h, :], start=True, stop=True) for h, D_ in HD]
                            kb.pe(fns, [NT[k].b, N_[k].b], [PB[7]])
                            kb.op("act", lambda: A.copy(out=NT[k + 1][:], in_=hv(7)), [PB[7]], [NT[k + 1].b])
                        src, dst = PTt[k % 2], PTt[(k + 1) % 2]
                        fns = [lambda h=h, D_=D_: PEe.matmul(psum[D_, 0, H(h)], lhsT=N_[k + 1][D_, h, :], rhs=src[D_, h, :], start=True, stop=True) for h, D_ in HD]
                        kb.pe(fns, [N_[k + 1].b, src.b], [PB[0]])
                        kb.op("dve", lambda: V.tensor_tensor(out=dst[:], in0=hv(0), in1=src[:], op=ALU.add), [PB[0], src.b], [dst.b])
                    TTf = PTt[1]
                    fns = [lambda h=h, D_=D_: PEe.matmul(psum[D_, 1, H(h)], lhsT=TTf[D_, h, :], rhs=Wn[D_, h, :], start=True, stop=True) for h, D_ in HD]
                    kb.pe(fns, [TTf.b, Wn.b], [PB[1]])
                    kb.op("dve", lambda: V.tensor_copy(out=U[:], in_=hv(1)), [PB[1]], [U.b])
                    fns = []
                    for h, D_ in HD:
                        fns.append(lambda h=h, D_=D_: PEe.matmul(psum[D_, 2, H(h)], lhsT=qg[D_, h, j, 64:128], rhs=Zb[D_, h, :], start=True, stop=False))
                        fns.append(lambda h=h, D_=D_: PEe.matmul(psum[D_, 2, H(h)], lhsT=BabT[D_, h, :], rhs=U[D_, h, :], start=False, stop=False))
                        fns.append(lambda h=h, D_=D_: PEe.matmul(psum[D_, 2, H(h)], lhsT=BakT[D_, h, :], rhs=vv[D_, j, H(h)], start=False, stop=True))
                    kb.pe(fns, bb(qg, vv) + [Zb.b, BabT.b, U.b, BakT.b], [PB[2]])
                    ys = Yst[yi[0] % 2]
                    yi[0] += 1
                    kb.op("act", lambda: A.copy(out=ys[:], in_=psum[:, 2, :]), [PB[2]], [ys.b])
                    kb.dma("sp", ys.b.name + "f", YS[0][tokf:tokf + 64, :], ys[0:64, :], reads=[ys.b])
                    kb.dma("sp", ys.b.name + "b", YS[1][tokb:tokb + 64, :], ys[64:128, :], reads=[ys.b])
                    fns = []
                    for h, D_ in HD:
                        fns.append(lambda h=h, D_=D_: PEe.matmul(psum[D_, 3, H(h)], lhsT=ak[D_, j, H(h)], rhs=U[D_, h, :], start=True, stop=False))
                        fns.append(lambda h=h, D_=D_: PEe.matmul(psum[D_, 3, H(h)], lhsT=kk_[D_, j, H(h)], rhs=vv[D_, j, H(h)], start=False, stop=True))
                    kb.pe(fns, bb(ak, kk_, vv) + [U.b], [PB[3]])
                    kb.op("dve", lambda: V.tensor_tensor(out=Z[:], in0=hv(3), in1=Z[:], op=ALU.add), [PB[3], Z.b], [Z.b])
                    jb = nchk - 1 - j
                    kb.op("dve", lambda: V.tensor_tensor(out=Z[0:64], in0=Z[0:64], in1=gc[0:64, :, j:j + 1].to_broadcast([64, 8, 64]), op=ALU.mult), [Z.b, gc.b], [Z.b])
                    kb.op("pool", lambda: P.tensor_tensor(out=Z[64:128], in0=Z[64:128], in1=gc[64:128, :, jb:jb + 1].to_broadcast([64, 8, 64]), op=ALU.mult), [Z.b, gc.b2], [Z.b])
                    kb.op("act", lambda: A.copy(out=Zb[:], in_=Z[:]), [Z.b], [Zb.b])

                kb.op("dve", lambda: V.memset(Z[:], 0.0), [], [Z.b])
                kb.op("dve", lambda: V.memset(Zb[:], 0.0), [], [Zb.b])
                load(0, 0)
                for s in range(nB):
                    par = s % 2
                    if s + 1 < nB:
                        load(s + 1, 1 - par)
                    t0f, nb, _ = blocks[fo[s]]
                    t0b, _, _ = blocks[bo[s]]
                    nchk = nb // 64
                    for j in range(nchk):
                        chunk(par, j, nchk, t0f + j * 64, t0b + (nchk - 1 - j) * 64)
                kb.barrier()

        def phase_C3(l, s2, part=0, nparts=1):
            if True:
                bk = 6 + part
                bc = T(s2, nc, "bcg", [128, 2, 512], F32)
                kb.dma("sp", "bcg%d" % part, bc[:].rearrange("p a n -> p (a n)"), BCd[:, l * 1024:(l + 1) * 1024], writes=[bc.b])
                def mk(nm, shape, dt):
                    return [T(s2, nc, "%s%d_%d" % (nm, i, part), shape, dt) for i in range(2)]
                yf, yb = mk("yf", [128, 512], F32), mk("yb", [128, 512], F32)
                vz = mk("vz", [128, 1024], BF16)
                rk_ = mk("rk_", [128, 2, 4, 128], BF16)
                rkr = mk("rkr", [128, 4, 128], BF16)
                y = mk("y_", [128, 512], F32)
                ysq = mk("ysq", [128, 512], F32)
                st_ = mk("st_", [128, 6, 8], F32)
                sz = mk("sz", [128, 512], F32)
                bon = mk("bon", [128, 512], F32)
                yg = mk("yg", [128, 512], BF16)
                ygT = mk("ygT", [128, 4, 128], BF16)
                for ti, tt in enumerate(range(part, NTT, nparts)):
                    p = ti % 2
                    ts_ = slice(tt * 128, (tt + 1) * 128)
                    kb.dma("sp", yf[p].b.name, yf[p][:], YS[0][ts_, :], writes=[yf[p].b])
                    kb.dma("sp", yb[p].b.name, yb[p][:], YS[1][ts_, :], writes=[yb[p].b])
                    kb.dma("sp", vz[p].b.name, vz[p][:], PT[ts_, 0:1024], writes=[vz[p].b])
                    kb.dma("sp", rk_[p].b.name, rk_[p][:].rearrange("p a c t -> p (a c) t"), PF[0:1024, ts_].rearrange("(c p) t -> p c t", p=128), writes=[rk_[p].b])
                    for c in range(4):
                        kb.op("dve", lambda: V.scalar_tensor_tensor(out=rkr[p][:, c, :], in0=rk_[p][:, 0, c, :], scalar=pp[:, l, 56 + c:57 + c], in1=rk_[p][:, 1, c, :],
                                                                     op0=ALU.mult, op1=ALU.mult), [rk_[p].b, pp.b], [rkr[p].b])
                    kb.pe([lambda c=c: PEe.matmul(psum[:, bk, 0:8], lhsT=rkr[p][:, c, :], rhs=sel_b[:, c * 8:(c + 1) * 8], start=(c == 0), stop=(c == 3)) for c in range(4)],
                          [rkr[p].b, konb.b], [PB[bk]])
                    S_ = st_[p]
                    kb.op("act", lambda: A.copy(out=S_[:, 5, :], in_=psum[:, bk, 0:8]), [PB[bk]], [S_.b])
                    kb.op("pool", lambda: P.tensor_tensor(out=y[p][:], in0=yf[p][:], in1=yb[p][:], op=ALU.add), [yf[p].b, yb[p].b], [y[p].b])
                    y3 = y[p][:].rearrange("p (h x) -> p h x", h=8)
                    kb.op("dve", lambda: V.tensor_reduce(out=S_[:, 0, :], in_=y3, axis=AX.X, op=ALU.add), [y[p].b], [S_.b])
                    kb.op("act", lambda: A.activation(out=ysq[p][:], in_=y[p][:], func=AF.Square), [y[p].b], [ysq[p].b])
                    kb.op("dve", lambda: V.tensor_reduce(out=S_[:, 1, :], in_=ysq[p][:].rearrange("p (h x) -> p h x", h=8), axis=AX.X, op=ALU.add), [ysq[p].b], [S_.b])
                    kb.op("dve", lambda: V.tensor_scalar(out=S_[:, 2, :], in0=S_[:, 0, :], scalar1=1.0 / 64, scalar2=None, op0=ALU.mult), [S_.b], [S_.b])
                    kb.op("dve", lambda: V.tensor_tensor(out=S_[:, 3, :], in0=S_[:, 2, :], in1=S_[:, 2, :], op=ALU.mult), [S_.b], [S_.b])
                    kb.op("dve", lambda: V.scalar_tensor_tensor(out=S_[:, 3, :], in0=S_[:, 1, :], scalar=1.0 / 64, in1=S_[:, 3, :], op0=ALU.mult, op1=ALU.subtract), [S_.b], [S_.b])
                    kb.op("act", lambda: A.activation(out=S_[:, 4, :], in_=S_[:, 3, :], func=AF.Sqrt, bias=GN_EPS, scale=1.0), [S_.b], [S_.b])
                    kb.op("dve", lambda: V.reciprocal(out=S_[:, 4, :], in_=S_[:, 4, :]), [S_.b], [S_.b])
                    kb.op("dve", lambda: V.tensor_tensor(out=y3, in0=y3, in1=S_[:, 2, :].unsqueeze(2).to_broadcast([128, 8, 64]), op=ALU.subtract), [y[p].b, S_.b], [y[p].b])
                    kb.op("dve", lambda: V.tensor_tensor(out=y3, in0=y3, in1=S_[:, 4, :].unsqueeze(2).to_broadcast([128, 8, 64]), op=ALU.mult), [y[p].b, S_.b], [y[p].b])
                    kb.op("pool", lambda: P.tensor_tensor(out=y[p][:], in0=y[p][:], in1=bc[:, 0, :], op=ALU.mult), [y[p].b, bc.b], [y[p].b])
                    kb.op("pool", lambda: P.tensor_tensor(out=y[p][:], in0=y[p][:], in1=bc[:, 1, :], op=ALU.add), [y[p].b, bc.b], [y[p].b])
                    kb.op("dve", lambda: V.tensor_tensor(out=bon[p][:].rearrange("p (h x) -> p h x", h=8), in0=vz[p][:, 0:512].rearrange("p (h x) -> p h x", h=8),
                                                         in1=S_[:, 5, :].unsqueeze(2).to_broadcast([128, 8, 64]), op=ALU.mult), [vz[p].b, S_.b], [bon[p].b])
                    kb.op("pool", lambda: P.tensor_tensor(out=y[p][:], in0=y[p][:], in1=bon[p][:], op=ALU.add), [y[p].b, bon[p].b], [y[p].b])
                    kb.op("act", lambda: A.activation(out=sz[p][:], in_=vz[p][:, 512:1024], func=AF.Silu), [vz[p].b], [sz[p].b])
                    kb.op("dve", lambda: V.tensor_tensor(out=yg[p][:], in0=y[p][:], in1=sz[p][:], op=ALU.mult), [y[p].b, sz[p].b], [yg[p].b])
                    kb.pe([lambda c=c: PEe.transpose(psbf[bk][:, 512 + c * 128:512 + (c + 1) * 128], yg[p][:, c * 128:(c + 1) * 128], ident_b) for c in range(4)], [yg[p].b, konb.b], [PB[bk]])
                    kb.op("act", lambda: A.copy(out=ygT[p][:].rearrange("p c t -> p (c t)"), in_=psbf[bk][:, 512:1024]), [PB[bk]], [ygT[p].b])
                    kb.dma("sp", ygT[p].b.name, YG[0:512, ts_].rearrange("(c p) t -> p c t", p=128), ygT[p][:], reads=[ygT[p].b])
                    yield

        def phase_D(l, s2):
            if True:
                gbz = T(s2, nc, "gbz", [128, 2, 4, 512], BF16)
                gcu = T(s2, nc, "gcu", [128, 2, 4, 514], BF16)
                gu = T(s2, nc, "gu", [128, 4, 514], F32)
                acc = T(s2, nc, "acc", [128, 4, 512], F32)
                szd = T(s2, nc, "szd", [128, 4, 512], F32)
                od = T(s2, nc, "od", [128, 4, 512], BF16)
                for bi, (t0, nb, col) in enumerate(blocks):
                    s0, s1 = (0, CTX) if col == 1 else (CTX, TT)
                    lo, hi = max(s0, t0 - 1), min(s1, t0 + nb + 1)
                    kb.op("pool", lambda: P.memset(gcu[:], 0.0), [], [gcu.b])
                    off = lo - (t0 - 1)
                    for j, r0 in enumerate([2816, 3328]):
                        kb.dma("sp", "gcu%d" % j, gcu[:, j, :, off:off + hi - lo], PF[r0:r0 + 512, lo:hi].rearrange("(c p) t -> p c t", p=128), writes=[gcu.b])
                    for j, r0 in enumerate([2304, 3840]):
                        kb.dma("sp", "gbz%d" % j, gbz[:, j, :, 0:nb], PF[r0:r0 + 512, t0:t0 + nb].rearrange("(c p) t -> p c t", p=128), writes=[gbz.b])
                    kb.op("dve", lambda: V.tensor_tensor(out=gu[:, :, 0:nb + 2], in0=gcu[:, 0, :, 0:nb + 2], in1=gcu[:, 1, :, 0:nb + 2], op=ALU.mult), [gcu.b], [gu.b])
                    for c in range(4):
                        cw = lambda j: pp[:, l, 60 + j * 4 + c:61 + j * 4 + c]
                        kb.op("dve", lambda: V.tensor_scalar(out=acc[:, c, 0:nb], in0=gu[:, c, 0:nb], scalar1=cw(0), scalar2=None, op0=ALU.mult), [gu.b, pp.b], [acc.b])
                        kb.op("dve", lambda: V.scalar_tensor_tensor(out=acc[:, c, 0:nb], in0=gu[:, c, 1:nb + 1], scalar=cw(1), in1=acc[:, c, 0:nb], op0=ALU.mult, op1=ALU.add), [gu.b, pp.b, acc.b], [acc.b])
                        kb.op("dve", lambda: V.scalar_tensor_tensor(out=acc[:, c, 0:nb], in0=gu[:, c, 2:nb + 2], scalar=cw(2), in1=acc[:, c, 0:nb], op0=ALU.mult, op1=ALU.add), [gu.b, pp.b, acc.b], [acc.b])
                        kb.op("dve", lambda: V.scalar_tensor_tensor(out=acc[:, c, 0:nb], in0=acc[:, c, 0:nb], scalar=pp[:, l, 72 + c:73 + c], in1=gbz[:, 0, c, 0:nb], op0=ALU.add, op1=ALU.mult),
                              [acc.b, pp.b, gbz.b], [acc.b])
                        yield
                    kb.op("act", lambda: A.activation(out=szd[:, :, 0:nb], in_=gbz[:, 1, :, 0:nb], func=AF.Silu), [gbz.b], [szd.b])
                    kb.op("pool", lambda: P.tensor_tensor(out=od[:, :, 0:nb], in0=acc[:, :, 0:nb], in1=szd[:, :, 0:nb], op=ALU.mult), [acc.b, szd.b], [od.b])
                    kb.dma("sp", "od", YG[512:1024, t0:t0 + nb].rearrange("(c p) t -> p c t", p=128), od[:, :, 0:nb], reads=[od.b])

        def phase_E1(l, s2):
            if True:
                pg = T(s2, nc, "pg", [128, 2, 128], BF16)
                qk = [T(s2, nc, "qk%d" % i, [128, 8, 512], BF16) for i in range(2)]
                cs_ = [T(s2, nc, "cs_%d" % i, [128, 2, 512], F32) for i in range(2)]
                sqe = [T(s2, nc, "sqe%d" % i, [128, 512], BF16) for i in range(2)]
                ri = [T(s2, nc, "ri%d" % i, [128, 512], F32) for i in range(2)]
                t1 = [T(s2, nc, "t1_%d" % i, [128, 512], F32) for i in range(2)]
                t2 = [T(s2, nc, "t2_%d" % i, [128, 512], F32) for i in range(2)]
                oq = [T(s2, nc, "oq%d" % i, [128, 512], BF16) for i in range(2)]
                for j in range(2):
                    kb.op("dve", lambda: V.tensor_scalar(out=pg[:, j, :], in0=perm_f, scalar1=pp[:, l, 76 + j:77 + j], scalar2=None, op0=ALU.mult), [kon.b, pp.b], [pg.b])
                for bi, (t0, nb, col) in enumerate(blocks):
                    S = slice(0, nb)
                    q_, c_ = qk[bi % 2], cs_[bi % 2]
                    kb.dma_group("sp", q_.b.name, [(q_[:, 0:4, S], PF[4352:4864, t0:t0 + nb].rearrange("(c p) t -> p c t", p=128)),
                                                   (q_[:, 4:8, S], PF[4864:5376, t0:t0 + nb].rearrange("(c p) t -> p c t", p=128))], writes=[q_.b])
                    kb.dma_group("sp", c_.b.name, [(c_[:, 0, S], ROPE[:, t0:t0 + nb]), (c_[:, 1, S], ROPE[:, TT + t0:TT + t0 + nb])], writes=[c_.b])
                    for j in range(8):
                        w = j // 4
                        p = j % 2
                        kb.op("act", lambda: A.activation(out=sqe[p][:, S], in_=q_[:, j, S], func=AF.Square), [q_.b], [sqe[p].b])
                        kb.pe([lambda: PEe.matmul(psum[:, 4 + p, S], lhsT=blk_b, rhs=sqe[p][:, S], start=True, stop=True)], [sqe[p].b, konb.b], [PB[4 + p]])
                        kb.pe([lambda: PEe.matmul(psum[:, 6 + p, S], lhsT=pg[:, w, :], rhs=q_[:, j, S], start=True, stop=True)], [pg.b, q_.b], [PB[6 + p]])
                        kb.op("act", lambda: A.activation(out=ri[p][:, S], in_=psum[:, 4 + p, S], func=AF.Ln, bias=epsc[:, 0:1], scale=1.0 / 64), [PB[4 + p], epsc.b], [ri[p].b])
                        kb.op("act", lambda: A.activation(out=ri[p][:, S], in_=ri[p][:, S], func=AF.Exp, scale=-0.5), [ri[p].b], [ri[p].b])
                        kb.op("dve", lambda: V.scalar_tensor_tensor(out=t1[p][:, S], in0=q_[:, j, S], scalar=pp[:, l, 76 + w:77 + w], in1=c_[:, 0, S], op0=ALU.mult, op1=ALU.mult),
                              [q_.b, pp.b, c_.b], [t1[p].b])
                        kb.op("dve", lambda: V.tensor_tensor(out=t2[p][:, S], in0=psum[:, 6 + p, S], in1=c_[:, 1, S], op=ALU.mult), [PB[6 + p], c_.b], [t2[p].b])
                        kb.op("pool", lambda: P.tensor_tensor(out=t1[p][:, S], in0=t1[p][:, S], in1=t2[p][:, S], op=ALU.add), [t1[p].b, t2[p].b], [t1[p].b])
                        kb.op("pool", lambda: P.tensor_tensor(out=oq[p][:, S], in0=t1[p][:, S], in1=ri[p][:, S], op=ALU.mult), [t1[p].b, ri[p].b], [oq[p].b])
                        dst = QRs if w == 0 else KRs
                        kb.dma("sp", oq[p].b.name, dst[(j % 4) * 128:(j % 4 + 1) * 128, t0:t0 + nb], oq[p][:, S], reads=[oq[p].b])
                        yield

        def phase_E2(l, last, s2):
            if True:
                QR = T(s2, nc, "QR", [128, 4, TT], BF16)
                KR = T(s2, nc, "KR", [128, 4, TT], BF16)
                VD = T(s2, nc, "VD", [128, NTT, 512], BF16)
                PTs = [[T(s2, nc, "PTs%d_%d" % (i, c), [128, 512], BF16) for c in range(2)] for i in range(3)]
                accD = [T(s2, nc, "accD%d" % i, [128, 512], F32) for i in range(2)]
                accP = [T(s2, nc, "accP%d" % i, [128, 512], F32) for i in range(2)]
                o0 = [T(s2, nc, "o0_%d" % i, [128, 512], F32) for i in range(2)]
                r0 = T(s2, nc, "r0", [128, 512], F32)
                r0a = T(s2, nc, "r0a", [128, 512], F32)
                sq2 = T(s2, nc, "sq2", [128, 512], BF16)
                rst = T(s2, nc, "rst", [128, 512], F32)
                zc = T(s2, nc, "zc", [128, 512], BF16)
                szc = T(s2, nc, "szc", [128, 512], F32)
                oc = T(s2, nc, "oc", [128, 512], BF16)
                kb.dma("sp", "VD", VD[:], PT[:, 1024:1536].rearrange("(t p) n -> p t n", p=128), writes=[VD.b])
                kb.dma("sp", "QR", QR[:], QRs.rearrange("(h p) t -> p h t", p=128), writes=[QR.b])
                kb.dma("sp", "KR", KR[:], KRs.rearrange("(h p) t -> p h t", p=128), writes=[KR.b])
                pending = []

                def flush(n):
                    for _ in range(min(n, len(pending))):
                        pending.pop(0)()
                n_ = 0
                bidx = 0
                for h in range(4):
                    for bi, (t0, nb, col) in enumerate(blocks):
                        if col == 1 and last:
                            continue
                        S = slice(0, nb)
                        kts = list(range(CTX // 128)) if col == 1 else list(range(NTT))
                        aD, aP = accD[bidx % 2], accP[bidx % 2]
                        bidx += 1

                        def smm(i):
                            kt = kts[i]
                            b0 = 2 * ((n_ + i) % 2)
                            kb.pe([lambda c=c: PEe.matmul(psum[:, b0 + c, S], lhsT=KR[c * 64:(c + 1) * 64, h, kt * 128:(kt + 1) * 128], rhs=QR[c * 64:(c + 1) * 64, h, t0:t0 + nb],
                                                          start=True, stop=True) for c in range(2)], [KR.b, QR.b], [PB[b0], PB[b0 + 1]])
                        smm(0)
                        for i, kt in enumerate(kts):
                            if i + 1 < len(kts):
                                smm(i + 1)
                            b0 = 2 * ((n_ + i) % 2)
                            pt = PTs[(n_ + i) % 3]
                            first, lastk = (i == 0), (i == len(kts) - 1)
                            for c in range(2):
                                kb.op("act", lambda: A.activation(out=pt[c][:, S], in_=psum[:, b0 + c, S], func=AF.Exp, scale=0.125), [PB[b0 + c]], [pt[c].b])
                                fns = [lambda: PEe.matmul(psum[:, 4 + c, S], lhsT=VD[:, kt, h * 128:(h + 1) * 128], rhs=pt[c][:, S], start=first, stop=lastk)]
                                wr = [PB[4 + c]]
                                if c == 0:
                                    fns.append(lambda: PEe.matmul(psum[:, 7, S], lhsT=ones_b, rhs=pt[0][:, S], start=first, stop=lastk))
                                    wr.append(PB[7])
                                kb.pe(fns, [VD.b, pt[c].b, konb.b], wr)
                            if i % 2 == 0:
                                if i == 0:
                                    kb.op("dve", lambda: V.tensor_copy(out=aD[:, S], in_=pt[1][:, S]), [pt[1].b], [aD.b])
                                else:
                                    kb.op("dve", lambda: V.tensor_tensor(out=aD[:, S], in0=aD[:, S], in1=pt[1][:, S], op=ALU.add), [pt[1].b, aD.b], [aD.b])
                            else:
                                if i == 1:
                                    kb.op("pool", lambda: P.tensor_copy(out=aP[:, S], in_=pt[1][:, S]), [pt[1].b], [aP.b])
                                else:
                                    kb.op("pool", lambda: P.tensor_tensor(out=aP[:, S], in0=aP[:, S], in1=pt[1][:, S], op=ALU.add), [pt[1].b, aP.b], [aP.b])
                            flush(1)
                            yield
                        n_ += len(kts)
                        flush(len(pending))
                        kb.op("act", lambda: A.copy(out=o0[0][:, S], in_=psum[:, 4, S]), [PB[4]], [o0[0].b])
                        kb.op("dve", lambda: V.tensor_copy(out=o0[1][:, S], in_=psum[:, 5, S]), [PB[5]], [o0[1].b])
                        kb.op("act", lambda: A.copy(out=r0a[:, S], in_=psum[:, 7, S]), [PB[7]], [r0a.b])

                        def mk_epilogue(h=h, t0=t0, nb=nb, S=S, aD=aD, aP=aP):
                            ops = []
                            ops.append(lambda: kb.dma("sp", "zc", zc[:, S], PF[5888 + h * 128:5888 + (h + 1) * 128, t0:t0 + nb], writes=[zc.b]))
                            ops.append(lambda: kb.op("dve", lambda: V.reciprocal(out=r0a[:, S], in_=r0a[:, S]), [r0a.b], [r0a.b]))
                            ops.append(lambda: kb.op("dve", lambda: V.tensor_tensor(out=o0[0][:, S], in0=o0[0][:, S], in1=r0a[:, S], op=ALU.mult), [o0[0].b, r0a.b], [o0[0].b]))
                            ops.append(lambda: kb.pe([lambda: PEe.matmul(psum[:, 6, S], lhsT=ones_f, rhs=aD[:, S], start=True, stop=False),
                                                      lambda: PEe.matmul(psum[:, 6, S], lhsT=ones_f, rhs=aP[:, S], start=False, stop=True)], [aD.b, aP.b, kon.b], [PB[6]]))
                            ops.append(lambda: kb.op("dve", lambda: V.reciprocal(out=r0[:, S], in_=psum[:, 6, S]), [PB[6]], [r0.b]))
                            ops.append(lambda: kb.op("dve", lambda: V.tensor_tensor(out=o0[1][:, S], in0=o0[1][:, S], in1=r0[:, S], op=ALU.mult), [o0[1].b, r0.b], [o0[1].b]))
                            ops.append(lambda: kb.op("dve", lambda: V.scalar_tensor_tensor(out=o0[0][:, S], in0=o0[1][:, S], scalar=lamc[:, l, 0:1], in1=o0[0][:, S], op0=ALU.mult, op1=ALU.add),
                                                     [o0[0].b, o0[1].b, lamc.b], [o0[0].b]))
                            ops.append(lambda: kb.op("act", lambda: A.activation(out=sq2[:, S], in_=o0[0][:, S], func=AF.Square), [o0[0].b], [sq2.b]))
                            ops.append(lambda: kb.pe([lambda: PEe.matmul(psum[:, 6, S], lhsT=ones_b, rhs=sq2[:, S], start=True, stop=True)], [sq2.b, konb.b], [PB[6]]))
                            ops.append(lambda: kb.op("act", lambda: A.activation(out=rst[:, S], in_=psum[:, 6, S], func=AF.Ln, bias=epsc[:, 0:1], scale=1.0 / 128), [PB[6], epsc.b], [rst.b]))
                            ops.append(lambda: kb.op("act", lambda: A.activation(out=rst[:, S], in_=rst[:, S], func=AF.Exp, scale=-0.5), [rst.b], [rst.b]))
                            ops.append(lambda: kb.op("dve", lambda: V.scalar_tensor_tensor(out=o0[0][:, S], in0=o0[0][:, S], scalar=sgl[:, l:l + 1], in1=rst[:, S], op0=ALU.mult, op1=ALU.mult),
                                                     [o0[0].b, sgl.b, rst.b], [o0[0].b]))
                            ops.append(lambda: kb.op("act", lambda: A.activation(out=szc[:, S], in_=zc[:, S], func=AF.Silu), [zc.b], [szc.b]))
                            ops.append(lambda: kb.op("pool", lambda: P.tensor_tensor(out=oc[:, S], in0=o0[0][:, S], in1=szc[:, S], op=ALU.mult), [o0[0].b, szc.b], [oc.b]))
                            ops.append(lambda: kb.dma("sp", "oc", YG[1024 + h * 128:1024 + (h + 1) * 128, t0:t0 + nb], oc[:, S], reads=[oc.b]))
                            return ops
                        pending.extend(mk_epilogue())
                flush(len(pending))

        def phase_F(l, last):
            with ExitStack() as s2:
                wbr = T(s2, nc, "wbr", [128, 12, 1024], BF16)
                wou = T(s2, nc, "wou", [128, 8, 1024], BF16)
                ygb = [T(s2, nc, "ygb%d" % i, [128, 12, 512], BF16) for i in range(2)]
                gm = [T(s2, nc, "gm%d" % i, [128, 24, 512], BF16) for i in range(2)]
                xa = [T(s2, nc, "xaf%d" % i, [128, 8, 512], F32) for i in range(2)]
                xn = T(s2, nc, "xnf", [128, 8, 512], F32)
                mT = T(s2, nc, "mT", [128, 8, 512], BF16)
                sg = [T(s2, nc, "sgf%d" % i, [128, 512], F32) for i in range(3)]
                ma = [T(s2, nc, "ma%d" % i, [128, 512], F32) for i in range(3)]
                stg = [xa[0], xa[1], xn]
                for i in range(5):
                    w = stg[i % 3]
                    wv = w[:].rearrange("p c n -> p (c n)").rearrange("p (a m) -> p a m", a=4)
                    src = w_br[l][i * 512:(i + 1) * 512, :] if i < 3 else w_out[l][(i - 3) * 512:(i - 2) * 512, :]
                    kb.dma("sp", w.b.name, wv, src.rearrange("(c p) n -> p c n", p=128), writes=[w.b])
                    dstt = wbr[:, i * 4:(i + 1) * 4, :] if i < 3 else wou[:, (i - 3) * 4:(i - 2) * 4, :]
                    dstb = wbr.b if i < 3 else wou.b
                    kb.op("pool", lambda: P.tensor_copy(out=dstt, in_=wv), [w.b], [dstb])
                todo = [(bi, blk) for bi, blk in enumerate(blocks) if not (blk[2] == 1 and last)]

                def loads(k):
                    bi, (t0, nb, col) = todo[k]
                    p = k % 2
                    S = slice(0, nb)
                    kb.dma("sp", ygb[p].b.name, ygb[p][:, :, S], YG[:, t0:t0 + nb].rearrange("(c p) t -> p c t", p=128), writes=[ygb[p].b])
                    kb.dma("sp", gm[p].b.name, gm[p][:, :, S], PF[6400:9472, t0:t0 + nb].rearrange("(c p) t -> p c t", p=128), writes=[gm[p].b])
                    kb.dma("sp", xa[p].b.name, xa[p][:, :, S], xTv[:, :, t0:t0 + nb], writes=[xa[p].b])
                loads(0)
                for k, (bi, (t0, nb, col)) in enumerate(todo):
                    if k + 1 < len(todo):
                        loads(k + 1)
                    p = k % 2
                    S = slice(0, nb)
                    for dc in range(8):
                        for i in range(3):
                            kb.pe([lambda w=w: PEe.matmul(psum[:, 1 + i, S], lhsT=wbr[:, i * 4 + w, dc * 128:(dc + 1) * 128], rhs=ygb[p][:, i * 4 + w, S], start=(w == 0), stop=(w == 3))
                                   for w in range(4)], [wbr.b, ygb[p].b], [PB[1 + i]])
                            kb.op("act", lambda: A.activation(out=sg[i][:, S], in_=gm[p][:, i * 8 + dc, S], func=AF.Sigmoid), [gm[p].b], [sg[i].b])
                            kb.op("dve", lambda: V.tensor_tensor(out=ma[i][:, S], in0=psum[:, 1 + i, S], in1=sg[i][:, S], op=ALU.mult), [PB[1 + i], sg[i].b], [ma[i].b])
                        kb.op("pool", lambda: P.tensor_tensor(out=ma[0][:, S], in0=ma[0][:, S], in1=ma[1][:, S], op=ALU.add), [ma[0].b, ma[1].b], [ma[0].b])
                        kb.op("pool", lambda: P.tensor_tensor(out=mT[:, dc, S], in0=ma[0][:, S], in1=ma[2][:, S], op=ALU.add), [ma[0].b, ma[2].b], [mT.b])
                    for d2 in range(8):
                        bank = 4 + d2 % 2
                        kb.pe([lambda dc=dc: PEe.matmul(psum[:, bank, S], lhsT=wou[:, dc, d2 * 128:(d2 + 1) * 128], rhs=mT[:, dc, S], start=(dc == 0), stop=(dc == 7)) for dc in range(8)],
                              [wou.b, mT.b], [PB[bank]])
                        kb.op("dve", lambda: V.scalar_tensor_tensor(out=xn[:, d2, S], in0=psum[:, bank, S], scalar=modT[:, l, 16 + d2, col:col + 1], in1=xa[p][:, d2, S], op0=ALU.mult, op1=ALU.add),
                              [PB[bank], modT.b, xa[p].b], [xn.b])
                    kb.dma("sp", "xnf", xTv[:, :, t0:t0 + nb], xn[:, :, S], reads=[xn.b])
                kb.barrier()

        def drain(g):
            for _ in g:
                pass

        def layer(l):
            last = (l == DEPTH - 1)
            phase_AB(l)
            with ExitStack() as sg:
                kb.coop.run([(lambda: drain(phase_C1(l, sg)), 8), (lambda: drain(phase_E1(l, sg)), 3), (lambda: drain(phase_D(l, sg)), 1)])
                kb.barrier()
            phase_C2(l)
            with ExitStack() as sg:
                drain(phase_E2(l, last, sg))
                kb.barrier()
            with ExitStack() as sg:
                kb.coop.run([(lambda: drain(phase_C3(l, sg, 0, 2)), 1), (lambda: drain(phase_C3(l, sg, 1, 2)), 1)])
                kb.barrier()
            phase_F(l, last)

        for l in range(DEPTH):
            layer(l)
        with ExitStack() as s2:
            xb_ = [T(s2, nc, "xo%d" % i, [128, 8, 512], F32) for i in range(2)]
            for bi, (t0, nb, col) in enumerate(blocks[1:]):
                x_ = xb_[bi % 2]
                kb.dma("sp", x_.b.name, x_[:, :, 0:nb], xTv[:, :, t0:t0 + nb], writes=[x_.b])
                kb.dma("sp", x_.b.name, yout.rearrange("(c p) t -> p c t", p=128)[:, :, t0 - CTX:t0 - CTX + nb], x_[:, :, 0:nb], reads=[x_.b])
            if dbg:
                for k_, shp in dbg.items():
                    src = {"PF": PF, "PX": PX, "PT": PT, "YG": YG, "YS0": YS[0], "YS1": YS[1], "xTs": xT, "AKT0": AKT[0], "GCs0": GCs[0], "GCs1": GCs[1]}[k_]
                    kb.dma("sp", "dbg", dbgo[k_], src, reads=[])
            kb.barrier()
    build.n_inst = kb.n_inst
    build.n_sem = len(kb.sems)
    return nc


def host_consts(SEQ, DEPTH):
    TT = CTX + SEQ
    kon = np.zeros((128, 1024), np.float32)
    kon[:, 0:128] = np.eye(128)
    kon[:, 128:256] = 1.0
    p = np.arange(128)
    kon[:, 256:384] = (p[:, None] // 64 == p[None, :] // 64)
    partner = np.where((p % 32) < 16, p + 16, p - 16)
    perm = np.zeros((128, 128), np.float32)
    perm[partner, p] = 1.0
    kon[:, 384:512] = perm
    sel = np.zeros((128, 4, 8), np.float32)
    for c in range(4):
        for q in range(128):
            sel[q, c, 2 * c + q // 64] = 1.0
    kon[:, 512:544] = sel.reshape(128, 32)
    i = np.arange(64)
    SU = (i[:, None] < i[None, :]); SL = (i[:, None] > i[None, :]); IU = (i[:, None] <= i[None, :]); IL = (i[:, None] >= i[None, :]); I = np.eye(64)
    rep = lambda ms: np.stack([np.broadcast_to(m[:, None, :], (64, 8, 64)) for m in ms], axis=1).astype(np.float32).reshape(64, 4 * 512)
    msk = np.concatenate([rep((SU, IU, SL, I)), rep((SL, IL, SU, I))], axis=0)
    n_freq = 16
    inv_freq = (10000.0 ** (-np.arange(n_freq, dtype=np.float32) / n_freq)).astype(np.float32)
    tok = np.arange(SEQ)
    row = (tok // 64).astype(np.float32)
    colp = (tok % 64).astype(np.float32)
    d = p % 64
    half = d // 32
    j = d % 16
    which = (d % 32) // 16
    pos = np.where(half[:, None] == 0, row[None, :], colp[None, :]).astype(np.float32)
    ang = (pos * inv_freq[j][:, None]).astype(np.float32)
    cosT = np.ones((128, TT), np.float32)
    sinT = np.zeros((128, TT), np.float32)
    cosT[:, CTX:] = np.cos(ang)
    sinT[:, CTX:] = np.sin(ang) * np.where(which == 0, -1.0, 1.0)[:, None]
    rope = np.concatenate([cosT, sinT], axis=1).astype(np.float32)
    ltri = np.ones((128, 512), np.float32)
    ltri[:, 0::64] = 0.0
    return dict(KON=kon, MSK=np.ascontiguousarray(msk), ROPE=np.ascontiguousarray(rope), LTRI=ltri)


def host_inputs(inp, SEQ, DEPTH, b):
    L = DEPTH
    f = lambda a: np.asarray(a, np.float32)
    xT = np.concatenate([f(inp["ctx"])[b], f(inp["x"])[b]], axis=0).T
    PPa = np.zeros((128, L, NPL), np.float32)
    colT = lambda v: v.reshape(-1, 128).T
    for l in range(L):
        PPa[:, l, 0:8] = colT(f(inp["norm_g"])[l])
        PPa[:, l, 8:32] = colT(f(inp["b_mod"])[l])
        PPa[:, l, 32:40] = colT(f(inp["decay_w0"])[l].reshape(-1))
        PPa[:, l, 40:48] = colT(f(inp["iclr_a0"])[l].reshape(-1))
        PPa[:, l, 48:52] = colT(f(inp["k_k"])[l])
        PPa[:, l, 52:56] = colT(f(inp["k_a"])[l])
        PPa[:, l, 56:60] = colT(f(inp["r_k"])[l].reshape(-1))
        PPa[:, l, 60:72] = colT(f(inp["conv_w"])[l].reshape(-1))
        PPa[:, l, 72:76] = colT(f(inp["conv_b"])[l])
        PPa[:, l, 76] = np.tile(f(inp["qk_norm_g"])[l, 0], 2)
        PPa[:, l, 77] = np.tile(f(inp["qk_norm_g"])[l, 1], 2)
        PPa[:, l, 78] = f(inp["subln_g"])[l]
    CC = np.stack([colT(f(inp["c"])[b]), colT(f(inp["c_ctx"]))], axis=2).reshape(128, 16)
    BC = np.stack([np.broadcast_to(f(inp["gn_g"])[:L, None, :], (L, 128, 512)), np.broadcast_to(f(inp["gn_b"])[:L, None, :], (L, 128, 512))], axis=2)
    BC = np.ascontiguousarray(BC.transpose(1, 0, 2, 3)).reshape(128, L * 1024)
    LAM = f(inp["lambda_qk"])[:L].reshape(1, L * 256)
    return dict(xT=np.ascontiguousarray(xT), PP=PPa.reshape(128, L * NPL), CC=np.ascontiguousarray(CC), BC=BC, LAM=np.ascontiguousarray(LAM),
                w_mod=f(inp["w_mod"])[:L], w_in=f(inp["w_in"])[:L], dup=f(inp["decay_up"])[:L].reshape(L, 128, 512), iup=f(inp["iclr_up"])[:L].reshape(L, 128, 512),
                w_br=f(inp["w_branch"])[:L].reshape(L, 1536, D), w_out=f(inp["w_out"])[:L])


def run(inp, SEQ, DEPTH, dbg=None):
    nc = build(SEQ, DEPTH, dbg)
    consts = host_consts(SEQ, DEPTH)
    nb = np.asarray(inp["x"]).shape[0]
    in_maps = []
    for b in range(nb):
        m = host_inputs(inp, SEQ, DEPTH, b)
        m.update(consts)
        in_maps.append(m)
    res = run_bass_kernel_spmd(nc, in_maps, core_ids=list(range(nb)))
    out = np.stack([np.asarray(r["yout"]).T for r in res.results], axis=0)
    return out.astype(np.float32), res


def kernel(**inputs):
    out, _ = run(inputs, 4096, 4)
    return out
```

```python
import math
import threading
import numpy as np
from contextlib import ExitStack
import concourse.bass as bass
import concourse.mybir as mybir
from concourse.bass_utils import run_bass_kernel_spmd

F32 = mybir.dt.float32
BF16 = mybir.dt.bfloat16
AF = mybir.ActivationFunctionType
ALU = mybir.AluOpType
AX = mybir.AxisListType

D = 1024
CTX = 256
NIN = 9472
NPL = 79
C0 = math.exp(-0.5)
GN_EPS = 64e-5
RMS_EPS = 1e-6


class Buf:
    __slots__ = ("name", "w", "r")

    def __init__(self, name):
        self.name = name
        self.w = None
        self.r = []


class Coop:
    def __init__(self):
        self.cur = None

    def run(self, specs):
        n = len(specs)
        self.evs = [threading.Event() for _ in range(n)]
        self.alive = [True] * n
        self.quota = [q for _, q in specs]
        self.used = [0] * n
        self.done_ev = threading.Event()
        self.err = None

        def wrap(i, fn):
            self.evs[i].wait()
            self.evs[i].clear()
            self.cur = i
            try:
                fn()
            except BaseException as e:
                self.err = e
            self.alive[i] = False
            self._pass(i)
        ths = [threading.Thread(target=wrap, args=(i, f)) for i, (f, _) in enumerate(specs)]
        for t in ths:
            t.start()
        self.evs[0].set()
        self.done_ev.wait()
        for t in ths:
            t.join()
        self.cur = None
        if self.err is not None:
            raise self.err

    def _pass(self, i):
        n = len(self.alive)
        for k in range(1, n + 1):
            j = (i + k) % n
            if self.alive[j]:
                if j == i:
                    return True
                self.evs[j].set()
                return False
        self.done_ev.set()
        return False

    def switch(self):
        i = self.cur
        if i is None:
            return
        self.used[i] += 1
        if self.used[i] < self.quota[i]:
            return
        self.used[i] = 0
        if not self._pass(i):
            self.evs[i].wait()
            self.evs[i].clear()
            self.cur = i


class KB:
    def __init__(self, nc, stack):
        self.nc = nc
        self.stack = stack
        self.eng = {"pe": nc.tensor, "dve": nc.vector, "act": nc.scalar, "pool": nc.gpsimd, "sp": nc.sync}
        self.sems = {}
        self.cnt = {}
        self.seen = {e: {} for e in self.eng}
        for e in self.eng:
            self.sems[e] = stack.enter_context(nc.semaphore("s_" + e))
            self.cnt[e] = 0
        self.n_inst = 0
        self.dma_names = {}
        self.dma_pool = []
        self.coop = Coop()

    def dma_sem(self, name):
        if name not in self.dma_names:
            i = len(self.dma_names)
            if i >= len(self.dma_pool):
                key = "d_%d" % i
                self.sems[key] = self.stack.enter_context(self.nc.semaphore(key))
                self.cnt[key] = 0
                self.dma_pool.append(key)
            self.dma_names[name] = self.dma_pool[i]
        return self.dma_names[name]

    def phase_reset(self):
        self.dma_names = {}

    def _wait(self, e, ev):
        if ev is None:
            return
        key, val = ev
        if key == "pe" and e == "pe":
            return
        if self.seen[e].get(key, 0) >= val:
            return
        self.eng[e].wait_ge(self.sems[key], val)
        self.seen[e][key] = val
        self.n_inst += 1

    def deps(self, e, reads, writes):
        for b in reads:
            self._wait(e, b.w)
        for b in writes:
            self._wait(e, b.w)
            for ev in b.r:
                self._wait(e, ev)

    def done(self, ev, reads, writes):
        for b in reads:
            b.r.append(ev)
            if len(b.r) > 16:
                d = {}
                for k, v in b.r:
                    d[k] = max(d.get(k, 0), v)
                b.r = list(d.items())
        for b in writes:
            b.w = ev
            b.r = []

    def op(self, e, fn, reads=(), writes=()):
        self.deps(e, reads, writes)
        ins = fn()
        self.cnt[e] += 1
        ins.then_inc(self.sems[e], 1)
        ev = (e, self.cnt[e])
        self.done(ev, reads, writes)
        self.n_inst += 1
        self.coop.switch()
        return ev

    def pe(self, fns, reads=(), writes=()):
        self.deps("pe", reads, writes)
        ins = None
        for fn in fns:
            ins = fn()
            self.n_inst += 1
        self.cnt["pe"] += 1
        ins.then_inc(self.sems["pe"], 1)
        ev = ("pe", self.cnt["pe"])
        self.done(ev, reads, writes)
        self.coop.switch()
        return ev

    def dma(self, q, semname, out, in_, reads=(), writes=()):
        key = self.dma_sem(semname)
        self.deps(q, reads, writes)
        ins = self.eng[q].dma_start(out=out, in_=in_)
        self.cnt[key] += 16
        ins.then_inc(self.sems[key], 16)
        ev = (key, self.cnt[key])
        self.done(ev, reads, writes)
        self.n_inst += 1
        self.coop.switch()
        return ev

    def dma_group(self, q, semname, pairs, reads=(), writes=()):
        key = self.dma_sem(semname)
        self.deps(q, reads, writes)
        for out, in_ in pairs:
            ins = self.eng[q].dma_start(out=out, in_=in_)
            self.cnt[key] += 16
            ins.then_inc(self.sems[key], 16)
            self.n_inst += 1
        ev = (key, self.cnt[key])
        self.done(ev, reads, writes)
        self.coop.switch()
        return ev

    def barrier(self):
        for e in self.eng:
            for k in self.cnt:
                if self.cnt[k] > 0:
                    self._wait(e, (k, self.cnt[k]))
        self.phase_reset()


class T:
    uid = [0]

    def __init__(self, st, nc, name, shape, dt, nbuf=1):
        T.uid[0] += 1
        self.t = st.enter_context(nc.sbuf_tensor("%s_u%d" % (name, T.uid[0]), shape, dt))
        self.b = Buf(name)

    def __getitem__(self, k):
        return self.t[k]


SEGS = [(0, 512, "F"), (512, 512, "F"), (1024, 512, "T"), (1536, 256, "X"), (1792, 512, "T"),
        (2304, 512, "F"), (2816, 512, "F"), (3328, 512, "F"), (3840, 512, "F"),
        (4352, 512, "F"), (4864, 512, "F"), (5376, 512, "T"), (5888, 512, "F")] + \
       [(6400 + 512 * i, 512, "F") for i in range(6)]
TCOL = {1024: 0, 1792: 512, 5376: 1024}


def build(SEQ, DEPTH, dbg=None):
    TT = CTX + SEQ
    NCH = TT // 64
    blocks = [(0, CTX, 1)] + [(CTX + 512 * i, 512, 0) for i in range(SEQ // 512)]
    nc = bass.Bass("TRN2", target_bir_lowering=False)
    dram = lambda n, s, d, k="Internal": nc.dram_tensor(n, s, d, kind=k).ap()
    xT_in = dram("xT", [D, TT], F32, "ExternalInput")
    PP = dram("PP", [128, DEPTH * NPL], F32, "ExternalInput")
    CCd = dram("CC", [128, 16], F32, "ExternalInput")
    BCd = dram("BC", [128, DEPTH * 2 * 512], F32, "ExternalInput")
    LAMd = dram("LAM", [1, DEPTH * 256], F32, "ExternalInput")
    KON = dram("KON", [128, 1024], F32, "ExternalInput")
    MSK = dram("MSK", [128, 4 * 512], F32, "ExternalInput")
    ROPE = dram("ROPE", [128, 2 * TT], F32, "ExternalInput")
    LTRI = dram("LTRI", [128, 512], F32, "ExternalInput")
    w_mod = dram("w_mod", [DEPTH, D, 3 * D], F32, "ExternalInput")
    w_in = dram("w_in", [DEPTH, D, NIN], F32, "ExternalInput")
    dup = dram("dup", [DEPTH, 128, 512], F32, "ExternalInput")
    iup = dram("iup", [DEPTH, 128, 512], F32, "ExternalInput")
    w_br = dram("w_br", [DEPTH, 1536, D], F32, "ExternalInput")
    w_out = dram("w_out", [DEPTH, D, D], F32, "ExternalInput")
    yout = dram("yout", [D, SEQ], F32, "ExternalOutput")
    xT = dram("xTs", [D, TT], F32)
    PF = dram("PF", [NIN, TT], BF16)
    PX = dram("PX", [256, TT], F32)
    PT = dram("PT", [TT, 1536], BF16)
    QGs = [dram("QGs%d" % d, [512, NCH, 2, 64], BF16) for d in range(2)]
    AKT = [dram("AKT%d" % d, [512, TT], BF16) for d in range(2)]
    KIT = [dram("KIT%d" % d, [512, TT], BF16) for d in range(2)]
    AKK = [dram("AKK%d" % d, [TT, 512], BF16) for d in range(2)]
    KIK = [dram("KIK%d" % d, [TT, 512], BF16) for d in range(2)]
    GCs = [dram("GCs%d" % d, [512, NCH], F32) for d in range(2)]
    YS = [dram("YS%d" % d, [TT, 512], F32) for d in range(2)]
    YG = dram("YG", [1536, TT], BF16)
    QRs = dram("QRs", [512, TT], BF16)
    KRs = dram("KRs", [512, TT], BF16)
    dbgo = {}
    if dbg:
        for n, shp in dbg.items():
            dbgo[n] = dram("dbg_" + n, shp, F32, "ExternalOutput")

    with ExitStack() as st:
        kb = KB(nc, st)
        V, A, P, PEe = nc.vector, nc.scalar, nc.gpsimd, nc.tensor
        psum = st.enter_context(nc.psum_tensor("psum", [128, 8, 512], F32))
        PB = [Buf("ps%d" % i) for i in range(8)]
        psbf = [psum[:, i, :].bitcast(BF16) for i in range(8)]

        kon = T(st, nc, "kon", [128, 1024], F32)
        konb = T(st, nc, "konb", [128, 1024], BF16)
        msk2 = T(st, nc, "msk2", [128, 4, 8, 64], F32)
        pp = T(st, nc, "pp", [128, DEPTH, NPL], F32)
        cc = T(st, nc, "cc", [128, 8, 2], F32)
        ltri = T(st, nc, "ltri", [128, 512], F32)
        modT = T(st, nc, "modT", [128, DEPTH, 24, 2], F32)
        gsc = T(st, nc, "gsc", [128, DEPTH, 8, 2], F32)
        omka = T(st, nc, "omka", [128, DEPTH, 4], F32)
        lamc = T(st, nc, "lamc", [128, DEPTH, 2], F32)
        sgl = T(st, nc, "sgl", [128, DEPTH], F32)
        kb.dma("sp", "c0", kon[:], KON[:, :], writes=[kon.b])
        kb.dma("sp", "c1", msk2[:].rearrange("p a h x -> p (a h x)"), MSK[:, :], writes=[msk2.b])
        kb.dma("sp", "c2", pp[:].rearrange("p l n -> p (l n)"), PP[:, :], writes=[pp.b])
        kb.dma("sp", "c3", cc[:].rearrange("p a b -> p (a b)"), CCd[:, :], writes=[cc.b])
        kb.dma("sp", "c4", ltri[:], LTRI[:, :], writes=[ltri.b])
        kb.op("dve", lambda: V.tensor_copy(out=konb[:], in_=kon[:]), [kon.b], [konb.b])
        epsc = T(st, nc, "epsc", [128, 2], F32)
        kb.op("pool", lambda: P.memset(epsc[:, 0:1], RMS_EPS), [], [epsc.b])
        kb.op("pool", lambda: P.memset(epsc[:, 1:2], 1e-12), [], [epsc.b])
        ident_f, ones_f = kon[:, 0:128], kon[:, 128:256]
        ident_b, ones_b, blk_b, perm_f = konb[:, 0:128], konb[:, 128:256], konb[:, 256:384], kon[:, 384:512]
        sel_b = konb[:, 512:544]

        with ExitStack() as s2:
            sc = T(s2, nc, "sc", [128, 8, 2], F32)
            wm = [T(s2, nc, "wm%d" % i, [128, 8, 512], F32) for i in range(2)]
            lamt = T(s2, nc, "lamt", [1, DEPTH, 4, 64], F32)
            lam2 = T(s2, nc, "lam2", [1, DEPTH, 2], F32)
            lam3 = T(s2, nc, "lam3", [1, DEPTH, 2], F32)
            kb.op("act", lambda: A.activation(out=sc[:], in_=cc[:], func=AF.Silu), [cc.b], [sc.b])
            i = 0
            for l in range(DEPTH):
                for nb6 in range(6):
                    w = wm[i % 2]
                    i += 1
                    kb.dma("sp", w.b.name, w[:], w_mod[l].rearrange("(c p) n -> p c n", p=128)[:, :, nb6 * 512:(nb6 + 1) * 512], writes=[w.b])
                    fns = []
                    for n4 in range(4):
                        for c in range(8):
                            fns.append(lambda n4=n4, c=c, w=w: PEe.matmul(psum[:, 0, n4 * 2:n4 * 2 + 2], lhsT=w[:, c, n4 * 128:(n4 + 1) * 128], rhs=sc[:, c, :], start=(c == 0), stop=(c == 7)))
                    kb.pe(fns, [w.b, sc.b], [PB[0]])
                    kb.op("dve", lambda l=l, nb6=nb6: V.tensor_tensor(out=modT[:, l, nb6 * 4:(nb6 + 1) * 4, :], in0=psum[:, 0, 0:8].rearrange("p (a b) -> p a b", b=2),
                                                                   in1=pp[:, l, 8 + nb6 * 4:8 + (nb6 + 1) * 4].unsqueeze(2).to_broadcast([128, 4, 2]), op=ALU.add), [PB[0], pp.b], [modT.b])
                kb.op("dve", lambda l=l: V.tensor_scalar(out=gsc[:, l], in0=modT[:, l, 8:16, :], scalar1=1.0, scalar2=None, op0=ALU.add), [modT.b], [gsc.b])
                kb.op("dve", lambda l=l: V.tensor_tensor(out=gsc[:, l], in0=gsc[:, l], in1=pp[:, l, 0:8].unsqueeze(2).to_broadcast([128, 8, 2]), op=ALU.mult), [gsc.b, pp.b], [gsc.b])
                kb.op("dve", lambda l=l: V.tensor_scalar(out=omka[:, l, :], in0=pp[:, l, 52:56], scalar1=-1.0, scalar2=1.0, op0=ALU.mult, op1=ALU.add), [pp.b], [omka.b])
            kb.dma("sp", "c5", lamt[:].rearrange("p l a x -> p (l a x)"), LAMd[:, :], writes=[lamt.b])
            lv = lamt[:].rearrange("p l (a b) x -> p l a b x", b=2)
            kb.op("dve", lambda: V.tensor_tensor(out=lv[:, :, :, 0, :], in0=lv[:, :, :, 0, :], in1=lv[:, :, :, 1, :], op=ALU.mult), [lamt.b], [lamt.b])
            kb.op("dve", lambda: V.tensor_reduce(out=lam2[:], in_=lv[:, :, :, 0, :], axis=AX.X, op=ALU.add), [lamt.b], [lam2.b])
            kb.op("act", lambda: A.activation(out=lam3[:], in_=lam2[:], func=AF.Exp), [lam2.b], [lam3.b])
            kb.op("dve", lambda: V.tensor_tensor(out=lam2[:, :, 0], in0=lam3[:, :, 1], in1=lam3[:, :, 0], op=ALU.subtract), [lam3.b], [lam2.b])
            for l in range(DEPTH):
                li = 0.8 - 0.6 * math.exp(-0.3 * l)
                kb.op("dve", lambda l=l, li=li: V.tensor_scalar(out=lam2[:, l, 0:1], in0=lam2[:, l, 0:1], scalar1=-li, scalar2=None, op0=ALU.add), [lam2.b], [lam2.b])
                kb.op("dve", lambda l=l, li=li: V.tensor_scalar(out=sgl[:, l:l + 1], in0=pp[:, l, 78:79], scalar1=(1.0 - li), scalar2=None, op0=ALU.mult), [pp.b], [sgl.b])
            kb.pe([lambda: PEe.matmul(psum[:, 0, 0:2 * DEPTH], lhsT=ones_f[0:1, :], rhs=lam2[:].rearrange("p l a -> p (l a)"), start=True, stop=True)], [lam2.b, kon.b], [PB[0]])
            kb.op("dve", lambda: V.tensor_copy(out=lamc[:].rearrange("p l a -> p (l a)"), in_=psum[:, 0, 0:2 * DEPTH]), [PB[0]], [lamc.b])
            kb.barrier()

        xTv = xT.rearrange("(c p) t -> p c t", p=128)
        xIv = xT_in.rearrange("(c p) t -> p c t", p=128)
        yOv = yout.rearrange("(c p) t -> p c t", p=128)
        NTT = TT // 128

        def blk_of(tt):
            t = tt * 128
            for bi, (t0, nb, col) in enumerate(blocks):
                if t0 <= t < t0 + nb:
                    return bi

        evac_rr = [0]

        def evac(out, in_, rb, wb, scale=None):
            evac_rr[0] += 1
            if evac_rr[0] % 2 == 0:
                if scale is None:
                    return kb.op("act", lambda: A.copy(out=out, in_=in_), rb, wb)
                return kb.op("act", lambda: A.mul(out=out, in_=in_, mul=scale), rb, wb)
            if scale is None:
                return kb.op("dve", lambda: V.tensor_copy(out=out, in_=in_), rb, wb)
            return kb.op("dve", lambda: V.tensor_scalar(out=out, in0=in_, scalar1=scale, scalar2=None, op0=ALU.mult), rb, wb)

        def run_threads(specs):
            th = [[g, 0, float(tot)] for g, tot in specs]
            while th:
                t_ = min(th, key=lambda x: x[1] / x[2])
                try:
                    next(t_[0])
                    t_[1] += 1
                except StopIteration:
                    th.remove(t_)

        def phase_AB(l):
            with ExitStack() as s2:
                hT = T(s2, nc, "hT", [128, 8, TT], BF16)
                hTb = [Buf("hT%d" % i) for i in range(len(blocks))]
                xa_ = [T(s2, nc, "xa%d" % i, [128, 8, 512], F32) for i in range(2)]
                sq_ = [T(s2, nc, "sq%d" % i, [128, 8, 512], BF16) for i in range(2)]
                rs_ = [T(s2, nc, "rs%d" % i, [128, 512], F32) for i in range(2)]
                tmp = [T(s2, nc, "tmpa%d" % i, [128, 512], F32) for i in range(2)]
                wf = [T(s2, nc, "wf%d" % i, [128, 8, 512], F32) for i in range(2)]
                wb = [T(s2, nc, "wb%d" % i, [128, 8, 512], BF16) for i in range(2)]
                ost = [T(s2, nc, "ost%d" % i, [128, 512], BF16) for i in range(4)]
                osx = [T(s2, nc, "osx%d" % i, [128, 512], F32) for i in range(2)]
                w_inv = w_in[l].rearrange("(c p) n -> p c n", p=128)

                def loadw(si):
                    c0, wd, kind = SEGS[si]
                    f, b = wf[si % 2], wb[si % 2]
                    kb.dma("sp", f.b.name, f[:, :, 0:wd], w_inv[:, :, c0:c0 + wd], writes=[f.b])
                    kb.op("pool", lambda: P.tensor_copy(out=b[:, :, 0:wd], in_=f[:, :, 0:wd]), [f.b], [b.b])

                loadw(0)

                def loadx(bi):
                    t0, nb, col = blocks[bi]
                    kb.dma("sp", xa_[bi % 2].b.name, xa_[bi % 2][:, :, 0:nb], (xIv if l == 0 else xTv)[:, :, t0:t0 + nb], writes=[xa_[bi % 2].b])
                loadx(0)
                for bi, (t0, nb, col) in enumerate(blocks):
                    if bi + 1 < len(blocks):
                        loadx(bi + 1)
                    xa, sq, rs = xa_[bi % 2], sq_[bi % 2], rs_[bi % 2]
                    kb.op("act", lambda: A.activation(out=sq[:, :, 0:nb], in_=xa[:, :, 0:nb], func=AF.Square), [xa.b], [sq.b])
                    kb.pe([lambda c=c: PEe.matmul(psum[:, bi % 2, 0:nb], lhsT=ones_b, rhs=sq[:, c, 0:nb], start=(c == 0), stop=(c == 7)) for c in range(8)],
                          [sq.b, konb.b], [PB[bi % 2]])
                    kb.op("act", lambda: A.activation(out=rs[:, 0:nb], in_=psum[:, bi % 2, 0:nb], func=AF.Ln, bias=epsc[:, 0:1], scale=1.0 / D), [PB[bi % 2], epsc.b], [rs.b])
                    kb.op("act", lambda: A.activation(out=rs[:, 0:nb], in_=rs[:, 0:nb], func=AF.Exp, scale=-0.5), [rs.b], [rs.b])
                    for c in range(8):
                        tm = tmp[c % 2]
                        kb.op("dve", lambda: V.scalar_tensor_tensor(out=tm[:, 0:nb], in0=xa[:, c, 0:nb], scalar=gsc[:, l, c, col:col + 1], in1=rs[:, 0:nb],
                                                                    op0=ALU.mult, op1=ALU.mult), [xa.b, rs.b, gsc.b], [tm.b])
                        kb.op("act", lambda: A.activation(out=hT[:, c, t0:t0 + nb], in_=tm[:, 0:nb], func=AF.Identity, bias=modT[:, l, c, col:col + 1], scale=1.0),
                              [tm.b, modT.b], [hTb[bi]])
                k = 0
                pk = 0
                for si, (c0, wd, kind) in enumerate(SEGS):
                    if si + 1 < len(SEGS):
                        loadw(si + 1)
                    b = wb[si % 2]
                    if kind in "FX":
                        for bi, (t0, nb, col) in enumerate(blocks):
                            for n4 in range(wd // 128):
                                bank = 2 + pk % 6
                                pk += 1
                                kb.pe([lambda c=c: PEe.matmul(psum[:, bank, 0:nb], lhsT=b[:, c, n4 * 128:(n4 + 1) * 128], rhs=hT[:, c, t0:t0 + nb], start=(c == 0), stop=(c == 7))
                                       for c in range(8)], [b.b, hTb[bi]], [PB[bank]])
                                if kind == "F":
                                    o = ost[k % 4]
                                    k += 1
                                    evac(o[:, 0:nb], psum[:, bank, 0:nb], [PB[bank]], [o.b])
                                    kb.dma("sp", o.b.name, PF[c0 + n4 * 128:c0 + (n4 + 1) * 128, t0:t0 + nb], o[:, 0:nb], reads=[o.b])
                                else:
                                    o = osx[k % 2]
                                    k += 1
                                    evac(o[:, 0:nb], psum[:, bank, 0:nb], [PB[bank]], [o.b])
                                    kb.dma("sp", o.b.name, PX[n4 * 128:(n4 + 1) * 128, t0:t0 + nb], o[:, 0:nb], reads=[o.b])
                    else:
                        for tt in range(NTT):
                            bank = 2 + pk % 6
                            pk += 1
                            kb.pe([lambda c=c: PEe.matmul(psum[:, bank, 0:512], lhsT=hT[:, c, tt * 128:(tt + 1) * 128], rhs=b[:, c, 0:512], start=(c == 0), stop=(c == 7))
                                   for c in range(8)], [b.b, hTb[blk_of(tt)]], [PB[bank]])
                            o = ost[k % 4]
                            k += 1
                            evac(o[:, :], psum[:, bank, :], [PB[bank]], [o.b])
                            kb.dma("sp", o.b.name, PT[tt * 128:(tt + 1) * 128, TCOL[c0]:TCOL[c0] + 512], o[:, :], reads=[o.b])
                kb.barrier()

        def phase_C1(l, s2):
            if True:
                rT = T(s2, nc, "rT", [128, 4, 512], BF16)
                kT = T(s2, nc, "kT", [128, 4, 512], BF16)
                lw = T(s2, nc, "lw", [128, 512], F32)
                la = T(s2, nc, "la", [128, 512], F32)
                tl = T(s2, nc, "tl", [128, 512], BF16)
                lab = T(s2, nc, "lab", [128, 512], BF16)
                upf = T(s2, nc, "upf", [128, 2, 512], F32)
                upb = T(s2, nc, "upb", [128, 2, 512], BF16)
                names = ["kkraw", "rk", "kk", "sg", "a_", "akk", "t_", "kd", "csf", "cs", "cse", "Ep", "Em", "Epv", "tmpb"]
                f = {n: T(s2, nc, "c1_" + n, [128, 512], F32) for n in names}
                sqk = T(s2, nc, "sqk", [128, 512], BF16)
                ob = {n: [T(s2, nc, "c1o_%s%d" % (n, i), [128, 512], BF16) for i in range(2)] for n in ["kkg", "rg", "aki", "ki"]}
                gcs = [T(s2, nc, "gcs%d" % i, [128, 8], F32) for i in range(2)]
                tk = [T(s2, nc, "tk%d" % i, [128, 2, 4, 128], BF16) for i in range(2)]
                kb.dma("sp", "upf", upf[:, 0, :], dup[l], writes=[upf.b])
                kb.dma("sp", "upf", upf[:, 1, :], iup[l], writes=[upf.b])
                kb.op("dve", lambda: V.tensor_copy(out=upb[:], in_=upf[:]), [upf.b], [upb.b])
                it = 0
                for bi, (t0, nb, col) in enumerate(blocks):
                    nchk = nb // 64
                    ch0 = t0 // 64
                    ntt = nb // 128
                    kb.dma("sp", "rT", rT[:, :, 0:nb], PF[0:512, t0:t0 + nb].rearrange("(c p) t -> p c t", p=128), writes=[rT.b])
                    kb.dma("sp", "kT", kT[:, :, 0:nb], PF[512:1024, t0:t0 + nb].rearrange("(c p) t -> p c t", p=128), writes=[kT.b])
                    kb.dma("sp", "lw", lw[:, 0:nb], PX[0:128, t0:t0 + nb], writes=[lw.b])
                    kb.dma("sp", "la", la[:, 0:nb], PX[128:256, t0:t0 + nb], writes=[la.b])
                    kb.op("act", lambda: A.activation(out=tl[:, 0:nb], in_=lw[:, 0:nb], func=AF.Tanh), [lw.b], [tl.b])
                    kb.op("dve", lambda: V.tensor_copy(out=lab[:, 0:nb], in_=la[:, 0:nb]), [la.b], [lab.b])
                    for cc_ in range(4):
                        S = slice(0, nb)
                        kb.op("dve", lambda: V.tensor_scalar(out=f["kkraw"][:, S], in0=kT[:, cc_, S], scalar1=pp[:, l, 48 + cc_:49 + cc_], scalar2=None, op0=ALU.mult),
                              [kT.b, pp.b], [f["kkraw"].b])
                        kb.op("act", lambda: A.activation(out=sqk[:, S], in_=f["kkraw"][:, S], func=AF.Square), [f["kkraw"].b], [sqk.b])
                        kb.pe([lambda: PEe.matmul(psum[:, 0, S], lhsT=blk_b, rhs=sqk[:, S], start=True, stop=True)], [sqk.b, konb.b], [PB[0]])
                        kb.op("act", lambda: A.activation(out=f["rk"][:, S], in_=psum[:, 0, S], func=AF.Ln, bias=epsc[:, 1:2], scale=1.0), [PB[0], epsc.b], [f["rk"].b])
                        kb.op("act", lambda: A.activation(out=f["rk"][:, S], in_=f["rk"][:, S], func=AF.Exp, scale=-0.5), [f["rk"].b], [f["rk"].b])
                        kb.op("dve", lambda: V.tensor_tensor(out=f["kk"][:, S], in0=f["kkraw"][:, S], in1=f["rk"][:, S], op=ALU.mult), [f["kkraw"].b, f["rk"].b], [f["kk"].b])
                        for d in range(2):
                            R = slice(d * 64, (d + 1) * 64)
                            C = slice(cc_ * 128, (cc_ + 1) * 128)
                            o = {n: ob[n][it % 2] for n in ob}
                            g_ = gcs[it % 2]
                            tk_ = tk[it % 2]
                            it += 1
                            kb.pe([lambda: PEe.matmul(psum[:, 1, S], lhsT=upb[R, 0, C], rhs=tl[R, S], start=True, stop=True)], [upb.b, tl.b], [PB[1]])
                            kb.op("act", lambda: A.activation(out=f["sg"][:, S], in_=psum[:, 1, S], func=AF.Sigmoid, bias=pp[:, l, 32 + d * 4 + cc_:33 + d * 4 + cc_], scale=1.0),
                                  [PB[1], pp.b], [f["sg"].b])
                            kb.pe([lambda: PEe.matmul(psum[:, 2, S], lhsT=upb[R, 1, C], rhs=lab[R, S], start=True, stop=True)], [upb.b, lab.b], [PB[2]])
                            kb.op("act", lambda: A.activation(out=f["a_"][:, S], in_=psum[:, 2, S], func=AF.Sigmoid, bias=pp[:, l, 40 + d * 4 + cc_:41 + d * 4 + cc_], scale=1.0),
                                  [PB[2], pp.b], [f["a_"].b])
                            kb.op("pool", lambda: P.tensor_tensor(out=f["akk"][:, S], in0=f["a_"][:, S], in1=f["kk"][:, S], op=ALU.mult), [f["a_"].b, f["kk"].b], [f["akk"].b])
                            kb.op("dve", lambda: V.tensor_scalar(out=f["t_"][:, S], in0=f["a_"][:, S], scalar1=pp[:, l, 52 + cc_:53 + cc_], scalar2=omka[:, l, cc_:cc_ + 1],
                                                                 op0=ALU.mult, op1=ALU.add), [f["a_"].b, pp.b, omka.b], [f["t_"].b])
                            kb.op("pool", lambda: P.tensor_tensor(out=f["kd"][:, S], in0=f["t_"][:, S], in1=kT[:, cc_, S], op=ALU.mult), [f["t_"].b, kT.b], [f["kd"].b])
                            kb.op("dve", lambda: V.tensor_tensor_scan(out=f["csf"][:, S], data0=ltri[:, S], data1=f["sg"][:, S], initial=0.0, op0=ALU.mult, op1=ALU.add),
                                  [ltri.b, f["sg"].b], [f["csf"].b])
                            if d == 0:
                                cs = f["csf"]
                            else:
                                cs = f["cs"]
                                kb.op("dve", lambda: V.tensor_tensor(out=f["tmpb"][:, S], in0=f["sg"][:, S], in1=f["csf"][:, S], op=ALU.subtract), [f["sg"].b, f["csf"].b], [f["tmpb"].b])
                                kb.op("dve", lambda: V.tensor_tensor(out=cs[:, S].rearrange("p (c x) -> p c x", x=64), in0=f["tmpb"][:, S].rearrange("p (c x) -> p c x", x=64),
                                                                     in1=f["csf"][:, S].rearrange("p (c x) -> p c x", x=64)[:, :, 63:64].to_broadcast([128, nchk, 64]), op=ALU.add),
                                      [f["tmpb"].b, f["csf"].b], [cs.b])
                            kb.op("pool", lambda: P.tensor_tensor(out=f["cse"][:, S], in0=cs[:, S], in1=f["sg"][:, S], op=ALU.subtract), [cs.b, f["sg"].b], [f["cse"].b])
                            kb.op("act", lambda: A.activation(out=f["Ep"][:, S], in_=cs[:, S], func=AF.Exp, scale=-C0), [cs.b], [f["Ep"].b])
                            kb.op("act", lambda: A.activation(out=f["Em"][:, S], in_=cs[:, S], func=AF.Exp, scale=C0), [cs.b], [f["Em"].b])
                            kb.op("act", lambda: A.activation(out=f["Epv"][:, S], in_=f["cse"][:, S], func=AF.Exp, scale=-C0), [f["cse"].b], [f["Epv"].b])
                            kb.op("dve", lambda: V.tensor_tensor(out=o["rg"][:, S], in0=rT[:, cc_, S], in1=f["Ep"][:, S], op=ALU.mult), [rT.b, f["Ep"].b], [o["rg"].b])
                            kb.op("pool", lambda: P.tensor_tensor(out=o["kkg"][:, S], in0=f["kk"][:, S], in1=f["Epv"][:, S], op=ALU.mult), [f["kk"].b, f["Epv"].b], [o["kkg"].b])
                            kb.op("dve", lambda: V.tensor_tensor(out=o["aki"][:, S], in0=f["akk"][:, S], in1=f["Em"][:, S], op=ALU.mult), [f["akk"].b, f["Em"].b], [o["aki"].b])
                            kb.op("pool", lambda: P.tensor_tensor(out=o["ki"][:, S], in0=f["kd"][:, S], in1=f["Em"][:, S], op=ALU.mult), [f["kd"].b, f["Em"].b], [o["ki"].b])
                            e_ = 63 if d == 0 else 0
                            kb.op("dve", lambda: V.tensor_copy(out=g_[:, 0:nchk], in_=f["Ep"][:, S].rearrange("p (c x) -> p c x", x=64)[:, :, e_]), [f["Ep"].b], [g_.b])
                            kb.dma("sp", o["kkg"].b.name, QGs[d][C, ch0:ch0 + nchk, 0, :], o["kkg"][:, S].rearrange("p (c x) -> p c x", x=64), reads=[o["kkg"].b])
                            kb.dma("sp", o["rg"].b.name, QGs[d][C, ch0:ch0 + nchk, 1, :], o["rg"][:, S].rearrange("p (c x) -> p c x", x=64), reads=[o["rg"].b])
                            kb.dma("sp", o["aki"].b.name, AKT[d][C, t0:t0 + nb], o["aki"][:, S], reads=[o["aki"].b])
                            kb.dma("sp", o["ki"].b.name, KIT[d][C, t0:t0 + nb], o["ki"][:, S], reads=[o["ki"].b])
                            kb.dma("sp", g_.b.name, GCs[d][C, ch0:ch0 + nchk], g_[:, 0:nchk], reads=[g_.b])
                            fns = []
                            for qi, nm in enumerate(["aki", "ki"]):
                                for tt in range(ntt):
                                    fns.append(lambda qi=qi, nm=nm, tt=tt: PEe.transpose(psbf[3][:, (qi * 4 + tt) * 128:(qi * 4 + tt + 1) * 128], o[nm][:, tt * 128:(tt + 1) * 128], ident_b))
                            kb.pe(fns, [o["aki"].b, o["ki"].b, konb.b], [PB[3]])
                            kb.op("act", lambda: A.copy(out=tk_[:].rearrange("p q t c -> p (q t c)"), in_=psbf[3][:, :]), [PB[3]], [tk_.b])
                            kb.dma("sp", tk_.b.name + "a", AKK[d][t0:t0 + nb, C].rearrange("(t p) c -> p t c", p=128), tk_[:, 0, 0:ntt, :], reads=[tk_.b])
                            kb.dma("sp", tk_.b.name + "b", KIK[d][t0:t0 + nb, C].rearrange("(t p) c -> p t c", p=128), tk_[:, 1, 0:ntt, :], reads=[tk_.b])
                            yield

        def phase_C2(l):
            with ExitStack() as s2:
                def mk(nm, shape, dt):
                    ts = [T(s2, nc, "%s%d" % (nm, i), shape, dt) for i in range(2)]
                    for t_ in ts:
                        t_.b2 = Buf(t_.b.name + "x")
                    return ts
                QG = mk("QG", [128, 8, 8, 128], BF16)
                AT = mk("AT", [128, 8, 512], BF16)
                KT_ = mk("KT", [128, 8, 512], BF16)
                AK = mk("AK", [128, 8, 512], BF16)
                KK_ = mk("KK", [128, 8, 512], BF16)
                VV = mk("VV", [128, 8, 512], BF16)
                GC = mk("GC", [128, 8, 8], F32)
                Z = T(s2, nc, "Z", [128, 8, 64], F32)
                Zb = T(s2, nc, "Zb", [128, 8, 64], BF16)
                NT = [T(s2, nc, "NT%d" % i, [128, 8, 64], BF16) for i in range(5)]
                N_ = [T(s2, nc, "N%d" % i, [128, 8, 64], BF16) for i in range(6)]
                PTt = [T(s2, nc, "PTt%d" % i, [128, 8, 64], BF16) for i in range(2)]
                BabT = T(s2, nc, "BabT", [128, 8, 64], BF16)
                AakT = T(s2, nc, "AakT", [128, 8, 64], BF16)
                BakT = T(s2, nc, "BakT", [128, 8, 64], BF16)
                Wn = T(s2, nc, "Wn", [128, 8, 64], BF16)
                U = T(s2, nc, "U", [128, 8, 64], BF16)
                Yst = [T(s2, nc, "Yst%d" % i, [128, 512], F32) for i in range(2)]
                yi = [0]
                DS = [slice(0, 64), slice(64, 128)]
                nB = len(blocks)
                fo = list(range(nB))
                bo = [0] + list(range(nB - 1, 0, -1))

                def hv(bank):
                    return psum[:, bank, :].rearrange("p (h i) -> p h i", h=8)

                def load(s, par):
                    for d in range(2):
                        t0, nb, col = blocks[fo[s] if d == 0 else bo[s]]
                        nchk, ch0 = nb // 64, t0 // 64
                        if d == 0:
                            csl = slice(ch0, ch0 + nchk)
                        else:
                            csl = slice(ch0 + nchk - 1, (ch0 - 1) if ch0 > 0 else None, -1)
                        D_ = DS[d]
                        bsel = (lambda t_: t_.b) if d == 0 else (lambda t_: t_.b2)
                        sfx = "f" if d == 0 else "b"
                        def ld(t_, dst, src):
                            kb.dma("sp", t_.b.name + sfx, dst, src, writes=[bsel(t_)])

                        def ldh(t_, dstf, srcf):
                            if d == 0:
                                ld(t_, dstf(slice(None)), srcf(slice(None)))
                            else:
                                kb.dma_group("sp", t_.b.name + sfx, [(dstf(h), srcf(h)) for h in range(8)], writes=[bsel(t_)])
                        qsrc = QGs[d].rearrange("(h k) c w i -> k h c (w i)", k=64)
                        ldh(QG[par], lambda h: QG[par][D_, h, 0:nchk, :], lambda h: qsrc[:, h, csl, :])
                        asrc = AKT[d].rearrange("(h k) (c x) -> k h c x", k=64, x=64)
                        ldh(AT[par], lambda h: AT[par][D_, h, 0:nb].rearrange("p (c x) -> p c x", x=64) if not isinstance(h, slice) else AT[par][D_, :, 0:nb].rearrange("p h (c x) -> p h c x", x=64),
                            lambda h: asrc[:, h, csl, :])
                        ksrc = KIT[d].rearrange("(h k) (c x) -> k h c x", k=64, x=64)
                        ldh(KT_[par], lambda h: KT_[par][D_, h, 0:nb].rearrange("p (c x) -> p c x", x=64) if not isinstance(h, slice) else KT_[par][D_, :, 0:nb].rearrange("p h (c x) -> p h c x", x=64),
                            lambda h: ksrc[:, h, csl, :])
                        ld(AK[par], AK[par][D_, 0:nchk, :], AKK[d].rearrange("(c p) n -> p c n", p=64)[:, csl, :])
                        ld(KK_[par], KK_[par][D_, 0:nchk, :], KIK[d].rearrange("(c p) n -> p c n", p=64)[:, csl, :])
                        ld(VV[par], VV[par][D_, 0:nchk, :], PT[:, 0:512].rearrange("(c p) n -> p c n", p=64)[:, csl, :])
                        ld(GC[par], GC[par][D_, :, 0:nchk], GCs[d].rearrange("(h k) c -> k h c", k=64)[:, :, ch0:ch0 + nchk])

                def chunk(par, j, nchk, tokf, tokb):
                    qg, at, kt, ak, kk_, vv, gc = QG[par], AT[par], KT_[par], AK[par], KK_[par], VV[par], GC[par]
                    bb = lambda *ts: [x for t_ in ts for x in (t_.b, t_.b2)]
                    cs = slice(j * 64, j * 64 + 64)
                    H = lambda h: slice(h * 64, (h + 1) * 64)
                    HD = [(h, D_) for h in range(8) for D_ in DS]
                    fns = []
                    for h, D_ in HD:
                        fns.append(lambda h=h, D_=D_: PEe.matmul(psum[D_, h // 4, (h % 4) * 128:(h % 4) * 128 + 128], lhsT=at[D_, h, cs], rhs=qg[D_, h, j, :], start=True, stop=True))
                    for h, D_ in HD:
                        fns.append(lambda h=h, D_=D_: PEe.matmul(psum[D_, 2 + h // 4, (h % 4) * 128:(h % 4) * 128 + 128], lhsT=kt[D_, h, cs], rhs=qg[D_, h, j, :], start=True, stop=True))
                    for h, D_ in HD:
                        fns.append(lambda h=h, D_=D_: PEe.matmul(psum[D_, 4, H(h)], lhsT=qg[D_, h, j, 0:64], rhs=at[D_, h, cs], start=True, stop=True))
                    kb.pe(fns, bb(qg, at, kt), [PB[0], PB[1], PB[2], PB[3], PB[4]])
                    P1v = psum[:, 0:2, :].rearrange("p b (h w i) -> p (b h) w i", h=4, w=2)
                    P2v = psum[:, 2:4, :].rearrange("p b (h w i) -> p (b h) w i", h=4, w=2)
                    kb.op("dve", lambda: V.scalar_tensor_tensor(out=NT[0][:], in0=P1v[:, :, 0, :], scalar=-1.0, in1=msk2[:, 0], op0=ALU.mult, op1=ALU.mult), [PB[0], PB[1], msk2.b], [NT[0].b])
                    kb.op("dve", lambda: V.scalar_tensor_tensor(out=N_[0][:], in0=hv(4), scalar=-1.0, in1=msk2[:, 2], op0=ALU.mult, op1=ALU.mult), [PB[4], msk2.b], [N_[0].b])
                    kb.op("dve", lambda: V.tensor_tensor(out=AakT[:], in0=P2v[:, :, 0, :], in1=msk2[:, 0], op=ALU.mult), [PB[2], PB[3], msk2.b], [AakT.b])
                    kb.op("dve", lambda: V.tensor_tensor(out=BabT[:], in0=P1v[:, :, 1, :], in1=msk2[:, 1], op=ALU.mult), [PB[0], PB[1], msk2.b], [BabT.b])
                    kb.op("dve", lambda: V.tensor_tensor(out=BakT[:], in0=P2v[:, :, 1, :], in1=msk2[:, 1], op=ALU.mult), [PB[2], PB[3], msk2.b], [BakT.b])
                    kb.op("pool", lambda: P.tensor_tensor(out=PTt[0][:], in0=NT[0][:], in1=msk2[:, 3], op=ALU.add), [NT[0].b, msk2.b], [PTt[0].b])
                    fns = []
                    for h, D_ in HD:
                        fns.append(lambda h=h, D_=D_: PEe.matmul(psum[D_, 5, H(h)], lhsT=qg[D_, h, j, 0:64], rhs=Zb[D_, h, :], start=True, stop=False))
                        fns.append(lambda h=h, D_=D_: PEe.matmul(psum[D_, 5, H(h)], lhsT=AakT[D_, h, :], rhs=vv[D_, j, H(h)], start=False, stop=True))
                    kb.pe(fns, bb(qg, vv) + [Zb.b, AakT.b], [PB[5]])
                    kb.op("act", lambda: A.mul(out=Wn[:], in_=hv(5), mul=-1.0), [PB[5]], [Wn.b])
                    for k in range(5):
                        fns = [lambda h=h, D_=D_: PEe.matmul(psum[D_, 6, H(h)], lhsT=NT[k][D_, h, :], rhs=N_[k][D_, h, :], start=True, stop=True) for h, D_ in HD]
                        kb.pe(fns, [NT[k].b, N_[k].b], [PB[6]])
                        kb.op("act", lambda: A.copy(out=N_[k + 1][:], in_=hv(6)), [PB[6]], [N_[k + 1].b])
                        if k < 4:
                            fns = [lambda h=h, D_=D_: PEe.matmul(psum[D_, 7, H(h)], lhsT=N_[k][D_, h, :], rhs=NT[k][D_, h, :], start=True, stop=True) for h, D_ in HD]
                            kb.pe(fns, [NT[k].b, N_[k].b], [PB[7]])
                            kb.op("act", lambda: A.copy(out=NT[k + 1][:], in_=hv(7)), [PB[7]], [NT[k + 1].b])
                        src, dst = PTt[k % 2], PTt[(k + 1) % 2]
                        fns = [lambda h=h, D_=D_: PEe.matmul(psum[D_, 0, H(h)], lhsT=N_[k + 1][D_, h, :], rhs=src[D_, h, :], start=True, stop=True) for h, D_ in HD]
                        kb.pe(fns, [N_[k + 1].b, src.b], [PB[0]])
                        kb.op("dve", lambda: V.tensor_tensor(out=dst[:], in0=hv(0), in1=src[:], op=ALU.add), [PB[0], src.b], [dst.b])
                    TTf = PTt[1]
                    fns = [lambda h=h, D_=D_: PEe.matmul(psum[D_, 1, H(h)], lhsT=TTf[D_, h, :], rhs=Wn[D_, h, :], start=True, stop=True) for h, D_ in HD]
                    kb.pe(fns, [TTf.b, Wn.b], [PB[1]])
                    kb.op("dve", lambda: V.tensor_copy(out=U[:], in_=hv(1)), [PB[1]], [U.b])
                    fns = []
                    for h, D_ in HD:
                        fns.append(lambda h=h, D_=D_: PEe.matmul(psum[D_, 2, H(h)], lhsT=qg[D_, h, j, 64:128], rhs=Zb[D_, h, :], start=True, stop=False))
                        fns.append(lambda h=h, D_=D_: PEe.matmul(psum[D_, 2, H(h)], lhsT=BabT[D_, h, :], rhs=U[D_, h, :], start=False, stop=False))
                        fns.append(lambda h=h, D_=D_: PEe.matmul(psum[D_, 2, H(h)], lhsT=BakT[D_, h, :], rhs=vv[D_, j, H(h)], start=False, stop=True))
                    kb.pe(fns, bb(qg, vv) + [Zb.b, BabT.b, U.b, BakT.b], [PB[2]])
                    ys = Yst[yi[0] % 2]
                    yi[0] += 1
                    kb.op("act", lambda: A.copy(out=ys[:], in_=psum[:, 2, :]), [PB[2]], [ys.b])
                    kb.dma("sp", ys.b.name + "f", YS[0][tokf:tokf + 64, :], ys[0:64, :], reads=[ys.b])
                    kb.dma("sp", ys.b.name + "b", YS[1][tokb:tokb + 64, :], ys[64:128, :], reads=[ys.b])
                    fns = []
                    for h, D_ in HD:
                        fns.append(lambda h=h, D_=D_: PEe.matmul(psum[D_, 3, H(h)], lhsT=ak[D_, j, H(h)], rhs=U[D_, h, :], start=True, stop=False))
                        fns.append(lambda h=h, D_=D_: PEe.matmul(psum[D_, 3, H(h)], lhsT=kk_[D_, j, H(h)], rhs=vv[D_, j, H(h)], start=False, stop=True))
                    kb.pe(fns, bb(ak, kk_, vv) + [U.b], [PB[3]])
                    kb.op("dve", lambda: V.tensor_tensor(out=Z[:], in0=hv(3), in1=Z[:], op=ALU.add), [PB[3], Z.b], [Z.b])
                    jb = nchk - 1 - j
                    kb.op("dve", lambda: V.tensor_tensor(out=Z[0:64], in0=Z[0:64], in1=gc[0:64, :, j:j + 1].to_broadcast([64, 8, 64]), op=ALU.mult), [Z.b, gc.b], [Z.b])
                    kb.op("pool", lambda: P.tensor_tensor(out=Z[64:128], in0=Z[64:128], in1=gc[64:128, :, jb:jb + 1].to_broadcast([64, 8, 64]), op=ALU.mult), [Z.b, gc.b2], [Z.b])
                    kb.op("act", lambda: A.copy(out=Zb[:], in_=Z[:]), [Z.b], [Zb.b])

                kb.op("dve", lambda: V.memset(Z[:], 0.0), [], [Z.b])
                kb.op("dve", lambda: V.memset(Zb[:], 0.0), [], [Zb.b])
                load(0, 0)
                for s in range(nB):
                    par = s % 2
                    if s + 1 < nB:
                        load(s + 1, 1 - par)
                    t0f, nb, _ = blocks[fo[s]]
                    t0b, _, _ = blocks[bo[s]]
                    nchk = nb // 64
                    for j in range(nchk):
                        chunk(par, j, nchk, t0f + j * 64, t0b + (nchk - 1 - j) * 64)
                kb.barrier()

        def phase_C3(l, s2, part=0, nparts=1):
            if True:
                bk = 6 + part
                bc = T(s2, nc, "bcg", [128, 2, 512], F32)
                kb.dma("sp", "bcg%d" % part, bc[:].rearrange("p a n -> p (a n)"), BCd[:, l * 1024:(l + 1) * 1024], writes=[bc.b])
                def mk(nm, shape, dt):
                    return [T(s2, nc, "%s%d_%d" % (nm, i, part), shape, dt) for i in range(2)]
                yf, yb = mk("yf", [128, 512], F32), mk("yb", [128, 512], F32)
                vz = mk("vz", [128, 1024], BF16)
                rk_ = mk("rk_", [128, 2, 4, 128], BF16)
                rkr = mk("rkr", [128, 4, 128], BF16)
                y = mk("y_", [128, 512], F32)
                ysq = mk("ysq", [128, 512], F32)
                st_ = mk("st_", [128, 6, 8], F32)
                sz = mk("sz", [128, 512], F32)
                bon = mk("bon", [128, 512], F32)
                yg = mk("yg", [128, 512], BF16)
                ygT = mk("ygT", [128, 4, 128], BF16)
                for ti, tt in enumerate(range(part, NTT, nparts)):
                    p = ti % 2
                    ts_ = slice(tt * 128, (tt + 1) * 128)
                    kb.dma("sp", yf[p].b.name, yf[p][:], YS[0][ts_, :], writes=[yf[p].b])
                    kb.dma("sp", yb[p].b.name, yb[p][:], YS[1][ts_, :], writes=[yb[p].b])
                    kb.dma("sp", vz[p].b.name, vz[p][:], PT[ts_, 0:1024], writes=[vz[p].b])
                    kb.dma("sp", rk_[p].b.name, rk_[p][:].rearrange("p a c t -> p (a c) t"), PF[0:1024, ts_].rearrange("(c p) t -> p c t", p=128), writes=[rk_[p].b])
                    for c in range(4):
                        kb.op("dve", lambda: V.scalar_tensor_tensor(out=rkr[p][:, c, :], in0=rk_[p][:, 0, c, :], scalar=pp[:, l, 56 + c:57 + c], in1=rk_[p][:, 1, c, :],
                                                                     op0=ALU.mult, op1=ALU.mult), [rk_[p].b, pp.b], [rkr[p].b])
                    kb.pe([lambda c=c: PEe.matmul(psum[:, bk, 0:8], lhsT=rkr[p][:, c, :], rhs=sel_b[:, c * 8:(c + 1) * 8], start=(c == 0), stop=(c == 3)) for c in range(4)],
                          [rkr[p].b, konb.b], [PB[bk]])
                    S_ = st_[p]
                    kb.op("act", lambda: A.copy(out=S_[:, 5, :], in_=psum[:, bk, 0:8]), [PB[bk]], [S_.b])
                    kb.op("pool", lambda: P.tensor_tensor(out=y[p][:], in0=yf[p][:], in1=yb[p][:], op=ALU.add), [yf[p].b, yb[p].b], [y[p].b])
                    y3 = y[p][:].rearrange("p (h x) -> p h x", h=8)
                    kb.op("dve", lambda: V.tensor_reduce(out=S_[:, 0, :], in_=y3, axis=AX.X, op=ALU.add), [y[p].b], [S_.b])
                    kb.op("act", lambda: A.activation(out=ysq[p][:], in_=y[p][:], func=AF.Square), [y[p].b], [ysq[p].b])
                    kb.op("dve", lambda: V.tensor_reduce(out=S_[:, 1, :], in_=ysq[p][:].rearrange("p (h x) -> p h x", h=8), axis=AX.X, op=ALU.add), [ysq[p].b], [S_.b])
                    kb.op("dve", lambda: V.tensor_scalar(out=S_[:, 2, :], in0=S_[:, 0, :], scalar1=1.0 / 64, scalar2=None, op0=ALU.mult), [S_.b], [S_.b])
                    kb.op("dve", lambda: V.tensor_tensor(out=S_[:, 3, :], in0=S_[:, 2, :], in1=S_[:, 2, :], op=ALU.mult), [S_.b], [S_.b])
                    kb.op("dve", lambda: V.scalar_tensor_tensor(out=S_[:, 3, :], in0=S_[:, 1, :], scalar=1.0 / 64, in1=S_[:, 3, :], op0=ALU.mult, op1=ALU.subtract), [S_.b], [S_.b])
                    kb.op("act", lambda: A.activation(out=S_[:, 4, :], in_=S_[:, 3, :], func=AF.Sqrt, bias=GN_EPS, scale=1.0), [S_.b], [S_.b])
                    kb.op("dve", lambda: V.reciprocal(out=S_[:, 4, :], in_=S_[:, 4, :]), [S_.b], [S_.b])
                    kb.op("dve", lambda: V.tensor_tensor(out=y3, in0=y3, in1=S_[:, 2, :].unsqueeze(2).to_broadcast([128, 8, 64]), op=ALU.subtract), [y[p].b, S_.b], [y[p].b])
                    kb.op("dve", lambda: V.tensor_tensor(out=y3, in0=y3, in1=S_[:, 4, :].unsqueeze(2).to_broadcast([128, 8, 64]), op=ALU.mult), [y[p].b, S_.b], [y[p].b])
                    kb.op("pool", lambda: P.tensor_tensor(out=y[p][:], in0=y[p][:], in1=bc[:, 0, :], op=ALU.mult), [y[p].b, bc.b], [y[p].b])
                    kb.op("pool", lambda: P.tensor_tensor(out=y[p][:], in0=y[p][:], in1=bc[:, 1, :], op=ALU.add), [y[p].b, bc.b], [y[p].b])
                    kb.op("dve", lambda: V.tensor_tensor(out=bon[p][:].rearrange("p (h x) -> p h x", h=8), in0=vz[p][:, 0:512].rearrange("p (h x) -> p h x", h=8),
                                                         in1=S_[:, 5, :].unsqueeze(2).to_broadcast([128, 8, 64]), op=ALU.mult), [vz[p].b, S_.b], [bon[p].b])
                    kb.op("pool", lambda: P.tensor_tensor(out=y[p][:], in0=y[p][:], in1=bon[p][:], op=ALU.add), [y[p].b, bon[p].b], [y[p].b])
                    kb.op("act", lambda: A.activation(out=sz[p][:], in_=vz[p][:, 512:1024], func=AF.Silu), [vz[p].b], [sz[p].b])
                    kb.op("dve", lambda: V.tensor_tensor(out=yg[p][:], in0=y[p][:], in1=sz[p][:], op=ALU.mult), [y[p].b, sz[p].b], [yg[p].b])
                    kb.pe([lambda c=c: PEe.transpose(psbf[bk][:, 512 + c * 128:512 + (c + 1) * 128], yg[p][:, c * 128:(c + 1) * 128], ident_b) for c in range(4)], [yg[p].b, konb.b], [PB[bk]])
                    kb.op("act", lambda: A.copy(out=ygT[p][:].rearrange("p c t -> p (c t)"), in_=psbf[bk][:, 512:1024]), [PB[bk]], [ygT[p].b])
                    kb.dma("sp", ygT[p].b.name, YG[0:512, ts_].rearrange("(c p) t -> p c t", p=128), ygT[p][:], reads=[ygT[p].b])
                    yield

        def phase_D(l, s2):
            if True:
                gbz = T(s2, nc, "gbz", [128, 2, 4, 512], BF16)
                gcu = T(s2, nc, "gcu", [128, 2, 4, 514], BF16)
                gu = T(s2, nc, "gu", [128, 4, 514], F32)
                acc = T(s2, nc, "acc", [128, 4, 512], F32)
                szd = T(s2, nc, "szd", [128, 4, 512], F32)
                od = T(s2, nc, "od", [128, 4, 512], BF16)
                for bi, (t0, nb, col) in enumerate(blocks):
                    s0, s1 = (0, CTX) if col == 1 else (CTX, TT)
                    lo, hi = max(s0, t0 - 1), min(s1, t0 + nb + 1)
                    kb.op("pool", lambda: P.memset(gcu[:], 0.0), [], [gcu.b])
                    off = lo - (t0 - 1)
                    for j, r0 in enumerate([2816, 3328]):
                        kb.dma("sp", "gcu%d" % j, gcu[:, j, :, off:off + hi - lo], PF[r0:r0 + 512, lo:hi].rearrange("(c p) t -> p c t", p=128), writes=[gcu.b])
                    for j, r0 in enumerate([2304, 3840]):
                        kb.dma("sp", "gbz%d" % j, gbz[:, j, :, 0:nb], PF[r0:r0 + 512, t0:t0 + nb].rearrange("(c p) t -> p c t", p=128), writes=[gbz.b])
                    kb.op("dve", lambda: V.tensor_tensor(out=gu[:, :, 0:nb + 2], in0=gcu[:, 0, :, 0:nb + 2], in1=gcu[:, 1, :, 0:nb + 2], op=ALU.mult), [gcu.b], [gu.b])
                    for c in range(4):
                        cw = lambda j: pp[:, l, 60 + j * 4 + c:61 + j * 4 + c]
                        kb.op("dve", lambda: V.tensor_scalar(out=acc[:, c, 0:nb], in0=gu[:, c, 0:nb], scalar1=cw(0), scalar2=None, op0=ALU.mult), [gu.b, pp.b], [acc.b])
                        kb.op("dve", lambda: V.scalar_tensor_tensor(out=acc[:, c, 0:nb], in0=gu[:, c, 1:nb + 1], scalar=cw(1), in1=acc[:, c, 0:nb], op0=ALU.mult, op1=ALU.add), [gu.b, pp.b, acc.b], [acc.b])
                        kb.op("dve", lambda: V.scalar_tensor_tensor(out=acc[:, c, 0:nb], in0=gu[:, c, 2:nb + 2], scalar=cw(2), in1=acc[:, c, 0:nb], op0=ALU.mult, op1=ALU.add), [gu.b, pp.b, acc.b], [acc.b])
                        kb.op("dve", lambda: V.scalar_tensor_tensor(out=acc[:, c, 0:nb], in0=acc[:, c, 0:nb], scalar=pp[:, l, 72 + c:73 + c], in1=gbz[:, 0, c, 0:nb], op0=ALU.add, op1=ALU.mult),
                              [acc.b, pp.b, gbz.b], [acc.b])
                        yield
                    kb.op("act", lambda: A.activation(out=szd[:, :, 0:nb], in_=gbz[:, 1, :, 0:nb], func=AF.Silu), [gbz.b], [szd.b])
                    kb.op("pool", lambda: P.tensor_tensor(out=od[:, :, 0:nb], in0=acc[:, :, 0:nb], in1=szd[:, :, 0:nb], op=ALU.mult), [acc.b, szd.b], [od.b])
                    kb.dma("sp", "od", YG[512:1024, t0:t0 + nb].rearrange("(c p) t -> p c t", p=128), od[:, :, 0:nb], reads=[od.b])

        def phase_E1(l, s2):
            if True:
                pg = T(s2, nc, "pg", [128, 2, 128], BF16)
                qk = [T(s2, nc, "qk%d" % i, [128, 8, 512], BF16) for i in range(2)]
                cs_ = [T(s2, nc, "cs_%d" % i, [128, 2, 512], F32) for i in range(2)]
                sqe = [T(s2, nc, "sqe%d" % i, [128, 512], BF16) for i in range(2)]
                ri = [T(s2, nc, "ri%d" % i, [128, 512], F32) for i in range(2)]
                t1 = [T(s2, nc, "t1_%d" % i, [128, 512], F32) for i in range(2)]
                t2 = [T(s2, nc, "t2_%d" % i, [128, 512], F32) for i in range(2)]
                oq = [T(s2, nc, "oq%d" % i, [128, 512], BF16) for i in range(2)]
                for j in range(2):
                    kb.op("dve", lambda: V.tensor_scalar(out=pg[:, j, :], in0=perm_f, scalar1=pp[:, l, 76 + j:77 + j], scalar2=None, op0=ALU.mult), [kon.b, pp.b], [pg.b])
                for bi, (t0, nb, col) in enumerate(blocks):
                    S = slice(0, nb)
                    q_, c_ = qk[bi % 2], cs_[bi % 2]
                    kb.dma_group("sp", q_.b.name, [(q_[:, 0:4, S], PF[4352:4864, t0:t0 + nb].rearrange("(c p) t -> p c t", p=128)),
                                                   (q_[:, 4:8, S], PF[4864:5376, t0:t0 + nb].rearrange("(c p) t -> p c t", p=128))], writes=[q_.b])
                    kb.dma_group("sp", c_.b.name, [(c_[:, 0, S], ROPE[:, t0:t0 + nb]), (c_[:, 1, S], ROPE[:, TT + t0:TT + t0 + nb])], writes=[c_.b])
                    for j in range(8):
                        w = j // 4
                        p = j % 2
                        kb.op("act", lambda: A.activation(out=sqe[p][:, S], in_=q_[:, j, S], func=AF.Square), [q_.b], [sqe[p].b])
                        kb.pe([lambda: PEe.matmul(psum[:, 4 + p, S], lhsT=blk_b, rhs=sqe[p][:, S], start=True, stop=True)], [sqe[p].b, konb.b], [PB[4 + p]])
                        kb.pe([lambda: PEe.matmul(psum[:, 6 + p, S], lhsT=pg[:, w, :], rhs=q_[:, j, S], start=True, stop=True)], [pg.b, q_.b], [PB[6 + p]])
                        kb.op("act", lambda: A.activation(out=ri[p][:, S], in_=psum[:, 4 + p, S], func=AF.Ln, bias=epsc[:, 0:1], scale=1.0 / 64), [PB[4 + p], epsc.b], [ri[p].b])
                        kb.op("act", lambda: A.activation(out=ri[p][:, S], in_=ri[p][:, S], func=AF.Exp, scale=-0.5), [ri[p].b], [ri[p].b])
                        kb.op("dve", lambda: V.scalar_tensor_tensor(out=t1[p][:, S], in0=q_[:, j, S], scalar=pp[:, l, 76 + w:77 + w], in1=c_[:, 0, S], op0=ALU.mult, op1=ALU.mult),
                              [q_.b, pp.b, c_.b], [t1[p].b])
                        kb.op("dve", lambda: V.tensor_tensor(out=t2[p][:, S], in0=psum[:, 6 + p, S], in1=c_[:, 1, S], op=ALU.mult), [PB[6 + p], c_.b], [t2[p].b])
                        kb.op("pool", lambda: P.tensor_tensor(out=t1[p][:, S], in0=t1[p][:, S], in1=t2[p][:, S], op=ALU.add), [t1[p].b, t2[p].b], [t1[p].b])
                        kb.op("pool", lambda: P.tensor_tensor(out=oq[p][:, S], in0=t1[p][:, S], in1=ri[p][:, S], op=ALU.mult), [t1[p].b, ri[p].b], [oq[p].b])
                        dst = QRs if w == 0 else KRs
                        kb.dma("sp", oq[p].b.name, dst[(j % 4) * 128:(j % 4 + 1) * 128, t0:t0 + nb], oq[p][:, S], reads=[oq[p].b])
                        yield

        def phase_E2(l, last, s2):
            if True:
                QR = T(s2, nc, "QR", [128, 4, TT], BF16)
                KR = T(s2, nc, "KR", [128, 4, TT], BF16)
                VD = T(s2, nc, "VD", [128, NTT, 512], BF16)
                PTs = [[T(s2, nc, "PTs%d_%d" % (i, c), [128, 512], BF16) for c in range(2)] for i in range(3)]
                accD = [T(s2, nc, "accD%d" % i, [128, 512], F32) for i in range(2)]
                accP = [T(s2, nc, "accP%d" % i, [128, 512], F32) for i in range(2)]
                o0 = [T(s2, nc, "o0_%d" % i, [128, 512], F32) for i in range(2)]
                r0 = T(s2, nc, "r0", [128, 512], F32)
                r0a = T(s2, nc, "r0a", [128, 512], F32)
                sq2 = T(s2, nc, "sq2", [128, 512], BF16)
                rst = T(s2, nc, "rst", [128, 512], F32)
                zc = T(s2, nc, "zc", [128, 512], BF16)
                szc = T(s2, nc, "szc", [128, 512], F32)
                oc = T(s2, nc, "oc", [128, 512], BF16)
                kb.dma("sp", "VD", VD[:], PT[:, 1024:1536].rearrange("(t p) n -> p t n", p=128), writes=[VD.b])
                kb.dma("sp", "QR", QR[:], QRs.rearrange("(h p) t -> p h t", p=128), writes=[QR.b])
                kb.dma("sp", "KR", KR[:], KRs.rearrange("(h p) t -> p h t", p=128), writes=[KR.b])
                pending = []

                def flush(n):
                    for _ in range(min(n, len(pending))):
                        pending.pop(0)()
                n_ = 0
                bidx = 0
                for h in range(4):
                    for bi, (t0, nb, col) in enumerate(blocks):
                        if col == 1 and last:
                            continue
                        S = slice(0, nb)
                        kts = list(range(CTX // 128)) if col == 1 else list(range(NTT))
                        aD, aP = accD[bidx % 2], accP[bidx % 2]
                        bidx += 1

                        def smm(i):
                            kt = kts[i]
                            b0 = 2 * ((n_ + i) % 2)
                            kb.pe([lambda c=c: PEe.matmul(psum[:, b0 + c, S], lhsT=KR[c * 64:(c + 1) * 64, h, kt * 128:(kt + 1) * 128], rhs=QR[c * 64:(c + 1) * 64, h, t0:t0 + nb],
                                                          start=True, stop=True) for c in range(2)], [KR.b, QR.b], [PB[b0], PB[b0 + 1]])
                        smm(0)
                        for i, kt in enumerate(kts):
                            if i + 1 < len(kts):
                                smm(i + 1)
                            b0 = 2 * ((n_ + i) % 2)
                            pt = PTs[(n_ + i) % 3]
                            first, lastk = (i == 0), (i == len(kts) - 1)
                            for c in range(2):
                                kb.op("act", lambda: A.activation(out=pt[c][:, S], in_=psum[:, b0 + c, S], func=AF.Exp, scale=0.125), [PB[b0 + c]], [pt[c].b])
                                fns = [lambda: PEe.matmul(psum[:, 4 + c, S], lhsT=VD[:, kt, h * 128:(h + 1) * 128], rhs=pt[c][:, S], start=first, stop=lastk)]
                                wr = [PB[4 + c]]
                                if c == 0:
                                    fns.append(lambda: PEe.matmul(psum[:, 7, S], lhsT=ones_b, rhs=pt[0][:, S], start=first, stop=lastk))
                                    wr.append(PB[7])
                                kb.pe(fns, [VD.b, pt[c].b, konb.b], wr)
                            if i % 2 == 0:
                                if i == 0:
                                    kb.op("dve", lambda: V.tensor_copy(out=aD[:, S], in_=pt[1][:, S]), [pt[1].b], [aD.b])
                                else:
                                    kb.op("dve", lambda: V.tensor_tensor(out=aD[:, S], in0=aD[:, S], in1=pt[1][:, S], op=ALU.add), [pt[1].b, aD.b], [aD.b])
                            else:
                                if i == 1:
                                    kb.op("pool", lambda: P.tensor_copy(out=aP[:, S], in_=pt[1][:, S]), [pt[1].b], [aP.b])
                                else:
                                    kb.op("pool", lambda: P.tensor_tensor(out=aP[:, S], in0=aP[:, S], in1=pt[1][:, S], op=ALU.add), [pt[1].b, aP.b], [aP.b])
                            flush(1)
                            yield
                        n_ += len(kts)
                        flush(len(pending))
                        kb.op("act", lambda: A.copy(out=o0[0][:, S], in_=psum[:, 4, S]), [PB[4]], [o0[0].b])
                        kb.op("dve", lambda: V.tensor_copy(out=o0[1][:, S], in_=psum[:, 5, S]), [PB[5]], [o0[1].b])
                        kb.op("act", lambda: A.copy(out=r0a[:, S], in_=psum[:, 7, S]), [PB[7]], [r0a.b])

                        def mk_epilogue(h=h, t0=t0, nb=nb, S=S, aD=aD, aP=aP):
                            ops = []
                            ops.append(lambda: kb.dma("sp", "zc", zc[:, S], PF[5888 + h * 128:5888 + (h + 1) * 128, t0:t0 + nb], writes=[zc.b]))
                            ops.append(lambda: kb.op("dve", lambda: V.reciprocal(out=r0a[:, S], in_=r0a[:, S]), [r0a.b], [r0a.b]))
                            ops.append(lambda: kb.op("dve", lambda: V.tensor_tensor(out=o0[0][:, S], in0=o0[0][:, S], in1=r0a[:, S], op=ALU.mult), [o0[0].b, r0a.b], [o0[0].b]))
                            ops.append(lambda: kb.pe([lambda: PEe.matmul(psum[:, 6, S], lhsT=ones_f, rhs=aD[:, S], start=True, stop=False),
                                                      lambda: PEe.matmul(psum[:, 6, S], lhsT=ones_f, rhs=aP[:, S], start=False, stop=True)], [aD.b, aP.b, kon.b], [PB[6]]))
                            ops.append(lambda: kb.op("dve", lambda: V.reciprocal(out=r0[:, S], in_=psum[:, 6, S]), [PB[6]], [r0.b]))
                            ops.append(lambda: kb.op("dve", lambda: V.tensor_tensor(out=o0[1][:, S], in0=o0[1][:, S], in1=r0[:, S], op=ALU.mult), [o0[1].b, r0.b], [o0[1].b]))
                            ops.append(lambda: kb.op("dve", lambda: V.scalar_tensor_tensor(out=o0[0][:, S], in0=o0[1][:, S], scalar=lamc[:, l, 0:1], in1=o0[0][:, S], op0=ALU.mult, op1=ALU.add),
                                                     [o0[0].b, o0[1].b, lamc.b], [o0[0].b]))
                            ops.append(lambda: kb.op("act", lambda: A.activation(out=sq2[:, S], in_=o0[0][:, S], func=AF.Square), [o0[0].b], [sq2.b]))
                            ops.append(lambda: kb.pe([lambda: PEe.matmul(psum[:, 6, S], lhsT=ones_b, rhs=sq2[:, S], start=True, stop=True)], [sq2.b, konb.b], [PB[6]]))
                            ops.append(lambda: kb.op("act", lambda: A.activation(out=rst[:, S], in_=psum[:, 6, S], func=AF.Ln, bias=epsc[:, 0:1], scale=1.0 / 128), [PB[6], epsc.b], [rst.b]))
                            ops.append(lambda: kb.op("act", lambda: A.activation(out=rst[:, S], in_=rst[:, S], func=AF.Exp, scale=-0.5), [rst.b], [rst.b]))
                            ops.append(lambda: kb.op("dve", lambda: V.scalar_tensor_tensor(out=o0[0][:, S], in0=o0[0][:, S], scalar=sgl[:, l:l + 1], in1=rst[:, S], op0=ALU.mult, op1=ALU.mult),
                                                     [o0[0].b, sgl.b, rst.b], [o0[0].b]))
                            ops.append(lambda: kb.op("act", lambda: A.activation(out=szc[:, S], in_=zc[:, S], func=AF.Silu), [zc.b], [szc.b]))
                            ops.append(lambda: kb.op("pool", lambda: P.tensor_tensor(out=oc[:, S], in0=o0[0][:, S], in1=szc[:, S], op=ALU.mult), [o0[0].b, szc.b], [oc.b]))
                            ops.append(lambda: kb.dma("sp", "oc", YG[1024 + h * 128:1024 + (h + 1) * 128, t0:t0 + nb], oc[:, S], reads=[oc.b]))
                            return ops
                        pending.extend(mk_epilogue())
                flush(len(pending))

        def phase_F(l, last):
            with ExitStack() as s2:
                wbr = T(s2, nc, "wbr", [128, 12, 1024], BF16)
                wou = T(s2, nc, "wou", [128, 8, 1024], BF16)
                ygb = [T(s2, nc, "ygb%d" % i, [128, 12, 512], BF16) for i in range(2)]
                gm = [T(s2, nc, "gm%d" % i, [128, 24, 512], BF16) for i in range(2)]
                xa = [T(s2, nc, "xaf%d" % i, [128, 8, 512], F32) for i in range(2)]
                xn = T(s2, nc, "xnf", [128, 8, 512], F32)
                mT = T(s2, nc, "mT", [128, 8, 512], BF16)
                sg = [T(s2, nc, "sgf%d" % i, [128, 512], F32) for i in range(3)]
                ma = [T(s2, nc, "ma%d" % i, [128, 512], F32) for i in range(3)]
                stg = [xa[0], xa[1], xn]
                for i in range(5):
                    w = stg[i % 3]
                    wv = w[:].rearrange("p c n -> p (c n)").rearrange("p (a m) -> p a m", a=4)
                    src = w_br[l][i * 512:(i + 1) * 512, :] if i < 3 else w_out[l][(i - 3) * 512:(i - 2) * 512, :]
                    kb.dma("sp", w.b.name, wv, src.rearrange("(c p) n -> p c n", p=128), writes=[w.b])
                    dstt = wbr[:, i * 4:(i + 1) * 4, :] if i < 3 else wou[:, (i - 3) * 4:(i - 2) * 4, :]
                    dstb = wbr.b if i < 3 else wou.b
                    kb.op("pool", lambda: P.tensor_copy(out=dstt, in_=wv), [w.b], [dstb])
                todo = [(bi, blk) for bi, blk in enumerate(blocks) if not (blk[2] == 1 and last)]

                def loads(k):
                    bi, (t0, nb, col) = todo[k]
                    p = k % 2
                    S = slice(0, nb)
                    kb.dma("sp", ygb[p].b.name, ygb[p][:, :, S], YG[:, t0:t0 + nb].rearrange("(c p) t -> p c t", p=128), writes=[ygb[p].b])
                    kb.dma("sp", gm[p].b.name, gm[p][:, :, S], PF[6400:9472, t0:t0 + nb].rearrange("(c p) t -> p c t", p=128), writes=[gm[p].b])
                    kb.dma("sp", xa[p].b.name, xa[p][:, :, S], (xIv if l == 0 else xTv)[:, :, t0:t0 + nb], writes=[xa[p].b])
                loads(0)
                for k, (bi, (t0, nb, col)) in enumerate(todo):
                    if k + 1 < len(todo):
                        loads(k + 1)
                    p = k % 2
                    S = slice(0, nb)
                    for dc in range(8):
                        for i in range(3):
                            kb.pe([lambda w=w: PEe.matmul(psum[:, 1 + i, S], lhsT=wbr[:, i * 4 + w, dc * 128:(dc + 1) * 128], rhs=ygb[p][:, i * 4 + w, S], start=(w == 0), stop=(w == 3))
                                   for w in range(4)], [wbr.b, ygb[p].b], [PB[1 + i]])
                            kb.op("act", lambda: A.activation(out=sg[i][:, S], in_=gm[p][:, i * 8 + dc, S], func=AF.Sigmoid), [gm[p].b], [sg[i].b])
                            kb.op("dve", lambda: V.tensor_tensor(out=ma[i][:, S], in0=psum[:, 1 + i, S], in1=sg[i][:, S], op=ALU.mult), [PB[1 + i], sg[i].b], [ma[i].b])
                        kb.op("pool", lambda: P.tensor_tensor(out=ma[0][:, S], in0=ma[0][:, S], in1=ma[1][:, S], op=ALU.add), [ma[0].b, ma[1].b], [ma[0].b])
                        kb.op("pool", lambda: P.tensor_tensor(out=mT[:, dc, S], in0=ma[0][:, S], in1=ma[2][:, S], op=ALU.add), [ma[0].b, ma[2].b], [mT.b])
                    for d2 in range(8):
                        bank = 4 + d2 % 2
                        kb.pe([lambda dc=dc: PEe.matmul(psum[:, bank, S], lhsT=wou[:, dc, d2 * 128:(d2 + 1) * 128], rhs=mT[:, dc, S], start=(dc == 0), stop=(dc == 7)) for dc in range(8)],
                              [wou.b, mT.b], [PB[bank]])
                        kb.op("dve", lambda: V.scalar_tensor_tensor(out=xn[:, d2, S], in0=psum[:, bank, S], scalar=modT[:, l, 16 + d2, col:col + 1], in1=xa[p][:, d2, S], op0=ALU.mult, op1=ALU.add),
                              [PB[bank], modT.b, xa[p].b], [xn.b])
                    if last:
                        kb.dma("sp", "xnf", yOv[:, :, t0 - CTX:t0 - CTX + nb], xn[:, :, S], reads=[xn.b])
                    else:
                        kb.dma("sp", "xnf", xTv[:, :, t0:t0 + nb], xn[:, :, S], reads=[xn.b])
                kb.barrier()

        def drain(g):
            for _ in g:
                pass

        def layer(l):
            last = (l == DEPTH - 1)
            phase_AB(l)
            with ExitStack() as sg:
                kb.coop.run([(lambda: drain(phase_C1(l, sg)), 8), (lambda: drain(phase_E1(l, sg)), 3), (lambda: drain(phase_D(l, sg)), 1)])
                kb.barrier()
            phase_C2(l)
            with ExitStack() as sg:
                drain(phase_E2(l, last, sg))
                kb.barrier()
            with ExitStack() as sg:
                kb.coop.run([(lambda: drain(phase_C3(l, sg, 0, 2)), 1), (lambda: drain(phase_C3(l, sg, 1, 2)), 1)])
                kb.barrier()
            phase_F(l, last)

        for l in range(DEPTH):
            layer(l)
        with ExitStack() as s2:
            if dbg:
                for k_, shp in dbg.items():
                    src = {"PF": PF, "PX": PX, "PT": PT, "YG": YG, "YS0": YS[0], "YS1": YS[1], "xTs": xT, "AKT0": AKT[0], "GCs0": GCs[0], "GCs1": GCs[1]}[k_]
                    kb.dma("sp", "dbg", dbgo[k_], src, reads=[])
            kb.barrier()
    build.n_inst = kb.n_inst
    build.n_sem = len(kb.sems)
    return nc


def host_consts(SEQ, DEPTH):
    TT = CTX + SEQ
    kon = np.zeros((128, 1024), np.float32)
    kon[:, 0:128] = np.eye(128)
    kon[:, 128:256] = 1.0
    p = np.arange(128)
    kon[:, 256:384] = (p[:, None] // 64 == p[None, :] // 64)
    partner = np.where((p % 32) < 16, p + 16, p - 16)
    perm = np.zeros((128, 128), np.float32)
    perm[partner, p] = 1.0
    kon[:, 384:512] = perm
    sel = np.zeros((128, 4, 8), np.float32)
    for c in range(4):
        for q in range(128):
            sel[q, c, 2 * c + q // 64] = 1.0
    kon[:, 512:544] = sel.reshape(128, 32)
    i = np.arange(64)
    SU = (i[:, None] < i[None, :]); SL = (i[:, None] > i[None, :]); IU = (i[:, None] <= i[None, :]); IL = (i[:, None] >= i[None, :]); I = np.eye(64)
    rep = lambda ms: np.stack([np.broadcast_to(m[:, None, :], (64, 8, 64)) for m in ms], axis=1).astype(np.float32).reshape(64, 4 * 512)
    msk = np.concatenate([rep((SU, IU, SL, I)), rep((SL, IL, SU, I))], axis=0)
    n_freq = 16
    inv_freq = (10000.0 ** (-np.arange(n_freq, dtype=np.float32) / n_freq)).astype(np.float32)
    tok = np.arange(SEQ)
    row = (tok // 64).astype(np.float32)
    colp = (tok % 64).astype(np.float32)
    d = p % 64
    half = d // 32
    j = d % 16
    which = (d % 32) // 16
    pos = np.where(half[:, None] == 0, row[None, :], colp[None, :]).astype(np.float32)
    ang = (pos * inv_freq[j][:, None]).astype(np.float32)
    cosT = np.ones((128, TT), np.float32)
    sinT = np.zeros((128, TT), np.float32)
    cosT[:, CTX:] = np.cos(ang)
    sinT[:, CTX:] = np.sin(ang) * np.where(which == 0, -1.0, 1.0)[:, None]
    rope = np.concatenate([cosT, sinT], axis=1).astype(np.float32)
    ltri = np.ones((128, 512), np.float32)
    ltri[:, 0::64] = 0.0
    return dict(KON=kon, MSK=np.ascontiguousarray(msk), ROPE=np.ascontiguousarray(rope), LTRI=ltri)


def host_inputs(inp, SEQ, DEPTH, b):
    L = DEPTH
    f = lambda a: np.asarray(a, np.float32)
    xT = np.concatenate([f(inp["ctx"])[b], f(inp["x"])[b]], axis=0).T
    PPa = np.zeros((128, L, NPL), np.float32)
    colT = lambda v: v.reshape(-1, 128).T
    for l in range(L):
        PPa[:, l, 0:8] = colT(f(inp["norm_g"])[l])
        PPa[:, l, 8:32] = colT(f(inp["b_mod"])[l])
        PPa[:, l, 32:40] = colT(f(inp["decay_w0"])[l].reshape(-1))
        PPa[:, l, 40:48] = colT(f(inp["iclr_a0"])[l].reshape(-1))
        PPa[:, l, 48:52] = colT(f(inp["k_k"])[l])
        PPa[:, l, 52:56] = colT(f(inp["k_a"])[l])
        PPa[:, l, 56:60] = colT(f(inp["r_k"])[l].reshape(-1))
        PPa[:, l, 60:72] = colT(f(inp["conv_w"])[l].reshape(-1))
        PPa[:, l, 72:76] = colT(f(inp["conv_b"])[l])
        PPa[:, l, 76] = np.tile(f(inp["qk_norm_g"])[l, 0], 2)
        PPa[:, l, 77] = np.tile(f(inp["qk_norm_g"])[l, 1], 2)
        PPa[:, l, 78] = f(inp["subln_g"])[l]
    CC = np.stack([colT(f(inp["c"])[b]), colT(f(inp["c_ctx"]))], axis=2).reshape(128, 16)
    BC = np.stack([np.broadcast_to(f(inp["gn_g"])[:L, None, :], (L, 128, 512)), np.broadcast_to(f(inp["gn_b"])[:L, None, :], (L, 128, 512))], axis=2)
    BC = np.ascontiguousarray(BC.transpose(1, 0, 2, 3)).reshape(128, L * 1024)
    LAM = f(inp["lambda_qk"])[:L].reshape(1, L * 256)
    return dict(xT=np.ascontiguousarray(xT), PP=PPa.reshape(128, L * NPL), CC=np.ascontiguousarray(CC), BC=BC, LAM=np.ascontiguousarray(LAM),
                w_mod=f(inp["w_mod"])[:L], w_in=f(inp["w_in"])[:L], dup=f(inp["decay_up"])[:L].reshape(L, 128, 512), iup=f(inp["iclr_up"])[:L].reshape(L, 128, 512),
                w_br=f(inp["w_branch"])[:L].reshape(L, 1536, D), w_out=f(inp["w_out"])[:L])


def run(inp, SEQ, DEPTH, dbg=None):
    nc = build(SEQ, DEPTH, dbg)
    consts = host_consts(SEQ, DEPTH)
    nb = np.asarray(inp["x"]).shape[0]
    in_maps = []
    for b in range(nb):
        m = host_inputs(inp, SEQ, DEPTH, b)
        m.update(consts)
        in_maps.append(m)
    res = run_bass_kernel_spmd(nc, in_maps, core_ids=list(range(nb)))
    out = np.stack([np.asarray(r["yout"]).T for r in res.results], axis=0)
    return out.astype(np.float32), res


def kernel(**inputs):
    out, _ = run(inputs, 4096, 4)
    return out
```
